# Optimizing a Trainium2 kernel written in Bass

```python
import jax, jax.numpy as jnp
from jax import lax
import numpy as np

D_MODEL = 1024
BATCH = 16
SEQ = 256
DEPTH = 1
DEC_BATCH = 4
DEC_SEQ = 4096
PAST_LEN = 256

GRID_W = 64
D_MIX = D_MODEL
D_CONV = D_MIX // 2
D_MLSTM = D_MIX - D_CONV
N_HEADS = 4
HEAD_DIM = D_MLSTM // N_HEADS
N_GATES = 4 * N_HEADS
D_IN = 3 * D_CONV + 4 * D_MLSTM + N_GATES
D_FF = ((8 * D_MODEL // 3 + 127) // 128) * 128
CHUNK = 128
EPS = 1e-6

kernel_name = 'hymba_conv_mlstm_diffusion_step'


def rmsnorm(x, g):
    xf = x.astype(jnp.float32)
    y = xf * lax.rsqrt(jnp.mean(xf * xf, axis=-1, keepdims=True) + EPS)
    return (y * g.astype(jnp.float32)).astype(x.dtype)


def conv3(x4, w, b, axis):
    n = x4.shape[axis]
    pad = [(0, 0)] * 4
    pad[axis] = (1, 1)
    xp = jnp.pad(x4, pad)
    prev = lax.slice_in_dim(xp, 0, n, axis=axis)
    nxt = lax.slice_in_dim(xp, 2, n + 2, axis=axis)
    return w[0] * prev + w[1] * x4 + w[2] * nxt + b


def mlstm_scan(q, k, v, log_i, log_f, C0, n0, m0):
    B, T, H, Dh = q.shape
    nc = T // CHUNK
    f32 = jnp.float32

    def blocks(a):
        a = a.astype(f32).reshape((B, nc, CHUNK) + a.shape[2:])
        return jnp.moveaxis(a, (1, 3), (0, 2))

    qb = blocks(q) * (Dh ** -0.5)
    kb, vb, ib, fb = blocks(k), blocks(v), blocks(log_i), blocks(log_f)
    mask = jnp.tril(jnp.ones((CHUNK, CHUNK), dtype=bool))

    def step(carry, inp):
        C, n, m = carry
        qc, kc, vc, ic, fc = inp
        b = jnp.cumsum(fc, axis=-1)
        dlog = b[..., :, None] - b[..., None, :] + ic[..., None, :]
        dlog = jnp.where(mask, dlog, -jnp.inf)
        inter = b + m[..., None]
        m_t = jnp.maximum(inter, jnp.max(dlog, axis=-1))
        w_intra = jnp.exp(dlog - m_t[..., None])
        w_inter = jnp.exp(inter - m_t)
        s = jnp.einsum('bhtd,bhsd->bhts', qc, kc) * w_intra
        num = jnp.einsum('bhts,bhsd->bhtd', s, vc) + w_inter[..., None] * jnp.einsum('bhvk,bhtk->bhtv', C, qc)
        den = jnp.sum(s, axis=-1) + w_inter * jnp.einsum('bhk,bhtk->bht', n, qc)
        h = num / jnp.maximum(jnp.abs(den), jnp.exp(-m_t))[..., None]
        bl = b[..., -1]
        g = bl[..., None] - b + ic
        m_new = jnp.maximum(bl + m, jnp.max(g, axis=-1))
        wg = jnp.exp(g - m_new[..., None])
        decay = jnp.exp(bl + m - m_new)
        C_new = decay[..., None, None] * C + jnp.einsum('bhs,bhsv,bhsk->bhvk', wg, vc, kc)
        n_new = decay[..., None] * n + jnp.einsum('bhs,bhsk->bhk', wg, kc)
        return (C_new, n_new, m_new), h

    (C, n, m), hb = lax.scan(step, (C0.astype(f32), n0.astype(f32), m0.astype(f32)), (qb, kb, vb, ib, fb))
    h = jnp.moveaxis(hb, (0, 2), (1, 3)).reshape(B, T, H, Dh)
    return h, C, n, m


def mlstm_bidir(q, k, v, gates, C0, n0, m0):
    flip = lambda a: a[:, ::-1]
    h_f, Cf, nf, mf = mlstm_scan(q, k, v, gates[:, :, 0], jax.nn.log_sigmoid(gates[:, :, 1]),
                                 C0[:, 0], n0[:, 0], m0[:, 0])
    h_b, Cb, nb, mb = mlstm_scan(flip(q), flip(k), flip(v), flip(gates[:, :, 2]),
                                 flip(jax.nn.log_sigmoid(gates[:, :, 3])), C0[:, 1], n0[:, 1], m0[:, 1])
    return h_f + flip(h_b), jnp.stack([Cf, Cb], 1), jnp.stack([nf, nb], 1), jnp.stack([mf, mb], 1)


def trunk_layer(x, mod, row_len, ffn_axis, C0, n0, m0, norm1, w_in, b_gate, conv_sc_w, conv_sc_b,
                mh_norm, w_out, norm2, w_up, conv_ffn_w, conv_ffn_b, w_down):
    B, T, _ = x.shape
    rows = T // row_len
    shift1, scale1, gate1, shift2, scale2, gate2 = jnp.split(mod, 6, axis=-1)
    h = rmsnorm(x, norm1) * (1 + scale1) + shift1
    z = h @ w_in
    idx = [D_CONV, 2 * D_CONV, 3 * D_CONV, 3 * D_CONV + D_MLSTM, 3 * D_CONV + 2 * D_MLSTM,
           3 * D_CONV + 3 * D_MLSTM, 3 * D_CONV + 4 * D_MLSTM]
    sc_b, sc_c, sc_x, q, k, v, o, g = jnp.split(z, idx, axis=-1)
    u = (sc_c * sc_x).reshape(B, rows, row_len, D_CONV)
    y_sc = sc_b * conv3(u, conv_sc_w, conv_sc_b, axis=2).reshape(B, T, D_CONV)
    heads = lambda a: a.reshape(B, T, N_HEADS, HEAD_DIM)
    gates = g.astype(jnp.float32).reshape(B, T, 4, N_HEADS) + b_gate.astype(jnp.float32)
    h_m, C, n, m = mlstm_bidir(heads(q), heads(k), heads(v), gates, C0, n0, m0)
    h_m = h_m * lax.rsqrt(jnp.mean(h_m * h_m, axis=-1, keepdims=True) + EPS)
    h_m = h_m * mh_norm.reshape(N_HEADS, HEAD_DIM).astype(jnp.float32)
    h_m = h_m.reshape(B, T, D_MLSTM).astype(x.dtype) * jax.nn.sigmoid(o)
    x = x + gate1 * (jnp.concatenate([y_sc, h_m], axis=-1) @ w_out)
    h2 = rmsnorm(x, norm2) * (1 + scale2) + shift2
    up = (h2 @ w_up).reshape(B, rows, row_len, 2 * D_FF)
    up = conv3(up, conv_ffn_w, conv_ffn_b, axis=ffn_axis).reshape(B, T, 2 * D_FF)
    a, gg = jnp.split(up, 2, axis=-1)
    x = x + gate2 * ((jax.nn.silu(gg) * a) @ w_down)
    return x, C, n, m


def setup_inputs(seed: int = 0) -> dict:
    key = jax.random.key(seed)
    ks = jax.random.split(key, 24)
    nrm = lambda k, shape, scale: scale * jax.random.normal(k, shape, jnp.float32)
    forget_bias = jnp.linspace(3.0, 6.0, N_HEADS)
    gate_offset = jnp.zeros((4, N_HEADS), jnp.float32).at[1].set(forget_bias).at[3].set(forget_bias)
    return {
        'x_prompt': nrm(ks[0], (BATCH, SEQ, D_MODEL), 1.0),
        'x_sample': nrm(ks[1], (DEC_BATCH, DEC_SEQ, D_MODEL), 1.0),
        'state_C': nrm(ks[2], (DEC_BATCH, DEPTH, 2, N_HEADS, HEAD_DIM, HEAD_DIM), 0.5),
        'state_n': nrm(ks[3], (DEC_BATCH, DEPTH, 2, N_HEADS, HEAD_DIM), 0.5),
        'state_m': nrm(ks[4], (DEC_BATCH, DEPTH, 2, N_HEADS), 1.0),
        'c': nrm(ks[5], (DEC_BATCH, D_MODEL), 1.0),
        'c_ctx': nrm(ks[6], (D_MODEL,), 1.0),
        'w_mod': nrm(ks[7], (DEPTH, D_MODEL, 6 * D_MODEL), D_MODEL ** -0.5),
        'b_mod': nrm(ks[8], (DEPTH, 6 * D_MODEL), 0.02),
        'norm1': 1.0 + nrm(ks[9], (DEPTH, D_MODEL), 0.02),
        'w_in': nrm(ks[10], (DEPTH, D_MODEL, D_IN), D_MODEL ** -0.5),
        'b_gate': gate_offset + nrm(ks[11], (DEPTH, 4, N_HEADS), 0.1),
        'conv_sc_w': nrm(ks[12], (DEPTH, 3, D_CONV), 3 ** -0.5),
        'conv_sc_b': nrm(ks[13], (DEPTH, D_CONV), 0.02),
        'mh_norm': 1.0 + nrm(ks[14], (DEPTH, D_MLSTM), 0.02),
        'w_out': nrm(ks[15], (DEPTH, D_MIX, D_MODEL), D_MIX ** -0.5),
        'norm2': 1.0 + nrm(ks[16], (DEPTH, D_MODEL), 0.02),
        'w_up': nrm(ks[17], (DEPTH, D_MODEL, 2 * D_FF), D_MODEL ** -0.5),
        'conv_ffn_w': nrm(ks[18], (DEPTH, 3, 2 * D_FF), 3 ** -0.5),
        'conv_ffn_b': nrm(ks[19], (DEPTH, 2 * D_FF), 0.02),
        'w_down': nrm(ks[20], (DEPTH, D_FF, D_MODEL), D_FF ** -0.5),
        'final_norm': 1.0 + nrm(ks[21], (D_MODEL,), 0.02),
    }


def reference(x_prompt, x_sample, state_C, state_n, state_m, c, c_ctx, w_mod, b_mod, norm1, w_in, b_gate,
              conv_sc_w, conv_sc_b, mh_norm, w_out, norm2, w_up, conv_ffn_w, conv_ffn_b, w_down, final_norm):
    B, S, _ = x_prompt.shape
    f32 = jnp.float32
    zC = jnp.zeros((B, 2, N_HEADS, HEAD_DIM, HEAD_DIM), f32)
    zn = jnp.zeros((B, 2, N_HEADS, HEAD_DIM), f32)
    zm = jnp.zeros((B, 2, N_HEADS), f32)
    xp, xs = x_prompt, x_sample
    new_C, new_n, new_m = [], [], []
    for l in range(DEPTH):
        mod_ctx = (jax.nn.silu(c_ctx) @ w_mod[l] + b_mod[l])[None, None, :]
        mod_lat = (jax.nn.silu(c) @ w_mod[l] + b_mod[l])[:, None, :]
        lw = (norm1[l], w_in[l], b_gate[l], conv_sc_w[l], conv_sc_b[l], mh_norm[l], w_out[l],
              norm2[l], w_up[l], conv_ffn_w[l], conv_ffn_b[l], w_down[l])
        xp, Cc, ncc, mc = trunk_layer(xp, mod_ctx, S, 2, zC, zn, zm, *lw)
        xs, _, _, _ = trunk_layer(xs, mod_lat, GRID_W, 1, state_C[:, l], state_n[:, l], state_m[:, l], *lw)
        new_C.append(Cc.astype(x_prompt.dtype))
        new_n.append(ncc.astype(x_prompt.dtype))
        new_m.append(mc.astype(x_prompt.dtype))
    y_prompt = rmsnorm(xp, final_norm)
    y_sample = rmsnorm(xs, final_norm)
    return (y_prompt, y_sample, jnp.stack(new_C, 1), jnp.stack(new_n, 1), jnp.stack(new_m, 1))
```

```python
import contextlib
import numpy as np
import concourse.bass as bass
import concourse.mybir as mybir
from concourse.bass_utils import run_bass_kernel_spmd

F32 = mybir.dt.float32
BF16 = mybir.dt.bfloat16
AF = mybir.ActivationFunctionType
ALU = mybir.AluOpType
AX = mybir.AxisListType

D = 1024
NT = 20
NTOK = 2560
DH = 128
NH = 4
DFF = 2816
NJ = 22
EPS = 1e-6
ENGS = ["sp", "act", "pool", "dve", "pe"]


class Scope:
    def __init__(self, nc):
        self.nc = nc
        self.stack = contextlib.ExitStack()

    def __enter__(self):
        self.stack.__enter__()
        return self

    def __exit__(self, *a):
        return self.stack.__exit__(*a)

    _n = [0]

    def sb(self, shape, dt=F32):
        Scope._n[0] += 1
        return self.stack.enter_context(self.nc.sbuf_tensor("sb%d" % Scope._n[0], list(shape), dt))

    def ps(self, shape, dt=F32):
        Scope._n[0] += 1
        return self.stack.enter_context(self.nc.psum_tensor("ps%d" % Scope._n[0], list(shape), dt))


class Op:
    __slots__ = ("eng", "fn", "waits", "sig", "val", "dma", "cc", "sem", "block", "extra", "store")

    def __init__(self, eng, fn, block):
        self.eng = eng
        self.fn = fn
        self.waits = []
        self.sig = False
        self.val = None
        self.dma = False
        self.cc = False
        self.sem = None
        self.block = block
        self.extra = []
        self.store = False


class Buf:
    __slots__ = ("name", "ws", "r", "parts", "gen_deps")

    def __init__(self, name=""):
        self.name = name
        self.ws = []
        self.r = []
        self.parts = False
        self.gen_deps = []


class Rec:
    def __init__(self, nc, nring=8):
        self.nc = nc
        self.engs = {"sp": nc.sync, "act": nc.scalar, "pool": nc.gpsimd, "dve": nc.vector, "pe": nc.tensor}
        self.pending = {e: [] for e in ENGS}
        self.block = 0
        self.csem = {e: nc.alloc_semaphore("c_" + e) for e in ["act", "pool", "dve", "pe"]}
        self.ccount = {e: 0 for e in self.csem}
        self.nring = nring
        self.dsem = {q: [nc.alloc_semaphore("d_%s%d" % (q, i)) for i in range(nring)] for q in ["sp", "pool"]}
        self.duse = {q: [0] * nring for q in self.dsem}
        self.dnext = {q: 0 for q in self.dsem}
        self.ccsem = nc.alloc_semaphore("ccsem")
        self.cccount = 0

    def _deps(self, op, reads, writes, parts=False):
        deps = {}
        is_async = op.dma or op.cc

        def add(d):
            if d is None:
                return
            d_async = d.dma or d.cc
            if not d_async and d.block != self.block:
                return
            if not d_async and not is_async and d.eng == op.eng and op.eng == "pe":
                return
            deps[id(d)] = d

        for b in reads:
            for ww in b.ws:
                add(ww)
        for b in writes:
            same_gen = parts and b.parts and not b.r and b.ws
            if same_gen:
                for gd in b.gen_deps:
                    add(gd)
            else:
                b.gen_deps = list(b.ws) + list(b.r)
                for ww in b.ws:
                    add(ww)
            for rr in b.r:
                add(rr)
        for d in deps.values():
            if not (d.dma or d.cc):
                d.sig = True
        op.waits = list(deps.values())
        for b in reads:
            if b in writes:
                continue
            if not is_async:
                b.r = [x for x in b.r if (x.dma or x.cc or x.eng != op.eng)]
            b.r.append(op)
        for b in writes:
            same_gen = parts and b.parts and not b.r and b.ws
            if same_gen:
                if not is_async:
                    b.ws = [x for x in b.ws if (x.dma or x.cc or x.eng != op.eng)]
                b.ws.append(op)
            else:
                b.ws = [op]
            b.parts = parts
            b.r = []

    def op(self, eng, fn, reads=(), writes=(), parts=False):
        o = Op(eng, fn, self.block)
        self._deps(o, list(reads), list(writes), parts)
        self.pending[eng].append(o)
        return o

    def dma(self, q, out, in_, reads=(), writes=(), **kw):
        o = Op(q, lambda e: e.dma_start(out=out, in_=in_, **kw), self.block)
        o.dma = True
        try:
            o.store = "DRam" in type(out.tensor).__name__
        except Exception:
            o.store = True
        slot = self.dnext[q]
        self.dnext[q] = (slot + 1) % self.nring
        prev = self.duse[q][slot]
        o.sem = self.dsem[q][slot]
        if prev > 0:
            o.extra.append((o.sem, 16 * prev))
        self.duse[q][slot] = prev + 1
        o.val = 16 * (prev + 1)
        self._deps(o, list(reads), list(writes))
        self.pending[q].append(o)
        return o

    def collective(self, fn, reads=(), writes=()):
        o = Op("pool", fn, self.block)
        o.cc = True
        self.cccount += 1
        o.sem = self.ccsem
        o.val = self.cccount
        self._deps(o, list(reads), list(writes))
        self.pending["pool"].append(o)
        return o

    def flush(self, final=False):
        if not final and not any(self.pending[e] for e in ENGS):
            return
        for e in ENGS:
            for o in self.pending[e]:
                if o.dma or o.cc:
                    continue
                if o.sig:
                    self.ccount[e] += 1
                    o.val = self.ccount[e]
        names = {"sp": "sync", "act": "scalar", "pool": "gpsimd", "dve": "vector", "pe": "tensor"}
        with self.nc.Block() as block:
            for e in ENGS:
                ops = self.pending[e]
                if not ops and not (final and e == "sp"):
                    continue

                def body(engine, ops=ops, e=e):
                    for o in ops:
                        w = {}
                        for (s, v) in o.extra:
                            w[s.num] = (s, max(v, w.get(s.num, (s, 0))[1]))
                        for d in o.waits:
                            s = d.sem if (d.dma or d.cc) else self.csem[d.eng]
                            w[s.num] = (s, max(d.val, w.get(s.num, (s, 0))[1]))
                        for (s, v) in w.values():
                            engine.wait_ge(s, v)
                        ins = o.fn(engine)
                        if o.dma:
                            ins.then_inc(o.sem, 16)
                        elif o.cc:
                            ins.then_inc(o.sem, 1)
                        elif o.sig:
                            ins.then_inc(self.csem[e], 1)
                    tail = {}
                    for o in ops:
                        if o.dma and o.store:
                            tail[o.sem.num] = (o.sem, max(o.val, tail.get(o.sem.num, (o.sem, 0))[1]))
                    for (sm, v) in tail.values():
                        engine.wait_ge(sm, v)
                    if final and e == "sp":
                        for q in self.dsem:
                            for i in range(self.nring):
                                if self.duse[q][i] > 0:
                                    engine.wait_ge(self.dsem[q][i], 16 * self.duse[q][i])

                getattr(block, names[e])(body)
        self.pending = {e: [] for e in ENGS}
        self.block += 1


def build_program(dbg=()):
    nc = bass.Bass("TRN2", target_bir_lowering=False)
    R = Rec(nc)
    dbg = set(dbg)
    dbg_outs = {}

    def din(name, shape, dt=F32):
        return nc.dram_tensor(name, list(shape), dt, kind="ExternalInput").ap()

    def dout(name, shape, dt=F32):
        return nc.dram_tensor(name, list(shape), dt, kind="ExternalOutput").ap()

    xin = din("xin", [NTOK, D])
    cT_d = din("cT", [128, 8, 2])
    wmod_d = din("w_mod_r", [12, 128, 8, 512])
    bmod_d = din("b_mod2", [2, 6144])
    n1c_d = din("norm1c", [128, 8])
    n2c_d = din("norm2c", [128, 8])
    wic_d = din("wic", [4, 128, 8, 384])
    wiqk_d = din("wiqk", [8, 128, 8, 128])
    wiv_d = din("wiv", [128, 8, 512])
    wio_d = din("wio", [128, 8, 512])
    wig_d = din("wig", [128, 8, 16])
    bg_d = din("bgate_row", [128, 16])
    csc_d = din("conv_sc", [128, 4, 4])
    mhn_d = din("mhn_row", [128, 512])
    wout_d = din("w_out_r", [128, 8, 1024])
    wup_d = din("w_up_r", [NJ, 128, 8, 256])
    cff_d = din("conv_ffn", [128, 44, 4])
    wdn_d = din("w_down_r", [128, NJ, 1024])
    fn_d = din("fn_row", [128, 1024])
    s0_d = din("s0", [128, 4, 129])
    m0_d = din("m0", [128, 4])
    sel_d = din("sel", [128, 2])
    ident_d = din("ident", [128, 128])
    anti_d = din("antiI", [128, 128])
    mk0_d = din("maskT0", [128, 128])
    mk1_d = din("maskT1", [128, 128])
    ones_d = din("ones", [128, 128])
    selj_d = din("selj", [2, 256])
    i2_d = din("i2", [2, 2])
    i8_d = din("i8", [8, 8])

    y_d = dout("y", [NTOK, D])
    oC_d = dout("oC", [2, 2, 4, 128, 128])
    on_d = dout("on", [2, 2, 4, 128])
    om_d = dout("om", [2, 2, 4])

    x1s_d = nc.dram_tensor("x1s", [NTOK, D], F32).ap()
    gate_d = nc.dram_tensor("gate_scr", [2, 2048], F32).ap()
    sx_in = nc.dram_tensor("sx_in", [128, 520], F32)
    sx_out = nc.dram_tensor("sx_out", [256, 520], F32)
    hx_in = nc.dram_tensor("hx_in", [64, 1024], F32)
    hx_out = nc.dram_tensor("hx_out", [128, 1024], F32)
    b_x1s = [Buf("x1s%d" % i) for i in range(NT)]
    b_gate_d = Buf("gate_d")
    b_sxin, b_sxout, b_hxin, b_hxout = Buf(), Buf(), Buf(), Buf()
    hx_scr = nc.dram_tensor("hx_scr", [64, D], F32).ap()
    b_hs = Buf()
    RG = [[0, 1], [2, 3], [4, 5], [6, 7]]

    def dump(name, ap, shape, buf, dt=F32):
        if name not in dbg:
            return
        t = dout("dbg_" + name, shape, dt)
        dbg_outs[name] = (shape, dt)
        R.dma("sp", t, ap, reads=[buf])

    def T(name, shape, dt=F32):
        return nc.sbuf_tensor(list(shape), dt)

    def PS(name, shape, dt=F32):
        return nc.psum_tensor(list(shape), dt)

    with Scope(nc) as _sc:
        ident_f = _sc.sb([128, 128])
        ones_f = _sc.sb([128, 128])
        tri0_f = _sc.sb([128, 128])
        tri1_f = _sc.sb([128, 128])
        ident_b = _sc.sb([128, 128], BF16)
        anti_b = _sc.sb([128, 128], BF16)
        mk0_b = _sc.sb([128, 128], BF16)
        mk1_b = _sc.sb([128, 128], BF16)
        selj = _sc.sb([2, 256])
        i2 = _sc.sb([2, 2])
        i8 = _sc.sb([8, 8])
        n1c = _sc.sb([128, 8])
        n2c = _sc.sb([128, 8])
        bg_row = _sc.sb([128, 16])
        csc = _sc.sb([128, 4, 4])
        cff = _sc.sb([128, 44, 4])
        sel = _sc.sb([128, 2])
        A1 = _sc.sb([128, 8, 2])
        B1 = _sc.sb([128, 8, 2])
        A2 = _sc.sb([128, 8, 2])
        B2 = _sc.sb([128, 8, 2])
        b_const = Buf("const")
        for (t, d) in [(ident_f, ident_d), (ones_f, ones_d), (tri0_f, mk0_d), (tri1_f, mk1_d),
                       (selj, selj_d), (i2, i2_d), (i8, i8_d), (n1c, n1c_d), (n2c, n2c_d),
                       (bg_row, bg_d), (csc, csc_d), (cff, cff_d), (sel, sel_d)]:
            R.dma("sp", t[:], d, writes=[b_const])
        for (t, d) in [(ident_b, ident_d), (anti_b, anti_d), (mk0_b, mk0_d), (mk1_b, mk1_d)]:
            R.dma("pool", t[:], d, writes=[b_const])
        b_mod = Buf("modcols")

        HT = nc.alloc_sbuf_tensor_at("HT", [128, 8, NTOK], BF16, offset=172032)
        b_ht = [Buf() for _ in range(NT)]
        HT_BYTES = 8 * NTOK * 2
        with Scope(nc) as _sc:
            cT_f = _sc.sb([128, 8, 2])
            cs_b = _sc.sb([128, 8, 2], BF16)
            bmod = _sc.sb([2, 6144])
            modrow = _sc.sb([2, 6144])
            wm0 = _sc.sb([128, 8, 512], BF16)
            wm1 = _sc.sb([128, 8, 512], BF16)
            wm2 = _sc.sb([128, 8, 512], BF16)
            tmpc = _sc.sb([128, 8, 2])
            pm0 = _sc.ps([128, 512])
            pm1 = _sc.ps([128, 512])
            pcol = _sc.ps([128, 512])
            b_ct, b_cs, b_bmod, b_modrow, b_tmpc = Buf(), Buf(), Buf(), Buf("modrow"), Buf()
            R.dma("sp", cT_f[:], cT_d, writes=[b_ct])
            R.dma("sp", bmod[:], bmod_d, writes=[b_bmod])
            R.op("act", lambda e: e.activation(out=cs_b[:], in_=cT_f[:], func=AF.Silu), reads=[b_ct], writes=[b_cs])
            wms = [wm0, wm1, wm2]
            b_wm = [Buf() for _ in range(3)]
            pms = [pm0, pm1]
            b_pm = [Buf(), Buf()]
            b_pcol = Buf()

            def colv(vi):
                return pcol[:, vi * 16:(vi + 1) * 16].rearrange("p (a b) -> p a b", b=2)

            def wchunk(n):
                s = n % 3
                R.dma("pool", wms[s][:], wmod_d[n], writes=[b_wm[s]])
                p = n % 2
                for kc in range(8):
                    R.op("pe", lambda e, kc=kc: e.matmul(pms[p][0:2, :], lhsT=cs_b[:, kc, :], rhs=wms[s][:, kc, :], start=(kc == 0), stop=(kc == 7)),
                         reads=[b_cs, b_wm[s]], writes=[b_pm[p]])
                R.op("dve", lambda e: e.tensor_tensor(out=modrow[:, n * 512:(n + 1) * 512], in0=pms[p][0:2, :], in1=bmod[:, n * 512:(n + 1) * 512], op=ALU.add),
                     reads=[b_pm[p], b_bmod], writes=[b_modrow])

            def cols(Aout, Bout, vsh, vsc, ncol):
                offs = [0, 1024, 3072, 4096]
                for vi in (vsh, vsc):
                    for kc in range(8):
                        c0 = (vi * 8 + kc) * 2
                        off = offs[vi]
                        R.op("pe", lambda e, off=off, kc=kc, c0=c0: e.matmul(pcol[:, c0:c0 + 2], lhsT=modrow[0:2, off + kc * 128: off + (kc + 1) * 128],
                                                                            rhs=i2[:, :], start=True, stop=True),
                             reads=[b_modrow, b_const], writes=[b_pcol])
                R.op("dve", lambda e: e.tensor_scalar(out=tmpc[:], in0=colv(vsc), scalar1=1.0, scalar2=None, op0=ALU.add), reads=[b_pcol], writes=[b_tmpc])
                R.op("dve", lambda e: e.tensor_tensor(out=Aout[:], in0=tmpc[:], in1=ncol[:, :].unsqueeze(2).to_broadcast([128, 8, 2]), op=ALU.mult),
                     reads=[b_tmpc, b_const], writes=[b_mod])
                R.op("dve", lambda e: e.tensor_copy(out=Bout[:], in_=colv(vsh)), reads=[b_pcol], writes=[b_mod])

            bs1, bs2, nb = make_builder(nc, R, _sc, HT, b_ht, list(range(NT)), lambda t: xin[t * 128:(t + 1) * 128, :], [None] * NT,
                                        A1, B1, b_mod, ident_b, b_const, lambda t: 0 if t < 4 else 1, [128] * NT)
            bs1(0)
            for n in range(4):
                wchunk(n)
            cols(A1, B1, 0, 1, n1c)
            nextc = 4
            for i in range(nb):
                if i + 1 < nb:
                    bs1(i + 1)
                bs2(i)
                if i % 2 == 1 and nextc < 12:
                    wchunk(nextc)
                    if nextc == 5:
                        R.dma("sp", gate_d[:, 0:1024], modrow[:, 2048:3072], reads=[b_modrow], writes=[b_gate_d])
                    if nextc == 9:
                        cols(A2, B2, 2, 3, n2c)
                    nextc += 1
            while nextc < 12:
                wchunk(nextc)
                if nextc == 9:
                    cols(A2, B2, 2, 3, n2c)
                nextc += 1
            R.dma("sp", gate_d[:, 1024:2048], modrow[:, 5120:6144], reads=[b_modrow], writes=[b_gate_d])
            dump("modrow", modrow[:], [2, 6144], b_modrow)
            R.flush()

        if True:

            with Scope(nc) as _sc:
                QT = _sc.sb([128, NT, 4, 128], BF16)
                KT = _sc.sb([128, NT, 4, 128], BF16)
                V1 = _sc.sb([128, NT, 4, 130], BF16)
                OG = _sc.sb([128, NT, 512], BF16)
                YSC = _sc.sb([128, NT, 4, 128], BF16)
                GZ = _sc.sb([128, NT, 16])
                RR = _sc.sb([128, NT, 8])
                CUM = _sc.sb([128, NT, 8])
                FTOT = _sc.sb([128, NT, 8])
                RMAX = _sc.sb([128, NT, 8])
                G1ROW = _sc.sb([128, 2, 1024])
                b_qt = [Buf() for _ in range(NT)]
                b_kt = [Buf() for _ in range(NT)]
                b_v1 = [Buf() for _ in range(NT)]
                b_og = [Buf() for _ in range(NT)]
                b_ysc = [Buf() for _ in range(NT)]
                b_gz = [Buf() for _ in range(NT)]
                b_gs = [Buf() for _ in range(NT)]
                b_g1 = Buf("g1row")
                for j in range(2):
                    R.dma("sp", G1ROW[:, j, :], gate_d[j:j + 1, 0:1024].to_broadcast([128, 1024]), reads=[b_gate_d], writes=[b_g1])
                for t in range(NT):
                    R.op("pool", lambda e, t=t: e.memset(V1[:, t, :, 128:130], 1.0), writes=[b_v1[t]], parts=True)

                if True:
                    dump("HT", HT[:], [128, 8, NTOK], b_ht[0], BF16)
                    phase1(nc, R, locals())
                    R.flush()
                for nm, tt, bb, shp, dtt in [("QT", QT, b_qt, [128, NT, 4, 128], BF16), ("KT", KT, b_kt, [128, NT, 4, 128], BF16),
                                             ("V1", V1, b_v1, [128, NT, 4, 130], BF16), ("OG", OG, b_og, [128, NT, 512], BF16),
                                             ("YSC", YSC, b_ysc, [128, NT, 4, 128], BF16), ("GZ", GZ, b_gz, [128, NT, 16], F32),
                                             ("RR", RR, b_gs, [128, NT, 8], F32), ("CUM", CUM, b_gs, [128, NT, 8], F32),
                                             ("FTOT", FTOT, b_gs, [128, NT, 8], F32), ("RMAX", RMAX, b_gs, [128, NT, 8], F32)]:
                    dump(nm, tt[:], shp, bb[0], dtt)
                with Scope(nc) as _sw:
                    WOUT = _sw.sb([128, 8, 1024], BF16)
                    b_wout = Buf()
                    R.dma("pool", WOUT[:], wout_d, writes=[b_wout])
                    phase2(nc, R, locals())
                    R.flush()
                    phase25(nc, R, locals())

            ffn_all(nc, R, locals())
        R.flush(final=True)
    return nc, dbg_outs


def phase25(nc, R, L):
    YSC, OG, G1ROW, WOUT = L["YSC"], L["OG"], L["G1ROW"], L["WOUT"]
    b_ysc, b_og, b_g1, b_wout, b_const, sel = L["b_ysc"], L["b_og"], L["b_g1"], L["b_wout"], L["b_const"], L["sel"]
    xin, x1s_d, b_x1s = L["xin"], L["x1s_d"], L["b_x1s"]
    hx_in, hx_out, b_hxin, b_hxout, hx_scr, b_hs, RG = (L[k] for k in ["hx_in", "hx_out", "b_hxin", "b_hxout", "hx_scr", "b_hs", "RG"])
    with Scope(nc) as _sc:
        XT = _sc.sb([128, 3, D], F32)
        X1 = _sc.sb([128, 4, 512], F32)
        HST = _sc.sb([64, 2, 512], F32)
        pw = [_sc.ps([128, 512], F32) for _ in range(4)]
        b_pw = [Buf() for _ in range(4)]
        b_xt = [Buf() for _ in range(3)]
        b_x1 = [Buf() for _ in range(4)]
        b_hst = Buf()
        order = [NT - 1] + list(range(NT - 1))
        cnt = 0
        for k, c in enumerate(order):
            typ = 0 if c < 4 else 1
            xs = k % 3
            R.dma("sp", XT[:, xs, :], xin[c * 128:(c + 1) * 128, :], writes=[b_xt[xs]])
            for half in range(2):
                pb = cnt % 4
                cnt += 1
                for kb in range(8):
                    if kb < 4:
                        R.op("pe", lambda e, kb=kb, pb=pb, c=c, half=half: e.matmul(pw[pb][:, :], lhsT=YSC[:, c, kb, :], rhs=WOUT[:, kb, half * 512:(half + 1) * 512],
                                                                                 start=(kb == 0), stop=False),
                             reads=[b_ysc[c], b_wout], writes=[b_pw[pb]])
                    else:
                        R.op("pe", lambda e, kb=kb, pb=pb, c=c, half=half: e.matmul(pw[pb][:, :], lhsT=OG[:, c, (kb - 4) * 128:(kb - 3) * 128], rhs=WOUT[:, kb, half * 512:(half + 1) * 512],
                                                                                 start=False, stop=(kb == 7)),
                             reads=[b_og[c], b_wout], writes=[b_pw[pb]])
                R.op("dve", lambda e, pb=pb, half=half, typ=typ: e.tensor_tensor(out=X1[:, pb, :], in0=pw[pb][:, :], in1=G1ROW[:, typ, half * 512:(half + 1) * 512], op=ALU.mult),
                     reads=[b_pw[pb], b_g1], writes=[b_x1[pb]])
                R.op("pool", lambda e, pb=pb, half=half, xs=xs: e.tensor_tensor(out=XT[:, xs, half * 512:(half + 1) * 512], in0=XT[:, xs, half * 512:(half + 1) * 512],
                                                                              in1=X1[:, pb, :], op=ALU.add),
                     reads=[b_x1[pb], b_xt[xs]], writes=[b_xt[xs]])
            R.dma("sp", x1s_d[c * 128:(c + 1) * 128, :], XT[:, xs, :], reads=[b_xt[xs]], writes=[b_x1s[c]])
            if c == NT - 1:
                R.dma("sp", hx_in[:, :], XT[64:128, xs, :], reads=[b_xt[xs]], writes=[b_hxin])
                R.collective(lambda e: e.collective_compute("AllGather", ALU.bypass, replica_groups=RG, ins=[hx_in.ap().opt()], outs=[hx_out.ap().opt()]),
                             reads=[b_hxin], writes=[b_hxout])
        for hf in range(2):
            cs_ = slice(hf * 512, (hf + 1) * 512)
            R.dma("pool", HST[:, 0, :], hx_out[0:64, cs_], reads=[b_hxout], writes=[b_hst])
            R.dma("pool", HST[:, 1, :], hx_out[64:128, cs_], reads=[b_hxout], writes=[b_hst])
            R.op("dve", lambda e: e.tensor_scalar(out=HST[:, 0, :], in0=HST[:, 0, :], scalar1=sel[0:64, 1:2], scalar2=None, op0=ALU.mult), reads=[b_hst, b_const], writes=[b_hst])
            R.op("dve", lambda e: e.tensor_scalar(out=HST[:, 1, :], in0=HST[:, 1, :], scalar1=sel[0:64, 0:1], scalar2=None, op0=ALU.mult), reads=[b_hst, b_const], writes=[b_hst])
            R.op("dve", lambda e: e.tensor_tensor(out=HST[:, 0, :], in0=HST[:, 0, :], in1=HST[:, 1, :], op=ALU.add), reads=[b_hst], writes=[b_hst])
            R.dma("pool", hx_scr[:, cs_], HST[:, 0, :], reads=[b_hst], writes=[b_hs])
        R.flush()


def make_builder(nc, R, _sc, HT, b_ht, tiles, src_of, src_bufs, A, B, b_mod, ident_b, b_const, typ_of, ntoks, col_of=None, anti=None):
    xts = [_sc.sb([128, D], F32) for _ in range(3)]
    xns = [_sc.sb([128, D], BF16) for _ in range(2)]
    sqj = _sc.sb([128, D], BF16)
    ss = _sc.sb([128, 4], F32)
    rs = _sc.sb([128, 4], F32)
    pts = [[_sc.ps([128, 8, 128], BF16) for _ in range(2)] for _ in range(2)]
    ptf = [_sc.ps([128, 4, 128], F32) for _ in range(2)] if anti is not None else None
    b_xt = [Buf() for _ in range(3)]
    b_xn = [Buf() for _ in range(2)]
    b_pt = [[Buf(), Buf()] for _ in range(2)]
    b_ss = [Buf() for _ in range(4)]
    b_rs = [Buf() for _ in range(4)]
    b_sq = Buf()

    def s1(i):
        t = tiles[i]
        n = ntoks[i]
        xs, ns, q = i % 3, i % 2, i % 4
        R.dma("sp", xts[xs][0:n, :], src_of(t), reads=[src_bufs[i]] if src_bufs[i] is not None else [], writes=[b_xt[xs]])
        R.op("act", lambda e: e.activation(out=sqj[0:n, :], in_=xts[xs][0:n, :], func=AF.Square, accum_out=ss[0:n, q:q + 1]),
             reads=[b_xt[xs]], writes=[b_sq, b_ss[q]])
        R.op("act", lambda e: e.activation(out=rs[0:n, q:q + 1], in_=ss[0:n, q:q + 1], func=AF.Sqrt, bias=EPS, scale=1.0 / D),
             reads=[b_ss[q]], writes=[b_rs[q]])
        R.op("dve", lambda e: e.reciprocal(out=rs[0:n, q:q + 1], in_=rs[0:n, q:q + 1]), reads=[b_rs[q]], writes=[b_rs[q]])
        R.op("dve", lambda e: e.tensor_scalar(out=xns[ns][0:n, :], in0=xts[xs][0:n, :], scalar1=rs[0:n, q:q + 1], scalar2=None, op0=ALU.mult),
             reads=[b_xt[xs], b_rs[q]], writes=[b_xn[ns]])

    def s2(i):
        t = tiles[i]
        n = ntoks[i]
        ns, ps_ = i % 2, i % 2
        c0 = col_of(t) if col_of else t * 128
        for kc in range(8):
            eo, k2 = kc % 2, kc // 2
            if anti is None:
                R.op("pe", lambda e, kc=kc, eo=eo, k2=k2: e.transpose(pts[ps_][eo][:, k2, 0:n], xns[ns][0:n, kc * 128:(kc + 1) * 128], ident_b[0:n, 0:n]),
                     reads=[b_xn[ns], b_const], writes=[b_pt[ps_][eo]])
            else:
                R.op("pe", lambda e, kc=kc, eo=eo, k2=k2: e.matmul(ptf[eo][:, k2, 0:n], lhsT=xns[ns][0:n, kc * 128:(kc + 1) * 128], rhs=anti[0:n, 0:n], start=True, stop=True),
                     reads=[b_xn[ns], b_const], writes=[b_pt[ps_][eo]])
        j = typ_of(t)
        for kc in range(8):
            eo, k2 = kc % 2, kc // 2
            src_ps = pts[ps_][eo] if anti is None else ptf[eo]
            if eo == 0:
                R.op("act", lambda e, kc=kc, k2=k2, src_ps=src_ps: e.activation(out=HT[:, kc, c0:c0 + n], in_=src_ps[:, k2, 0:n], func=AF.Identity,
                                                                               scale=A[:, kc, j:j + 1], bias=B[:, kc, j:j + 1]),
                     reads=[b_pt[ps_][0], b_mod], writes=[b_ht[i]], parts=True)
            else:
                R.op("dve", lambda e, kc=kc, k2=k2, src_ps=src_ps: e.tensor_scalar(out=HT[:, kc, c0:c0 + n], in0=src_ps[:, k2, 0:n],
                                                                                  scalar1=A[:, kc, j:j + 1], scalar2=B[:, kc, j:j + 1], op0=ALU.mult, op1=ALU.add),
                     reads=[b_pt[ps_][1], b_mod], writes=[b_ht[i]], parts=True)

    return s1, s2, len(tiles)


def build_hT(nc, R, HT, b_ht, tiles, src_of, src_bufs, A, B, b_mod, ident_b, b_const, typ_of, ntoks, col_of=None, anti=None):
    with Scope(nc) as _sc:
        s1, s2, nt = make_builder(nc, R, _sc, HT, b_ht, tiles, src_of, src_bufs, A, B, b_mod, ident_b, b_const, typ_of, ntoks, col_of, anti)
        s1(0)
        for i in range(nt):
            if i + 1 < nt:
                s1(i + 1)
            s2(i)
        R.flush()


def phase1(nc, R, L):
    HT, b_ht = L["HT"], L["b_ht"]
    QT, KT, V1, OG, YSC, GZ, RR, CUM, FTOT, RMAX = (L[k] for k in ["QT", "KT", "V1", "OG", "YSC", "GZ", "RR", "CUM", "FTOT", "RMAX"])
    b_qt, b_kt, b_v1, b_og, b_ysc, b_gz, b_gs = (L[k] for k in ["b_qt", "b_kt", "b_v1", "b_og", "b_ysc", "b_gz", "b_gs"])
    csc, bg_row, b_const = L["csc"], L["bg_row"], L["b_const"]
    tri0_f, tri1_f, ones_f, ident_f, i8 = L["tri0_f"], L["tri1_f"], L["ones_f"], L["ident_f"], L["i8"]
    wic_d, wiqk_d, wiv_d, wio_d, wig_d = L["wic_d"], L["wiqk_d"], L["wiv_d"], L["wio_d"], L["wig_d"]
    with Scope(nc) as _sc:
        ws0 = _sc.sb([128, 8, 512], BF16)
        ws1 = _sc.sb([128, 8, 512], BF16)
        wgt = _sc.sb([128, 8, 16], BF16)
        scc0 = _sc.sb([128, 512], F32)
        scc1 = _sc.sb([128, 512], F32)
        u0 = _sc.sb([128, 512], F32)
        u1 = _sc.sb([128, 512], F32)
        ac0 = _sc.sb([128, 512], F32)
        ac1 = _sc.sb([128, 512], F32)
        LTALL = _sc.sb([128, NT, 8], F32)
        MX80 = _sc.sb([80, 2], F32)
        DG80 = _sc.sb([80, 2, 80], F32)
        p0 = _sc.ps([128, 512], F32)
        p1 = _sc.ps([128, 512], F32)
        p2 = _sc.ps([128, 512], F32)
        p3 = _sc.ps([128, 512], F32)
        p4 = _sc.ps([128, 512], F32)
        p5 = _sc.ps([128, 512], F32)
        p6 = _sc.ps([128, 512], F32)
        p7 = _sc.ps([128, 512], F32)
        assert nc.sbuf_bytes_remaining >= 8 * NTOK * 2 + 64, nc.sbuf_bytes_remaining
        ws = [ws0, ws1]
        b_ws = [Buf(), Buf()]
        pp = [p0, p1, p2, p3, p4, p5, p6, p7]
        b_pp = [Buf() for _ in range(8)]
        sccs, us, acs = [scc0, scc1], [u0, u1], [ac0, ac1]
        b_scc, b_u, b_ac = [Buf(), Buf()], [Buf(), Buf()], [Buf(), Buf()]
        wslot = [0]

        def next_w():
            s = wslot[0] % 2
            wslot[0] += 1
            return s

        def grp_bufs(bl, g):
            return [bl[4 * g + i] for i in range(4)]

        step = 0
        for cb in range(4):
            s = next_w()
            R.dma("pool", ws[s][:, :, 0:384], wic_d[cb], writes=[b_ws[s]])
            for g in range(5):
                pb = (step % 2) * 3
                k2 = step % 2
                step += 1
                hbufs = grp_bufs(b_ht, g)
                for part in range(3):
                    for kc in range(8):
                        R.op("pe", lambda e, s=s, part=part, kc=kc, g=g, pb=pb: e.matmul(pp[pb + part][:, :], lhsT=ws[s][:, kc, part * 128:(part + 1) * 128],
                                                                                     rhs=HT[:, kc, g * 512:(g + 1) * 512], start=(kc == 0), stop=(kc == 7)),
                             reads=[b_ws[s]] + hbufs, writes=[b_pp[pb + part]])
                W = 256 if g == 0 else 64

                def v3(ap, W=W):
                    return ap.rearrange("p (r w) -> p r w", w=W)

                R.op("act", lambda e, k2=k2, pb=pb: e.activation(out=sccs[k2][:, :], in_=pp[pb + 1][:, :], func=AF.Copy),
                     reads=[b_pp[pb + 1]], writes=[b_scc[k2]])
                R.op("dve", lambda e, k2=k2, pb=pb: e.tensor_tensor(out=us[k2][:, :], in0=pp[pb + 2][:, :], in1=sccs[k2][:, :], op=ALU.mult),
                     reads=[b_pp[pb + 2], b_scc[k2]], writes=[b_u[k2]])
                R.op("act", lambda e, k2=k2, cb=cb: e.activation(out=acs[k2][:, :], in_=us[k2][:, :], func=AF.Identity, scale=csc[:, cb, 1:2], bias=csc[:, cb, 3:4]),
                     reads=[b_u[k2], b_const], writes=[b_ac[k2]])
                R.op("dve", lambda e, k2=k2, cb=cb, v3=v3, W=W: e.scalar_tensor_tensor(out=v3(acs[k2][:, :])[:, :, 1:W], in0=v3(us[k2][:, :])[:, :, 0:W - 1], scalar=csc[:, cb, 0:1],
                                                                                      in1=v3(acs[k2][:, :])[:, :, 1:W], op0=ALU.mult, op1=ALU.add),
                     reads=[b_u[k2], b_ac[k2], b_const], writes=[b_ac[k2]])
                R.op("dve", lambda e, k2=k2, cb=cb, v3=v3, W=W: e.scalar_tensor_tensor(out=v3(acs[k2][:, :])[:, :, 0:W - 1], in0=v3(us[k2][:, :])[:, :, 1:W], scalar=csc[:, cb, 2:3],
                                                                                      in1=v3(acs[k2][:, :])[:, :, 0:W - 1], op0=ALU.mult, op1=ALU.add),
                     reads=[b_u[k2], b_ac[k2], b_const], writes=[b_ac[k2]])
                R.op("dve", lambda e, k2=k2, pb=pb, g=g, cb=cb: e.tensor_tensor(out=YSC[:, 4 * g:4 * g + 4, cb, :], in0=acs[k2][:, :].rearrange("p (a b) -> p a b", b=128),
                                                                              in1=pp[pb][:, :].rearrange("p (a b) -> p a b", b=128), op=ALU.mult),
                     reads=[b_ac[k2], b_pp[pb]], writes=grp_bufs(b_ysc, g), parts=True)
        for blk in range(8):
            s = next_w()
            R.dma("pool", ws[s][:, :, 0:128], wiqk_d[blk], writes=[b_ws[s]])
            hh = blk % 4
            for g in range(5):
                pb = 6 + (step % 2)
                step += 1
                hbufs = grp_bufs(b_ht, g)
                for kc in range(8):
                    R.op("pe", lambda e, s=s, kc=kc, g=g, pb=pb: e.matmul(pp[pb][:, :], lhsT=ws[s][:, kc, 0:128], rhs=HT[:, kc, g * 512:(g + 1) * 512],
                                                                     start=(kc == 0), stop=(kc == 7)),
                         reads=[b_ws[s]] + hbufs, writes=[b_pp[pb]])
                if blk < 4:
                    R.op("act", lambda e, pb=pb, g=g, hh=hh: e.activation(out=QT[:, 4 * g:4 * g + 4, hh, :], in_=pp[pb][:, :].rearrange("p (a b) -> p a b", b=128),
                                                                        func=AF.Copy, scale=float(DH ** -0.5)),
                         reads=[b_pp[pb]], writes=grp_bufs(b_qt, g), parts=True)
                else:
                    R.op("dve", lambda e, pb=pb, g=g, hh=hh: e.tensor_copy(out=KT[:, 4 * g:4 * g + 4, hh, :], in_=pp[pb][:, :].rearrange("p (a b) -> p a b", b=128)),
                         reads=[b_pp[pb]], writes=grp_bufs(b_kt, g), parts=True)
        for which in range(2):
            s = next_w()
            R.dma("pool", ws[s][:, :, :], wiv_d if which == 0 else wio_d, writes=[b_ws[s]])
            for t in range(NT):
                pb = step % 6
                step += 1
                for kc in range(8):
                    R.op("pe", lambda e, s=s, kc=kc, t=t, pb=pb: e.matmul(pp[pb][:, :], lhsT=HT[:, kc, t * 128:(t + 1) * 128], rhs=ws[s][:, kc, :],
                                                                     start=(kc == 0), stop=(kc == 7)),
                         reads=[b_ws[s], b_ht[t]], writes=[b_pp[pb]])
                if which == 0:
                    eng = "dve" if t % 2 == 0 else "act"
                    if eng == "dve":
                        R.op("dve", lambda e, pb=pb, t=t: e.tensor_copy(out=V1[:, t, :, 0:128], in_=pp[pb][:, :].rearrange("p (a b) -> p a b", b=128)),
                             reads=[b_pp[pb]], writes=[b_v1[t]], parts=True)
                    else:
                        R.op("act", lambda e, pb=pb, t=t: e.activation(out=V1[:, t, :, 0:128], in_=pp[pb][:, :].rearrange("p (a b) -> p a b", b=128), func=AF.Copy),
                             reads=[b_pp[pb]], writes=[b_v1[t]], parts=True)
                else:
                    R.op("act", lambda e, pb=pb, t=t: e.activation(out=OG[:, t, :], in_=pp[pb][:, :], func=AF.Sigmoid),
                         reads=[b_pp[pb]], writes=[b_og[t]])
        b_wg = Buf()
        R.dma("pool", wgt[:], wig_d, writes=[b_wg])
        b_lt = [Buf() for _ in range(4)]
        b_mx8 = [Buf() for _ in range(4)]
        b_dg8 = [Buf() for _ in range(4)]
        for t in range(NT):
            pb = step % 6
            step += 1
            q4 = t % 4
            for kc in range(8):
                R.op("pe", lambda e, kc=kc, t=t, pb=pb: e.matmul(pp[pb][:, 0:16], lhsT=HT[:, kc, t * 128:(t + 1) * 128], rhs=wgt[:, kc, :],
                                                                start=(kc == 0), stop=(kc == 7)),
                     reads=[b_wg, b_ht[t]], writes=[b_pp[pb]])
            R.op("dve", lambda e, pb=pb, t=t: e.tensor_tensor(out=GZ[:, t, :], in0=pp[pb][:, 0:16], in1=bg_row[:, :], op=ALU.add),
                 reads=[b_pp[pb], b_const], writes=[b_gz[t]])
        b_ltall, b_pga, b_mx80, b_dg80 = Buf(), Buf(), Buf(), Buf()
        gz5 = GZ[:, :, :].rearrange("p t (d y h) -> p t d y h", d=2, y=2)
        lt4 = LTALL[:, :, :].rearrange("p t (d h) -> p t d h", d=2)
        R.op("act", lambda e: e.activation(out=lt4, in_=gz5[:, :, :, 1, :], func=AF.Exp, scale=-1.0), reads=b_gz, writes=[b_ltall])
        R.op("act", lambda e: e.activation(out=LTALL[:, :, :], in_=LTALL[:, :, :], func=AF.Ln, bias=1.0, scale=1.0), reads=[b_ltall], writes=[b_ltall])
        pA, pB, pC = pp[6], pp[7], pp[0]
        R.op("pe", lambda e: e.matmul(pA[:, 0:80].rearrange("p (t h) -> p t h", h=4), lhsT=tri0_f[:, :], rhs=LTALL[:, :, 0:4], start=True, stop=True),
             reads=[b_ltall, b_const], writes=[b_pp[6]])
        R.op("pe", lambda e: e.matmul(pA[:, 80:160].rearrange("p (t h) -> p t h", h=4), lhsT=tri1_f[:, :], rhs=LTALL[:, :, 4:8], start=True, stop=True),
             reads=[b_ltall, b_const], writes=[b_pp[6]])
        R.op("pe", lambda e: e.matmul(pA[:, 160:320].rearrange("p (t h) -> p t h", h=8), lhsT=ones_f[:, :], rhs=LTALL[:, :, :], start=True, stop=True),
             reads=[b_ltall, b_const], writes=[b_pp[6]])
        R.op("dve", lambda e: e.tensor_copy(out=CUM[:, :, 0:4], in_=pA[:, 0:80].rearrange("p (t h) -> p t h", h=4)), reads=[b_pp[6]], writes=b_gs, parts=True)
        R.op("dve", lambda e: e.tensor_copy(out=CUM[:, :, 4:8], in_=pA[:, 80:160].rearrange("p (t h) -> p t h", h=4)), reads=[b_pp[6]], writes=b_gs, parts=True)
        R.op("dve", lambda e: e.tensor_copy(out=FTOT[:, :, :], in_=pA[:, 160:320].rearrange("p (t h) -> p t h", h=8)), reads=[b_pp[6]], writes=b_gs, parts=True)
        R.op("dve", lambda e: e.tensor_tensor(out=RR[:, :, :].rearrange("p t (d h) -> p t d h", d=2), in0=gz5[:, :, :, 0, :],
                                              in1=CUM[:, :, :].rearrange("p t (d h) -> p t d h", d=2), op=ALU.add),
             reads=b_gz + b_gs, writes=b_gs)
        for k in range(2):
            R.op("pe", lambda e, k=k: e.matmul(pB[0:80, k * 128:(k + 1) * 128], lhsT=RR[:, :, :].rearrange("p t h -> p (t h)")[:, k * 80:(k + 1) * 80], rhs=ident_f[:, :], start=True, stop=True),
                 reads=b_gs + [b_const], writes=[b_pp[7]])
        R.op("dve", lambda e: e.tensor_reduce(out=MX80[:, :], in_=pB[0:80, 0:256].rearrange("p (k n) -> p k n", k=2), axis=AX.X, op=ALU.max),
             reads=[b_pp[7]], writes=[b_mx80])
        for k in range(2):
            R.op("dve", lambda e, k=k: e.tensor_scalar(out=DG80[:, k, :], in0=ident_f[0:80, 0:80], scalar1=MX80[:, k:k + 1], scalar2=None, op0=ALU.mult),
                 reads=[b_mx80, b_const], writes=[b_dg80])
        for k in range(2):
            R.op("pe", lambda e, k=k: e.matmul(pC[:, k * 80:(k + 1) * 80], lhsT=ones_f[0:80, :], rhs=DG80[:, k, :], start=True, stop=True),
                 reads=[b_dg80, b_const], writes=[b_pp[0]])
        R.op("dve", lambda e: e.tensor_copy(out=RMAX[:, :, :], in_=pC[:, 0:160].rearrange("p (t h) -> p t h", h=8)), reads=[b_pp[0]], writes=b_gs)
        R.flush()


def phase2(nc, R, L):
    QT, KT, V1, OG, YSC, GZ, RR, CUM, FTOT, RMAX, G1ROW = (L[k] for k in ["QT", "KT", "V1", "OG", "YSC", "GZ", "RR", "CUM", "FTOT", "RMAX", "G1ROW"])
    b_qt, b_kt, b_v1, b_og, b_ysc, b_gs, b_g1 = (L[k] for k in ["b_qt", "b_kt", "b_v1", "b_og", "b_ysc", "b_gs", "b_g1"])
    b_const, ident_b, ident_f, mk0_b, mk1_b, sel = (L[k] for k in ["b_const", "ident_b", "ident_f", "mk0_b", "mk1_b", "sel"])
    xin, x1s_d, b_x1s = L["xin"], L["x1s_d"], L["b_x1s"]
    wout_d, mhn_d, s0_d, m0_d = L["wout_d"], L["mhn_d"], L["s0_d"], L["m0_d"]
    oC_d, on_d, om_d = L["oC_d"], L["on_d"], L["om_d"]
    sx_in, sx_out, hx_in, b_sxin, b_sxout, b_hxin, RG = (L[k] for k in ["sx_in", "sx_out", "hx_in", "b_sxin", "b_sxout", "b_hxin", "RG"])
    dump = L["dump"]
    masks = [mk0_b, mk1_b]
    with Scope(nc) as _sc:
        SNAP = _sc.sb([128, NT, 4, 130], BF16)
        MHN = _sc.sb([128, 512], F32)
        MX = _sc.sb([128, NT, 8], F32)
        MPRE = _sc.sb([128, NT + 1, 8], F32)
        WG = _sc.sb([128, NT, 8], F32)
        DEC = _sc.sb([128, NT, 8], F32)
        CLP = _sc.sb([128, NT, 8], F32)
        TMPG = _sc.sb([128, NT, 8], F32)
        MFIN = _sc.sb([128, 3, 8], F32)
        SST = _sc.sb([128, 2, 4, 129], F32)
        SDB = _sc.sb([128, 4, 130], BF16)
        VW = _sc.sb([128, 2, 4, 130], BF16)
        KTK = _sc.sb([128, 4, 128], BF16)
        KTK2 = _sc.sb([128, 2, 4, 128], BF16)
        KTK2 = _sc.sb([128, 2, 4, 128], BF16)
        PT = _sc.sb([128, 2, 4, 128], BF16)
        DN = _sc.sb([128, 12], F32)
        HACC = _sc.sb([128, 2, 512], F32)
        HG0 = _sc.sb([128, 512], F32)
        HG = _sc.sb([128, 512], BF16)
        HMT = _sc.sb([128, 4, 128], BF16)
        SQ4 = _sc.sb([128, 4, 128], BF16)
        SS4 = _sc.sb([128, 2, 4], F32)
        CTR = _sc.sb([128, 2, 128], F32)
        SXG = _sc.sb([128, 2, 520], F32)
        SXT = _sc.sb([128, 520], F32)
        pST = _sc.ps([128, 512], F32)
        pND = _sc.ps([128, 3, 512], F32)
        pUP = _sc.ps([128, 2, 512], F32)
        pTR = _sc.ps([128, 1024], BF16)
        pWO = _sc.ps([128, 512], F32)
        b_snap = [Buf() for _ in range(NT)]
        b_mhn = Buf()
        R.dma("sp", MHN[:], mhn_d, writes=[b_mhn])
        b_chain = Buf("chain")
        b_tmpg = Buf()
        b_mfin = Buf()
        b_sst = [[Buf() for _ in range(4)] for _ in range(2)]
        b_pst, b_pnd, b_pwo = Buf(), [Buf() for _ in range(3)], Buf()
        _ptr = Buf()
        b_ptr = [_ptr, _ptr]
        _pup = Buf()
        b_pup = [_pup, _pup]
        b_sdb, b_vw, b_ktk, b_ptt = Buf(), [Buf(), Buf()], Buf(), [Buf(), Buf()]
        b_ktk2 = [Buf(), Buf()]
        b_ktk2 = [Buf(), Buf()]
        b_dn = Buf()
        b_hacc, b_ss4, b_xt = [[Buf() for _ in range(4)] for _ in range(2)], [Buf(), Buf()], [Buf(), Buf()]
        b_hg0, b_hg, b_hmt, b_sq, b_x1 = Buf(), Buf(), Buf(), Buf(), Buf()
        b_x1h = [Buf(), Buf()]
        b_sq4 = [Buf() for _ in range(4)]
        b_ctr = [Buf(), Buf()]
        b_sxg, b_sxt = Buf(), Buf()
        b_sxs = b_sxg
        cnt = {"ctr": 0}

        def nxt(k, n):
            v = cnt[k] % n
            cnt[k] += 1
            return v

        seqs = {"P0": [0, 1], "P1": [2, 3], "S": list(range(4, NT))}

        def d4(d):
            return slice(d * 4, d * 4 + 4)

        def nd_slot(i):
            return pND[:, i // 3, (i % 3) * 129:(i % 3) * 129 + 129]

        def up_slot(h):
            return pUP[:, h // 3, (h % 3) * 129:(h % 3) * 129 + 129]

        def chain(seq, d, init):
            tiles = seqs[seq] if d == 0 else seqs[seq][::-1]
            sidx = {"P0": 0, "P1": 1, "S": 2}[seq]
            first = tiles[0]
            if init is None:
                R.op("dve", lambda e: e.memset(MPRE[:, first, d4(d)], 0.0), writes=[b_chain])
            else:
                ap, bufs = init
                R.op("dve", lambda e: e.tensor_copy(out=MPRE[:, first, d4(d)], in_=ap), reads=bufs, writes=[b_chain])
            for i, c in enumerate(tiles):
                R.op("dve", lambda e, c=c: e.tensor_tensor(out=MX[:, c, d4(d)], in0=MPRE[:, c, d4(d)], in1=RMAX[:, c, d4(d)], op=ALU.max),
                     reads=[b_chain, b_gs[c]], writes=[b_chain])
                if i + 1 < len(tiles):
                    nx = tiles[i + 1]
                    R.op("dve", lambda e, c=c, nx=nx: e.tensor_tensor(out=MPRE[:, nx, d4(d)], in0=MX[:, c, d4(d)], in1=FTOT[:, c, d4(d)], op=ALU.subtract),
                         reads=[b_chain, b_gs[c]], writes=[b_chain])
                else:
                    R.op("dve", lambda e, c=c: e.tensor_tensor(out=MFIN[:, sidx, d4(d)], in0=MX[:, c, d4(d)], in1=FTOT[:, c, d4(d)], op=ALU.subtract),
                         reads=[b_chain, b_gs[c]], writes=[b_mfin])

        def weights(t0, t1, d):
            gs = [b_gs[t] for t in range(t0, t1)]
            for (dst, a, b_) in [(WG, RR, MX), (DEC, MPRE, MX), (CLP, CUM, MX)]:
                R.op("dve", lambda e, a=a, b_=b_: e.tensor_tensor(out=TMPG[:, t0:t1, d4(d)], in0=a[:, t0:t1, d4(d)], in1=b_[:, t0:t1, d4(d)], op=ALU.subtract),
                     reads=[b_chain] + gs, writes=[b_tmpg])
                R.op("act", lambda e, dst=dst: e.activation(out=dst[:, t0:t1, d4(d)], in_=TMPG[:, t0:t1, d4(d)], func=AF.Exp),
                     reads=[b_tmpg], writes=[b_chain])

        def st_vw(c, d):
            for h in range(4):
                col = d * 4 + h
                R.op("act", lambda e, h=h, col=col: e.activation(out=VW[:, d, h, 0:129], in_=V1[:, c, h, 0:129], func=AF.Identity, scale=WG[:, c, col:col + 1]),
                     reads=[b_v1[c], b_chain], writes=[b_vw[d]], parts=True)

        def st_ktr(c):
            for h in range(4):
                R.op("pe", lambda e, h=h: e.transpose(pTR[:, h * 128:(h + 1) * 128], KT[:, c, h, :], ident_b[:, :]), reads=[b_kt[c], b_const], writes=[b_ptr[0]])
            R.op("act", lambda e: e.activation(out=KTK[:, :, :], in_=pTR[:, 0:512].rearrange("p (a b) -> p a b", b=128), func=AF.Copy), reads=[b_ptr[0]], writes=[b_ktk])

        def st_upd(c, d):
            for h in range(4):
                R.op("pe", lambda e, h=h: e.matmul(up_slot(h), lhsT=KTK[:, h, :], rhs=VW[:, d, h, 0:129], start=True, stop=True),
                     reads=[b_ktk, b_vw[d]], writes=[b_pup[h // 3]])
            for h in range(4):
                col = d * 4 + h
                R.op("dve", lambda e, h=h, col=col: e.scalar_tensor_tensor(out=SST[:, d, h, :], in0=SST[:, d, h, :], scalar=DEC[:, c, col:col + 1], in1=up_slot(h),
                                                                            op0=ALU.mult, op1=ALU.add),
                     reads=[b_sst[d][h], b_chain, b_pup[h // 3]], writes=[b_sst[d][h]])

        pSTb = pST[:, :].bitcast(BF16)

        def state_only_pass(seq):
            cl = seqs[seq]

            def prep(c, par):
                trb = pTR if par == 0 else pSTb
                btr = b_ptr[0] if par == 0 else b_pst
                for h in range(4):
                    R.op("act", lambda e, h=h: e.activation(out=VW[:, par, h, 0:129], in_=V1[:, c, h, 0:129], func=AF.Identity, scale=WG[:, c, h:h + 1]),
                         reads=[b_v1[c], b_chain], writes=[b_vw[par]], parts=True)
                for h in range(4):
                    R.op("pe", lambda e, h=h: e.transpose(trb[:, h * 128:(h + 1) * 128], KT[:, c, h, :], ident_b[:, :]),
                         reads=[b_kt[c], b_const], writes=[btr])
                R.op("act", lambda e: e.activation(out=KTK2[:, par, :, :], in_=trb[:, 0:512].rearrange("p (a b) -> p a b", b=128), func=AF.Copy),
                     reads=[btr], writes=[b_ktk2[par]])

            def upd(c, par):
                def slot(h):
                    return up_slot(h) if par == 0 else nd_slot(h)
                bb = [b_pup[0]] if par == 0 else [b_pnd[0], b_pnd[1]]
                for h in range(4):
                    R.op("pe", lambda e, h=h: e.matmul(slot(h), lhsT=KTK2[:, par, h, :], rhs=VW[:, par, h, 0:129], start=True, stop=True),
                         reads=[b_ktk2[par], b_vw[par]], writes=bb)
                for h in range(4):
                    R.op("act", lambda e, h=h: e.activation(out=SNAP[:, c, h, 0:129], in_=SST[:, 0, h, :], func=AF.Identity, scale=DEC[:, c, h:h + 1]),
                         reads=[b_sst[0][h], b_chain], writes=[b_snap[c]], parts=True)
                for h in range(4):
                    R.op("dve", lambda e, h=h: e.scalar_tensor_tensor(out=SST[:, 0, h, :], in0=SST[:, 0, h, :], scalar=DEC[:, c, h:h + 1], in1=slot(h),
                                                                       op0=ALU.mult, op1=ALU.add),
                         reads=[b_sst[0][h], b_chain] + bb, writes=[b_sst[0][h]])

            prep(cl[0], 0)
            for k, c in enumerate(cl):
                if k + 1 < len(cl):
                    prep(cl[k + 1], (k + 1) % 2)
                upd(c, k % 2)

        def emit_state(seq, d):
            si = {"P0": 0, "P1": 1}[seq]
            for h in range(4):
                cs = nxt("ctr", 2)
                R.op("pe", lambda e, h=h: e.matmul(pWO[:, 0:128], lhsT=SST[:, d, h, 0:128], rhs=ident_f[:, :], start=True, stop=True),
                     reads=[b_sst[d][h], b_const], writes=[b_pwo])
                R.op("act", lambda e, cs=cs: e.activation(out=CTR[:, cs, :], in_=pWO[:, 0:128], func=AF.Copy), reads=[b_pwo], writes=[b_ctr[cs]])
                R.dma("sp", oC_d[si, d, h, :, :], CTR[:, cs, :], reads=[b_ctr[cs]])
                R.dma("sp", on_d[si, d, h, :].rearrange("(p o) -> p o", o=1), SST[:, d, h, 128:129], reads=[b_sst[d][h]])
            R.dma("sp", om_d[si, d, :].rearrange("(o h) -> o h", o=1), MFIN[0:1, si, d4(d)], reads=[b_mfin])

        def init_state(d, src=None):
            for h in range(4):
                if src is None:
                    R.op("dve", lambda e, h=h: e.memset(SST[:, d, h, :], 0.0), writes=[b_sst[d][h]])
                else:
                    ap_of, bufs = src
                    R.op("dve", lambda e, h=h: e.tensor_copy(out=SST[:, d, h, :], in_=ap_of(h)), reads=bufs, writes=[b_sst[d][h]])

        def full_pass(seq):
            typ = 1 if seq == "S" else 0
            cl = seqs[seq][::-1]

            def s1a(c, par):
                for h in range(4):
                    R.op("act", lambda e, h=h: e.activation(out=SDB[:, h, 0:129], in_=SST[:, 1, h, :], func=AF.Identity, scale=DEC[:, c, 4 + h:5 + h]),
                         reads=[b_sst[1][h], b_chain], writes=[b_sdb], parts=True)
                st_vw(c, 0)
                st_vw(c, 1)
                for h in range(4):
                    R.op("pe", lambda e, h=h: e.matmul(pST[:, h * 128:(h + 1) * 128], lhsT=KT[:, c, h, :], rhs=QT[:, c, h, :], start=True, stop=True),
                         reads=[b_kt[c], b_qt[c]], writes=[b_pst])
                st_ktr(c)

            def s1b(c, par):
                for d in range(2):
                    R.op("dve", lambda e, d=d: e.tensor_tensor(out=PT[:, d, :, :], in0=pST[:, :].rearrange("p (a b) -> p a b", b=128),
                                                               in1=masks[d][:, :].unsqueeze(1).to_broadcast([128, 4, 128]), op=ALU.mult),
                         reads=[b_pst, b_const], writes=[b_ptt[d]])
                for d in range(2):
                    for h in range(4):
                        i = d * 4 + h
                        R.op("pe", lambda e, d=d, h=h, i=i: e.matmul(nd_slot(i), lhsT=PT[:, d, h, :], rhs=VW[:, d, h, 0:129], start=True, stop=False),
                             reads=[b_ptt[d], b_vw[d]], writes=[b_pnd[i // 3]])
                        if d == 0:
                            R.op("pe", lambda e, h=h, i=i: e.matmul(nd_slot(i), lhsT=QT[:, c, h, :], rhs=SNAP[:, c, h, 0:129], start=False, stop=True),
                                 reads=[b_qt[c], b_snap[c]], writes=[b_pnd[i // 3]])
                        else:
                            R.op("pe", lambda e, h=h, i=i: e.matmul(nd_slot(i), lhsT=QT[:, c, h, :], rhs=SDB[:, h, 0:129], start=False, stop=True),
                                 reads=[b_qt[c], b_sdb], writes=[b_pnd[i // 3]])
                st_upd(c, 1)

            def s1c(c, par):
                R.op("act", lambda e: e.activation(out=DN[:, 0:6].rearrange("p (b t) -> p b t", t=3), in_=pND[:, 0:2, 0:387].rearrange("p b (t w) -> p b t w", w=129)[:, :, :, 128],
                                                   func=AF.Abs),
                     reads=b_pnd, writes=[b_dn])
                R.op("act", lambda e: e.activation(out=DN[:, 6:8], in_=pND[:, 2, 0:258].rearrange("p (t w) -> p t w", w=129)[:, :, 128], func=AF.Abs),
                     reads=b_pnd, writes=[b_dn])
                R.op("dve", lambda e: e.tensor_tensor(out=DN[:, 0:8], in0=DN[:, 0:8], in1=CLP[:, c, :], op=ALU.max), reads=[b_dn, b_chain], writes=[b_dn])
                R.op("dve", lambda e: e.reciprocal(out=DN[:, 0:8], in_=DN[:, 0:8]), reads=[b_dn], writes=[b_dn])
                for h in range(4):
                    R.op("dve", lambda e, h=h: e.tensor_scalar(out=HACC[:, par, h * 128:(h + 1) * 128], in0=nd_slot(h)[:, 0:128], scalar1=DN[:, h:h + 1], scalar2=None, op0=ALU.mult),
                         reads=[b_pnd[h // 3], b_dn], writes=[b_hacc[par][h]])
                for h in range(4):
                    i = 4 + h
                    R.op("dve", lambda e, h=h, i=i: e.scalar_tensor_tensor(out=HACC[:, par, h * 128:(h + 1) * 128], in0=nd_slot(i)[:, 0:128], scalar=DN[:, i:i + 1],
                                                                            in1=HACC[:, par, h * 128:(h + 1) * 128], op0=ALU.mult, op1=ALU.add),
                         reads=[b_pnd[i // 3], b_dn, b_hacc[par][h]], writes=[b_hacc[par][h]])

            def hg0(c):
                R.op("pool", lambda e: e.tensor_tensor(out=HG0[:, :], in0=OG[:, c, :], in1=MHN[:, :], op=ALU.mult), reads=[b_og[c], b_mhn], writes=[b_hg0])

            def s2a(c, par):
                for h in range(4):
                    R.op("act", lambda e, h=h: e.activation(out=SQ4[:, h, :], in_=HACC[:, par, h * 128:(h + 1) * 128], func=AF.Square, accum_out=SS4[:, par, h:h + 1]),
                         reads=[b_hacc[par][h]], writes=[b_sq4[h], b_ss4[par]], parts=True)
                R.op("act", lambda e: e.activation(out=SS4[:, par, :], in_=SS4[:, par, :], func=AF.Sqrt, bias=EPS, scale=1.0 / DH),
                     reads=[b_ss4[par]], writes=[b_ss4[par]])
                R.op("dve", lambda e: e.reciprocal(out=SS4[:, par, :], in_=SS4[:, par, :]), reads=[b_ss4[par]], writes=[b_ss4[par]])
                for h in range(4):
                    R.op("dve", lambda e, h=h: e.scalar_tensor_tensor(out=HG[:, h * 128:(h + 1) * 128], in0=HACC[:, par, h * 128:(h + 1) * 128], scalar=SS4[:, par, h:h + 1],
                                                                       in1=HG0[:, h * 128:(h + 1) * 128], op0=ALU.mult, op1=ALU.mult),
                         reads=[b_hacc[par][h], b_ss4[par], b_hg0], writes=[b_hg], parts=True)

            def s2b(c, par):
                for h in range(4):
                    R.op("pe", lambda e, h=h: e.transpose(pTR[:, 512 + h * 128:512 + (h + 1) * 128], HG[:, h * 128:(h + 1) * 128], ident_b[:, :]),
                         reads=[b_hg, b_const], writes=[b_ptr[1]])
                R.op("act", lambda e: e.activation(out=OG[:, c, :].rearrange("p (a b) -> p a b", b=128), in_=pTR[:, 512:1024].rearrange("p (a b) -> p a b", b=128), func=AF.Copy),
                     reads=[b_ptr[1]], writes=[b_og[c]])

            def s2c(c, par):
                pass

            prev = None
            for k, c in enumerate(cl):
                par = k % 2
                s1a(c, par)
                if prev is not None:
                    s2a(*prev)
                s1b(c, par)
                if prev is not None:
                    s2b(*prev)
                s1c(c, par)
                if prev is not None:
                    s2c(*prev)
                hg0(c)
                prev = (c, par)
            s2a(*prev)
            s2b(*prev)
            s2c(*prev)

        chain("P0", 0, None)
        chain("P1", 0, None)
        R.dma("sp", SXT[:, 0:4], m0_d, writes=[b_sxt])
        chain("S", 0, (SXT[:, 0:4], [b_sxt]))
        weights(0, NT, 0)
        R.dma("sp", SXG[:, 0, 0:516].rearrange("p (h v) -> p h v", v=129), s0_d, writes=[b_sxs])
        init_state(0, (lambda h: SXG[:, 0, h * 129:(h + 1) * 129], [b_sxs]))
        state_only_pass("S")
        for h in range(4):
            R.op("dve", lambda e, h=h: e.tensor_copy(out=SXG[:, 0, h * 129:(h + 1) * 129], in_=SST[:, 0, h, :]), reads=[b_sst[0][h]], writes=[b_sxs])
        R.op("dve", lambda e: e.tensor_copy(out=SXG[:, 0, 516:520], in_=MFIN[:, 2, 0:4]), reads=[b_mfin], writes=[b_sxs])
        R.dma("pool", sx_in[:, :], SXG[:, 0, :], reads=[b_sxs], writes=[b_sxin])
        R.collective(lambda e: e.collective_compute("AllGather", ALU.bypass, replica_groups=RG, ins=[sx_in.ap().opt()], outs=[sx_out.ap().opt()]),
                     reads=[b_sxin], writes=[b_sxout])
        chain("P0", 1, None)
        chain("P1", 1, None)
        weights(0, 4, 1)
        for seq in ["P0", "P1"]:
            init_state(0)
            state_only_pass(seq)
            emit_state(seq, 0)
            init_state(1)
            full_pass(seq)
            emit_state(seq, 1)
        R.dma("pool", SXG[:, :, :], sx_out.ap().rearrange("(r p) n -> p r n", p=128), reads=[b_sxout], writes=[b_sxg])
        R.op("dve", lambda e: e.tensor_scalar(out=SXT[:, :], in0=SXG[:, 0, :], scalar1=sel[:, 1:2], scalar2=None, op0=ALU.mult), reads=[b_sxg, b_const], writes=[b_sxt])
        R.op("dve", lambda e: e.scalar_tensor_tensor(out=SXT[:, :], in0=SXG[:, 1, :], scalar=sel[:, 0:1], in1=SXT[:, :], op0=ALU.mult, op1=ALU.add),
             reads=[b_sxg, b_sxt, b_const], writes=[b_sxt])
        chain("S", 1, (SXT[:, 516:520], [b_sxt]))
        weights(4, NT, 1)
        init_state(1, (lambda h: SXT[:, h * 129:(h + 1) * 129], [b_sxt]))
        full_pass("S")
        dump("MX", MX[:], [128, NT, 8], b_chain)
        dump("WG", WG[:], [128, NT, 8], b_chain)
        dump("SNAP", SNAP[:], [128, NT, 4, 130], b_snap[0], BF16)
        R.flush()


def ffn_up(nc, R, L, _sc, seg, H2T, b_h2t, ACTS, b_acts, n_own, n_all, ntile, nbank, side=None):
    cff, b_const, wup_d = L["cff"], L["b_const"], L["wup_d"]
    U = _sc.sb([128, 2, n_all], F32)
    ACA = _sc.sb([128, 2, n_own], F32)
    ACG = _sc.sb([128, 2, n_own], F32)
    WU = _sc.sb([128, 3, 8, 256], BF16)
    fp = [_sc.ps([128, 512], F32) for _ in range(nbank)]
    b_fp = [Buf() for _ in range(nbank)]
    b_wu = [Buf() for _ in range(3)]
    b_u = [Buf(), Buf()]
    b_aca = [Buf(), Buf()]
    b_acg = [Buf(), Buf()]
    groups = [(g0, min(512, n_all - g0)) for g0 in range(0, n_all, 512)]
    pcount = 0

    def finish(j):
        g2 = j % 2
        R.op("act", lambda e: e.activation(out=ACG[:, g2, :], in_=ACG[:, g2, :], func=AF.Silu), reads=[b_acg[g2]], writes=[b_acg[g2]])
        R.op("pool", lambda e: e.tensor_tensor(out=ACTS[:, j, :], in0=ACG[:, g2, :], in1=ACA[:, g2, :], op=ALU.mult),
             reads=[b_acg[g2], b_aca[g2]], writes=[b_acts[j]])

    for j in range(min(2, NJ)):
        R.dma("pool", WU[:, j % 3, :, :], wup_d[j], writes=[b_wu[j % 3]])
    for j in range(NJ):
        s = j % 3
        if j + 2 < NJ:
            R.dma("pool", WU[:, (j + 2) % 3, :, :], wup_d[j + 2], writes=[b_wu[(j + 2) % 3]])
        a2 = j % 2
        for part in range(2):
            blk = j + part * NJ
            us = part
            acc = ACA[:, a2, :] if part == 0 else ACG[:, a2, :]
            b_acc = b_aca[a2] if part == 0 else b_acg[a2]
            for (g0, gn) in groups:
                pb = pcount % nbank
                pcount += 1
                hb = [b_h2t[min(ti, ntile)] for ti in range(g0 // 128, (g0 + gn + 127) // 128)]
                for kc in range(8):
                    R.op("pe", lambda e, kc=kc, g0=g0, gn=gn, pb=pb, s=s, part=part: e.matmul(fp[pb][:, 0:gn], lhsT=WU[:, s, kc, part * 128:(part + 1) * 128],
                                                                                         rhs=H2T[:, kc, g0:g0 + gn], start=(kc == 0), stop=(kc == 7)),
                         reads=[b_wu[s]] + hb, writes=[b_fp[pb]])
                R.op("act", lambda e, g0=g0, gn=gn, pb=pb, us=us: e.activation(out=U[:, us, g0:g0 + gn], in_=fp[pb][:, 0:gn], func=AF.Copy),
                     reads=[b_fp[pb]], writes=[b_u[us]], parts=True)
                if g0 < n_own:
                    on = min(gn, n_own - g0)
                    R.op("act", lambda e, g0=g0, on=on, pb=pb, acc=acc, blk=blk: e.activation(out=acc[:, g0:g0 + on], in_=fp[pb][:, 0:on], func=AF.Identity,
                                                                                         scale=cff[:, blk, 1:2], bias=cff[:, blk, 3:4]),
                         reads=[b_fp[pb], b_const], writes=[b_acc], parts=True)
            if seg == "prompt":
                def v3(ap):
                    return ap.rearrange("p (r w) -> p r w", w=256)
                R.op("dve", lambda e, acc=acc, us=us, blk=blk, v3=v3: e.scalar_tensor_tensor(out=v3(acc)[:, :, 1:256], in0=v3(U[:, us, 0:512])[:, :, 0:255], scalar=cff[:, blk, 0:1],
                                                                                        in1=v3(acc)[:, :, 1:256], op0=ALU.mult, op1=ALU.add),
                     reads=[b_u[us], b_acc, b_const], writes=[b_acc])
                R.op("dve", lambda e, acc=acc, us=us, blk=blk, v3=v3: e.scalar_tensor_tensor(out=v3(acc)[:, :, 0:255], in0=v3(U[:, us, 0:512])[:, :, 1:256], scalar=cff[:, blk, 2:3],
                                                                                        in1=v3(acc)[:, :, 0:255], op0=ALU.mult, op1=ALU.add),
                     reads=[b_u[us], b_acc, b_const], writes=[b_acc])
            else:
                R.op("dve", lambda e, acc=acc, us=us, blk=blk: e.scalar_tensor_tensor(out=acc[:, 64:2048], in0=U[:, us, 0:1984], scalar=cff[:, blk, 0:1],
                                                                                 in1=acc[:, 64:2048], op0=ALU.mult, op1=ALU.add),
                     reads=[b_u[us], b_acc, b_const], writes=[b_acc])
                R.op("dve", lambda e, acc=acc, us=us, blk=blk: e.scalar_tensor_tensor(out=acc[:, 0:2048], in0=U[:, us, 64:2112], scalar=cff[:, blk, 2:3],
                                                                                 in1=acc[:, 0:2048], op0=ALU.mult, op1=ALU.add),
                     reads=[b_u[us], b_acc, b_const], writes=[b_acc])
        if j >= 1:
            finish(j - 1)
        if side is not None:
            side(j)
    finish(NJ - 1)


def ffn_down_consts(nc, R, L, _sc, typ):
    wdn_d, fn_d, gate_d, b_gate_d = L["wdn_d"], L["fn_d"], L["gate_d"], L["b_gate_d"]
    WD = _sc.sb([128, NJ, D], BF16)
    FNR = _sc.sb([128, D], F32)
    G2ROW = _sc.sb([128, D], F32)
    b_wd = [Buf() for _ in range(NJ // 2)]
    b_fnr, b_g2 = Buf(), Buf()
    for jj in range(0, NJ, 2):
        R.dma("pool", WD[:, jj:jj + 2, :], wdn_d[:, jj:jj + 2, :], writes=[b_wd[jj // 2]])
    R.dma("sp", FNR[:], fn_d, writes=[b_fnr])
    R.dma("sp", G2ROW[:], gate_d[typ:typ + 1, 1024:2048].to_broadcast([128, 1024]), reads=[b_gate_d], writes=[b_g2])
    return WD, FNR, G2ROW, b_wd, b_fnr, b_g2


def ffn_down(nc, R, L, _sc, tiles, ACTS, b_acts, dc):
    x1s_d, b_x1s, y_d = L["x1s_d"], L["b_x1s"], L["y_d"]
    WD, FNR, G2ROW, b_wd, b_fnr, b_g2 = dc
    X1L = _sc.sb([128, 2, D], F32)
    X2 = _sc.sb([128, 2, D], F32)
    YT = _sc.sb([128, 2, D], F32)
    SQ2 = _sc.sb([128, D], BF16)
    SSD = _sc.sb([128, 4], F32)
    dp = [_sc.ps([128, 512], F32) for _ in range(4)]
    b_dp = [Buf() for _ in range(4)]
    b_sq2 = Buf()
    b_x1l, b_x2, b_yt = [Buf(), Buf()], [Buf(), Buf()], [Buf(), Buf()]
    b_ssd = [Buf() for _ in range(4)]
    pc = 0
    for i, t in enumerate(tiles):
        hs = i % 2
        q = i % 4
        R.dma("sp", X1L[:, hs, :], x1s_d[t * 128:(t + 1) * 128, :], reads=[b_x1s[t]], writes=[b_x1l[hs]])
        for half in range(2):
            pb = pc % 4
            pc += 1
            for j in range(NJ):
                R.op("pe", lambda e, j=j, i=i, half=half, pb=pb: e.matmul(dp[pb][:, :], lhsT=ACTS[:, j, i * 128:(i + 1) * 128], rhs=WD[:, j, half * 512:(half + 1) * 512],
                                                                        start=(j == 0), stop=(j == NJ - 1)),
                     reads=[b_acts[j], b_wd[j // 2]], writes=[b_dp[pb]])
            R.op("dve", lambda e, half=half, pb=pb, hs=hs: e.tensor_tensor(out=X2[:, hs, half * 512:(half + 1) * 512], in0=dp[pb][:, :], in1=G2ROW[:, half * 512:(half + 1) * 512], op=ALU.mult),
                 reads=[b_dp[pb], b_g2], writes=[b_x2[hs]], parts=True)
            R.op("pool", lambda e, half=half, hs=hs: e.tensor_tensor(out=X2[:, hs, half * 512:(half + 1) * 512], in0=X2[:, hs, half * 512:(half + 1) * 512],
                                                                   in1=X1L[:, hs, half * 512:(half + 1) * 512], op=ALU.add),
                 reads=[b_x2[hs], b_x1l[hs]], writes=[b_x2[hs]])
        R.op("act", lambda e, hs=hs, q=q: e.activation(out=SQ2[:, :], in_=X2[:, hs, :], func=AF.Square, accum_out=SSD[:, q:q + 1]),
             reads=[b_x2[hs]], writes=[b_sq2, b_ssd[q]])
        R.op("act", lambda e, q=q: e.activation(out=SSD[:, q:q + 1], in_=SSD[:, q:q + 1], func=AF.Sqrt, bias=EPS, scale=1.0 / D),
             reads=[b_ssd[q]], writes=[b_ssd[q]])
        R.op("dve", lambda e, q=q: e.reciprocal(out=SSD[:, q:q + 1], in_=SSD[:, q:q + 1]),
             reads=[b_ssd[q]], writes=[b_ssd[q]])
        R.op("dve", lambda e, hs=hs, q=q: e.scalar_tensor_tensor(out=YT[:, hs, :], in0=X2[:, hs, :], scalar=SSD[:, q:q + 1], in1=FNR[:, :], op0=ALU.mult, op1=ALU.mult),
             reads=[b_x2[hs], b_ssd[q], b_fnr], writes=[b_yt[hs]])
        R.dma("sp", y_d[t * 128:(t + 1) * 128, :], YT[:, hs, :], reads=[b_yt[hs]])


def ffn_all(nc, R, L):
    A2, B2, b_mod, ident_b, anti_b, b_const, sel = (L[k] for k in ["A2", "B2", "b_mod", "ident_b", "anti_b", "b_const", "sel"])
    x1s_d, b_x1s = L["x1s_d"], L["b_x1s"]
    hx_in, hx_out, b_hxin, b_hxout, RG = L["hx_in"], L["hx_out"], L["b_hxin"], L["b_hxout"], L["RG"]
    dump = L["dump"]
    p_tiles = [0, 1, 2, 3]
    s_tiles = list(range(4, NT))
    NS = len(s_tiles)
    hx_scr, b_hs = L["hx_scr"], L["b_hs"]
    with Scope(nc) as _so:
        H2Ts = _so.sb([128, 8, 2112], BF16)
        b_h2ts = [Buf() for _ in range(NS + 1)]
        with Scope(nc) as _sp:
            ACTSp = _sp.sb([128, NJ, 512], BF16)
            b_actsp = [Buf() for _ in range(NJ)]
            H2Tp = _sp.sb([128, 8, 512], BF16)
            b_h2tp = [Buf() for _ in range(5)]
            dcp = ffn_down_consts(nc, R, L, _sp, 0)
            with Scope(nc) as _s1:
                build_hT(nc, R, H2Tp, b_h2tp, p_tiles, lambda t: x1s_d[t * 128:(t + 1) * 128, :], [b_x1s[t] for t in p_tiles], A2, B2, b_mod, ident_b, b_const,
                         lambda t: 0, [128] * 4)
            with Scope(nc) as _s2:
                s1, s2, nb = make_builder(nc, R, _s2, H2Ts, b_h2ts, s_tiles, lambda t: x1s_d[t * 128:(t + 1) * 128, :], [b_x1s[t] for t in s_tiles],
                                          A2, B2, b_mod, ident_b, b_const, lambda t: 1, [128] * NS, col_of=lambda t: (t - 4) * 128)
                state = {"i": 0}
                s1(0)

                def side(j):
                    i = state["i"]
                    if i < nb:
                        if i + 1 < nb:
                            s1(i + 1)
                        s2(i)
                        state["i"] = i + 1

                ffn_up(nc, R, L, _s2, "prompt", H2Tp, b_h2tp, ACTSp, b_actsp, 512, 512, 4, 4, side=side)
                while state["i"] < nb:
                    side(0)
                R.flush()
            with Scope(nc) as _s3:
                ffn_down(nc, R, L, _s3, p_tiles, ACTSp, b_actsp, dcp)
                R.flush()
        build_hT(nc, R, H2Ts, [b_h2ts[NS]], [0], lambda t: hx_scr, [b_hs], A2, B2, b_mod, ident_b, b_const,
                 lambda t: 1, [64], col_of=lambda t: 2048, anti=anti_b)
        dump("H2T_sample", H2Ts[:], [128, 8, 2112], b_h2ts[0], BF16)
        with Scope(nc) as _ss:
            ACTSs = _ss.sb([128, NJ, 2048], BF16)
            b_actss = [Buf() for _ in range(NJ)]
            with Scope(nc) as _s4:
                ffn_up(nc, R, L, _s4, "sample", H2Ts, b_h2ts, ACTSs, b_actss, 2048, 2112, NS, 8)
                R.flush()
            dump("ACTS_sample", ACTSs[:], [128, NJ, 2048], b_actss[0], BF16)
            with Scope(nc) as _s5:
                dcs = ffn_down_consts(nc, R, L, _s5, 1)
                ffn_down(nc, R, L, _s5, s_tiles, ACTSs, b_actss, dcs)
                R.flush()


_CACHE = {}


def _consts():
    i = np.arange(128)
    ident = np.eye(128, dtype=np.float32)
    anti = np.zeros((128, 128), np.float32)
    anti[np.arange(64), 63 - np.arange(64)] = 1.0
    mk0 = (i[:, None] <= i[None, :]).astype(np.float32)
    mk1 = (i[:, None] >= i[None, :]).astype(np.float32)
    selj = np.zeros((2, 256), np.float32)
    selj[0, 0:128] = 1.0
    selj[1, 128:256] = 1.0
    return dict(ident=ident, antiI=anti, maskT0=mk0, maskT1=mk1, ones=np.ones((128, 128), np.float32),
                selj=selj, i2=np.eye(2, dtype=np.float32), i8=np.eye(8, dtype=np.float32))


def _kmajor(w):
    return np.ascontiguousarray(w.reshape(8, 128, -1).transpose(1, 0, 2))


def make_in_maps(x_prompt, x_sample, state_C, state_n, state_m, c, c_ctx, w_mod, b_mod, norm1, w_in, b_gate,
                 conv_sc_w, conv_sc_b, mh_norm, w_out, norm2, w_up, conv_ffn_w, conv_ffn_b, w_down, final_norm):
    f = np.float32
    A = lambda a: np.asarray(a, dtype=f)
    x_prompt, x_sample, state_C, state_n, state_m, c, c_ctx = map(A, (x_prompt, x_sample, state_C, state_n, state_m, c, c_ctx))
    w_mod, b_mod, norm1, w_in, b_gate = A(w_mod)[0], A(b_mod)[0], A(norm1)[0], A(w_in)[0], A(b_gate)[0]
    conv_sc_w, conv_sc_b, mh_norm, w_out, norm2 = A(conv_sc_w)[0], A(conv_sc_b)[0], A(mh_norm)[0], A(w_out)[0], A(norm2)[0]
    w_up, conv_ffn_w, conv_ffn_b, w_down, final_norm = A(w_up)[0], A(conv_ffn_w)[0], A(conv_ffn_b)[0], A(w_down)[0], A(final_norm)
    cst = _consts()
    shared = dict(cst)
    shared["w_mod_r"] = np.ascontiguousarray(_kmajor(w_mod).reshape(128, 8, 12, 512).transpose(2, 0, 1, 3))
    shared["b_mod2"] = np.ascontiguousarray(np.broadcast_to(b_mod[None, :], (2, 6144)))
    shared["norm1c"] = np.ascontiguousarray(norm1.reshape(8, 128).T)
    shared["norm2c"] = np.ascontiguousarray(norm2.reshape(8, 128).T)
    wk = _kmajor(w_in)
    shared["wic"] = np.ascontiguousarray(np.stack([np.concatenate([wk[:, :, cb * 128:(cb + 1) * 128], wk[:, :, 512 + cb * 128:512 + (cb + 1) * 128],
                                                                   wk[:, :, 1024 + cb * 128:1024 + (cb + 1) * 128]], axis=2) for cb in range(4)]))
    shared["wiqk"] = np.ascontiguousarray(np.stack([wk[:, :, 1536 + b * 128:1536 + (b + 1) * 128] for b in range(8)]))
    shared["wiv"] = np.ascontiguousarray(wk[:, :, 2560:3072])
    shared["wio"] = np.ascontiguousarray(wk[:, :, 3072:3584])
    shared["mhn_row"] = np.ascontiguousarray(np.broadcast_to(mh_norm[None, :], (128, 512)))
    shared["w_out_r"] = _kmajor(w_out)
    wu = _kmajor(w_up)
    shared["w_up_r"] = np.ascontiguousarray(np.stack([np.concatenate([wu[:, :, j * 128:(j + 1) * 128], wu[:, :, DFF + j * 128:DFF + (j + 1) * 128]], axis=2)
                                                      for j in range(NJ)]))
    shared["w_down_r"] = np.ascontiguousarray(w_down.reshape(NJ, 128, D).transpose(1, 0, 2))
    shared["fn_row"] = np.ascontiguousarray(np.broadcast_to(final_norm[None, :], (128, D)))
    maps = []
    for core in range(8):
        par = core % 2
        b = core // 2
        m = dict(shared)
        ps = [x_prompt[2 * core], x_prompt[2 * core + 1]]
        xs = x_sample[b, 0:2048] if par == 0 else x_sample[b, 2048:4096]
        if par == 1:
            ps = [p[::-1] for p in ps]
            xs = xs[::-1]
        m["xin"] = np.ascontiguousarray(np.concatenate(ps + [xs], axis=0))
        cv = np.stack([c_ctx, c[b]], axis=1)
        m["cT"] = np.ascontiguousarray(cv.reshape(8, 128, 2).transpose(1, 0, 2))
        gperm = [0, 1, 2, 3] if par == 0 else [2, 3, 0, 1]
        wg = wk[:, :, 3584:3600].reshape(128, 8, 4, 4)[:, :, gperm, :].reshape(128, 8, 16)
        m["wig"] = np.ascontiguousarray(wg)
        bg = b_gate[gperm, :].reshape(16)
        m["bgate_row"] = np.ascontiguousarray(np.broadcast_to(bg[None, :], (128, 16)))
        tap = [0, 1, 2] if par == 0 else [2, 1, 0]
        csc = np.stack([conv_sc_w[tap[0]], conv_sc_w[tap[1]], conv_sc_w[tap[2]], conv_sc_b], axis=1)
        m["conv_sc"] = np.ascontiguousarray(csc.reshape(4, 128, 4).transpose(1, 0, 2))
        cf = np.stack([conv_ffn_w[tap[0]], conv_ffn_w[tap[1]], conv_ffn_w[tap[2]], conv_ffn_b], axis=1)
        m["conv_ffn"] = np.ascontiguousarray(cf.reshape(44, 128, 4).transpose(1, 0, 2))
        dsel = par
        C0 = state_C[b, 0, dsel]
        n0 = state_n[b, 0, dsel]
        s0 = np.concatenate([C0.transpose(2, 0, 1), n0.T[:, :, None]], axis=2)
        m["s0"] = np.ascontiguousarray(s0)
        m["m0"] = np.ascontiguousarray(np.broadcast_to(state_m[b, 0, dsel][None, :], (128, 4)))
        sel = np.zeros((128, 2), f)
        sel[:, par] = 1.0
        m["sel"] = sel
        maps.append(m)
    return maps


def assemble(results):
    f = np.float32
    y_prompt = np.zeros((16, 256, D), f)
    y_sample = np.zeros((4, 4096, D), f)
    new_C = np.zeros((16, 1, 2, 4, 128, 128), f)
    new_n = np.zeros((16, 1, 2, 4, 128), f)
    new_m = np.zeros((16, 1, 2, 4), f)
    for core in range(8):
        r = results[core]
        par = core % 2
        b = core // 2
        y = np.asarray(r["y"], dtype=f)
        for i in range(2):
            yp = y[i * 256:(i + 1) * 256]
            y_prompt[2 * core + i] = yp[::-1] if par else yp
            for dl in range(2):
                dg = dl if par == 0 else 1 - dl
                new_C[2 * core + i, 0, dg] = r["oC"][i, dl]
                new_n[2 * core + i, 0, dg] = r["on"][i, dl]
                new_m[2 * core + i, 0, dg] = r["om"][i, dl]
        ys = y[512:2560]
        if par == 0:
            y_sample[b, 0:2048] = ys
        else:
            y_sample[b, 2048:4096] = ys[::-1]
    return (y_prompt, y_sample, new_C, new_n, new_m)


def kernel(**inputs):
    maps = make_in_maps(**inputs)
    if "nc" not in _CACHE:
        _CACHE["nc"] = build_program()[0]
    res = run_bass_kernel_spmd(_CACHE["nc"], maps, core_ids=list(range(8)))
    return assemble(res.results)
```

```python
import contextlib
import numpy as np
import concourse.bass as bass
import concourse.mybir as mybir
from concourse.bass_utils import run_bass_kernel_spmd

F32 = mybir.dt.float32
BF16 = mybir.dt.bfloat16
AF = mybir.ActivationFunctionType
ALU = mybir.AluOpType
AX = mybir.AxisListType

D = 1024
NT = 20
NTOK = 2560
DH = 128
NH = 4
DFF = 2816
NJ = 22
EPS = 1e-6
EO_OF = [0, 1, 1, 1, 0, 1, 1, 1]
K2_OF = [0, 0, 1, 2, 1, 3, 4, 5]
ENGS = ["sp", "act", "pool", "dve", "pe"]


class Scope:
    def __init__(self, nc):
        self.nc = nc
        self.stack = contextlib.ExitStack()

    def __enter__(self):
        self.stack.__enter__()
        return self

    def __exit__(self, *a):
        return self.stack.__exit__(*a)

    _n = [0]

    def sb(self, shape, dt=F32):
        Scope._n[0] += 1
        return self.stack.enter_context(self.nc.sbuf_tensor("sb%d" % Scope._n[0], list(shape), dt))

    def ps(self, shape, dt=F32):
        Scope._n[0] += 1
        return self.stack.enter_context(self.nc.psum_tensor("ps%d" % Scope._n[0], list(shape), dt))


class Op:
    __slots__ = ("eng", "fn", "waits", "sig", "val", "dma", "cc", "sem", "block", "extra", "store")

    def __init__(self, eng, fn, block):
        self.eng = eng
        self.fn = fn
        self.waits = []
        self.sig = False
        self.val = None
        self.dma = False
        self.cc = False
        self.sem = None
        self.block = block
        self.extra = []
        self.store = False


class Buf:
    __slots__ = ("name", "ws", "r", "parts", "gen_deps")

    def __init__(self, name=""):
        self.name = name
        self.ws = []
        self.r = []
        self.parts = False
        self.gen_deps = []


class Rec:
    def __init__(self, nc, nring=8):
        self.nc = nc
        self.engs = {"sp": nc.sync, "act": nc.scalar, "pool": nc.gpsimd, "dve": nc.vector, "pe": nc.tensor}
        self.pending = {e: [] for e in ENGS}
        self.block = 0
        self.csem = {e: nc.alloc_semaphore("c_" + e) for e in ["act", "pool", "dve", "pe"]}
        self.ccount = {e: 0 for e in self.csem}
        self.nring = nring
        self.dsem = {q: [nc.alloc_semaphore("d_%s%d" % (q, i)) for i in range(nring)] for q in ["sp", "pool"]}
        self.duse = {q: [0] * nring for q in self.dsem}
        self.dnext = {q: 0 for q in self.dsem}
        self.ccsem = nc.alloc_semaphore("ccsem")
        self.cccount = 0

    def _deps(self, op, reads, writes, parts=False):
        deps = {}
        is_async = op.dma or op.cc

        def add(d):
            if d is None:
                return
            d_async = d.dma or d.cc
            if not d_async and d.block != self.block:
                return
            if not d_async and not is_async and d.eng == op.eng and op.eng == "pe":
                return
            deps[id(d)] = d

        for b in reads:
            for ww in b.ws:
                add(ww)
        for b in writes:
            same_gen = parts and b.parts and not b.r and b.ws
            if same_gen:
                for gd in b.gen_deps:
                    add(gd)
            else:
                b.gen_deps = list(b.ws) + list(b.r)
                for ww in b.ws:
                    add(ww)
            for rr in b.r:
                add(rr)
        for d in deps.values():
            if not (d.dma or d.cc):
                d.sig = True
        op.waits = list(deps.values())
        for b in reads:
            if b in writes:
                continue
            if not is_async:
                b.r = [x for x in b.r if (x.dma or x.cc or x.eng != op.eng)]
            b.r.append(op)
        for b in writes:
            same_gen = parts and b.parts and not b.r and b.ws
            if same_gen:
                if not is_async:
                    b.ws = [x for x in b.ws if (x.dma or x.cc or x.eng != op.eng)]
                b.ws.append(op)
            else:
                b.ws = [op]
            b.parts = parts
            b.r = []

    def op(self, eng, fn, reads=(), writes=(), parts=False):
        o = Op(eng, fn, self.block)
        self._deps(o, list(reads), list(writes), parts)
        self.pending[eng].append(o)
        return o

    def dma(self, q, out, in_, reads=(), writes=(), **kw):
        o = Op(q, lambda e: e.dma_start(out=out, in_=in_, **kw), self.block)
        o.dma = True
        try:
            o.store = "DRam" in type(out.tensor).__name__
        except Exception:
            o.store = True
        slot = self.dnext[q]
        self.dnext[q] = (slot + 1) % self.nring
        prev = self.duse[q][slot]
        o.sem = self.dsem[q][slot]
        if prev > 0:
            o.extra.append((o.sem, 16 * prev))
        self.duse[q][slot] = prev + 1
        o.val = 16 * (prev + 1)
        self._deps(o, list(reads), list(writes))
        self.pending[q].append(o)
        return o

    def collective(self, fn, reads=(), writes=()):
        o = Op("pool", fn, self.block)
        o.cc = True
        self.cccount += 1
        o.sem = self.ccsem
        o.val = self.cccount
        self._deps(o, list(reads), list(writes))
        self.pending["pool"].append(o)
        return o

    def flush(self, final=False):
        if not final and not any(self.pending[e] for e in ENGS):
            return
        for e in ENGS:
            for o in self.pending[e]:
                if o.dma or o.cc:
                    continue
                if o.sig:
                    self.ccount[e] += 1
                    o.val = self.ccount[e]
        names = {"sp": "sync", "act": "scalar", "pool": "gpsimd", "dve": "vector", "pe": "tensor"}
        with self.nc.Block() as block:
            for e in ENGS:
                ops = self.pending[e]
                if not ops and not (final and e == "sp"):
                    continue

                def body(engine, ops=ops, e=e):
                    for o in ops:
                        w = {}
                        for (s, v) in o.extra:
                            w[s.num] = (s, max(v, w.get(s.num, (s, 0))[1]))
                        for d in o.waits:
                            s = d.sem if (d.dma or d.cc) else self.csem[d.eng]
                            w[s.num] = (s, max(d.val, w.get(s.num, (s, 0))[1]))
                        for (s, v) in w.values():
                            engine.wait_ge(s, v)
                        ins = o.fn(engine)
                        if o.dma:
                            ins.then_inc(o.sem, 16)
                        elif o.cc:
                            ins.then_inc(o.sem, 1)
                        elif o.sig:
                            ins.then_inc(self.csem[e], 1)
                    tail = {}
                    for o in ops:
                        if o.dma and o.store:
                            tail[o.sem.num] = (o.sem, max(o.val, tail.get(o.sem.num, (o.sem, 0))[1]))
                    for (sm, v) in tail.values():
                        engine.wait_ge(sm, v)
                    if final and e == "sp":
                        for q in self.dsem:
                            for i in range(self.nring):
                                if self.duse[q][i] > 0:
                                    engine.wait_ge(self.dsem[q][i], 16 * self.duse[q][i])

                getattr(block, names[e])(body)
        self.pending = {e: [] for e in ENGS}
        self.block += 1


def build_program(dbg=()):
    nc = bass.Bass("TRN2", target_bir_lowering=False)
    R = Rec(nc)
    dbg = set(dbg)
    dbg_outs = {}

    def din(name, shape, dt=F32):
        return nc.dram_tensor(name, list(shape), dt, kind="ExternalInput").ap()

    def dout(name, shape, dt=F32):
        return nc.dram_tensor(name, list(shape), dt, kind="ExternalOutput").ap()

    xin = din("xin", [NTOK, D])
    cT_d = din("cT", [128, 8, 2])
    wmod_d = din("w_mod_r", [12, 128, 8, 512])
    bmod_d = din("b_mod2", [2, 6144])
    n1c_d = din("norm1c", [128, 8])
    n2c_d = din("norm2c", [128, 8])
    wic_d = din("wic", [4, 128, 8, 384])
    wiqk_d = din("wiqk", [8, 128, 8, 128])
    wiv_d = din("wiv", [128, 8, 512])
    wio_d = din("wio", [128, 8, 512])
    wig_d = din("wig", [128, 8, 16])
    bg_d = din("bgate_row", [128, 16])
    csc_d = din("conv_sc", [128, 4, 4])
    mhn_d = din("mhn_row", [128, 512])
    wout_d = din("w_out_r", [128, 8, 1024])
    wup_d = din("w_up_r", [NJ, 128, 8, 256])
    cff_d = din("conv_ffn", [128, 44, 4])
    wdn_d = din("w_down_r", [128, NJ, 1024])
    fn_d = din("fn_row", [128, 1024])
    s0_d = din("s0", [128, 4, 129])
    m0_d = din("m0", [128, 4])
    sel_d = din("sel", [128, 2])
    ident_d = din("ident", [128, 128])
    anti_d = din("antiI", [128, 128])
    mk0_d = din("maskT0", [128, 128])
    mk1_d = din("maskT1", [128, 128])
    ones_d = din("ones", [128, 128])
    selj_d = din("selj", [2, 256])
    i2_d = din("i2", [2, 2])
    i8_d = din("i8", [8, 8])

    y_d = dout("y", [NTOK, D])
    oC_d = dout("oC", [2, 2, 4, 128, 128])
    on_d = dout("on", [2, 2, 4, 128])
    om_d = dout("om", [2, 2, 4])

    x1s_d = nc.dram_tensor("x1s", [NTOK, D], F32).ap()
    gate_d = nc.dram_tensor("gate_scr", [2, 2048], F32).ap()
    sx_in = nc.dram_tensor("sx_in", [128, 520], F32)
    sx_out = nc.dram_tensor("sx_out", [256, 520], F32)
    hx_in = nc.dram_tensor("hx_in", [64, 1024], F32)
    hx_out = nc.dram_tensor("hx_out", [128, 1024], F32)
    b_x1s = [Buf("x1s%d" % i) for i in range(NT)]
    b_gate_d = Buf("gate_d")
    b_sxin, b_sxout, b_hxin, b_hxout = Buf(), Buf(), Buf(), Buf()
    hx_scr = nc.dram_tensor("hx_scr", [64, D], F32).ap()
    b_hs = Buf()
    RG = [[0, 1], [2, 3], [4, 5], [6, 7]]

    def dump(name, ap, shape, buf, dt=F32):
        if name not in dbg:
            return
        t = dout("dbg_" + name, shape, dt)
        dbg_outs[name] = (shape, dt)
        R.dma("sp", t, ap, reads=[buf])

    def T(name, shape, dt=F32):
        return nc.sbuf_tensor(list(shape), dt)

    def PS(name, shape, dt=F32):
        return nc.psum_tensor(list(shape), dt)

    with Scope(nc) as _sc:
        ident_f = _sc.sb([128, 128])
        ones_f = _sc.sb([128, 128])
        tri0_f = _sc.sb([128, 128])
        tri1_f = _sc.sb([128, 128])
        ident_b = _sc.sb([128, 128], BF16)
        anti_b = _sc.sb([128, 128], BF16)
        mk0_b = _sc.sb([128, 128], BF16)
        mk1_b = _sc.sb([128, 128], BF16)
        selj = _sc.sb([2, 256])
        i2 = _sc.sb([2, 2])
        i8 = _sc.sb([8, 8])
        n1c = _sc.sb([128, 8])
        n2c = _sc.sb([128, 8])
        bg_row = _sc.sb([128, 16])
        csc = _sc.sb([128, 4, 4])
        cff = _sc.sb([128, 44, 4])
        sel = _sc.sb([128, 2])
        A1 = _sc.sb([128, 8, 2])
        B1 = _sc.sb([128, 8, 2])
        A2 = _sc.sb([128, 8, 2])
        B2 = _sc.sb([128, 8, 2])
        b_const = Buf("const")
        for (t, d) in [(ident_f, ident_d), (ones_f, ones_d), (tri0_f, mk0_d), (tri1_f, mk1_d),
                       (selj, selj_d), (i2, i2_d), (i8, i8_d), (n1c, n1c_d), (n2c, n2c_d),
                       (bg_row, bg_d), (csc, csc_d), (cff, cff_d), (sel, sel_d)]:
            R.dma("sp", t[:], d, writes=[b_const])
        for (t, d) in [(ident_b, ident_d), (anti_b, anti_d), (mk0_b, mk0_d), (mk1_b, mk1_d)]:
            R.dma("pool", t[:], d, writes=[b_const])
        b_mod = Buf("modcols")

        HT = nc.alloc_sbuf_tensor_at("HT", [128, 8, NTOK], BF16, offset=172032)
        b_ht = [Buf() for _ in range(NT)]
        HT_BYTES = 8 * NTOK * 2
        with Scope(nc) as _sc:
            cT_f = _sc.sb([128, 8, 2])
            cs_b = _sc.sb([128, 8, 2], BF16)
            bmod = _sc.sb([2, 6144])
            modrow = _sc.sb([2, 6144])
            wm0 = _sc.sb([128, 8, 512], BF16)
            wm1 = _sc.sb([128, 8, 512], BF16)
            wm2 = _sc.sb([128, 8, 512], BF16)
            tmpc = _sc.sb([128, 8, 2])
            pm0 = _sc.ps([128, 512])
            pm1 = _sc.ps([128, 512])
            pcol = _sc.ps([128, 512])
            b_ct, b_cs, b_bmod, b_modrow, b_tmpc = Buf(), Buf(), Buf(), Buf("modrow"), Buf()
            R.dma("sp", cT_f[:], cT_d, writes=[b_ct])
            R.dma("sp", bmod[:], bmod_d, writes=[b_bmod])
            R.op("act", lambda e: e.activation(out=cs_b[:], in_=cT_f[:], func=AF.Silu), reads=[b_ct], writes=[b_cs])
            wms = [wm0, wm1, wm2]
            b_wm = [Buf() for _ in range(3)]
            pms = [pm0, pm1]
            b_pm = [Buf(), Buf()]
            b_pcol = Buf()

            def colv(vi):
                return pcol[:, vi * 16:(vi + 1) * 16].rearrange("p (a b) -> p a b", b=2)

            def wchunk(n):
                s = n % 3
                R.dma("pool", wms[s][:], wmod_d[n], writes=[b_wm[s]])
                p = n % 2
                for kc in range(8):
                    R.op("pe", lambda e, kc=kc: e.matmul(pms[p][0:2, :], lhsT=cs_b[:, kc, :], rhs=wms[s][:, kc, :], start=(kc == 0), stop=(kc == 7)),
                         reads=[b_cs, b_wm[s]], writes=[b_pm[p]])
                R.op("dve", lambda e: e.tensor_tensor(out=modrow[:, n * 512:(n + 1) * 512], in0=pms[p][0:2, :], in1=bmod[:, n * 512:(n + 1) * 512], op=ALU.add),
                     reads=[b_pm[p], b_bmod], writes=[b_modrow])

            def cols(Aout, Bout, vsh, vsc, ncol):
                offs = [0, 1024, 3072, 4096]
                for vi in (vsh, vsc):
                    for kc in range(8):
                        c0 = (vi * 8 + kc) * 2
                        off = offs[vi]
                        R.op("pe", lambda e, off=off, kc=kc, c0=c0: e.matmul(pcol[:, c0:c0 + 2], lhsT=modrow[0:2, off + kc * 128: off + (kc + 1) * 128],
                                                                            rhs=i2[:, :], start=True, stop=True),
                             reads=[b_modrow, b_const], writes=[b_pcol])
                R.op("dve", lambda e: e.tensor_scalar(out=tmpc[:], in0=colv(vsc), scalar1=1.0, scalar2=None, op0=ALU.add), reads=[b_pcol], writes=[b_tmpc])
                R.op("dve", lambda e: e.tensor_tensor(out=Aout[:], in0=tmpc[:], in1=ncol[:, :].unsqueeze(2).to_broadcast([128, 8, 2]), op=ALU.mult),
                     reads=[b_tmpc, b_const], writes=[b_mod])
                R.op("dve", lambda e: e.tensor_copy(out=Bout[:], in_=colv(vsh)), reads=[b_pcol], writes=[b_mod])

            bs1, bs2, nb = make_builder(nc, R, _sc, HT, b_ht, list(range(NT)), lambda t: xin[t * 128:(t + 1) * 128, :], [None] * NT,
                                        A1, B1, b_mod, ident_b, b_const, lambda t: 0 if t < 4 else 1, [128] * NT)
            bs1(0)
            for n in range(4):
                wchunk(n)
            cols(A1, B1, 0, 1, n1c)
            nextc = 4
            for i in range(nb):
                if i + 1 < nb:
                    bs1(i + 1)
                bs2(i)
                if i % 2 == 1 and nextc < 12:
                    wchunk(nextc)
                    if nextc == 5:
                        R.dma("sp", gate_d[:, 0:1024], modrow[:, 2048:3072], reads=[b_modrow], writes=[b_gate_d])
                    if nextc == 9:
                        cols(A2, B2, 2, 3, n2c)
                    nextc += 1
            while nextc < 12:
                wchunk(nextc)
                if nextc == 9:
                    cols(A2, B2, 2, 3, n2c)
                nextc += 1
            R.dma("sp", gate_d[:, 1024:2048], modrow[:, 5120:6144], reads=[b_modrow], writes=[b_gate_d])
            dump("modrow", modrow[:], [2, 6144], b_modrow)
            R.flush()

        if True:

            with Scope(nc) as _sc:
                QT = _sc.sb([128, NT, 4, 128], BF16)
                KT = _sc.sb([128, NT, 4, 128], BF16)
                V1 = _sc.sb([128, NT, 4, 130], BF16)
                OG = _sc.sb([128, NT, 512], BF16)
                YSC = _sc.sb([128, NT, 4, 128], BF16)
                GZ = _sc.sb([128, NT, 16])
                RR = _sc.sb([128, NT, 8])
                CUM = _sc.sb([128, NT, 8])
                FTOT = _sc.sb([128, NT, 8])
                RMAX = _sc.sb([128, NT, 8])
                G1ROW = _sc.sb([128, 2, 1024])
                b_qt = [Buf() for _ in range(NT)]
                b_kt = [Buf() for _ in range(NT)]
                b_v1 = [Buf() for _ in range(NT)]
                b_og = [Buf() for _ in range(NT)]
                b_ysc = [Buf() for _ in range(NT)]
                b_gz = [Buf() for _ in range(NT)]
                b_gs = [Buf() for _ in range(NT)]
                b_g1 = Buf("g1row")
                for j in range(2):
                    R.dma("sp", G1ROW[:, j, :], gate_d[j:j + 1, 0:1024].to_broadcast([128, 1024]), reads=[b_gate_d], writes=[b_g1])
                for t in range(NT):
                    R.op("pool", lambda e, t=t: e.memset(V1[:, t, :, 128:130], 1.0), writes=[b_v1[t]], parts=True)

                if True:
                    dump("HT", HT[:], [128, 8, NTOK], b_ht[0], BF16)
                    phase1(nc, R, locals())
                    R.flush()
                for nm, tt, bb, shp, dtt in [("QT", QT, b_qt, [128, NT, 4, 128], BF16), ("KT", KT, b_kt, [128, NT, 4, 128], BF16),
                                             ("V1", V1, b_v1, [128, NT, 4, 130], BF16), ("OG", OG, b_og, [128, NT, 512], BF16),
                                             ("YSC", YSC, b_ysc, [128, NT, 4, 128], BF16), ("GZ", GZ, b_gz, [128, NT, 16], F32),
                                             ("RR", RR, b_gs, [128, NT, 8], F32), ("CUM", CUM, b_gs, [128, NT, 8], F32),
                                             ("FTOT", FTOT, b_gs, [128, NT, 8], F32), ("RMAX", RMAX, b_gs, [128, NT, 8], F32)]:
                    dump(nm, tt[:], shp, bb[0], dtt)
                phase2(nc, R, locals())
                R.flush()

            ffn_all(nc, R, locals())
        R.flush(final=True)
    return nc, dbg_outs


def make_builder(nc, R, _sc, HT, b_ht, tiles, src_of, src_bufs, A, B, b_mod, ident_b, b_const, typ_of, ntoks, col_of=None, anti=None):
    xts = [_sc.sb([128, D], F32) for _ in range(3)]
    xns = [_sc.sb([128, D], BF16) for _ in range(2)]
    sqj = _sc.sb([128, D], BF16)
    ss = _sc.sb([128, 4], F32)
    rs = _sc.sb([128, 4], F32)
    pts = [[_sc.ps([128, 8, 128], BF16) for _ in range(2)] for _ in range(2)]
    ptf = [_sc.ps([128, 8, 64], F32) for _ in range(2)] if anti is not None else None
    b_xt = [Buf() for _ in range(3)]
    b_xn = [Buf() for _ in range(2)]
    b_pt = [[Buf(), Buf()] for _ in range(2)]
    b_ss = [Buf() for _ in range(4)]
    b_rs = [Buf() for _ in range(4)]
    b_sq = Buf()

    def s1(i):
        t = tiles[i]
        n = ntoks[i]
        xs, ns, q = i % 3, i % 2, i % 4
        R.dma("sp", xts[xs][0:n, :], src_of(t), reads=[src_bufs[i]] if src_bufs[i] is not None else [], writes=[b_xt[xs]])
        R.op("act", lambda e: e.activation(out=sqj[0:n, :], in_=xts[xs][0:n, :], func=AF.Square, accum_out=ss[0:n, q:q + 1]),
             reads=[b_xt[xs]], writes=[b_sq, b_ss[q]])
        R.op("act", lambda e: e.activation(out=rs[0:n, q:q + 1], in_=ss[0:n, q:q + 1], func=AF.Sqrt, bias=EPS, scale=1.0 / D),
             reads=[b_ss[q]], writes=[b_rs[q]])
        R.op("dve", lambda e: e.reciprocal(out=rs[0:n, q:q + 1], in_=rs[0:n, q:q + 1]), reads=[b_rs[q]], writes=[b_rs[q]])
        R.op("dve", lambda e: e.tensor_scalar(out=xns[ns][0:n, :], in0=xts[xs][0:n, :], scalar1=rs[0:n, q:q + 1], scalar2=None, op0=ALU.mult),
             reads=[b_xt[xs], b_rs[q]], writes=[b_xn[ns]])

    def s2(i):
        t = tiles[i]
        n = ntoks[i]
        ns, ps_ = i % 2, i % 2
        c0 = col_of(t) if col_of else t * 128
        for kc in range(8):
            eo, k2 = EO_OF[kc], K2_OF[kc]
            if anti is None:
                R.op("pe", lambda e, kc=kc, eo=eo, k2=k2: e.transpose(pts[ps_][eo][:, k2, 0:n], xns[ns][0:n, kc * 128:(kc + 1) * 128], ident_b[0:n, 0:n]),
                     reads=[b_xn[ns], b_const], writes=[b_pt[ps_][eo]])
            else:
                R.op("pe", lambda e, kc=kc, eo=eo, k2=k2: e.matmul(ptf[eo][:, k2, 0:n], lhsT=xns[ns][0:n, kc * 128:(kc + 1) * 128], rhs=anti[0:n, 0:n], start=True, stop=True),
                     reads=[b_xn[ns], b_const], writes=[b_pt[ps_][eo]])
        j = typ_of(t)
        for kc in range(8):
            eo, k2 = EO_OF[kc], K2_OF[kc]
            src_ps = pts[ps_][eo] if anti is None else ptf[eo]
            if eo == 0:
                R.op("act", lambda e, kc=kc, k2=k2, src_ps=src_ps: e.activation(out=HT[:, kc, c0:c0 + n], in_=src_ps[:, k2, 0:n], func=AF.Identity,
                                                                               scale=A[:, kc, j:j + 1], bias=B[:, kc, j:j + 1]),
                     reads=[b_pt[ps_][0], b_mod], writes=[b_ht[i]], parts=True)
            else:
                R.op("dve", lambda e, kc=kc, k2=k2, src_ps=src_ps: e.tensor_scalar(out=HT[:, kc, c0:c0 + n], in0=src_ps[:, k2, 0:n],
                                                                                  scalar1=A[:, kc, j:j + 1], scalar2=B[:, kc, j:j + 1], op0=ALU.mult, op1=ALU.add),
                     reads=[b_pt[ps_][1], b_mod], writes=[b_ht[i]], parts=True)

    return s1, s2, len(tiles)


def build_hT(nc, R, HT, b_ht, tiles, src_of, src_bufs, A, B, b_mod, ident_b, b_const, typ_of, ntoks, col_of=None, anti=None):
    with Scope(nc) as _sc:
        s1, s2, nt = make_builder(nc, R, _sc, HT, b_ht, tiles, src_of, src_bufs, A, B, b_mod, ident_b, b_const, typ_of, ntoks, col_of, anti)
        s1(0)
        for i in range(nt):
            if i + 1 < nt:
                s1(i + 1)
            s2(i)
        R.flush()


def phase1(nc, R, L):
    HT, b_ht = L["HT"], L["b_ht"]
    QT, KT, V1, OG, YSC, GZ, RR, CUM, FTOT, RMAX = (L[k] for k in ["QT", "KT", "V1", "OG", "YSC", "GZ", "RR", "CUM", "FTOT", "RMAX"])
    b_qt, b_kt, b_v1, b_og, b_ysc, b_gz, b_gs = (L[k] for k in ["b_qt", "b_kt", "b_v1", "b_og", "b_ysc", "b_gz", "b_gs"])
    csc, bg_row, b_const = L["csc"], L["bg_row"], L["b_const"]
    tri0_f, tri1_f, ones_f, ident_f, i8 = L["tri0_f"], L["tri1_f"], L["ones_f"], L["ident_f"], L["i8"]
    wic_d, wiqk_d, wiv_d, wio_d, wig_d = L["wic_d"], L["wiqk_d"], L["wiv_d"], L["wio_d"], L["wig_d"]
    with Scope(nc) as _sc:
        ws0 = _sc.sb([128, 8, 512], BF16)
        ws1 = _sc.sb([128, 8, 512], BF16)
        wgt = _sc.sb([128, 8, 16], BF16)
        scc0 = _sc.sb([128, 512], F32)
        scc1 = _sc.sb([128, 512], F32)
        u0 = _sc.sb([128, 512], F32)
        u1 = _sc.sb([128, 512], F32)
        ac0 = _sc.sb([128, 512], F32)
        ac1 = _sc.sb([128, 512], F32)
        LTALL = _sc.sb([128, NT, 8], F32)
        MX80 = _sc.sb([80, 2], F32)
        DG80 = _sc.sb([80, 2, 80], F32)
        p0 = _sc.ps([128, 512], F32)
        p1 = _sc.ps([128, 512], F32)
        p2 = _sc.ps([128, 512], F32)
        p3 = _sc.ps([128, 512], F32)
        p4 = _sc.ps([128, 512], F32)
        p5 = _sc.ps([128, 512], F32)
        p6 = _sc.ps([128, 512], F32)
        p7 = _sc.ps([128, 512], F32)
        assert nc.sbuf_bytes_remaining >= 8 * NTOK * 2 + 64, nc.sbuf_bytes_remaining
        ws = [ws0, ws1]
        b_ws = [Buf(), Buf()]
        pp = [p0, p1, p2, p3, p4, p5, p6, p7]
        b_pp = [Buf() for _ in range(8)]
        sccs, us, acs = [scc0, scc1], [u0, u1], [ac0, ac1]
        b_scc, b_u, b_ac = [Buf(), Buf()], [Buf(), Buf()], [Buf(), Buf()]
        wslot = [0]

        def next_w():
            s = wslot[0] % 2
            wslot[0] += 1
            return s

        def grp_bufs(bl, g):
            return [bl[4 * g + i] for i in range(4)]

        step = 0
        for cb in range(4):
            s = next_w()
            R.dma("pool", ws[s][:, :, 0:384], wic_d[cb], writes=[b_ws[s]])
            for g in range(5):
                pb = (step % 2) * 3
                k2 = step % 2
                step += 1
                hbufs = grp_bufs(b_ht, g)
                for part in range(3):
                    for kc in range(8):
                        R.op("pe", lambda e, s=s, part=part, kc=kc, g=g, pb=pb: e.matmul(pp[pb + part][:, :], lhsT=ws[s][:, kc, part * 128:(part + 1) * 128],
                                                                                     rhs=HT[:, kc, g * 512:(g + 1) * 512], start=(kc == 0), stop=(kc == 7)),
                             reads=[b_ws[s]] + hbufs, writes=[b_pp[pb + part]])
                W = 256 if g == 0 else 64

                def v3(ap, W=W):
                    return ap.rearrange("p (r w) -> p r w", w=W)

                R.op("act", lambda e, k2=k2, pb=pb: e.activation(out=sccs[k2][:, :], in_=pp[pb + 1][:, :], func=AF.Copy),
                     reads=[b_pp[pb + 1]], writes=[b_scc[k2]])
                R.op("dve", lambda e, k2=k2, pb=pb: e.tensor_tensor(out=us[k2][:, :], in0=pp[pb + 2][:, :], in1=sccs[k2][:, :], op=ALU.mult),
                     reads=[b_pp[pb + 2], b_scc[k2]], writes=[b_u[k2]])
                R.op("act", lambda e, k2=k2, cb=cb: e.activation(out=acs[k2][:, :], in_=us[k2][:, :], func=AF.Identity, scale=csc[:, cb, 1:2], bias=csc[:, cb, 3:4]),
                     reads=[b_u[k2], b_const], writes=[b_ac[k2]])
                R.op("dve", lambda e, k2=k2, cb=cb, v3=v3, W=W: e.scalar_tensor_tensor(out=v3(acs[k2][:, :])[:, :, 1:W], in0=v3(us[k2][:, :])[:, :, 0:W - 1], scalar=csc[:, cb, 0:1],
                                                                                      in1=v3(acs[k2][:, :])[:, :, 1:W], op0=ALU.mult, op1=ALU.add),
                     reads=[b_u[k2], b_ac[k2], b_const], writes=[b_ac[k2]])
                R.op("dve", lambda e, k2=k2, cb=cb, v3=v3, W=W: e.scalar_tensor_tensor(out=v3(acs[k2][:, :])[:, :, 0:W - 1], in0=v3(us[k2][:, :])[:, :, 1:W], scalar=csc[:, cb, 2:3],
                                                                                      in1=v3(acs[k2][:, :])[:, :, 0:W - 1], op0=ALU.mult, op1=ALU.add),
                     reads=[b_u[k2], b_ac[k2], b_const], writes=[b_ac[k2]])
                R.op("dve", lambda e, k2=k2, pb=pb, g=g, cb=cb: e.tensor_tensor(out=YSC[:, 4 * g:4 * g + 4, cb, :], in0=acs[k2][:, :].rearrange("p (a b) -> p a b", b=128),
                                                                              in1=pp[pb][:, :].rearrange("p (a b) -> p a b", b=128), op=ALU.mult),
                     reads=[b_ac[k2], b_pp[pb]], writes=grp_bufs(b_ysc, g), parts=True)
        for blk in range(8):
            s = next_w()
            R.dma("pool", ws[s][:, :, 0:128], wiqk_d[blk], writes=[b_ws[s]])
            hh = blk % 4
            for g in range(5):
                pb = 6 + (step % 2)
                step += 1
                hbufs = grp_bufs(b_ht, g)
                for kc in range(8):
                    R.op("pe", lambda e, s=s, kc=kc, g=g, pb=pb: e.matmul(pp[pb][:, :], lhsT=ws[s][:, kc, 0:128], rhs=HT[:, kc, g * 512:(g + 1) * 512],
                                                                     start=(kc == 0), stop=(kc == 7)),
                         reads=[b_ws[s]] + hbufs, writes=[b_pp[pb]])
                if blk < 4:
                    R.op("act", lambda e, pb=pb, g=g, hh=hh: e.activation(out=QT[:, 4 * g:4 * g + 4, hh, :], in_=pp[pb][:, :].rearrange("p (a b) -> p a b", b=128),
                                                                        func=AF.Copy, scale=float(DH ** -0.5)),
                         reads=[b_pp[pb]], writes=grp_bufs(b_qt, g), parts=True)
                else:
                    R.op("dve", lambda e, pb=pb, g=g, hh=hh: e.tensor_copy(out=KT[:, 4 * g:4 * g + 4, hh, :], in_=pp[pb][:, :].rearrange("p (a b) -> p a b", b=128)),
                         reads=[b_pp[pb]], writes=grp_bufs(b_kt, g), parts=True)
        for which in range(2):
            s = next_w()
            R.dma("pool", ws[s][:, :, :], wiv_d if which == 0 else wio_d, writes=[b_ws[s]])
            for t in range(NT):
                pb = step % 6
                step += 1
                for kc in range(8):
                    R.op("pe", lambda e, s=s, kc=kc, t=t, pb=pb: e.matmul(pp[pb][:, :], lhsT=HT[:, kc, t * 128:(t + 1) * 128], rhs=ws[s][:, kc, :],
                                                                     start=(kc == 0), stop=(kc == 7)),
                         reads=[b_ws[s], b_ht[t]], writes=[b_pp[pb]])
                if which == 0:
                    eng = "dve" if t % 2 == 0 else "act"
                    if eng == "dve":
                        R.op("dve", lambda e, pb=pb, t=t: e.tensor_copy(out=V1[:, t, :, 0:128], in_=pp[pb][:, :].rearrange("p (a b) -> p a b", b=128)),
                             reads=[b_pp[pb]], writes=[b_v1[t]], parts=True)
                    else:
                        R.op("act", lambda e, pb=pb, t=t: e.activation(out=V1[:, t, :, 0:128], in_=pp[pb][:, :].rearrange("p (a b) -> p a b", b=128), func=AF.Copy),
                             reads=[b_pp[pb]], writes=[b_v1[t]], parts=True)
                else:
                    R.op("act", lambda e, pb=pb, t=t: e.activation(out=OG[:, t, :], in_=pp[pb][:, :], func=AF.Sigmoid),
                         reads=[b_pp[pb]], writes=[b_og[t]])
        b_wg = Buf()
        R.dma("pool", wgt[:], wig_d, writes=[b_wg])
        b_lt = [Buf() for _ in range(4)]
        b_mx8 = [Buf() for _ in range(4)]
        b_dg8 = [Buf() for _ in range(4)]
        for t in range(NT):
            pb = step % 6
            step += 1
            q4 = t % 4
            for kc in range(8):
                R.op("pe", lambda e, kc=kc, t=t, pb=pb: e.matmul(pp[pb][:, 0:16], lhsT=HT[:, kc, t * 128:(t + 1) * 128], rhs=wgt[:, kc, :],
                                                                start=(kc == 0), stop=(kc == 7)),
                     reads=[b_wg, b_ht[t]], writes=[b_pp[pb]])
            R.op("dve", lambda e, pb=pb, t=t: e.tensor_tensor(out=GZ[:, t, :], in0=pp[pb][:, 0:16], in1=bg_row[:, :], op=ALU.add),
                 reads=[b_pp[pb], b_const], writes=[b_gz[t]])
        b_ltall, b_pga, b_mx80, b_dg80 = Buf(), Buf(), Buf(), Buf()
        gz5 = GZ[:, :, :].rearrange("p t (d y h) -> p t d y h", d=2, y=2)
        lt4 = LTALL[:, :, :].rearrange("p t (d h) -> p t d h", d=2)
        R.op("act", lambda e: e.activation(out=lt4, in_=gz5[:, :, :, 1, :], func=AF.Exp, scale=-1.0), reads=b_gz, writes=[b_ltall])
        R.op("act", lambda e: e.activation(out=LTALL[:, :, :], in_=LTALL[:, :, :], func=AF.Ln, bias=1.0, scale=1.0), reads=[b_ltall], writes=[b_ltall])
        pA, pB, pC = pp[6], pp[7], pp[0]
        R.op("pe", lambda e: e.matmul(pA[:, 0:80].rearrange("p (t h) -> p t h", h=4), lhsT=tri0_f[:, :], rhs=LTALL[:, :, 0:4], start=True, stop=True),
             reads=[b_ltall, b_const], writes=[b_pp[6]])
        R.op("pe", lambda e: e.matmul(pA[:, 80:160].rearrange("p (t h) -> p t h", h=4), lhsT=tri1_f[:, :], rhs=LTALL[:, :, 4:8], start=True, stop=True),
             reads=[b_ltall, b_const], writes=[b_pp[6]])
        R.op("pe", lambda e: e.matmul(pA[:, 160:320].rearrange("p (t h) -> p t h", h=8), lhsT=ones_f[:, :], rhs=LTALL[:, :, :], start=True, stop=True),
             reads=[b_ltall, b_const], writes=[b_pp[6]])
        R.op("dve", lambda e: e.tensor_copy(out=CUM[:, :, 0:4], in_=pA[:, 0:80].rearrange("p (t h) -> p t h", h=4)), reads=[b_pp[6]], writes=b_gs, parts=True)
        R.op("dve", lambda e: e.tensor_copy(out=CUM[:, :, 4:8], in_=pA[:, 80:160].rearrange("p (t h) -> p t h", h=4)), reads=[b_pp[6]], writes=b_gs, parts=True)
        R.op("dve", lambda e: e.tensor_copy(out=FTOT[:, :, :], in_=pA[:, 160:320].rearrange("p (t h) -> p t h", h=8)), reads=[b_pp[6]], writes=b_gs, parts=True)
        R.op("dve", lambda e: e.tensor_tensor(out=RR[:, :, :].rearrange("p t (d h) -> p t d h", d=2), in0=gz5[:, :, :, 0, :],
                                              in1=CUM[:, :, :].rearrange("p t (d h) -> p t d h", d=2), op=ALU.add),
             reads=b_gz + b_gs, writes=b_gs)
        for k in range(2):
            R.op("pe", lambda e, k=k: e.matmul(pB[0:80, k * 128:(k + 1) * 128], lhsT=RR[:, :, :].rearrange("p t h -> p (t h)")[:, k * 80:(k + 1) * 80], rhs=ident_f[:, :], start=True, stop=True),
                 reads=b_gs + [b_const], writes=[b_pp[7]])
        R.op("dve", lambda e: e.tensor_reduce(out=MX80[:, :], in_=pB[0:80, 0:256].rearrange("p (k n) -> p k n", k=2), axis=AX.X, op=ALU.max),
             reads=[b_pp[7]], writes=[b_mx80])
        for k in range(2):
            R.op("dve", lambda e, k=k: e.tensor_scalar(out=DG80[:, k, :], in0=ident_f[0:80, 0:80], scalar1=MX80[:, k:k + 1], scalar2=None, op0=ALU.mult),
                 reads=[b_mx80, b_const], writes=[b_dg80])
        for k in range(2):
            R.op("pe", lambda e, k=k: e.matmul(pC[:, k * 80:(k + 1) * 80], lhsT=ones_f[0:80, :], rhs=DG80[:, k, :], start=True, stop=True),
                 reads=[b_dg80, b_const], writes=[b_pp[0]])
        R.op("dve", lambda e: e.tensor_copy(out=RMAX[:, :, :], in_=pC[:, 0:160].rearrange("p (t h) -> p t h", h=8)), reads=[b_pp[0]], writes=b_gs)
        R.flush()


def phase2(nc, R, L):
    QT, KT, V1, OG, YSC, GZ, RR, CUM, FTOT, RMAX, G1ROW = (L[k] for k in ["QT", "KT", "V1", "OG", "YSC", "GZ", "RR", "CUM", "FTOT", "RMAX", "G1ROW"])
    b_qt, b_kt, b_v1, b_og, b_ysc, b_gs, b_g1 = (L[k] for k in ["b_qt", "b_kt", "b_v1", "b_og", "b_ysc", "b_gs", "b_g1"])
    b_const, ident_b, ident_f, mk0_b, mk1_b, sel = (L[k] for k in ["b_const", "ident_b", "ident_f", "mk0_b", "mk1_b", "sel"])
    xin, x1s_d, b_x1s = L["xin"], L["x1s_d"], L["b_x1s"]
    wout_d, mhn_d, s0_d, m0_d = L["wout_d"], L["mhn_d"], L["s0_d"], L["m0_d"]
    oC_d, on_d, om_d = L["oC_d"], L["on_d"], L["om_d"]
    sx_in, sx_out, hx_in, b_sxin, b_sxout, b_hxin, RG = (L[k] for k in ["sx_in", "sx_out", "hx_in", "b_sxin", "b_sxout", "b_hxin", "RG"])
    dump = L["dump"]
    masks = [mk0_b, mk1_b]
    with Scope(nc) as _sc:
        SNAP = _sc.sb([128, NT, 4, 130], BF16)
        WOUT = _sc.sb([128, 8, 1024], BF16)
        MHN = _sc.sb([128, 512], F32)
        MX = _sc.sb([128, NT, 8], F32)
        MPRE = _sc.sb([128, NT + 1, 8], F32)
        WG = _sc.sb([128, NT, 8], F32)
        DEC = _sc.sb([128, NT, 8], F32)
        CLP = _sc.sb([128, NT, 8], F32)
        TMPG = _sc.sb([128, NT, 8], F32)
        MFIN = _sc.sb([128, 3, 8], F32)
        SST = _sc.sb([128, 2, 4, 129], F32)
        SDB = _sc.sb([128, 4, 130], BF16)
        VW = _sc.sb([128, 2, 4, 130], BF16)
        KTK = _sc.sb([128, 4, 128], BF16)
        KTK2 = _sc.sb([128, 2, 4, 128], BF16)
        KTK2 = _sc.sb([128, 2, 4, 128], BF16)
        PT = _sc.sb([128, 2, 4, 128], BF16)
        DN = _sc.sb([128, 12], F32)
        HACC = _sc.sb([128, 2, 512], F32)
        HG0 = _sc.sb([128, 512], F32)
        HG = _sc.sb([128, 512], BF16)
        HMT = _sc.sb([128, 4, 128], BF16)
        SQ4 = _sc.sb([128, 4, 128], BF16)
        SS4 = _sc.sb([128, 2, 4], F32)
        XT = _sc.sb([128, 2, D], F32)
        X1 = _sc.sb([128, D], F32)
        CTR = _sc.sb([128, 2, 128], F32)
        SXG = _sc.sb([128, 2, 520], F32)
        SXT = _sc.sb([128, 520], F32)
        pST = _sc.ps([128, 512], F32)
        pND = _sc.ps([128, 3, 512], F32)
        pUP = _sc.ps([128, 2, 512], F32)
        pTR = _sc.ps([128, 1024], BF16)
        pWO = _sc.ps([128, 512], F32)
        b_snap = [Buf() for _ in range(NT)]
        b_wout, b_mhn = Buf(), Buf()
        R.dma("pool", WOUT[:], wout_d, writes=[b_wout])
        R.dma("sp", MHN[:], mhn_d, writes=[b_mhn])
        b_chain = Buf("chain")
        b_tmpg = Buf()
        b_mfin = Buf()
        b_sst = [[Buf() for _ in range(4)] for _ in range(2)]
        b_pst, b_pnd, b_ptr, b_pwo = Buf(), [Buf() for _ in range(3)], [Buf(), Buf()], Buf()
        _pup = Buf()
        b_pup = [_pup, _pup]
        b_sdb, b_vw, b_ktk, b_ptt = Buf(), [Buf(), Buf()], Buf(), [Buf(), Buf()]
        b_ktk2 = [Buf(), Buf()]
        b_ktk2 = [Buf(), Buf()]
        b_dn = Buf()
        b_hacc, b_ss4, b_xt = [[Buf() for _ in range(4)] for _ in range(2)], [Buf(), Buf()], [Buf(), Buf()]
        b_hg0, b_hg, b_hmt, b_sq, b_x1 = Buf(), Buf(), Buf(), Buf(), Buf()
        b_x1h = [Buf(), Buf()]
        b_sq4 = [Buf() for _ in range(4)]
        b_ctr = [Buf(), Buf()]
        b_sxg, b_sxt = Buf(), Buf()
        b_sxs = b_sxg
        cnt = {"ctr": 0}

        def nxt(k, n):
            v = cnt[k] % n
            cnt[k] += 1
            return v

        seqs = {"P0": [0, 1], "P1": [2, 3], "S": list(range(4, NT))}

        def d4(d):
            return slice(d * 4, d * 4 + 4)

        def nd_slot(i):
            return pND[:, i // 3, (i % 3) * 129:(i % 3) * 129 + 129]

        def up_slot(h):
            return pUP[:, h // 3, (h % 3) * 129:(h % 3) * 129 + 129]

        def chain(seq, d, init):
            tiles = seqs[seq] if d == 0 else seqs[seq][::-1]
            sidx = {"P0": 0, "P1": 1, "S": 2}[seq]
            first = tiles[0]
            if init is None:
                R.op("dve", lambda e: e.memset(MPRE[:, first, d4(d)], 0.0), writes=[b_chain])
            else:
                ap, bufs = init
                R.op("dve", lambda e: e.tensor_copy(out=MPRE[:, first, d4(d)], in_=ap), reads=bufs, writes=[b_chain])
            for i, c in enumerate(tiles):
                R.op("dve", lambda e, c=c: e.tensor_tensor(out=MX[:, c, d4(d)], in0=MPRE[:, c, d4(d)], in1=RMAX[:, c, d4(d)], op=ALU.max),
                     reads=[b_chain, b_gs[c]], writes=[b_chain])
                if i + 1 < len(tiles):
                    nx = tiles[i + 1]
                    R.op("dve", lambda e, c=c, nx=nx: e.tensor_tensor(out=MPRE[:, nx, d4(d)], in0=MX[:, c, d4(d)], in1=FTOT[:, c, d4(d)], op=ALU.subtract),
                         reads=[b_chain, b_gs[c]], writes=[b_chain])
                else:
                    R.op("dve", lambda e, c=c: e.tensor_tensor(out=MFIN[:, sidx, d4(d)], in0=MX[:, c, d4(d)], in1=FTOT[:, c, d4(d)], op=ALU.subtract),
                         reads=[b_chain, b_gs[c]], writes=[b_mfin])

        def weights(t0, t1, d):
            gs = [b_gs[t] for t in range(t0, t1)]
            for (dst, a, b_) in [(WG, RR, MX), (DEC, MPRE, MX), (CLP, CUM, MX)]:
                R.op("dve", lambda e, a=a, b_=b_: e.tensor_tensor(out=TMPG[:, t0:t1, d4(d)], in0=a[:, t0:t1, d4(d)], in1=b_[:, t0:t1, d4(d)], op=ALU.subtract),
                     reads=[b_chain] + gs, writes=[b_tmpg])
                R.op("act", lambda e, dst=dst: e.activation(out=dst[:, t0:t1, d4(d)], in_=TMPG[:, t0:t1, d4(d)], func=AF.Exp),
                     reads=[b_tmpg], writes=[b_chain])

        def st_vw(c, d):
            for h in range(4):
                col = d * 4 + h
                R.op("act", lambda e, h=h, col=col: e.activation(out=VW[:, d, h, 0:129], in_=V1[:, c, h, 0:129], func=AF.Identity, scale=WG[:, c, col:col + 1]),
                     reads=[b_v1[c], b_chain], writes=[b_vw[d]], parts=True)

        def st_ktr(c):
            for h in range(4):
                R.op("pe", lambda e, h=h: e.transpose(pTR[:, h * 128:(h + 1) * 128], KT[:, c, h, :], ident_b[:, :]), reads=[b_kt[c], b_const], writes=[b_ptr[0]])
            R.op("act", lambda e: e.activation(out=KTK[:, :, :], in_=pTR[:, 0:512].rearrange("p (a b) -> p a b", b=128), func=AF.Copy), reads=[b_ptr[0]], writes=[b_ktk])

        def st_upd(c, d):
            for h in range(4):
                R.op("pe", lambda e, h=h: e.matmul(up_slot(h), lhsT=KTK[:, h, :], rhs=VW[:, d, h, 0:129], start=True, stop=True),
                     reads=[b_ktk, b_vw[d]], writes=[b_pup[h // 3]])
            for h in range(4):
                col = d * 4 + h
                R.op("dve", lambda e, h=h, col=col: e.scalar_tensor_tensor(out=SST[:, d, h, :], in0=SST[:, d, h, :], scalar=DEC[:, c, col:col + 1], in1=up_slot(h),
                                                                            op0=ALU.mult, op1=ALU.add),
                     reads=[b_sst[d][h], b_chain, b_pup[h // 3]], writes=[b_sst[d][h]])

        pSTb = pST[:, :].bitcast(BF16)

        def state_only_pass(seq):
            cl = seqs[seq]

            def prep(c, par):
                trb = pTR if par == 0 else pSTb
                btr = b_ptr[0] if par == 0 else b_pst
                for h in range(4):
                    R.op("act", lambda e, h=h: e.activation(out=VW[:, par, h, 0:129], in_=V1[:, c, h, 0:129], func=AF.Identity, scale=WG[:, c, h:h + 1]),
                         reads=[b_v1[c], b_chain], writes=[b_vw[par]], parts=True)
                for h in range(4):
                    R.op("pe", lambda e, h=h: e.transpose(trb[:, h * 128:(h + 1) * 128], KT[:, c, h, :], ident_b[:, :]),
                         reads=[b_kt[c], b_const], writes=[btr])
                R.op("act", lambda e: e.activation(out=KTK2[:, par, :, :], in_=trb[:, 0:512].rearrange("p (a b) -> p a b", b=128), func=AF.Copy),
                     reads=[btr], writes=[b_ktk2[par]])

            def upd(c, par):
                def slot(h):
                    return up_slot(h) if par == 0 else nd_slot(h)
                bb = [b_pup[0]] if par == 0 else [b_pnd[0], b_pnd[1]]
                for h in range(4):
                    R.op("pe", lambda e, h=h: e.matmul(slot(h), lhsT=KTK2[:, par, h, :], rhs=VW[:, par, h, 0:129], start=True, stop=True),
                         reads=[b_ktk2[par], b_vw[par]], writes=bb)
                for h in range(4):
                    R.op("act", lambda e, h=h: e.activation(out=SNAP[:, c, h, 0:129], in_=SST[:, 0, h, :], func=AF.Identity, scale=DEC[:, c, h:h + 1]),
                         reads=[b_sst[0][h], b_chain], writes=[b_snap[c]], parts=True)
                for h in range(4):
                    R.op("dve", lambda e, h=h: e.scalar_tensor_tensor(out=SST[:, 0, h, :], in0=SST[:, 0, h, :], scalar=DEC[:, c, h:h + 1], in1=slot(h),
                                                                       op0=ALU.mult, op1=ALU.add),
                         reads=[b_sst[0][h], b_chain] + bb, writes=[b_sst[0][h]])

            prep(cl[0], 0)
            for k, c in enumerate(cl):
                if k + 1 < len(cl):
                    prep(cl[k + 1], (k + 1) % 2)
                upd(c, k % 2)

        def emit_state(seq, d):
            si = {"P0": 0, "P1": 1}[seq]
            for h in range(4):
                cs = nxt("ctr", 2)
                R.op("pe", lambda e, h=h: e.matmul(pWO[:, 0:128], lhsT=SST[:, d, h, 0:128], rhs=ident_f[:, :], start=True, stop=True),
                     reads=[b_sst[d][h], b_const], writes=[b_pwo])
                R.op("act", lambda e, cs=cs: e.activation(out=CTR[:, cs, :], in_=pWO[:, 0:128], func=AF.Copy), reads=[b_pwo], writes=[b_ctr[cs]])
                R.dma("sp", oC_d[si, d, h, :, :], CTR[:, cs, :], reads=[b_ctr[cs]])
                R.dma("sp", on_d[si, d, h, :].rearrange("(p o) -> p o", o=1), SST[:, d, h, 128:129], reads=[b_sst[d][h]])
            R.dma("sp", om_d[si, d, :].rearrange("(o h) -> o h", o=1), MFIN[0:1, si, d4(d)], reads=[b_mfin])

        def init_state(d, src=None):
            for h in range(4):
                if src is None:
                    R.op("dve", lambda e, h=h: e.memset(SST[:, d, h, :], 0.0), writes=[b_sst[d][h]])
                else:
                    ap_of, bufs = src
                    R.op("dve", lambda e, h=h: e.tensor_copy(out=SST[:, d, h, :], in_=ap_of(h)), reads=bufs, writes=[b_sst[d][h]])

        def full_pass(seq):
            typ = 1 if seq == "S" else 0
            cl = seqs[seq][::-1]

            def s1a(c, par):
                R.dma("sp", XT[:, par, :], xin[c * 128:(c + 1) * 128, :], writes=[b_xt[par]])
                for h in range(4):
                    R.op("act", lambda e, h=h: e.activation(out=SDB[:, h, 0:129], in_=SST[:, 1, h, :], func=AF.Identity, scale=DEC[:, c, 4 + h:5 + h]),
                         reads=[b_sst[1][h], b_chain], writes=[b_sdb], parts=True)
                st_vw(c, 0)
                st_vw(c, 1)
                for h in range(4):
                    R.op("pe", lambda e, h=h: e.matmul(pST[:, h * 128:(h + 1) * 128], lhsT=KT[:, c, h, :], rhs=QT[:, c, h, :], start=True, stop=True),
                         reads=[b_kt[c], b_qt[c]], writes=[b_pst])
                st_ktr(c)

            def s1b(c, par):
                for d in range(2):
                    R.op("dve", lambda e, d=d: e.tensor_tensor(out=PT[:, d, :, :], in0=pST[:, :].rearrange("p (a b) -> p a b", b=128),
                                                               in1=masks[d][:, :].unsqueeze(1).to_broadcast([128, 4, 128]), op=ALU.mult),
                         reads=[b_pst, b_const], writes=[b_ptt[d]])
                for d in range(2):
                    for h in range(4):
                        i = d * 4 + h
                        R.op("pe", lambda e, d=d, h=h, i=i: e.matmul(nd_slot(i), lhsT=PT[:, d, h, :], rhs=VW[:, d, h, 0:129], start=True, stop=False),
                             reads=[b_ptt[d], b_vw[d]], writes=[b_pnd[i // 3]])
                        if d == 0:
                            R.op("pe", lambda e, h=h, i=i: e.matmul(nd_slot(i), lhsT=QT[:, c, h, :], rhs=SNAP[:, c, h, 0:129], start=False, stop=True),
                                 reads=[b_qt[c], b_snap[c]], writes=[b_pnd[i // 3]])
                        else:
                            R.op("pe", lambda e, h=h, i=i: e.matmul(nd_slot(i), lhsT=QT[:, c, h, :], rhs=SDB[:, h, 0:129], start=False, stop=True),
                                 reads=[b_qt[c], b_sdb], writes=[b_pnd[i // 3]])
                st_upd(c, 1)

            def s1c(c, par):
                R.op("act", lambda e: e.activation(out=DN[:, 0:6].rearrange("p (b t) -> p b t", t=3), in_=pND[:, 0:2, 0:387].rearrange("p b (t w) -> p b t w", w=129)[:, :, :, 128],
                                                   func=AF.Abs),
                     reads=b_pnd, writes=[b_dn])
                R.op("act", lambda e: e.activation(out=DN[:, 6:8], in_=pND[:, 2, 0:258].rearrange("p (t w) -> p t w", w=129)[:, :, 128], func=AF.Abs),
                     reads=b_pnd, writes=[b_dn])
                R.op("dve", lambda e: e.tensor_tensor(out=DN[:, 0:8], in0=DN[:, 0:8], in1=CLP[:, c, :], op=ALU.max), reads=[b_dn, b_chain], writes=[b_dn])
                R.op("dve", lambda e: e.reciprocal(out=DN[:, 0:8], in_=DN[:, 0:8]), reads=[b_dn], writes=[b_dn])
                for h in range(4):
                    R.op("dve", lambda e, h=h: e.tensor_scalar(out=HACC[:, par, h * 128:(h + 1) * 128], in0=nd_slot(h)[:, 0:128], scalar1=DN[:, h:h + 1], scalar2=None, op0=ALU.mult),
                         reads=[b_pnd[h // 3], b_dn], writes=[b_hacc[par][h]])
                for h in range(4):
                    i = 4 + h
                    R.op("dve", lambda e, h=h, i=i: e.scalar_tensor_tensor(out=HACC[:, par, h * 128:(h + 1) * 128], in0=nd_slot(i)[:, 0:128], scalar=DN[:, i:i + 1],
                                                                            in1=HACC[:, par, h * 128:(h + 1) * 128], op0=ALU.mult, op1=ALU.add),
                         reads=[b_pnd[i // 3], b_dn, b_hacc[par][h]], writes=[b_hacc[par][h]])

            def hg0(c):
                R.op("pool", lambda e: e.tensor_tensor(out=HG0[:, :], in0=OG[:, c, :], in1=MHN[:, :], op=ALU.mult), reads=[b_og[c], b_mhn], writes=[b_hg0])

            def s2a(c, par):
                for h in range(4):
                    R.op("act", lambda e, h=h: e.activation(out=SQ4[:, h, :], in_=HACC[:, par, h * 128:(h + 1) * 128], func=AF.Square, accum_out=SS4[:, par, h:h + 1]),
                         reads=[b_hacc[par][h]], writes=[b_sq4[h], b_ss4[par]], parts=True)
                R.op("act", lambda e: e.activation(out=SS4[:, par, :], in_=SS4[:, par, :], func=AF.Sqrt, bias=EPS, scale=1.0 / DH),
                     reads=[b_ss4[par]], writes=[b_ss4[par]])
                R.op("dve", lambda e: e.reciprocal(out=SS4[:, par, :], in_=SS4[:, par, :]), reads=[b_ss4[par]], writes=[b_ss4[par]])
                for h in range(4):
                    R.op("dve", lambda e, h=h: e.scalar_tensor_tensor(out=HG[:, h * 128:(h + 1) * 128], in0=HACC[:, par, h * 128:(h + 1) * 128], scalar=SS4[:, par, h:h + 1],
                                                                       in1=HG0[:, h * 128:(h + 1) * 128], op0=ALU.mult, op1=ALU.mult),
                         reads=[b_hacc[par][h], b_ss4[par], b_hg0], writes=[b_hg], parts=True)

            def s2b(c, par):
                for h in range(4):
                    R.op("pe", lambda e, h=h: e.transpose(pTR[:, 512 + h * 128:512 + (h + 1) * 128], HG[:, h * 128:(h + 1) * 128], ident_b[:, :]),
                         reads=[b_hg, b_const], writes=[b_ptr[1]])
                R.op("act", lambda e: e.activation(out=HMT[:, :, :], in_=pTR[:, 512:1024].rearrange("p (a b) -> p a b", b=128), func=AF.Copy),
                     reads=[b_ptr[1]], writes=[b_hmt])
                wo_half(c, par, 0)

            def wo_half(c, par, half):
                for kb in range(8):
                    if kb < 4:
                        R.op("pe", lambda e, kb=kb: e.matmul(pWO[:, :], lhsT=YSC[:, c, kb, :], rhs=WOUT[:, kb, half * 512:(half + 1) * 512], start=(kb == 0), stop=False),
                             reads=[b_ysc[c], b_wout], writes=[b_pwo])
                    else:
                        R.op("pe", lambda e, kb=kb: e.matmul(pWO[:, :], lhsT=HMT[:, kb - 4, :], rhs=WOUT[:, kb, half * 512:(half + 1) * 512], start=False, stop=(kb == 7)),
                             reads=[b_hmt, b_wout], writes=[b_pwo])
                R.op("dve", lambda e: e.tensor_tensor(out=X1[:, half * 512:(half + 1) * 512], in0=pWO[:, :], in1=G1ROW[:, typ, half * 512:(half + 1) * 512], op=ALU.mult),
                     reads=[b_pwo, b_g1], writes=[b_x1h[half]])
                R.op("pool", lambda e: e.tensor_tensor(out=XT[:, par, half * 512:(half + 1) * 512], in0=XT[:, par, half * 512:(half + 1) * 512],
                                                       in1=X1[:, half * 512:(half + 1) * 512], op=ALU.add),
                     reads=[b_x1h[half], b_xt[par]], writes=[b_xt[par]])

            def s2c(c, par):
                wo_half(c, par, 1)
                R.dma("sp", x1s_d[c * 128:(c + 1) * 128, :], XT[:, par, :], reads=[b_xt[par]], writes=[b_x1s[c]])
                if c == NT - 1:
                    R.dma("sp", hx_in[:, :], XT[64:128, par, :], reads=[b_xt[par]], writes=[b_hxin])
                    hx_out, b_hxout, hx_scr, b_hs = L["hx_out"], L["b_hxout"], L["hx_scr"], L["b_hs"]
                    R.collective(lambda e: e.collective_compute("AllGather", ALU.bypass, replica_groups=RG, ins=[hx_in.ap().opt()], outs=[hx_out.ap().opt()]),
                                 reads=[b_hxin], writes=[b_hxout])

            prev = None
            for k, c in enumerate(cl):
                par = k % 2
                s1a(c, par)
                if prev is not None:
                    s2a(*prev)
                s1b(c, par)
                if prev is not None:
                    s2b(*prev)
                s1c(c, par)
                if prev is not None:
                    s2c(*prev)
                hg0(c)
                prev = (c, par)
            s2a(*prev)
            s2b(*prev)
            s2c(*prev)
            if seq == "S":
                hx_out, b_hxout, hx_scr, b_hs = L["hx_out"], L["b_hxout"], L["hx_scr"], L["b_hs"]
                for hf in range(2):
                    cs_ = slice(hf * 512, (hf + 1) * 512)
                    R.dma("pool", SXG[0:64, 0, 0:512], hx_out[0:64, cs_], reads=[b_hxout], writes=[b_sxg])
                    R.dma("pool", SXG[0:64, 1, 0:512], hx_out[64:128, cs_], reads=[b_hxout], writes=[b_sxg])
                    R.op("dve", lambda e: e.tensor_scalar(out=SXG[0:64, 0, 0:512], in0=SXG[0:64, 0, 0:512], scalar1=sel[0:64, 1:2], scalar2=None, op0=ALU.mult),
                         reads=[b_sxg, b_const], writes=[b_sxg])
                    R.op("dve", lambda e: e.tensor_scalar(out=SXG[0:64, 1, 0:512], in0=SXG[0:64, 1, 0:512], scalar1=sel[0:64, 0:1], scalar2=None, op0=ALU.mult),
                         reads=[b_sxg, b_const], writes=[b_sxg])
                    R.op("dve", lambda e: e.tensor_tensor(out=SXG[0:64, 0, 0:512], in0=SXG[0:64, 0, 0:512], in1=SXG[0:64, 1, 0:512], op=ALU.add),
                         reads=[b_sxg], writes=[b_sxg])
                    R.dma("pool", hx_scr[:, cs_], SXG[0:64, 0, 0:512], reads=[b_sxg], writes=[b_hs])

        chain("P0", 0, None)
        chain("P1", 0, None)
        R.dma("sp", SXT[:, 0:4], m0_d, writes=[b_sxt])
        chain("S", 0, (SXT[:, 0:4], [b_sxt]))
        weights(0, NT, 0)
        R.dma("sp", SXG[:, 0, 0:516].rearrange("p (h v) -> p h v", v=129), s0_d, writes=[b_sxs])
        init_state(0, (lambda h: SXG[:, 0, h * 129:(h + 1) * 129], [b_sxs]))
        state_only_pass("S")
        for h in range(4):
            R.op("dve", lambda e, h=h: e.tensor_copy(out=SXG[:, 0, h * 129:(h + 1) * 129], in_=SST[:, 0, h, :]), reads=[b_sst[0][h]], writes=[b_sxs])
        R.op("dve", lambda e: e.tensor_copy(out=SXG[:, 0, 516:520], in_=MFIN[:, 2, 0:4]), reads=[b_mfin], writes=[b_sxs])
        R.dma("pool", sx_in[:, :], SXG[:, 0, :], reads=[b_sxs], writes=[b_sxin])
        R.collective(lambda e: e.collective_compute("AllGather", ALU.bypass, replica_groups=RG, ins=[sx_in.ap().opt()], outs=[sx_out.ap().opt()]),
                     reads=[b_sxin], writes=[b_sxout])
        chain("P0", 1, None)
        chain("P1", 1, None)
        weights(0, 4, 1)
        for seq in ["P0", "P1"]:
            init_state(0)
            state_only_pass(seq)
            emit_state(seq, 0)
            init_state(1)
            full_pass(seq)
            emit_state(seq, 1)
        R.dma("pool", SXG[:, :, :], sx_out.ap().rearrange("(r p) n -> p r n", p=128), reads=[b_sxout], writes=[b_sxg])
        R.op("dve", lambda e: e.tensor_scalar(out=SXT[:, :], in0=SXG[:, 0, :], scalar1=sel[:, 1:2], scalar2=None, op0=ALU.mult), reads=[b_sxg, b_const], writes=[b_sxt])
        R.op("dve", lambda e: e.scalar_tensor_tensor(out=SXT[:, :], in0=SXG[:, 1, :], scalar=sel[:, 0:1], in1=SXT[:, :], op0=ALU.mult, op1=ALU.add),
             reads=[b_sxg, b_sxt, b_const], writes=[b_sxt])
        chain("S", 1, (SXT[:, 516:520], [b_sxt]))
        weights(4, NT, 1)
        init_state(1, (lambda h: SXT[:, h * 129:(h + 1) * 129], [b_sxt]))
        full_pass("S")
        dump("MX", MX[:], [128, NT, 8], b_chain)
        dump("WG", WG[:], [128, NT, 8], b_chain)
        dump("SNAP", SNAP[:], [128, NT, 4, 130], b_snap[0], BF16)
        R.flush()


def ffn_up(nc, R, L, _sc, seg, H2T, b_h2t, ACTS, b_acts, n_own, n_all, ntile, nbank, side=None):
    cff, b_const, wup_d = L["cff"], L["b_const"], L["wup_d"]
    U = _sc.sb([128, 2, n_all], F32)
    ACA = _sc.sb([128, 2, n_own], F32)
    ACG = _sc.sb([128, 2, n_own], F32)
    WU = _sc.sb([128, 3, 8, 256], BF16)
    fp = [_sc.ps([128, 512], F32) for _ in range(nbank)]
    b_fp = [Buf() for _ in range(nbank)]
    b_wu = [Buf() for _ in range(3)]
    b_u = [Buf(), Buf()]
    b_aca = [Buf(), Buf()]
    b_acg = [Buf(), Buf()]
    groups = [(g0, min(512, n_all - g0)) for g0 in range(0, n_all, 512)]
    pcount = 0

    def finish(j):
        g2 = j % 2
        R.op("act", lambda e: e.activation(out=ACG[:, g2, :], in_=ACG[:, g2, :], func=AF.Silu), reads=[b_acg[g2]], writes=[b_acg[g2]])
        R.op("pool", lambda e: e.tensor_tensor(out=ACTS[:, j, :], in0=ACG[:, g2, :], in1=ACA[:, g2, :], op=ALU.mult),
             reads=[b_acg[g2], b_aca[g2]], writes=[b_acts[j]])

    for j in range(min(2, NJ)):
        R.dma("pool", WU[:, j % 3, :, :], wup_d[j], writes=[b_wu[j % 3]])
    for j in range(NJ):
        s = j % 3
        if j + 2 < NJ:
            R.dma("pool", WU[:, (j + 2) % 3, :, :], wup_d[j + 2], writes=[b_wu[(j + 2) % 3]])
        a2 = j % 2
        for part in range(2):
            blk = j + part * NJ
            us = part
            acc = ACA[:, a2, :] if part == 0 else ACG[:, a2, :]
            b_acc = b_aca[a2] if part == 0 else b_acg[a2]
            for (g0, gn) in groups:
                pb = pcount % nbank
                pcount += 1
                hb = [b_h2t[min(ti, ntile)] for ti in range(g0 // 128, (g0 + gn + 127) // 128)]
                for kc in range(8):
                    R.op("pe", lambda e, kc=kc, g0=g0, gn=gn, pb=pb, s=s, part=part: e.matmul(fp[pb][:, 0:gn], lhsT=WU[:, s, kc, part * 128:(part + 1) * 128],
                                                                                         rhs=H2T[:, kc, g0:g0 + gn], start=(kc == 0), stop=(kc == 7)),
                         reads=[b_wu[s]] + hb, writes=[b_fp[pb]])
                R.op("act", lambda e, g0=g0, gn=gn, pb=pb, us=us: e.activation(out=U[:, us, g0:g0 + gn], in_=fp[pb][:, 0:gn], func=AF.Copy),
                     reads=[b_fp[pb]], writes=[b_u[us]], parts=True)
                if g0 < n_own:
                    on = min(gn, n_own - g0)
                    R.op("act", lambda e, g0=g0, on=on, pb=pb, acc=acc, blk=blk: e.activation(out=acc[:, g0:g0 + on], in_=fp[pb][:, 0:on], func=AF.Identity,
                                                                                         scale=cff[:, blk, 1:2], bias=cff[:, blk, 3:4]),
                         reads=[b_fp[pb], b_const], writes=[b_acc], parts=True)
            if seg == "prompt":
                def v3(ap):
                    return ap.rearrange("p (r w) -> p r w", w=256)
                R.op("dve", lambda e, acc=acc, us=us, blk=blk, v3=v3: e.scalar_tensor_tensor(out=v3(acc)[:, :, 1:256], in0=v3(U[:, us, 0:512])[:, :, 0:255], scalar=cff[:, blk, 0:1],
                                                                                        in1=v3(acc)[:, :, 1:256], op0=ALU.mult, op1=ALU.add),
                     reads=[b_u[us], b_acc, b_const], writes=[b_acc])
                R.op("dve", lambda e, acc=acc, us=us, blk=blk, v3=v3: e.scalar_tensor_tensor(out=v3(acc)[:, :, 0:255], in0=v3(U[:, us, 0:512])[:, :, 1:256], scalar=cff[:, blk, 2:3],
                                                                                        in1=v3(acc)[:, :, 0:255], op0=ALU.mult, op1=ALU.add),
                     reads=[b_u[us], b_acc, b_const], writes=[b_acc])
            else:
                R.op("dve", lambda e, acc=acc, us=us, blk=blk: e.scalar_tensor_tensor(out=acc[:, 64:2048], in0=U[:, us, 0:1984], scalar=cff[:, blk, 0:1],
                                                                                 in1=acc[:, 64:2048], op0=ALU.mult, op1=ALU.add),
                     reads=[b_u[us], b_acc, b_const], writes=[b_acc])
                R.op("dve", lambda e, acc=acc, us=us, blk=blk: e.scalar_tensor_tensor(out=acc[:, 0:2048], in0=U[:, us, 64:2112], scalar=cff[:, blk, 2:3],
                                                                                 in1=acc[:, 0:2048], op0=ALU.mult, op1=ALU.add),
                     reads=[b_u[us], b_acc, b_const], writes=[b_acc])
        if j >= 1:
            finish(j - 1)
        if side is not None:
            side(j)
    finish(NJ - 1)


def ffn_down_consts(nc, R, L, _sc, typ):
    wdn_d, fn_d, gate_d, b_gate_d = L["wdn_d"], L["fn_d"], L["gate_d"], L["b_gate_d"]
    WD = _sc.sb([128, NJ, D], BF16)
    FNR = _sc.sb([128, D], F32)
    G2ROW = _sc.sb([128, D], F32)
    b_wd = [Buf() for _ in range(NJ // 2)]
    b_fnr, b_g2 = Buf(), Buf()
    for jj in range(0, NJ, 2):
        R.dma("pool", WD[:, jj:jj + 2, :], wdn_d[:, jj:jj + 2, :], writes=[b_wd[jj // 2]])
    R.dma("sp", FNR[:], fn_d, writes=[b_fnr])
    R.dma("sp", G2ROW[:], gate_d[typ:typ + 1, 1024:2048].to_broadcast([128, 1024]), reads=[b_gate_d], writes=[b_g2])
    return WD, FNR, G2ROW, b_wd, b_fnr, b_g2


def ffn_down(nc, R, L, _sc, tiles, ACTS, b_acts, dc):
    x1s_d, b_x1s, y_d = L["x1s_d"], L["b_x1s"], L["y_d"]
    WD, FNR, G2ROW, b_wd, b_fnr, b_g2 = dc
    X1L = _sc.sb([128, 2, D], F32)
    X2 = _sc.sb([128, 2, D], F32)
    YT = _sc.sb([128, 2, D], F32)
    SQ2 = _sc.sb([128, D], BF16)
    SSD = _sc.sb([128, 4], F32)
    dp = [_sc.ps([128, 512], F32) for _ in range(4)]
    b_dp = [Buf() for _ in range(4)]
    b_sq2 = Buf()
    b_x1l, b_x2, b_yt = [Buf(), Buf()], [Buf(), Buf()], [Buf(), Buf()]
    b_ssd = [Buf() for _ in range(4)]
    pc = 0
    for i, t in enumerate(tiles):
        hs = i % 2
        q = i % 4
        R.dma("sp", X1L[:, hs, :], x1s_d[t * 128:(t + 1) * 128, :], reads=[b_x1s[t]], writes=[b_x1l[hs]])
        for half in range(2):
            pb = pc % 4
            pc += 1
            for j in range(NJ):
                R.op("pe", lambda e, j=j, i=i, half=half, pb=pb: e.matmul(dp[pb][:, :], lhsT=ACTS[:, j, i * 128:(i + 1) * 128], rhs=WD[:, j, half * 512:(half + 1) * 512],
                                                                        start=(j == 0), stop=(j == NJ - 1)),
                     reads=[b_acts[j], b_wd[j // 2]], writes=[b_dp[pb]])
            R.op("dve", lambda e, half=half, pb=pb, hs=hs: e.tensor_tensor(out=X2[:, hs, half * 512:(half + 1) * 512], in0=dp[pb][:, :], in1=G2ROW[:, half * 512:(half + 1) * 512], op=ALU.mult),
                 reads=[b_dp[pb], b_g2], writes=[b_x2[hs]], parts=True)
            R.op("pool", lambda e, half=half, hs=hs: e.tensor_tensor(out=X2[:, hs, half * 512:(half + 1) * 512], in0=X2[:, hs, half * 512:(half + 1) * 512],
                                                                   in1=X1L[:, hs, half * 512:(half + 1) * 512], op=ALU.add),
                 reads=[b_x2[hs], b_x1l[hs]], writes=[b_x2[hs]])
        R.op("act", lambda e, hs=hs, q=q: e.activation(out=SQ2[:, :], in_=X2[:, hs, :], func=AF.Square, accum_out=SSD[:, q:q + 1]),
             reads=[b_x2[hs]], writes=[b_sq2, b_ssd[q]])
        R.op("act", lambda e, q=q: e.activation(out=SSD[:, q:q + 1], in_=SSD[:, q:q + 1], func=AF.Sqrt, bias=EPS, scale=1.0 / D),
             reads=[b_ssd[q]], writes=[b_ssd[q]])
        R.op("dve", lambda e, q=q: e.reciprocal(out=SSD[:, q:q + 1], in_=SSD[:, q:q + 1]),
             reads=[b_ssd[q]], writes=[b_ssd[q]])
        R.op("dve", lambda e, hs=hs, q=q: e.scalar_tensor_tensor(out=YT[:, hs, :], in0=X2[:, hs, :], scalar=SSD[:, q:q + 1], in1=FNR[:, :], op0=ALU.mult, op1=ALU.mult),
             reads=[b_x2[hs], b_ssd[q], b_fnr], writes=[b_yt[hs]])
        R.dma("sp", y_d[t * 128:(t + 1) * 128, :], YT[:, hs, :], reads=[b_yt[hs]])


def ffn_all(nc, R, L):
    A2, B2, b_mod, ident_b, anti_b, b_const, sel = (L[k] for k in ["A2", "B2", "b_mod", "ident_b", "anti_b", "b_const", "sel"])
    x1s_d, b_x1s = L["x1s_d"], L["b_x1s"]
    hx_in, hx_out, b_hxin, b_hxout, RG = L["hx_in"], L["hx_out"], L["b_hxin"], L["b_hxout"], L["RG"]
    dump = L["dump"]
    p_tiles = [0, 1, 2, 3]
    s_tiles = list(range(4, NT))
    NS = len(s_tiles)
    hx_scr, b_hs = L["hx_scr"], L["b_hs"]
    with Scope(nc) as _so:
        H2Ts = _so.sb([128, 8, 2112], BF16)
        b_h2ts = [Buf() for _ in range(NS + 1)]
        with Scope(nc) as _sp:
            ACTSp = _sp.sb([128, NJ, 512], BF16)
            b_actsp = [Buf() for _ in range(NJ)]
            H2Tp = _sp.sb([128, 8, 512], BF16)
            b_h2tp = [Buf() for _ in range(5)]
            dcp = ffn_down_consts(nc, R, L, _sp, 0)
            with Scope(nc) as _s1:
                build_hT(nc, R, H2Tp, b_h2tp, p_tiles, lambda t: x1s_d[t * 128:(t + 1) * 128, :], [b_x1s[t] for t in p_tiles], A2, B2, b_mod, ident_b, b_const,
                         lambda t: 0, [128] * 4)
            with Scope(nc) as _s2:
                s1, s2, nb = make_builder(nc, R, _s2, H2Ts, b_h2ts, s_tiles, lambda t: x1s_d[t * 128:(t + 1) * 128, :], [b_x1s[t] for t in s_tiles],
                                          A2, B2, b_mod, ident_b, b_const, lambda t: 1, [128] * NS, col_of=lambda t: (t - 4) * 128)
                state = {"i": 0}
                s1(0)

                def side(j):
                    i = state["i"]
                    if i < nb:
                        if i + 1 < nb:
                            s1(i + 1)
                        s2(i)
                        state["i"] = i + 1

                ffn_up(nc, R, L, _s2, "prompt", H2Tp, b_h2tp, ACTSp, b_actsp, 512, 512, 4, 4, side=side)
                while state["i"] < nb:
                    side(0)
                R.flush()
            with Scope(nc) as _s3:
                ffn_down(nc, R, L, _s3, p_tiles, ACTSp, b_actsp, dcp)
                R.flush()
        build_hT(nc, R, H2Ts, [b_h2ts[NS]], [0], lambda t: hx_scr, [b_hs], A2, B2, b_mod, ident_b, b_const,
                 lambda t: 1, [64], col_of=lambda t: 2048, anti=anti_b)
        dump("H2T_sample", H2Ts[:], [128, 8, 2112], b_h2ts[0], BF16)
        with Scope(nc) as _ss:
            ACTSs = _ss.sb([128, NJ, 2048], BF16)
            b_actss = [Buf() for _ in range(NJ)]
            with Scope(nc) as _s4:
                ffn_up(nc, R, L, _s4, "sample", H2Ts, b_h2ts, ACTSs, b_actss, 2048, 2112, NS, 8)
                R.flush()
            dump("ACTS_sample", ACTSs[:], [128, NJ, 2048], b_actss[0], BF16)
            with Scope(nc) as _s5:
                dcs = ffn_down_consts(nc, R, L, _s5, 1)
                ffn_down(nc, R, L, _s5, s_tiles, ACTSs, b_actss, dcs)
                R.flush()


_CACHE = {}


def _consts():
    i = np.arange(128)
    ident = np.eye(128, dtype=np.float32)
    anti = np.zeros((128, 128), np.float32)
    anti[np.arange(64), 63 - np.arange(64)] = 1.0
    mk0 = (i[:, None] <= i[None, :]).astype(np.float32)
    mk1 = (i[:, None] >= i[None, :]).astype(np.float32)
    selj = np.zeros((2, 256), np.float32)
    selj[0, 0:128] = 1.0
    selj[1, 128:256] = 1.0
    return dict(ident=ident, antiI=anti, maskT0=mk0, maskT1=mk1, ones=np.ones((128, 128), np.float32),
                selj=selj, i2=np.eye(2, dtype=np.float32), i8=np.eye(8, dtype=np.float32))


def _kmajor(w):
    return np.ascontiguousarray(w.reshape(8, 128, -1).transpose(1, 0, 2))


def make_in_maps(x_prompt, x_sample, state_C, state_n, state_m, c, c_ctx, w_mod, b_mod, norm1, w_in, b_gate,
                 conv_sc_w, conv_sc_b, mh_norm, w_out, norm2, w_up, conv_ffn_w, conv_ffn_b, w_down, final_norm):
    f = np.float32
    A = lambda a: np.asarray(a, dtype=f)
    x_prompt, x_sample, state_C, state_n, state_m, c, c_ctx = map(A, (x_prompt, x_sample, state_C, state_n, state_m, c, c_ctx))
    w_mod, b_mod, norm1, w_in, b_gate = A(w_mod)[0], A(b_mod)[0], A(norm1)[0], A(w_in)[0], A(b_gate)[0]
    conv_sc_w, conv_sc_b, mh_norm, w_out, norm2 = A(conv_sc_w)[0], A(conv_sc_b)[0], A(mh_norm)[0], A(w_out)[0], A(norm2)[0]
    w_up, conv_ffn_w, conv_ffn_b, w_down, final_norm = A(w_up)[0], A(conv_ffn_w)[0], A(conv_ffn_b)[0], A(w_down)[0], A(final_norm)
    cst = _consts()
    shared = dict(cst)
    shared["w_mod_r"] = np.ascontiguousarray(_kmajor(w_mod).reshape(128, 8, 12, 512).transpose(2, 0, 1, 3))
    shared["b_mod2"] = np.ascontiguousarray(np.broadcast_to(b_mod[None, :], (2, 6144)))
    shared["norm1c"] = np.ascontiguousarray(norm1.reshape(8, 128).T)
    shared["norm2c"] = np.ascontiguousarray(norm2.reshape(8, 128).T)
    wk = _kmajor(w_in)
    shared["wic"] = np.ascontiguousarray(np.stack([np.concatenate([wk[:, :, cb * 128:(cb + 1) * 128], wk[:, :, 512 + cb * 128:512 + (cb + 1) * 128],
                                                                   wk[:, :, 1024 + cb * 128:1024 + (cb + 1) * 128]], axis=2) for cb in range(4)]))
    shared["wiqk"] = np.ascontiguousarray(np.stack([wk[:, :, 1536 + b * 128:1536 + (b + 1) * 128] for b in range(8)]))
    shared["wiv"] = np.ascontiguousarray(wk[:, :, 2560:3072])
    shared["wio"] = np.ascontiguousarray(wk[:, :, 3072:3584])
    shared["mhn_row"] = np.ascontiguousarray(np.broadcast_to(mh_norm[None, :], (128, 512)))
    shared["w_out_r"] = _kmajor(w_out)
    wu = _kmajor(w_up)
    shared["w_up_r"] = np.ascontiguousarray(np.stack([np.concatenate([wu[:, :, j * 128:(j + 1) * 128], wu[:, :, DFF + j * 128:DFF + (j + 1) * 128]], axis=2)
                                                      for j in range(NJ)]))
    shared["w_down_r"] = np.ascontiguousarray(w_down.reshape(NJ, 128, D).transpose(1, 0, 2))
    shared["fn_row"] = np.ascontiguousarray(np.broadcast_to(final_norm[None, :], (128, D)))
    maps = []
    for core in range(8):
        par = core % 2
        b = core // 2
        m = dict(shared)
        ps = [x_prompt[2 * core], x_prompt[2 * core + 1]]
        xs = x_sample[b, 0:2048] if par == 0 else x_sample[b, 2048:4096]
        if par == 1:
            ps = [p[::-1] for p in ps]
            xs = xs[::-1]
        m["xin"] = np.ascontiguousarray(np.concatenate(ps + [xs], axis=0))
        cv = np.stack([c_ctx, c[b]], axis=1)
        m["cT"] = np.ascontiguousarray(cv.reshape(8, 128, 2).transpose(1, 0, 2))
        gperm = [0, 1, 2, 3] if par == 0 else [2, 3, 0, 1]
        wg = wk[:, :, 3584:3600].reshape(128, 8, 4, 4)[:, :, gperm, :].reshape(128, 8, 16)
        m["wig"] = np.ascontiguousarray(wg)
        bg = b_gate[gperm, :].reshape(16)
        m["bgate_row"] = np.ascontiguousarray(np.broadcast_to(bg[None, :], (128, 16)))
        tap = [0, 1, 2] if par == 0 else [2, 1, 0]
        csc = np.stack([conv_sc_w[tap[0]], conv_sc_w[tap[1]], conv_sc_w[tap[2]], conv_sc_b], axis=1)
        m["conv_sc"] = np.ascontiguousarray(csc.reshape(4, 128, 4).transpose(1, 0, 2))
        cf = np.stack([conv_ffn_w[tap[0]], conv_ffn_w[tap[1]], conv_ffn_w[tap[2]], conv_ffn_b], axis=1)
        m["conv_ffn"] = np.ascontiguousarray(cf.reshape(44, 128, 4).transpose(1, 0, 2))
        dsel = par
        C0 = state_C[b, 0, dsel]
        n0 = state_n[b, 0, dsel]
        s0 = np.concatenate([C0.transpose(2, 0, 1), n0.T[:, :, None]], axis=2)
        m["s0"] = np.ascontiguousarray(s0)
        m["m0"] = np.ascontiguousarray(np.broadcast_to(state_m[b, 0, dsel][None, :], (128, 4)))
        sel = np.zeros((128, 2), f)
        sel[:, par] = 1.0
        m["sel"] = sel
        maps.append(m)
    return maps


def assemble(results):
    f = np.float32
    y_prompt = np.zeros((16, 256, D), f)
    y_sample = np.zeros((4, 4096, D), f)
    new_C = np.zeros((16, 1, 2, 4, 128, 128), f)
    new_n = np.zeros((16, 1, 2, 4, 128), f)
    new_m = np.zeros((16, 1, 2, 4), f)
    for core in range(8):
        r = results[core]
        par = core % 2
        b = core // 2
        y = np.asarray(r["y"], dtype=f)
        for i in range(2):
            yp = y[i * 256:(i + 1) * 256]
            y_prompt[2 * core + i] = yp[::-1] if par else yp
            for dl in range(2):
                dg = dl if par == 0 else 1 - dl
                new_C[2 * core + i, 0, dg] = r["oC"][i, dl]
                new_n[2 * core + i, 0, dg] = r["on"][i, dl]
                new_m[2 * core + i, 0, dg] = r["om"][i, dl]
        ys = y[512:2560]
        if par == 0:
            y_sample[b, 0:2048] = ys
        else:
            y_sample[b, 2048:4096] = ys[::-1]
    return (y_prompt, y_sample, new_C, new_n, new_m)


def kernel(**inputs):
    maps = make_in_maps(**inputs)
    if "nc" not in _CACHE:
        _CACHE["nc"] = build_program()[0]
    res = run_bass_kernel_spmd(_CACHE["nc"], maps, core_ids=list(range(8)))
    return assemble(res.results)
```

```python
import contextlib
import numpy as np
import concourse.bass as bass
import concourse.mybir as mybir
from concourse.bass_utils import run_bass_kernel_spmd

F32 = mybir.dt.float32
BF16 = mybir.dt.bfloat16
AF = mybir.ActivationFunctionType
ALU = mybir.AluOpType
AX = mybir.AxisListType

D = 1024
NT = 20
NTOK = 2560
DH = 128
NH = 4
DFF = 2816
NJ = 22
EPS = 1e-6
ENGS = ["sp", "act", "pool", "dve", "pe"]


class Scope:
    def __init__(self, nc):
        self.nc = nc
        self.stack = contextlib.ExitStack()

    def __enter__(self):
        self.stack.__enter__()
        return self

    def __exit__(self, *a):
        return self.stack.__exit__(*a)

    _n = [0]

    def sb(self, shape, dt=F32):
        Scope._n[0] += 1
        return self.stack.enter_context(self.nc.sbuf_tensor("sb%d" % Scope._n[0], list(shape), dt))

    def ps(self, shape, dt=F32):
        Scope._n[0] += 1
        return self.stack.enter_context(self.nc.psum_tensor("ps%d" % Scope._n[0], list(shape), dt))


class Op:
    __slots__ = ("eng", "fn", "waits", "sig", "val", "dma", "cc", "sem", "block", "extra", "store")

    def __init__(self, eng, fn, block):
        self.eng = eng
        self.fn = fn
        self.waits = []
        self.sig = False
        self.val = None
        self.dma = False
        self.cc = False
        self.sem = None
        self.block = block
        self.extra = []
        self.store = False


class Buf:
    __slots__ = ("name", "ws", "r", "parts", "gen_deps")

    def __init__(self, name=""):
        self.name = name
        self.ws = []
        self.r = []
        self.parts = False
        self.gen_deps = []


class Rec:
    def __init__(self, nc, nring=8):
        self.nc = nc
        self.engs = {"sp": nc.sync, "act": nc.scalar, "pool": nc.gpsimd, "dve": nc.vector, "pe": nc.tensor}
        self.pending = {e: [] for e in ENGS}
        self.block = 0
        self.csem = {e: nc.alloc_semaphore("c_" + e) for e in ["act", "pool", "dve", "pe"]}
        self.ccount = {e: 0 for e in self.csem}
        self.nring = nring
        self.dsem = {q: [nc.alloc_semaphore("d_%s%d" % (q, i)) for i in range(nring)] for q in ["sp", "pool"]}
        self.duse = {q: [0] * nring for q in self.dsem}
        self.dnext = {q: 0 for q in self.dsem}
        self.ccsem = nc.alloc_semaphore("ccsem")
        self.cccount = 0

    def _deps(self, op, reads, writes, parts=False):
        deps = {}
        is_async = op.dma or op.cc

        def add(d):
            if d is None:
                return
            d_async = d.dma or d.cc
            if not d_async and d.block != self.block:
                return
            if not d_async and not is_async and d.eng == op.eng and op.eng == "pe":
                return
            deps[id(d)] = d

        for b in reads:
            for ww in b.ws:
                add(ww)
        for b in writes:
            same_gen = parts and b.parts and not b.r and b.ws
            if same_gen:
                for gd in b.gen_deps:
                    add(gd)
            else:
                b.gen_deps = list(b.ws) + list(b.r)
                for ww in b.ws:
                    add(ww)
            for rr in b.r:
                add(rr)
        for d in deps.values():
            if not (d.dma or d.cc):
                d.sig = True
        op.waits = list(deps.values())
        for b in reads:
            if b in writes:
                continue
            if not is_async:
                b.r = [x for x in b.r if (x.dma or x.cc or x.eng != op.eng)]
            b.r.append(op)
        for b in writes:
            same_gen = parts and b.parts and not b.r and b.ws
            if same_gen:
                if not is_async:
                    b.ws = [x for x in b.ws if (x.dma or x.cc or x.eng != op.eng)]
                b.ws.append(op)
            else:
                b.ws = [op]
            b.parts = parts
            b.r = []

    def op(self, eng, fn, reads=(), writes=(), parts=False):
        o = Op(eng, fn, self.block)
        self._deps(o, list(reads), list(writes), parts)
        self.pending[eng].append(o)
        return o

    def dma(self, q, out, in_, reads=(), writes=(), parts=False, **kw):
        o = Op(q, lambda e: e.dma_start(out=out, in_=in_, **kw), self.block)
        o.dma = True
        try:
            o.store = "DRam" in type(out.tensor).__name__
        except Exception:
            o.store = True
        slot = self.dnext[q]
        self.dnext[q] = (slot + 1) % self.nring
        prev = self.duse[q][slot]
        o.sem = self.dsem[q][slot]
        if prev > 0:
            o.extra.append((o.sem, 16 * prev))
        self.duse[q][slot] = prev + 1
        o.val = 16 * (prev + 1)
        self._deps(o, list(reads), list(writes), parts)
        self.pending[q].append(o)
        return o

    def collective(self, fn, reads=(), writes=()):
        o = Op("pool", fn, self.block)
        o.cc = True
        self.cccount += 1
        o.sem = self.ccsem
        o.val = self.cccount
        self._deps(o, list(reads), list(writes))
        self.pending["pool"].append(o)
        return o

    def flush(self, final=False):
        if not final and not any(self.pending[e] for e in ENGS):
            return
        for e in ENGS:
            for o in self.pending[e]:
                if o.dma or o.cc:
                    continue
                if o.sig:
                    self.ccount[e] += 1
                    o.val = self.ccount[e]
        names = {"sp": "sync", "act": "scalar", "pool": "gpsimd", "dve": "vector", "pe": "tensor"}
        with self.nc.Block() as block:
            for e in ENGS:
                ops = self.pending[e]
                if not ops and not (final and e == "sp"):
                    continue

                def body(engine, ops=ops, e=e):
                    for o in ops:
                        w = {}
                        for (s, v) in o.extra:
                            w[s.num] = (s, max(v, w.get(s.num, (s, 0))[1]))
                        for d in o.waits:
                            s = d.sem if (d.dma or d.cc) else self.csem[d.eng]
                            w[s.num] = (s, max(d.val, w.get(s.num, (s, 0))[1]))
                        for (s, v) in w.values():
                            engine.wait_ge(s, v)
                        ins = o.fn(engine)
                        if o.dma:
                            ins.then_inc(o.sem, 16)
                        elif o.cc:
                            ins.then_inc(o.sem, 1)
                        elif o.sig:
                            ins.then_inc(self.csem[e], 1)
                    tail = {}
                    for o in ops:
                        if o.dma and o.store:
                            tail[o.sem.num] = (o.sem, max(o.val, tail.get(o.sem.num, (o.sem, 0))[1]))
                    for (sm, v) in tail.values():
                        engine.wait_ge(sm, v)
                    if final and e == "sp":
                        for q in self.dsem:
                            for i in range(self.nring):
                                if self.duse[q][i] > 0:
                                    engine.wait_ge(self.dsem[q][i], 16 * self.duse[q][i])

                getattr(block, names[e])(body)
        self.pending = {e: [] for e in ENGS}
        self.block += 1


def build_program(dbg=()):
    nc = bass.Bass("TRN2", target_bir_lowering=False)
    R = Rec(nc)
    dbg = set(dbg)
    dbg_outs = {}

    def din(name, shape, dt=F32):
        return nc.dram_tensor(name, list(shape), dt, kind="ExternalInput").ap()

    def dout(name, shape, dt=F32):
        return nc.dram_tensor(name, list(shape), dt, kind="ExternalOutput").ap()

    xin = din("xin", [NTOK, D])
    cT_d = din("cT", [128, 8, 2])
    wmod_d = din("w_mod_r", [12, 128, 8, 512])
    bmod_d = din("b_mod2", [2, 6144])
    n1c_d = din("norm1c", [128, 8])
    n2c_d = din("norm2c", [128, 8])
    wic_d = din("wic", [4, 128, 8, 384])
    wiqk_d = din("wiqk", [8, 128, 8, 128])
    wiv_d = din("wiv", [128, 8, 512])
    wio_d = din("wio", [128, 8, 512])
    wig_d = din("wig", [128, 8, 16])
    bg_d = din("bgate_row", [128, 16])
    csc_d = din("conv_sc", [128, 4, 4])
    mhn_d = din("mhn_row", [128, 512])
    wout_d = din("w_out_r", [128, 8, 1024])
    wup_d = din("w_up_r", [NJ, 128, 8, 256])
    cff_d = din("conv_ffn", [128, 44, 4])
    wdn_d = din("w_down_r", [128, NJ, 1024])
    fn_d = din("fn_row", [128, 1024])
    s0_d = din("s0", [128, 4, 129])
    m0_d = din("m0", [128, 4])
    sel_d = din("sel", [128, 2])
    ident_d = din("ident", [128, 128])
    anti_d = din("antiI", [128, 128])
    mk0_d = din("maskT0", [128, 128])
    mk1_d = din("maskT1", [128, 128])
    ones_d = din("ones", [128, 128])
    selj_d = din("selj", [2, 256])
    i2_d = din("i2", [2, 2])
    i8_d = din("i8", [8, 8])

    y_d = dout("y", [NTOK, D])
    oC_d = dout("oC", [2, 2, 4, 128, 128])
    on_d = dout("on", [2, 2, 4, 128])
    om_d = dout("om", [2, 2, 4])

    x1s_d = nc.dram_tensor("x1s", [NTOK, D], F32).ap()
    gate_d = nc.dram_tensor("gate_scr", [2, 2048], F32).ap()
    sx_in = nc.dram_tensor("sx_in", [128, 520], F32)
    sx_out = nc.dram_tensor("sx_out", [256, 520], F32)
    hx_in = nc.dram_tensor("hx_in", [64, 1024], F32)
    hx_out = nc.dram_tensor("hx_out", [128, 1024], F32)
    b_x1s = [Buf("x1s%d" % i) for i in range(NT)]
    b_gate_d = Buf("gate_d")
    b_sxin, b_sxout, b_hxin, b_hxout = Buf(), Buf(), Buf(), Buf()
    hx_scr = nc.dram_tensor("hx_scr", [64, D], F32).ap()
    b_hs = Buf()
    RG = [[0, 1], [2, 3], [4, 5], [6, 7]]

    def dump(name, ap, shape, buf, dt=F32):
        if name not in dbg:
            return
        t = dout("dbg_" + name, shape, dt)
        dbg_outs[name] = (shape, dt)
        R.dma("sp", t, ap, reads=[buf])

    def T(name, shape, dt=F32):
        return nc.sbuf_tensor(list(shape), dt)

    def PS(name, shape, dt=F32):
        return nc.psum_tensor(list(shape), dt)

    with Scope(nc) as _sc:
        ident_f = _sc.sb([128, 128])
        ones_f = _sc.sb([128, 128])
        tri0_f = _sc.sb([128, 128])
        tri1_f = _sc.sb([128, 128])
        ident_b = _sc.sb([128, 128], BF16)
        anti_b = _sc.sb([128, 128], BF16)
        mk0_b = _sc.sb([128, 128], BF16)
        mk1_b = _sc.sb([128, 128], BF16)
        selj = _sc.sb([2, 256])
        i2 = _sc.sb([2, 2])
        i8 = _sc.sb([8, 8])
        n1c = _sc.sb([128, 8])
        n2c = _sc.sb([128, 8])
        bg_row = _sc.sb([128, 16])
        csc = _sc.sb([128, 4, 4])
        cff = _sc.sb([128, 44, 4])
        sel = _sc.sb([128, 2])
        A1 = _sc.sb([128, 8, 2])
        B1 = _sc.sb([128, 8, 2])
        A2 = _sc.sb([128, 8, 2])
        B2 = _sc.sb([128, 8, 2])
        b_const = Buf("const")
        JN = _sc.sb([128, 2])

        def load_consts():
            for (t, d) in [(i2, i2_d), (n1c, n1c_d), (ident_f, ident_d), (ones_f, ones_d), (tri0_f, mk0_d), (tri1_f, mk1_d),
                           (selj, selj_d), (i8, i8_d), (n2c, n2c_d),
                           (bg_row, bg_d), (csc, csc_d), (cff, cff_d), (sel, sel_d)]:
                R.dma("sp", t[:], d, writes=[b_const], parts=True)
            for (t, d) in [(ident_b, ident_d), (anti_b, anti_d), (mk0_b, mk0_d), (mk1_b, mk1_d)]:
                R.dma("pool", t[:], d, writes=[b_const], parts=True)
            R.op("dve", lambda e: e.memset(JN[:, :], 0.0), reads=[b_const], writes=[b_const])
        b_mod = Buf("modcols")

        HT = nc.alloc_sbuf_tensor_at("HT", [128, 8, NTOK], BF16, offset=172032)
        b_ht = [Buf() for _ in range(NT)]
        HT_BYTES = 8 * NTOK * 2
        with Scope(nc) as _sc:
            cT_f = _sc.sb([128, 8, 2])
            cs_b = _sc.sb([128, 8, 2], BF16)
            bmod = _sc.sb([2, 6144])
            modrow = _sc.sb([2, 6144])
            wm0 = _sc.sb([128, 8, 512], BF16)
            wm1 = _sc.sb([128, 8, 512], BF16)
            wm2 = _sc.sb([128, 8, 512], BF16)
            tmpc = _sc.sb([128, 8, 2])
            pm0 = _sc.ps([128, 512])
            pm1 = _sc.ps([128, 512])
            pcol = _sc.ps([128, 512])
            b_ct, b_cs, b_bmod, b_modrow, b_tmpc = Buf(), Buf(), Buf(), Buf("modrow"), Buf()
            R.dma("sp", cT_f[:], cT_d, writes=[b_ct])
            R.dma("sp", bmod[:], bmod_d, writes=[b_bmod])
            R.op("act", lambda e: e.activation(out=cs_b[:], in_=cT_f[:], func=AF.Silu), reads=[b_ct], writes=[b_cs])
            wms = [wm0, wm1, wm2]
            b_wm = [Buf() for _ in range(3)]
            pms = [pm0, pm1]
            b_pm = [Buf(), Buf()]
            for n in range(3):
                R.dma("pool", wms[n][:], wmod_d[n], writes=[b_wm[n]])
            load_consts()
            b_pcol = Buf()

            def colv(vi):
                return pcol[:, vi * 16:(vi + 1) * 16].rearrange("p (a b) -> p a b", b=2)

            def wchunk(n):
                s = n % 3
                if n >= 3:
                    R.dma("pool", wms[s][:], wmod_d[n], writes=[b_wm[s]])
                p = n % 2
                for kc in range(8):
                    R.op("pe", lambda e, kc=kc: e.matmul(pms[p][0:2, :], lhsT=cs_b[:, kc, :], rhs=wms[s][:, kc, :], start=(kc == 0), stop=(kc == 7)),
                         reads=[b_cs, b_wm[s]], writes=[b_pm[p]])
                R.op("dve", lambda e: e.tensor_tensor(out=modrow[:, n * 512:(n + 1) * 512], in0=pms[p][0:2, :], in1=bmod[:, n * 512:(n + 1) * 512], op=ALU.add),
                     reads=[b_pm[p], b_bmod], writes=[b_modrow])

            def cols(Aout, Bout, vsh, vsc, ncol):
                offs = [0, 1024, 3072, 4096]
                for vi in (vsh, vsc):
                    for kc in range(8):
                        c0 = (vi * 8 + kc) * 2
                        off = offs[vi]
                        R.op("pe", lambda e, off=off, kc=kc, c0=c0: e.matmul(pcol[:, c0:c0 + 2], lhsT=modrow[0:2, off + kc * 128: off + (kc + 1) * 128],
                                                                            rhs=i2[:, :], start=True, stop=True),
                             reads=[b_modrow, b_const], writes=[b_pcol])
                R.op("dve", lambda e: e.tensor_scalar(out=tmpc[:], in0=colv(vsc), scalar1=1.0, scalar2=None, op0=ALU.add), reads=[b_pcol], writes=[b_tmpc])
                R.op("dve", lambda e: e.tensor_tensor(out=Aout[:], in0=tmpc[:], in1=ncol[:, :].unsqueeze(2).to_broadcast([128, 8, 2]), op=ALU.mult),
                     reads=[b_tmpc, b_const], writes=[b_mod])
                R.op("dve", lambda e: e.tensor_copy(out=Bout[:], in_=colv(vsh)), reads=[b_pcol], writes=[b_mod])

            bs1, bs2, nb = make_builder(nc, R, _sc, HT, b_ht, list(range(NT)), lambda t: xin[t * 128:(t + 1) * 128, :], [None] * NT,
                                        A1, B1, b_mod, ident_b, b_const, lambda t: 0 if t < 4 else 1, [128] * NT)
            bs1(0)
            for n in range(4):
                wchunk(n)
            cols(A1, B1, 0, 1, n1c)
            nextc = 4
            for i in range(nb):
                if i + 1 < nb:
                    bs1(i + 1)
                bs2(i)
                if i % 2 == 1 and nextc < 12:
                    wchunk(nextc)
                    if nextc == 5:
                        R.dma("sp", gate_d[:, 0:1024], modrow[:, 2048:3072], reads=[b_modrow], writes=[b_gate_d])
                    if nextc == 9:
                        cols(A2, B2, 2, 3, n2c)
                    nextc += 1
            while nextc < 12:
                wchunk(nextc)
                if nextc == 9:
                    cols(A2, B2, 2, 3, n2c)
                nextc += 1
            R.dma("sp", gate_d[:, 1024:2048], modrow[:, 5120:6144], reads=[b_modrow], writes=[b_gate_d])
            dump("modrow", modrow[:], [2, 6144], b_modrow)
            R.flush()

        if True:

            with Scope(nc) as _sc:
                QT = _sc.sb([128, NT, 4, 128], BF16)
                KT = _sc.sb([128, NT, 4, 128], BF16)
                V1 = _sc.sb([128, NT, 4, 130], BF16)
                OG = _sc.sb([128, NT, 512], BF16)
                YSC = _sc.sb([128, NT, 4, 128], BF16)
                GZ = _sc.sb([128, NT, 16])
                RR = _sc.sb([128, NT, 8])
                CUM = _sc.sb([128, NT, 8])
                FTOT = _sc.sb([128, NT, 8])
                RMAX = _sc.sb([128, NT, 8])
                G1ROW = _sc.sb([128, 2, 1024])
                b_qt = [Buf() for _ in range(NT)]
                b_kt = [Buf() for _ in range(NT)]
                b_v1 = [Buf() for _ in range(NT)]
                b_og = [Buf() for _ in range(NT)]
                b_ysc = [Buf() for _ in range(NT)]
                b_gz = [Buf() for _ in range(NT)]
                b_gs = [Buf() for _ in range(NT)]
                b_g1 = Buf("g1row")
                for j in range(2):
                    R.dma("sp", G1ROW[:, j, :], gate_d[j:j + 1, 0:1024].to_broadcast([128, 1024]), reads=[b_gate_d], writes=[b_g1])
                for t in range(NT):
                    R.op("pool", lambda e, t=t: e.memset(V1[:, t, :, 128:130], 1.0), writes=[b_v1[t]], parts=True)

                if True:
                    dump("HT", HT[:], [128, 8, NTOK], b_ht[0], BF16)
                    phase1(nc, R, locals())
                    R.flush()
                for nm, tt, bb, shp, dtt in [("QT", QT, b_qt, [128, NT, 4, 128], BF16), ("KT", KT, b_kt, [128, NT, 4, 128], BF16),
                                             ("V1", V1, b_v1, [128, NT, 4, 130], BF16), ("OG", OG, b_og, [128, NT, 512], BF16),
                                             ("YSC", YSC, b_ysc, [128, NT, 4, 128], BF16), ("GZ", GZ, b_gz, [128, NT, 16], F32),
                                             ("RR", RR, b_gs, [128, NT, 8], F32), ("CUM", CUM, b_gs, [128, NT, 8], F32),
                                             ("FTOT", FTOT, b_gs, [128, NT, 8], F32), ("RMAX", RMAX, b_gs, [128, NT, 8], F32)]:
                    dump(nm, tt[:], shp, bb[0], dtt)
                phase2(nc, R, locals())
                R.flush()

            ffn_all(nc, R, locals())
        R.flush(final=True)
    return nc, dbg_outs


def make_builder(nc, R, _sc, HT, b_ht, tiles, src_of, src_bufs, A, B, b_mod, ident_b, b_const, typ_of, ntoks, col_of=None, anti=None):
    xts = [_sc.sb([128, D], F32) for _ in range(3)]
    xns = [_sc.sb([128, D], BF16) for _ in range(2)]
    sqj = _sc.sb([128, D], BF16)
    ss = _sc.sb([128, 4], F32)
    rs = _sc.sb([128, 4], F32)
    pts = [[_sc.ps([128, 8, 128], BF16) for _ in range(2)] for _ in range(2)]
    ptf = [_sc.ps([128, 4, 128], F32) for _ in range(2)] if anti is not None else None
    b_xt = [Buf() for _ in range(3)]
    b_xn = [Buf() for _ in range(2)]
    b_pt = [[Buf(), Buf()] for _ in range(2)]
    b_ss = [Buf() for _ in range(4)]
    b_rs = [Buf() for _ in range(4)]
    b_sq = Buf()

    def s1(i):
        t = tiles[i]
        n = ntoks[i]
        xs, ns, q = i % 3, i % 2, i % 4
        R.dma("sp", xts[xs][0:n, :], src_of(t), reads=[src_bufs[i]] if src_bufs[i] is not None else [], writes=[b_xt[xs]])
        R.op("act", lambda e: e.activation(out=sqj[0:n, :], in_=xts[xs][0:n, :], func=AF.Square, accum_out=ss[0:n, q:q + 1]),
             reads=[b_xt[xs]], writes=[b_sq, b_ss[q]])
        R.op("act", lambda e: e.activation(out=rs[0:n, q:q + 1], in_=ss[0:n, q:q + 1], func=AF.Sqrt, bias=EPS, scale=1.0 / D),
             reads=[b_ss[q]], writes=[b_rs[q]])
        R.op("dve", lambda e: e.reciprocal(out=rs[0:n, q:q + 1], in_=rs[0:n, q:q + 1]), reads=[b_rs[q]], writes=[b_rs[q]])
        R.op("dve", lambda e: e.tensor_scalar(out=xns[ns][0:n, :], in0=xts[xs][0:n, :], scalar1=rs[0:n, q:q + 1], scalar2=None, op0=ALU.mult),
             reads=[b_xt[xs], b_rs[q]], writes=[b_xn[ns]])

    def s2(i):
        t = tiles[i]
        n = ntoks[i]
        ns, ps_ = i % 2, i % 2
        c0 = col_of(t) if col_of else t * 128
        for kc in range(8):
            eo, k2 = kc % 2, kc // 2
            if anti is None:
                R.op("pe", lambda e, kc=kc, eo=eo, k2=k2: e.transpose(pts[ps_][eo][:, k2, 0:n], xns[ns][0:n, kc * 128:(kc + 1) * 128], ident_b[0:n, 0:n]),
                     reads=[b_xn[ns], b_const], writes=[b_pt[ps_][eo]])
            else:
                R.op("pe", lambda e, kc=kc, eo=eo, k2=k2: e.matmul(ptf[eo][:, k2, 0:n], lhsT=xns[ns][0:n, kc * 128:(kc + 1) * 128], rhs=anti[0:n, 0:n], start=True, stop=True),
                     reads=[b_xn[ns], b_const], writes=[b_pt[ps_][eo]])
        j = typ_of(t)
        for kc in range(8):
            eo, k2 = kc % 2, kc // 2
            src_ps = pts[ps_][eo] if anti is None else ptf[eo]
            if eo == 0:
                R.op("act", lambda e, kc=kc, k2=k2, src_ps=src_ps: e.activation(out=HT[:, kc, c0:c0 + n], in_=src_ps[:, k2, 0:n], func=AF.Identity,
                                                                               scale=A[:, kc, j:j + 1], bias=B[:, kc, j:j + 1]),
                     reads=[b_pt[ps_][0], b_mod], writes=[b_ht[i]], parts=True)
            else:
                R.op("dve", lambda e, kc=kc, k2=k2, src_ps=src_ps: e.tensor_scalar(out=HT[:, kc, c0:c0 + n], in0=src_ps[:, k2, 0:n],
                                                                                  scalar1=A[:, kc, j:j + 1], scalar2=B[:, kc, j:j + 1], op0=ALU.mult, op1=ALU.add),
                     reads=[b_pt[ps_][1], b_mod], writes=[b_ht[i]], parts=True)

    return s1, s2, len(tiles)


def build_hT(nc, R, HT, b_ht, tiles, src_of, src_bufs, A, B, b_mod, ident_b, b_const, typ_of, ntoks, col_of=None, anti=None):
    with Scope(nc) as _sc:
        s1, s2, nt = make_builder(nc, R, _sc, HT, b_ht, tiles, src_of, src_bufs, A, B, b_mod, ident_b, b_const, typ_of, ntoks, col_of, anti)
        s1(0)
        for i in range(nt):
            if i + 1 < nt:
                s1(i + 1)
            s2(i)
        R.flush()


def phase1(nc, R, L):
    HT, b_ht = L["HT"], L["b_ht"]
    QT, KT, V1, OG, YSC, GZ, RR, CUM, FTOT, RMAX = (L[k] for k in ["QT", "KT", "V1", "OG", "YSC", "GZ", "RR", "CUM", "FTOT", "RMAX"])
    b_qt, b_kt, b_v1, b_og, b_ysc, b_gz, b_gs = (L[k] for k in ["b_qt", "b_kt", "b_v1", "b_og", "b_ysc", "b_gz", "b_gs"])
    csc, bg_row, b_const = L["csc"], L["bg_row"], L["b_const"]
    tri0_f, tri1_f, ones_f, ident_f, i8 = L["tri0_f"], L["tri1_f"], L["ones_f"], L["ident_f"], L["i8"]
    wic_d, wiqk_d, wiv_d, wio_d, wig_d = L["wic_d"], L["wiqk_d"], L["wiv_d"], L["wio_d"], L["wig_d"]
    with Scope(nc) as _sc:
        ws0 = _sc.sb([128, 8, 512], BF16)
        ws1 = _sc.sb([128, 8, 512], BF16)
        wgt = _sc.sb([128, 8, 16], BF16)
        scc0 = _sc.sb([128, 512], F32)
        scc1 = _sc.sb([128, 512], F32)
        u0 = _sc.sb([128, 512], F32)
        u1 = _sc.sb([128, 512], F32)
        ac0 = _sc.sb([128, 512], F32)
        ac1 = _sc.sb([128, 512], F32)
        LTALL = _sc.sb([128, NT, 8], F32)
        MX80 = _sc.sb([80, 2], F32)
        DG80 = _sc.sb([80, 2, 80], F32)
        p0 = _sc.ps([128, 512], F32)
        p1 = _sc.ps([128, 512], F32)
        p2 = _sc.ps([128, 512], F32)
        p3 = _sc.ps([128, 512], F32)
        p4 = _sc.ps([128, 512], F32)
        p5 = _sc.ps([128, 512], F32)
        p6 = _sc.ps([128, 512], F32)
        p7 = _sc.ps([128, 512], F32)
        assert nc.sbuf_bytes_remaining >= 8 * NTOK * 2 + 64, nc.sbuf_bytes_remaining
        ws = [ws0, ws1]
        b_ws = [Buf(), Buf()]
        pp = [p0, p1, p2, p3, p4, p5, p6, p7]
        b_pp = [Buf() for _ in range(8)]
        sccs, us, acs = [scc0, scc1], [u0, u1], [ac0, ac1]
        b_scc, b_u, b_ac = [Buf(), Buf()], [Buf(), Buf()], [Buf(), Buf()]
        wslot = [0]

        def next_w():
            s = wslot[0] % 2
            wslot[0] += 1
            return s

        def grp_bufs(bl, g):
            return [bl[4 * g + i] for i in range(4)]

        step = 0
        for cb in range(4):
            s = next_w()
            R.dma("pool", ws[s][:, :, 0:384], wic_d[cb], writes=[b_ws[s]])
            for g in range(5):
                pb = (step % 2) * 3
                k2 = step % 2
                step += 1
                hbufs = grp_bufs(b_ht, g)
                for part in range(3):
                    for kc in range(8):
                        R.op("pe", lambda e, s=s, part=part, kc=kc, g=g, pb=pb: e.matmul(pp[pb + part][:, :], lhsT=ws[s][:, kc, part * 128:(part + 1) * 128],
                                                                                     rhs=HT[:, kc, g * 512:(g + 1) * 512], start=(kc == 0), stop=(kc == 7)),
                             reads=[b_ws[s]] + hbufs, writes=[b_pp[pb + part]])
                W = 256 if g == 0 else 64

                def v3(ap, W=W):
                    return ap.rearrange("p (r w) -> p r w", w=W)

                R.op("act", lambda e, k2=k2, pb=pb: e.activation(out=sccs[k2][:, :], in_=pp[pb + 1][:, :], func=AF.Copy),
                     reads=[b_pp[pb + 1]], writes=[b_scc[k2]])
                R.op("dve", lambda e, k2=k2, pb=pb: e.tensor_tensor(out=us[k2][:, :], in0=pp[pb + 2][:, :], in1=sccs[k2][:, :], op=ALU.mult),
                     reads=[b_pp[pb + 2], b_scc[k2]], writes=[b_u[k2]])
                R.op("act", lambda e, k2=k2, cb=cb: e.activation(out=acs[k2][:, :], in_=us[k2][:, :], func=AF.Identity, scale=csc[:, cb, 1:2], bias=csc[:, cb, 3:4]),
                     reads=[b_u[k2], b_const], writes=[b_ac[k2]])
                R.op("dve", lambda e, k2=k2, cb=cb, v3=v3, W=W: e.scalar_tensor_tensor(out=v3(acs[k2][:, :])[:, :, 1:W], in0=v3(us[k2][:, :])[:, :, 0:W - 1], scalar=csc[:, cb, 0:1],
                                                                                      in1=v3(acs[k2][:, :])[:, :, 1:W], op0=ALU.mult, op1=ALU.add),
                     reads=[b_u[k2], b_ac[k2], b_const], writes=[b_ac[k2]])
                R.op("dve", lambda e, k2=k2, cb=cb, v3=v3, W=W: e.scalar_tensor_tensor(out=v3(acs[k2][:, :])[:, :, 0:W - 1], in0=v3(us[k2][:, :])[:, :, 1:W], scalar=csc[:, cb, 2:3],
                                                                                      in1=v3(acs[k2][:, :])[:, :, 0:W - 1], op0=ALU.mult, op1=ALU.add),
                     reads=[b_u[k2], b_ac[k2], b_const], writes=[b_ac[k2]])
                R.op("dve", lambda e, k2=k2, pb=pb, g=g, cb=cb: e.tensor_tensor(out=YSC[:, 4 * g:4 * g + 4, cb, :], in0=acs[k2][:, :].rearrange("p (a b) -> p a b", b=128),
                                                                              in1=pp[pb][:, :].rearrange("p (a b) -> p a b", b=128), op=ALU.mult),
                     reads=[b_ac[k2], b_pp[pb]], writes=grp_bufs(b_ysc, g), parts=True)
        for blk in range(8):
            s = next_w()
            R.dma("pool", ws[s][:, :, 0:128], wiqk_d[blk], writes=[b_ws[s]])
            hh = blk % 4
            for g in range(5):
                pb = 6 + (step % 2)
                step += 1
                hbufs = grp_bufs(b_ht, g)
                for kc in range(8):
                    R.op("pe", lambda e, s=s, kc=kc, g=g, pb=pb: e.matmul(pp[pb][:, :], lhsT=ws[s][:, kc, 0:128], rhs=HT[:, kc, g * 512:(g + 1) * 512],
                                                                     start=(kc == 0), stop=(kc == 7)),
                         reads=[b_ws[s]] + hbufs, writes=[b_pp[pb]])
                if blk < 4:
                    R.op("act", lambda e, pb=pb, g=g, hh=hh: e.activation(out=QT[:, 4 * g:4 * g + 4, hh, :], in_=pp[pb][:, :].rearrange("p (a b) -> p a b", b=128),
                                                                        func=AF.Copy, scale=float(DH ** -0.5)),
                         reads=[b_pp[pb]], writes=grp_bufs(b_qt, g), parts=True)
                else:
                    R.op("dve", lambda e, pb=pb, g=g, hh=hh: e.tensor_copy(out=KT[:, 4 * g:4 * g + 4, hh, :], in_=pp[pb][:, :].rearrange("p (a b) -> p a b", b=128)),
                         reads=[b_pp[pb]], writes=grp_bufs(b_kt, g), parts=True)
        for which in range(2):
            s = next_w()
            R.dma("pool", ws[s][:, :, :], wiv_d if which == 0 else wio_d, writes=[b_ws[s]])
            for t in range(NT):
                pb = step % 6
                step += 1
                for kc in range(8):
                    R.op("pe", lambda e, s=s, kc=kc, t=t, pb=pb: e.matmul(pp[pb][:, :], lhsT=HT[:, kc, t * 128:(t + 1) * 128], rhs=ws[s][:, kc, :],
                                                                     start=(kc == 0), stop=(kc == 7)),
                         reads=[b_ws[s], b_ht[t]], writes=[b_pp[pb]])
                if which == 0:
                    eng = "dve" if t % 2 == 0 else "act"
                    if eng == "dve":
                        R.op("dve", lambda e, pb=pb, t=t: e.tensor_copy(out=V1[:, t, :, 0:128], in_=pp[pb][:, :].rearrange("p (a b) -> p a b", b=128)),
                             reads=[b_pp[pb]], writes=[b_v1[t]], parts=True)
                    else:
                        R.op("act", lambda e, pb=pb, t=t: e.activation(out=V1[:, t, :, 0:128], in_=pp[pb][:, :].rearrange("p (a b) -> p a b", b=128), func=AF.Copy),
                             reads=[b_pp[pb]], writes=[b_v1[t]], parts=True)
                else:
                    R.op("act", lambda e, pb=pb, t=t: e.activation(out=OG[:, t, :], in_=pp[pb][:, :], func=AF.Sigmoid),
                         reads=[b_pp[pb]], writes=[b_og[t]])
        b_wg = Buf()
        R.dma("pool", wgt[:], wig_d, writes=[b_wg])
        b_lt = [Buf() for _ in range(4)]
        b_mx8 = [Buf() for _ in range(4)]
        b_dg8 = [Buf() for _ in range(4)]
        for t in range(NT):
            pb = step % 6
            step += 1
            q4 = t % 4
            for kc in range(8):
                R.op("pe", lambda e, kc=kc, t=t, pb=pb: e.matmul(pp[pb][:, 0:16], lhsT=HT[:, kc, t * 128:(t + 1) * 128], rhs=wgt[:, kc, :],
                                                                start=(kc == 0), stop=(kc == 7)),
                     reads=[b_wg, b_ht[t]], writes=[b_pp[pb]])
            R.op("dve", lambda e, pb=pb, t=t: e.tensor_tensor(out=GZ[:, t, :], in0=pp[pb][:, 0:16], in1=bg_row[:, :], op=ALU.add),
                 reads=[b_pp[pb], b_const], writes=[b_gz[t]])
        b_ltall, b_pga, b_mx80, b_dg80 = Buf(), Buf(), Buf(), Buf()
        gz5 = GZ[:, :, :].rearrange("p t (d y h) -> p t d y h", d=2, y=2)
        lt4 = LTALL[:, :, :].rearrange("p t (d h) -> p t d h", d=2)
        R.op("act", lambda e: e.activation(out=lt4, in_=gz5[:, :, :, 1, :], func=AF.Exp, scale=-1.0), reads=b_gz, writes=[b_ltall])
        R.op("act", lambda e: e.activation(out=LTALL[:, :, :], in_=LTALL[:, :, :], func=AF.Ln, bias=1.0, scale=1.0), reads=[b_ltall], writes=[b_ltall])
        pA, pB, pC = pp[6], pp[7], pp[0]
        R.op("pe", lambda e: e.matmul(pA[:, 0:80].rearrange("p (t h) -> p t h", h=4), lhsT=tri0_f[:, :], rhs=LTALL[:, :, 0:4], start=True, stop=True),
             reads=[b_ltall, b_const], writes=[b_pp[6]])
        R.op("pe", lambda e: e.matmul(pA[:, 80:160].rearrange("p (t h) -> p t h", h=4), lhsT=tri1_f[:, :], rhs=LTALL[:, :, 4:8], start=True, stop=True),
             reads=[b_ltall, b_const], writes=[b_pp[6]])
        R.op("pe", lambda e: e.matmul(pA[:, 160:320].rearrange("p (t h) -> p t h", h=8), lhsT=ones_f[:, :], rhs=LTALL[:, :, :], start=True, stop=True),
             reads=[b_ltall, b_const], writes=[b_pp[6]])
        R.op("dve", lambda e: e.tensor_copy(out=CUM[:, :, 0:4], in_=pA[:, 0:80].rearrange("p (t h) -> p t h", h=4)), reads=[b_pp[6]], writes=b_gs, parts=True)
        R.op("dve", lambda e: e.tensor_copy(out=CUM[:, :, 4:8], in_=pA[:, 80:160].rearrange("p (t h) -> p t h", h=4)), reads=[b_pp[6]], writes=b_gs, parts=True)
        R.op("dve", lambda e: e.tensor_copy(out=FTOT[:, :, :], in_=pA[:, 160:320].rearrange("p (t h) -> p t h", h=8)), reads=[b_pp[6]], writes=b_gs, parts=True)
        R.op("dve", lambda e: e.tensor_tensor(out=RR[:, :, :].rearrange("p t (d h) -> p t d h", d=2), in0=gz5[:, :, :, 0, :],
                                              in1=CUM[:, :, :].rearrange("p t (d h) -> p t d h", d=2), op=ALU.add),
             reads=b_gz + b_gs, writes=b_gs)
        for k in range(2):
            R.op("pe", lambda e, k=k: e.matmul(pB[0:80, k * 128:(k + 1) * 128], lhsT=RR[:, :, :].rearrange("p t h -> p (t h)")[:, k * 80:(k + 1) * 80], rhs=ident_f[:, :], start=True, stop=True),
                 reads=b_gs + [b_const], writes=[b_pp[7]])
        R.op("dve", lambda e: e.tensor_reduce(out=MX80[:, :], in_=pB[0:80, 0:256].rearrange("p (k n) -> p k n", k=2), axis=AX.X, op=ALU.max),
             reads=[b_pp[7]], writes=[b_mx80])
        for k in range(2):
            R.op("dve", lambda e, k=k: e.tensor_scalar(out=DG80[:, k, :], in0=ident_f[0:80, 0:80], scalar1=MX80[:, k:k + 1], scalar2=None, op0=ALU.mult),
                 reads=[b_mx80, b_const], writes=[b_dg80])
        for k in range(2):
            R.op("pe", lambda e, k=k: e.matmul(pC[:, k * 80:(k + 1) * 80], lhsT=ones_f[0:80, :], rhs=DG80[:, k, :], start=True, stop=True),
                 reads=[b_dg80, b_const], writes=[b_pp[0]])
        R.op("dve", lambda e: e.tensor_copy(out=RMAX[:, :, :], in_=pC[:, 0:160].rearrange("p (t h) -> p t h", h=8)), reads=[b_pp[0]], writes=b_gs)
        R.flush()


def phase2(nc, R, L):
    QT, KT, V1, OG, YSC, GZ, RR, CUM, FTOT, RMAX, G1ROW = (L[k] for k in ["QT", "KT", "V1", "OG", "YSC", "GZ", "RR", "CUM", "FTOT", "RMAX", "G1ROW"])
    b_qt, b_kt, b_v1, b_og, b_ysc, b_gs, b_g1 = (L[k] for k in ["b_qt", "b_kt", "b_v1", "b_og", "b_ysc", "b_gs", "b_g1"])
    b_const, ident_b, ident_f, mk0_b, mk1_b, sel = (L[k] for k in ["b_const", "ident_b", "ident_f", "mk0_b", "mk1_b", "sel"])
    xin, x1s_d, b_x1s = L["xin"], L["x1s_d"], L["b_x1s"]
    wout_d, mhn_d, s0_d, m0_d = L["wout_d"], L["mhn_d"], L["s0_d"], L["m0_d"]
    oC_d, on_d, om_d = L["oC_d"], L["on_d"], L["om_d"]
    sx_in, sx_out, hx_in, b_sxin, b_sxout, b_hxin, RG = (L[k] for k in ["sx_in", "sx_out", "hx_in", "b_sxin", "b_sxout", "b_hxin", "RG"])
    dump = L["dump"]
    masks = [mk0_b, mk1_b]
    with Scope(nc) as _sc:
        SNAP = _sc.sb([128, NT, 4, 130], BF16)
        WOUT = _sc.sb([128, 8, 1024], BF16)
        MHN = _sc.sb([128, 512], F32)
        MX = _sc.sb([128, NT, 8], F32)
        MPRE = _sc.sb([128, NT + 1, 8], F32)
        WG = _sc.sb([128, NT, 8], F32)
        DEC = _sc.sb([128, NT, 8], F32)
        CLP = _sc.sb([128, NT, 8], F32)
        TMPG = _sc.sb([128, NT, 8], F32)
        MFIN = _sc.sb([128, 3, 8], F32)
        SST = _sc.sb([128, 2, 4, 129], F32)
        SDB = _sc.sb([128, 4, 130], BF16)
        VW = _sc.sb([128, 2, 4, 130], BF16)
        KTK = _sc.sb([128, 4, 128], BF16)
        KTK2 = _sc.sb([128, 2, 4, 128], BF16)
        KTK2 = _sc.sb([128, 2, 4, 128], BF16)
        PT = _sc.sb([128, 2, 4, 128], BF16)
        DN = _sc.sb([128, 12], F32)
        HACC = _sc.sb([128, 2, 512], F32)
        HG0 = _sc.sb([128, 512], F32)
        HG = _sc.sb([128, 512], BF16)
        HMT = _sc.sb([128, 4, 128], BF16)
        SQ4 = _sc.sb([128, 4, 128], BF16)
        SS4 = _sc.sb([128, 2, 4], F32)
        XT = _sc.sb([128, 2, D], F32)
        X1 = _sc.sb([128, D], F32)
        CTR = _sc.sb([128, 2, 128], F32)
        SXG = _sc.sb([128, 2, 520], F32)
        SXT = _sc.sb([128, 520], F32)
        pST = _sc.ps([128, 512], F32)
        pND = _sc.ps([128, 3, 512], F32)
        pUP = _sc.ps([128, 2, 512], F32)
        pTR = _sc.ps([128, 1024], BF16)
        pWO = _sc.ps([128, 512], F32)
        b_snap = [Buf() for _ in range(NT)]
        b_wout, b_mhn = Buf(), Buf()
        R.dma("pool", WOUT[:], wout_d, writes=[b_wout])
        R.dma("sp", MHN[:], mhn_d, writes=[b_mhn])
        b_chain = Buf("chain")
        b_tmpg = Buf()
        b_mfin = Buf()
        b_sst = [[Buf() for _ in range(4)] for _ in range(2)]
        b_pst, b_pnd, b_ptr, b_pwo = Buf(), [Buf() for _ in range(3)], [Buf(), Buf()], Buf()
        _pup = Buf()
        b_pup = [_pup, _pup]
        b_sdb, b_vw, b_ktk, b_ptt = Buf(), [Buf(), Buf()], Buf(), [Buf(), Buf()]
        b_ktk2 = [Buf(), Buf()]
        b_ktk2 = [Buf(), Buf()]
        b_dn = Buf()
        b_hacc, b_ss4, b_xt = [[Buf() for _ in range(4)] for _ in range(2)], [Buf(), Buf()], [Buf(), Buf()]
        b_hg0, b_hg, b_hmt, b_sq, b_x1 = Buf(), Buf(), Buf(), Buf(), Buf()
        b_x1h = [Buf(), Buf()]
        b_sq4 = [Buf() for _ in range(4)]
        b_ctr = [Buf(), Buf()]
        b_sxg, b_sxt = Buf(), Buf()
        b_sxs = b_sxg
        cnt = {"ctr": 0}

        def nxt(k, n):
            v = cnt[k] % n
            cnt[k] += 1
            return v

        seqs = {"P0": [0, 1], "P1": [2, 3], "S": list(range(4, NT))}

        def d4(d):
            return slice(d * 4, d * 4 + 4)

        def nd_slot(i):
            return pND[:, i // 3, (i % 3) * 129:(i % 3) * 129 + 129]

        def up_slot(h):
            return pUP[:, h // 3, (h % 3) * 129:(h % 3) * 129 + 129]

        def chain(seq, d, init):
            tiles = seqs[seq] if d == 0 else seqs[seq][::-1]
            sidx = {"P0": 0, "P1": 1, "S": 2}[seq]
            first = tiles[0]
            if init is None:
                R.op("dve", lambda e: e.memset(MPRE[:, first, d4(d)], 0.0), writes=[b_chain])
            else:
                ap, bufs = init
                R.op("dve", lambda e: e.tensor_copy(out=MPRE[:, first, d4(d)], in_=ap), reads=bufs, writes=[b_chain])
            for i, c in enumerate(tiles):
                R.op("dve", lambda e, c=c: e.tensor_tensor(out=MX[:, c, d4(d)], in0=MPRE[:, c, d4(d)], in1=RMAX[:, c, d4(d)], op=ALU.max),
                     reads=[b_chain, b_gs[c]], writes=[b_chain])
                if i + 1 < len(tiles):
                    nx = tiles[i + 1]
                    R.op("dve", lambda e, c=c, nx=nx: e.tensor_tensor(out=MPRE[:, nx, d4(d)], in0=MX[:, c, d4(d)], in1=FTOT[:, c, d4(d)], op=ALU.subtract),
                         reads=[b_chain, b_gs[c]], writes=[b_chain])
                else:
                    R.op("dve", lambda e, c=c: e.tensor_tensor(out=MFIN[:, sidx, d4(d)], in0=MX[:, c, d4(d)], in1=FTOT[:, c, d4(d)], op=ALU.subtract),
                         reads=[b_chain, b_gs[c]], writes=[b_mfin])

        def weights(t0, t1, d):
            gs = [b_gs[t] for t in range(t0, t1)]
            for (dst, a, b_) in [(WG, RR, MX), (DEC, MPRE, MX), (CLP, CUM, MX)]:
                R.op("dve", lambda e, a=a, b_=b_: e.tensor_tensor(out=TMPG[:, t0:t1, d4(d)], in0=a[:, t0:t1, d4(d)], in1=b_[:, t0:t1, d4(d)], op=ALU.subtract),
                     reads=[b_chain] + gs, writes=[b_tmpg])
                R.op("act", lambda e, dst=dst: e.activation(out=dst[:, t0:t1, d4(d)], in_=TMPG[:, t0:t1, d4(d)], func=AF.Exp),
                     reads=[b_tmpg], writes=[b_chain])

        def st_vw(c, d):
            for h in range(4):
                col = d * 4 + h
                R.op("act", lambda e, h=h, col=col: e.activation(out=VW[:, d, h, 0:129], in_=V1[:, c, h, 0:129], func=AF.Identity, scale=WG[:, c, col:col + 1]),
                     reads=[b_v1[c], b_chain], writes=[b_vw[d]], parts=True)

        def st_ktr(c):
            for h in range(4):
                R.op("pe", lambda e, h=h: e.transpose(pTR[:, h * 128:(h + 1) * 128], KT[:, c, h, :], ident_b[:, :]), reads=[b_kt[c], b_const], writes=[b_ptr[0]])
            R.op("act", lambda e: e.activation(out=KTK[:, :, :], in_=pTR[:, 0:512].rearrange("p (a b) -> p a b", b=128), func=AF.Copy), reads=[b_ptr[0]], writes=[b_ktk])

        def st_upd(c, d):
            for h in range(4):
                R.op("pe", lambda e, h=h: e.matmul(up_slot(h), lhsT=KTK[:, h, :], rhs=VW[:, d, h, 0:129], start=True, stop=True),
                     reads=[b_ktk, b_vw[d]], writes=[b_pup[h // 3]])
            for h in range(4):
                col = d * 4 + h
                R.op("dve", lambda e, h=h, col=col: e.scalar_tensor_tensor(out=SST[:, d, h, :], in0=SST[:, d, h, :], scalar=DEC[:, c, col:col + 1], in1=up_slot(h),
                                                                            op0=ALU.mult, op1=ALU.add),
                     reads=[b_sst[d][h], b_chain, b_pup[h // 3]], writes=[b_sst[d][h]])

        pSTb = pST[:, :].bitcast(BF16)

        def state_only_pass(seq):
            cl = seqs[seq]

            def prep(c, par):
                trb = pTR if par == 0 else pSTb
                btr = b_ptr[0] if par == 0 else b_pst
                for h in range(4):
                    R.op("act", lambda e, h=h: e.activation(out=VW[:, par, h, 0:129], in_=V1[:, c, h, 0:129], func=AF.Identity, scale=WG[:, c, h:h + 1]),
                         reads=[b_v1[c], b_chain], writes=[b_vw[par]], parts=True)
                for h in range(4):
                    R.op("pe", lambda e, h=h: e.transpose(trb[:, h * 128:(h + 1) * 128], KT[:, c, h, :], ident_b[:, :]),
                         reads=[b_kt[c], b_const], writes=[btr])
                R.op("act", lambda e: e.activation(out=KTK2[:, par, :, :], in_=trb[:, 0:512].rearrange("p (a b) -> p a b", b=128), func=AF.Copy),
                     reads=[btr], writes=[b_ktk2[par]])

            def upd(c, par):
                def slot(h):
                    return up_slot(h) if par == 0 else nd_slot(h)
                bb = [b_pup[0]] if par == 0 else [b_pnd[0], b_pnd[1]]
                for h in range(4):
                    R.op("pe", lambda e, h=h: e.matmul(slot(h), lhsT=KTK2[:, par, h, :], rhs=VW[:, par, h, 0:129], start=True, stop=True),
                         reads=[b_ktk2[par], b_vw[par]], writes=bb)
                for h in range(4):
                    R.op("act", lambda e, h=h: e.activation(out=SNAP[:, c, h, 0:129], in_=SST[:, 0, h, :], func=AF.Identity, scale=DEC[:, c, h:h + 1]),
                         reads=[b_sst[0][h], b_chain], writes=[b_snap[c]], parts=True)
                for h in range(4):
                    R.op("dve", lambda e, h=h: e.scalar_tensor_tensor(out=SST[:, 0, h, :], in0=SST[:, 0, h, :], scalar=DEC[:, c, h:h + 1], in1=slot(h),
                                                                       op0=ALU.mult, op1=ALU.add),
                         reads=[b_sst[0][h], b_chain] + bb, writes=[b_sst[0][h]])

            prep(cl[0], 0)
            for k, c in enumerate(cl):
                if k + 1 < len(cl):
                    prep(cl[k + 1], (k + 1) % 2)
                upd(c, k % 2)

        def emit_state(seq, d):
            si = {"P0": 0, "P1": 1}[seq]
            for h in range(4):
                cs = nxt("ctr", 2)
                R.op("pe", lambda e, h=h: e.matmul(pWO[:, 0:128], lhsT=SST[:, d, h, 0:128], rhs=ident_f[:, :], start=True, stop=True),
                     reads=[b_sst[d][h], b_const], writes=[b_pwo])
                R.op("act", lambda e, cs=cs: e.activation(out=CTR[:, cs, :], in_=pWO[:, 0:128], func=AF.Copy), reads=[b_pwo], writes=[b_ctr[cs]])
                R.dma("sp", oC_d[si, d, h, :, :], CTR[:, cs, :], reads=[b_ctr[cs]])
                R.dma("sp", on_d[si, d, h, :].rearrange("(p o) -> p o", o=1), SST[:, d, h, 128:129], reads=[b_sst[d][h]])
            R.dma("sp", om_d[si, d, :].rearrange("(o h) -> o h", o=1), MFIN[0:1, si, d4(d)], reads=[b_mfin])

        def init_state(d, src=None):
            for h in range(4):
                if src is None:
                    R.op("dve", lambda e, h=h: e.memset(SST[:, d, h, :], 0.0), writes=[b_sst[d][h]])
                else:
                    ap_of, bufs = src
                    R.op("dve", lambda e, h=h: e.tensor_copy(out=SST[:, d, h, :], in_=ap_of(h)), reads=bufs, writes=[b_sst[d][h]])

        def full_pass(seq):
            typ = 1 if seq == "S" else 0
            cl = seqs[seq][::-1]

            def s1a(c, par):
                R.dma("sp", XT[:, par, :], xin[c * 128:(c + 1) * 128, :], writes=[b_xt[par]])
                for h in range(4):
                    R.op("act", lambda e, h=h: e.activation(out=SDB[:, h, 0:129], in_=SST[:, 1, h, :], func=AF.Identity, scale=DEC[:, c, 4 + h:5 + h]),
                         reads=[b_sst[1][h], b_chain], writes=[b_sdb], parts=True)
                st_vw(c, 0)
                st_vw(c, 1)
                for h in range(4):
                    R.op("pe", lambda e, h=h: e.matmul(pST[:, h * 128:(h + 1) * 128], lhsT=KT[:, c, h, :], rhs=QT[:, c, h, :], start=True, stop=True),
                         reads=[b_kt[c], b_qt[c]], writes=[b_pst])
                st_ktr(c)

            def s1b(c, par):
                for d in range(2):
                    R.op("dve", lambda e, d=d: e.tensor_tensor(out=PT[:, d, :, :], in0=pST[:, :].rearrange("p (a b) -> p a b", b=128),
                                                               in1=masks[d][:, :].unsqueeze(1).to_broadcast([128, 4, 128]), op=ALU.mult),
                         reads=[b_pst, b_const], writes=[b_ptt[d]])
                for d in range(2):
                    for h in range(4):
                        i = d * 4 + h
                        R.op("pe", lambda e, d=d, h=h, i=i: e.matmul(nd_slot(i), lhsT=PT[:, d, h, :], rhs=VW[:, d, h, 0:129], start=True, stop=False),
                             reads=[b_ptt[d], b_vw[d]], writes=[b_pnd[i // 3]])
                        if d == 0:
                            R.op("pe", lambda e, h=h, i=i: e.matmul(nd_slot(i), lhsT=QT[:, c, h, :], rhs=SNAP[:, c, h, 0:129], start=False, stop=True),
                                 reads=[b_qt[c], b_snap[c]], writes=[b_pnd[i // 3]])
                        else:
                            R.op("pe", lambda e, h=h, i=i: e.matmul(nd_slot(i), lhsT=QT[:, c, h, :], rhs=SDB[:, h, 0:129], start=False, stop=True),
                                 reads=[b_qt[c], b_sdb], writes=[b_pnd[i // 3]])
                st_upd(c, 1)

            def s1c(c, par):
                R.op("act", lambda e: e.activation(out=DN[:, 0:6].rearrange("p (b t) -> p b t", t=3), in_=pND[:, 0:2, 0:387].rearrange("p b (t w) -> p b t w", w=129)[:, :, :, 128],
                                                   func=AF.Abs),
                     reads=b_pnd, writes=[b_dn])
                R.op("act", lambda e: e.activation(out=DN[:, 6:8], in_=pND[:, 2, 0:258].rearrange("p (t w) -> p t w", w=129)[:, :, 128], func=AF.Abs),
                     reads=b_pnd, writes=[b_dn])
                R.op("dve", lambda e: e.tensor_tensor(out=DN[:, 0:8], in0=DN[:, 0:8], in1=CLP[:, c, :], op=ALU.max), reads=[b_dn, b_chain], writes=[b_dn])
                R.op("dve", lambda e: e.reciprocal(out=DN[:, 0:8], in_=DN[:, 0:8]), reads=[b_dn], writes=[b_dn])
                for h in range(4):
                    R.op("dve", lambda e, h=h: e.tensor_scalar(out=HACC[:, par, h * 128:(h + 1) * 128], in0=nd_slot(h)[:, 0:128], scalar1=DN[:, h:h + 1], scalar2=None, op0=ALU.mult),
                         reads=[b_pnd[h // 3], b_dn], writes=[b_hacc[par][h]])
                for h in range(4):
                    i = 4 + h
                    R.op("dve", lambda e, h=h, i=i: e.scalar_tensor_tensor(out=HACC[:, par, h * 128:(h + 1) * 128], in0=nd_slot(i)[:, 0:128], scalar=DN[:, i:i + 1],
                                                                            in1=HACC[:, par, h * 128:(h + 1) * 128], op0=ALU.mult, op1=ALU.add),
                         reads=[b_pnd[i // 3], b_dn, b_hacc[par][h]], writes=[b_hacc[par][h]])

            def hg0(c):
                R.op("pool", lambda e: e.tensor_tensor(out=HG0[:, :], in0=OG[:, c, :], in1=MHN[:, :], op=ALU.mult), reads=[b_og[c], b_mhn], writes=[b_hg0])

            def s2a(c, par):
                for h in range(4):
                    R.op("act", lambda e, h=h: e.activation(out=SQ4[:, h, :], in_=HACC[:, par, h * 128:(h + 1) * 128], func=AF.Square, accum_out=SS4[:, par, h:h + 1]),
                         reads=[b_hacc[par][h]], writes=[b_sq4[h], b_ss4[par]], parts=True)
                R.op("act", lambda e: e.activation(out=SS4[:, par, :], in_=SS4[:, par, :], func=AF.Sqrt, bias=EPS, scale=1.0 / DH),
                     reads=[b_ss4[par]], writes=[b_ss4[par]])
                R.op("dve", lambda e: e.reciprocal(out=SS4[:, par, :], in_=SS4[:, par, :]), reads=[b_ss4[par]], writes=[b_ss4[par]])
                for h in range(4):
                    R.op("dve", lambda e, h=h: e.scalar_tensor_tensor(out=HG[:, h * 128:(h + 1) * 128], in0=HACC[:, par, h * 128:(h + 1) * 128], scalar=SS4[:, par, h:h + 1],
                                                                       in1=HG0[:, h * 128:(h + 1) * 128], op0=ALU.mult, op1=ALU.mult),
                         reads=[b_hacc[par][h], b_ss4[par], b_hg0], writes=[b_hg], parts=True)

            def s2b(c, par):
                for h in range(4):
                    R.op("pe", lambda e, h=h: e.transpose(pTR[:, 512 + h * 128:512 + (h + 1) * 128], HG[:, h * 128:(h + 1) * 128], ident_b[:, :]),
                         reads=[b_hg, b_const], writes=[b_ptr[1]])
                R.op("act", lambda e: e.activation(out=HMT[:, :, :], in_=pTR[:, 512:1024].rearrange("p (a b) -> p a b", b=128), func=AF.Copy),
                     reads=[b_ptr[1]], writes=[b_hmt])
                wo_half(c, par, 0)

            def wo_half(c, par, half):
                for kb in range(8):
                    if kb < 4:
                        R.op("pe", lambda e, kb=kb: e.matmul(pWO[:, :], lhsT=YSC[:, c, kb, :], rhs=WOUT[:, kb, half * 512:(half + 1) * 512], start=(kb == 0), stop=False),
                             reads=[b_ysc[c], b_wout], writes=[b_pwo])
                    else:
                        R.op("pe", lambda e, kb=kb: e.matmul(pWO[:, :], lhsT=HMT[:, kb - 4, :], rhs=WOUT[:, kb, half * 512:(half + 1) * 512], start=False, stop=(kb == 7)),
                             reads=[b_hmt, b_wout], writes=[b_pwo])
                R.op("dve", lambda e: e.tensor_tensor(out=X1[:, half * 512:(half + 1) * 512], in0=pWO[:, :], in1=G1ROW[:, typ, half * 512:(half + 1) * 512], op=ALU.mult),
                     reads=[b_pwo, b_g1], writes=[b_x1h[half]])
                R.op("pool", lambda e: e.tensor_tensor(out=XT[:, par, half * 512:(half + 1) * 512], in0=XT[:, par, half * 512:(half + 1) * 512],
                                                       in1=X1[:, half * 512:(half + 1) * 512], op=ALU.add),
                     reads=[b_x1h[half], b_xt[par]], writes=[b_xt[par]])

            def s2c(c, par):
                wo_half(c, par, 1)
                R.dma("sp", x1s_d[c * 128:(c + 1) * 128, :], XT[:, par, :], reads=[b_xt[par]], writes=[b_x1s[c]])
                if c == NT - 1:
                    R.dma("sp", hx_in[:, :], XT[64:128, par, :], reads=[b_xt[par]], writes=[b_hxin])
                    hx_out, b_hxout, hx_scr, b_hs = L["hx_out"], L["b_hxout"], L["hx_scr"], L["b_hs"]
                    R.collective(lambda e: e.collective_compute("AllGather", ALU.bypass, replica_groups=RG, ins=[hx_in.ap().opt()], outs=[hx_out.ap().opt()]),
                                 reads=[b_hxin], writes=[b_hxout])

            prev = None
            for k, c in enumerate(cl):
                par = k % 2
                s1a(c, par)
                if prev is not None:
                    s2a(*prev)
                s1b(c, par)
                if prev is not None:
                    s2b(*prev)
                s1c(c, par)
                if prev is not None:
                    s2c(*prev)
                hg0(c)
                prev = (c, par)
            s2a(*prev)
            s2b(*prev)
            s2c(*prev)
            if seq == "S":
                hx_out, b_hxout, hx_scr, b_hs = L["hx_out"], L["b_hxout"], L["hx_scr"], L["b_hs"]
                for hf in range(2):
                    cs_ = slice(hf * 512, (hf + 1) * 512)
                    R.dma("pool", SXG[0:64, 0, 0:512], hx_out[0:64, cs_], reads=[b_hxout], writes=[b_sxg])
                    R.dma("pool", SXG[0:64, 1, 0:512], hx_out[64:128, cs_], reads=[b_hxout], writes=[b_sxg])
                    R.op("dve", lambda e: e.tensor_scalar(out=SXG[0:64, 0, 0:512], in0=SXG[0:64, 0, 0:512], scalar1=sel[0:64, 1:2], scalar2=None, op0=ALU.mult),
                         reads=[b_sxg, b_const], writes=[b_sxg])
                    R.op("dve", lambda e: e.tensor_scalar(out=SXG[0:64, 1, 0:512], in0=SXG[0:64, 1, 0:512], scalar1=sel[0:64, 0:1], scalar2=None, op0=ALU.mult),
                         reads=[b_sxg, b_const], writes=[b_sxg])
                    R.op("dve", lambda e: e.tensor_tensor(out=SXG[0:64, 0, 0:512], in0=SXG[0:64, 0, 0:512], in1=SXG[0:64, 1, 0:512], op=ALU.add),
                         reads=[b_sxg], writes=[b_sxg])
                    R.dma("pool", hx_scr[:, cs_], SXG[0:64, 0, 0:512], reads=[b_sxg], writes=[b_hs])

        chain("P0", 0, None)
        chain("P1", 0, None)
        R.dma("sp", SXT[:, 0:4], m0_d, writes=[b_sxt])
        chain("S", 0, (SXT[:, 0:4], [b_sxt]))
        weights(0, NT, 0)
        R.dma("sp", SXG[:, 0, 0:516].rearrange("p (h v) -> p h v", v=129), s0_d, writes=[b_sxs])
        init_state(0, (lambda h: SXG[:, 0, h * 129:(h + 1) * 129], [b_sxs]))
        state_only_pass("S")
        for h in range(4):
            R.op("dve", lambda e, h=h: e.tensor_copy(out=SXG[:, 0, h * 129:(h + 1) * 129], in_=SST[:, 0, h, :]), reads=[b_sst[0][h]], writes=[b_sxs])
        R.op("dve", lambda e: e.tensor_copy(out=SXG[:, 0, 516:520], in_=MFIN[:, 2, 0:4]), reads=[b_mfin], writes=[b_sxs])
        R.dma("pool", sx_in[:, :], SXG[:, 0, :], reads=[b_sxs], writes=[b_sxin])
        R.collective(lambda e: e.collective_compute("AllGather", ALU.bypass, replica_groups=RG, ins=[sx_in.ap().opt()], outs=[sx_out.ap().opt()]),
                     reads=[b_sxin], writes=[b_sxout])
        chain("P0", 1, None)
        chain("P1", 1, None)
        weights(0, 4, 1)
        for seq in ["P0", "P1"]:
            init_state(0)
            state_only_pass(seq)
            emit_state(seq, 0)
            init_state(1)
            full_pass(seq)
            emit_state(seq, 1)
        R.dma("pool", SXG[:, :, :], sx_out.ap().rearrange("(r p) n -> p r n", p=128), reads=[b_sxout], writes=[b_sxg])
        R.op("dve", lambda e: e.tensor_scalar(out=SXT[:, :], in0=SXG[:, 0, :], scalar1=sel[:, 1:2], scalar2=None, op0=ALU.mult), reads=[b_sxg, b_const], writes=[b_sxt])
        R.op("dve", lambda e: e.scalar_tensor_tensor(out=SXT[:, :], in0=SXG[:, 1, :], scalar=sel[:, 0:1], in1=SXT[:, :], op0=ALU.mult, op1=ALU.add),
             reads=[b_sxg, b_sxt, b_const], writes=[b_sxt])
        chain("S", 1, (SXT[:, 516:520], [b_sxt]))
        weights(4, NT, 1)
        init_state(1, (lambda h: SXT[:, h * 129:(h + 1) * 129], [b_sxt]))
        full_pass("S")
        dump("MX", MX[:], [128, NT, 8], b_chain)
        dump("WG", WG[:], [128, NT, 8], b_chain)
        dump("SNAP", SNAP[:], [128, NT, 4, 130], b_snap[0], BF16)
        R.flush()


def ffn_up(nc, R, L, _sc, seg, H2T, b_h2t, ACTS, b_acts, n_own, n_all, ntile, nbank, side=None):
    cff, b_const, wup_d = L["cff"], L["b_const"], L["wup_d"]
    U = _sc.sb([128, 2, n_all], F32)
    ACA = _sc.sb([128, 2, n_own], F32)
    ACG = _sc.sb([128, 2, n_own], F32)
    WU = _sc.sb([128, 3, 8, 256], BF16)
    fp = [_sc.ps([128, 512], F32) for _ in range(nbank)]
    b_fp = [Buf() for _ in range(nbank)]
    b_wu = [Buf() for _ in range(3)]
    b_u = [Buf(), Buf()]
    b_aca = [Buf(), Buf()]
    b_acg = [Buf(), Buf()]
    groups = [(g0, min(512, n_all - g0)) for g0 in range(0, n_all, 512)]
    pcount = 0

    def finish(j):
        g2 = j % 2
        R.op("act", lambda e: e.activation(out=ACG[:, g2, :], in_=ACG[:, g2, :], func=AF.Silu), reads=[b_acg[g2]], writes=[b_acg[g2]])
        R.op("pool", lambda e: e.tensor_tensor(out=ACTS[:, j, :], in0=ACG[:, g2, :], in1=ACA[:, g2, :], op=ALU.mult),
             reads=[b_acg[g2], b_aca[g2]], writes=[b_acts[j]])

    for j in range(min(2, NJ)):
        R.dma("pool", WU[:, j % 3, :, :], wup_d[j], writes=[b_wu[j % 3]])
    for j in range(NJ):
        s = j % 3
        if j + 2 < NJ:
            R.dma("pool", WU[:, (j + 2) % 3, :, :], wup_d[j + 2], writes=[b_wu[(j + 2) % 3]])
        a2 = j % 2
        for part in range(2):
            blk = j + part * NJ
            us = part
            acc = ACA[:, a2, :] if part == 0 else ACG[:, a2, :]
            b_acc = b_aca[a2] if part == 0 else b_acg[a2]
            for (g0, gn) in groups:
                pb = pcount % nbank
                pcount += 1
                hb = [b_h2t[min(ti, ntile)] for ti in range(g0 // 128, (g0 + gn + 127) // 128)]
                for kc in range(8):
                    R.op("pe", lambda e, kc=kc, g0=g0, gn=gn, pb=pb, s=s, part=part: e.matmul(fp[pb][:, 0:gn], lhsT=WU[:, s, kc, part * 128:(part + 1) * 128],
                                                                                         rhs=H2T[:, kc, g0:g0 + gn], start=(kc == 0), stop=(kc == 7)),
                         reads=[b_wu[s]] + hb, writes=[b_fp[pb]])
                R.op("act", lambda e, g0=g0, gn=gn, pb=pb, us=us: e.activation(out=U[:, us, g0:g0 + gn], in_=fp[pb][:, 0:gn], func=AF.Copy),
                     reads=[b_fp[pb]], writes=[b_u[us]], parts=True)
                if g0 < n_own:
                    on = min(gn, n_own - g0)
                    R.op("act", lambda e, g0=g0, on=on, pb=pb, acc=acc, blk=blk: e.activation(out=acc[:, g0:g0 + on], in_=fp[pb][:, 0:on], func=AF.Identity,
                                                                                         scale=cff[:, blk, 1:2], bias=cff[:, blk, 3:4]),
                         reads=[b_fp[pb], b_const], writes=[b_acc], parts=True)
            if seg == "prompt":
                def v3(ap):
                    return ap.rearrange("p (r w) -> p r w", w=256)
                R.op("dve", lambda e, acc=acc, us=us, blk=blk, v3=v3: e.scalar_tensor_tensor(out=v3(acc)[:, :, 1:256], in0=v3(U[:, us, 0:512])[:, :, 0:255], scalar=cff[:, blk, 0:1],
                                                                                        in1=v3(acc)[:, :, 1:256], op0=ALU.mult, op1=ALU.add),
                     reads=[b_u[us], b_acc, b_const], writes=[b_acc])
                R.op("dve", lambda e, acc=acc, us=us, blk=blk, v3=v3: e.scalar_tensor_tensor(out=v3(acc)[:, :, 0:255], in0=v3(U[:, us, 0:512])[:, :, 1:256], scalar=cff[:, blk, 2:3],
                                                                                        in1=v3(acc)[:, :, 0:255], op0=ALU.mult, op1=ALU.add),
                     reads=[b_u[us], b_acc, b_const], writes=[b_acc])
            else:
                R.op("dve", lambda e, acc=acc, us=us, blk=blk: e.scalar_tensor_tensor(out=acc[:, 64:2048], in0=U[:, us, 0:1984], scalar=cff[:, blk, 0:1],
                                                                                 in1=acc[:, 64:2048], op0=ALU.mult, op1=ALU.add),
                     reads=[b_u[us], b_acc, b_const], writes=[b_acc])
                R.op("dve", lambda e, acc=acc, us=us, blk=blk: e.scalar_tensor_tensor(out=acc[:, 0:2048], in0=U[:, us, 64:2112], scalar=cff[:, blk, 2:3],
                                                                                 in1=acc[:, 0:2048], op0=ALU.mult, op1=ALU.add),
                     reads=[b_u[us], b_acc, b_const], writes=[b_acc])
        if j >= 1:
            finish(j - 1)
        if side is not None:
            side(j)
    finish(NJ - 1)


def ffn_down_consts(nc, R, L, _sc, typ):
    wdn_d, fn_d, gate_d, b_gate_d = L["wdn_d"], L["fn_d"], L["gate_d"], L["b_gate_d"]
    WD = _sc.sb([128, NJ, D], BF16)
    FNR = _sc.sb([128, D], F32)
    G2ROW = _sc.sb([128, D], F32)
    b_wd = [Buf() for _ in range(NJ // 2)]
    b_fnr, b_g2 = Buf(), Buf()
    for jj in range(0, NJ, 2):
        R.dma("pool", WD[:, jj:jj + 2, :], wdn_d[:, jj:jj + 2, :], writes=[b_wd[jj // 2]])
    R.dma("sp", FNR[:], fn_d, writes=[b_fnr])
    R.dma("sp", G2ROW[:], gate_d[typ:typ + 1, 1024:2048].to_broadcast([128, 1024]), reads=[b_gate_d], writes=[b_g2])
    return WD, FNR, G2ROW, b_wd, b_fnr, b_g2


def ffn_down(nc, R, L, _sc, tiles, ACTS, b_acts, dc):
    x1s_d, b_x1s, y_d = L["x1s_d"], L["b_x1s"], L["y_d"]
    WD, FNR, G2ROW, b_wd, b_fnr, b_g2 = dc
    X1L = _sc.sb([128, 2, D], F32)
    X2 = _sc.sb([128, 2, D], F32)
    YT = _sc.sb([128, 2, D], F32)
    SQ2 = _sc.sb([128, D], BF16)
    SSD = _sc.sb([128, 4], F32)
    dp = [_sc.ps([128, 512], F32) for _ in range(4)]
    b_dp = [Buf() for _ in range(4)]
    b_sq2 = Buf()
    b_x1l, b_x2, b_yt = [Buf(), Buf()], [Buf(), Buf()], [Buf(), Buf()]
    b_ssd = [Buf() for _ in range(4)]
    pc = 0
    for i, t in enumerate(tiles):
        hs = i % 2
        q = i % 4
        R.dma("sp", X1L[:, hs, :], x1s_d[t * 128:(t + 1) * 128, :], reads=[b_x1s[t]], writes=[b_x1l[hs]])
        for half in range(2):
            pb = pc % 4
            pc += 1
            for j in range(NJ):
                R.op("pe", lambda e, j=j, i=i, half=half, pb=pb: e.matmul(dp[pb][:, :], lhsT=ACTS[:, j, i * 128:(i + 1) * 128], rhs=WD[:, j, half * 512:(half + 1) * 512],
                                                                        start=(j == 0), stop=(j == NJ - 1)),
                     reads=[b_acts[j], b_wd[j // 2]], writes=[b_dp[pb]])
            R.op("dve", lambda e, half=half, pb=pb, hs=hs: e.tensor_tensor(out=X2[:, hs, half * 512:(half + 1) * 512], in0=dp[pb][:, :], in1=G2ROW[:, half * 512:(half + 1) * 512], op=ALU.mult),
                 reads=[b_dp[pb], b_g2], writes=[b_x2[hs]], parts=True)
            R.op("pool", lambda e, half=half, hs=hs: e.tensor_tensor(out=X2[:, hs, half * 512:(half + 1) * 512], in0=X2[:, hs, half * 512:(half + 1) * 512],
                                                                   in1=X1L[:, hs, half * 512:(half + 1) * 512], op=ALU.add),
                 reads=[b_x2[hs], b_x1l[hs]], writes=[b_x2[hs]])
        R.op("act", lambda e, hs=hs, q=q: e.activation(out=SQ2[:, :], in_=X2[:, hs, :], func=AF.Square, accum_out=SSD[:, q:q + 1]),
             reads=[b_x2[hs]], writes=[b_sq2, b_ssd[q]])
        R.op("act", lambda e, q=q: e.activation(out=SSD[:, q:q + 1], in_=SSD[:, q:q + 1], func=AF.Sqrt, bias=EPS, scale=1.0 / D),
             reads=[b_ssd[q]], writes=[b_ssd[q]])
        R.op("dve", lambda e, q=q: e.reciprocal(out=SSD[:, q:q + 1], in_=SSD[:, q:q + 1]),
             reads=[b_ssd[q]], writes=[b_ssd[q]])
        R.op("dve", lambda e, hs=hs, q=q: e.scalar_tensor_tensor(out=YT[:, hs, :], in0=X2[:, hs, :], scalar=SSD[:, q:q + 1], in1=FNR[:, :], op0=ALU.mult, op1=ALU.mult),
             reads=[b_x2[hs], b_ssd[q], b_fnr], writes=[b_yt[hs]])
        R.dma("sp", y_d[t * 128:(t + 1) * 128, :], YT[:, hs, :], reads=[b_yt[hs]])


def ffn_all(nc, R, L):
    A2, B2, b_mod, ident_b, anti_b, b_const, sel = (L[k] for k in ["A2", "B2", "b_mod", "ident_b", "anti_b", "b_const", "sel"])
    x1s_d, b_x1s = L["x1s_d"], L["b_x1s"]
    hx_in, hx_out, b_hxin, b_hxout, RG = L["hx_in"], L["hx_out"], L["b_hxin"], L["b_hxout"], L["RG"]
    dump = L["dump"]
    p_tiles = [0, 1, 2, 3]
    s_tiles = list(range(4, NT))
    NS = len(s_tiles)
    hx_scr, b_hs = L["hx_scr"], L["b_hs"]
    with Scope(nc) as _so:
        H2Ts = _so.sb([128, 8, 2112], BF16)
        b_h2ts = [Buf() for _ in range(NS + 1)]
        with Scope(nc) as _sp:
            ACTSp = _sp.sb([128, NJ, 512], BF16)
            b_actsp = [Buf() for _ in range(NJ)]
            H2Tp = _sp.sb([128, 8, 512], BF16)
            b_h2tp = [Buf() for _ in range(5)]
            dcp = ffn_down_consts(nc, R, L, _sp, 0)
            with Scope(nc) as _s1:
                build_hT(nc, R, H2Tp, b_h2tp, p_tiles, lambda t: x1s_d[t * 128:(t + 1) * 128, :], [b_x1s[t] for t in p_tiles], A2, B2, b_mod, ident_b, b_const,
                         lambda t: 0, [128] * 4)
            with Scope(nc) as _s2:
                s1, s2, nb = make_builder(nc, R, _s2, H2Ts, b_h2ts, s_tiles, lambda t: x1s_d[t * 128:(t + 1) * 128, :], [b_x1s[t] for t in s_tiles],
                                          A2, B2, b_mod, ident_b, b_const, lambda t: 1, [128] * NS, col_of=lambda t: (t - 4) * 128)
                state = {"i": 0}
                s1(0)

                def side(j):
                    i = state["i"]
                    if i < nb:
                        if i + 1 < nb:
                            s1(i + 1)
                        s2(i)
                        state["i"] = i + 1

                ffn_up(nc, R, L, _s2, "prompt", H2Tp, b_h2tp, ACTSp, b_actsp, 512, 512, 4, 4, side=side)
                while state["i"] < nb:
                    side(0)
                R.flush()
            with Scope(nc) as _s3:
                ffn_down(nc, R, L, _s3, p_tiles, ACTSp, b_actsp, dcp)
                R.flush()
        build_hT(nc, R, H2Ts, [b_h2ts[NS]], [0], lambda t: hx_scr, [b_hs], A2, B2, b_mod, ident_b, b_const,
                 lambda t: 1, [64], col_of=lambda t: 2048, anti=anti_b)
        dump("H2T_sample", H2Ts[:], [128, 8, 2112], b_h2ts[0], BF16)
        with Scope(nc) as _ss:
            ACTSs = _ss.sb([128, NJ, 2048], BF16)
            b_actss = [Buf() for _ in range(NJ)]
            with Scope(nc) as _s4:
                ffn_up(nc, R, L, _s4, "sample", H2Ts, b_h2ts, ACTSs, b_actss, 2048, 2112, NS, 8)
                R.flush()
            dump("ACTS_sample", ACTSs[:], [128, NJ, 2048], b_actss[0], BF16)
            with Scope(nc) as _s5:
                dcs = ffn_down_consts(nc, R, L, _s5, 1)
                ffn_down(nc, R, L, _s5, s_tiles, ACTSs, b_actss, dcs)
                R.flush()


_CACHE = {}


def _consts():
    i = np.arange(128)
    ident = np.eye(128, dtype=np.float32)
    anti = np.zeros((128, 128), np.float32)
    anti[np.arange(64), 63 - np.arange(64)] = 1.0
    mk0 = (i[:, None] <= i[None, :]).astype(np.float32)
    mk1 = (i[:, None] >= i[None, :]).astype(np.float32)
    selj = np.zeros((2, 256), np.float32)
    selj[0, 0:128] = 1.0
    selj[1, 128:256] = 1.0
    return dict(ident=ident, antiI=anti, maskT0=mk0, maskT1=mk1, ones=np.ones((128, 128), np.float32),
                selj=selj, i2=np.eye(2, dtype=np.float32), i8=np.eye(8, dtype=np.float32))


def _kmajor(w):
    return np.ascontiguousarray(w.reshape(8, 128, -1).transpose(1, 0, 2))


def make_in_maps(x_prompt, x_sample, state_C, state_n, state_m, c, c_ctx, w_mod, b_mod, norm1, w_in, b_gate,
                 conv_sc_w, conv_sc_b, mh_norm, w_out, norm2, w_up, conv_ffn_w, conv_ffn_b, w_down, final_norm):
    f = np.float32
    A = lambda a: np.asarray(a, dtype=f)
    x_prompt, x_sample, state_C, state_n, state_m, c, c_ctx = map(A, (x_prompt, x_sample, state_C, state_n, state_m, c, c_ctx))
    w_mod, b_mod, norm1, w_in, b_gate = A(w_mod)[0], A(b_mod)[0], A(norm1)[0], A(w_in)[0], A(b_gate)[0]
    conv_sc_w, conv_sc_b, mh_norm, w_out, norm2 = A(conv_sc_w)[0], A(conv_sc_b)[0], A(mh_norm)[0], A(w_out)[0], A(norm2)[0]
    w_up, conv_ffn_w, conv_ffn_b, w_down, final_norm = A(w_up)[0], A(conv_ffn_w)[0], A(conv_ffn_b)[0], A(w_down)[0], A(final_norm)
    cst = _consts()
    shared = dict(cst)
    shared["w_mod_r"] = np.ascontiguousarray(_kmajor(w_mod).reshape(128, 8, 12, 512).transpose(2, 0, 1, 3))
    shared["b_mod2"] = np.ascontiguousarray(np.broadcast_to(b_mod[None, :], (2, 6144)))
    shared["norm1c"] = np.ascontiguousarray(norm1.reshape(8, 128).T)
    shared["norm2c"] = np.ascontiguousarray(norm2.reshape(8, 128).T)
    wk = _kmajor(w_in)
    shared["wic"] = np.ascontiguousarray(np.stack([np.concatenate([wk[:, :, cb * 128:(cb + 1) * 128], wk[:, :, 512 + cb * 128:512 + (cb + 1) * 128],
                                                                   wk[:, :, 1024 + cb * 128:1024 + (cb + 1) * 128]], axis=2) for cb in range(4)]))
    shared["wiqk"] = np.ascontiguousarray(np.stack([wk[:, :, 1536 + b * 128:1536 + (b + 1) * 128] for b in range(8)]))
    shared["wiv"] = np.ascontiguousarray(wk[:, :, 2560:3072])
    shared["wio"] = np.ascontiguousarray(wk[:, :, 3072:3584])
    shared["mhn_row"] = np.ascontiguousarray(np.broadcast_to(mh_norm[None, :], (128, 512)))
    shared["w_out_r"] = _kmajor(w_out)
    wu = _kmajor(w_up)
    shared["w_up_r"] = np.ascontiguousarray(np.stack([np.concatenate([wu[:, :, j * 128:(j + 1) * 128], wu[:, :, DFF + j * 128:DFF + (j + 1) * 128]], axis=2)
                                                      for j in range(NJ)]))
    shared["w_down_r"] = np.ascontiguousarray(w_down.reshape(NJ, 128, D).transpose(1, 0, 2))
    shared["fn_row"] = np.ascontiguousarray(np.broadcast_to(final_norm[None, :], (128, D)))
    maps = []
    for core in range(8):
        par = core % 2
        b = core // 2
        m = dict(shared)
        ps = [x_prompt[2 * core], x_prompt[2 * core + 1]]
        xs = x_sample[b, 0:2048] if par == 0 else x_sample[b, 2048:4096]
        if par == 1:
            ps = [p[::-1] for p in ps]
            xs = xs[::-1]
        m["xin"] = np.ascontiguousarray(np.concatenate(ps + [xs], axis=0))
        cv = np.stack([c_ctx, c[b]], axis=1)
        m["cT"] = np.ascontiguousarray(cv.reshape(8, 128, 2).transpose(1, 0, 2))
        gperm = [0, 1, 2, 3] if par == 0 else [2, 3, 0, 1]
        wg = wk[:, :, 3584:3600].reshape(128, 8, 4, 4)[:, :, gperm, :].reshape(128, 8, 16)
        m["wig"] = np.ascontiguousarray(wg)
        bg = b_gate[gperm, :].reshape(16)
        m["bgate_row"] = np.ascontiguousarray(np.broadcast_to(bg[None, :], (128, 16)))
        tap = [0, 1, 2] if par == 0 else [2, 1, 0]
        csc = np.stack([conv_sc_w[tap[0]], conv_sc_w[tap[1]], conv_sc_w[tap[2]], conv_sc_b], axis=1)
        m["conv_sc"] = np.ascontiguousarray(csc.reshape(4, 128, 4).transpose(1, 0, 2))
        cf = np.stack([conv_ffn_w[tap[0]], conv_ffn_w[tap[1]], conv_ffn_w[tap[2]], conv_ffn_b], axis=1)
        m["conv_ffn"] = np.ascontiguousarray(cf.reshape(44, 128, 4).transpose(1, 0, 2))
        dsel = par
        C0 = state_C[b, 0, dsel]
        n0 = state_n[b, 0, dsel]
        s0 = np.concatenate([C0.transpose(2, 0, 1), n0.T[:, :, None]], axis=2)
        m["s0"] = np.ascontiguousarray(s0)
        m["m0"] = np.ascontiguousarray(np.broadcast_to(state_m[b, 0, dsel][None, :], (128, 4)))
        sel = np.zeros((128, 2), f)
        sel[:, par] = 1.0
        m["sel"] = sel
        maps.append(m)
    return maps


def assemble(results):
    f = np.float32
    y_prompt = np.zeros((16, 256, D), f)
    y_sample = np.zeros((4, 4096, D), f)
    new_C = np.zeros((16, 1, 2, 4, 128, 128), f)
    new_n = np.zeros((16, 1, 2, 4, 128), f)
    new_m = np.zeros((16, 1, 2, 4), f)
    for core in range(8):
        r = results[core]
        par = core % 2
        b = core // 2
        y = np.asarray(r["y"], dtype=f)
        for i in range(2):
            yp = y[i * 256:(i + 1) * 256]
            y_prompt[2 * core + i] = yp[::-1] if par else yp
            for dl in range(2):
                dg = dl if par == 0 else 1 - dl
                new_C[2 * core + i, 0, dg] = r["oC"][i, dl]
                new_n[2 * core + i, 0, dg] = r["on"][i, dl]
                new_m[2 * core + i, 0, dg] = r["om"][i, dl]
        ys = y[512:2560]
        if par == 0:
            y_sample[b, 0:2048] = ys
        else:
            y_sample[b, 2048:4096] = ys[::-1]
    return (y_prompt, y_sample, new_C, new_n, new_m)


def kernel(**inputs):
    maps = make_in_maps(**inputs)
    if "nc" not in _CACHE:
        _CACHE["nc"] = build_program()[0]
    res = run_bass_kernel_spmd(_CACHE["nc"], maps, core_ids=list(range(8)))
    return assemble(res.results)
```

```python
import contextlib
import numpy as np
import concourse.bass as bass
import concourse.mybir as mybir
from concourse.bass_utils import run_bass_kernel_spmd

F32 = mybir.dt.float32
BF16 = mybir.dt.bfloat16
AF = mybir.ActivationFunctionType
ALU = mybir.AluOpType
AX = mybir.AxisListType

D = 1024
NT = 20
NTOK = 2560
DH = 128
NH = 4
DFF = 2816
NJ = 22
EPS = 1e-6
ENGS = ["sp", "act", "pool", "dve", "pe"]


class Scope:
    def __init__(self, nc):
        self.nc = nc
        self.stack = contextlib.ExitStack()

    def __enter__(self):
        self.stack.__enter__()
        return self

    def __exit__(self, *a):
        return self.stack.__exit__(*a)

    _n = [0]

    def sb(self, shape, dt=F32):
        Scope._n[0] += 1
        return self.stack.enter_context(self.nc.sbuf_tensor("sb%d" % Scope._n[0], list(shape), dt))

    def ps(self, shape, dt=F32):
        Scope._n[0] += 1
        return self.stack.enter_context(self.nc.psum_tensor("ps%d" % Scope._n[0], list(shape), dt))


class Op:
    __slots__ = ("eng", "fn", "waits", "sig", "val", "dma", "cc", "sem", "block", "extra", "store")

    def __init__(self, eng, fn, block):
        self.eng = eng
        self.fn = fn
        self.waits = []
        self.sig = False
        self.val = None
        self.dma = False
        self.cc = False
        self.sem = None
        self.block = block
        self.extra = []
        self.store = False


class Buf:
    __slots__ = ("name", "ws", "r", "parts", "gen_deps")

    def __init__(self, name=""):
        self.name = name
        self.ws = []
        self.r = []
        self.parts = False
        self.gen_deps = []


class Rec:
    def __init__(self, nc, nring=8):
        self.nc = nc
        self.engs = {"sp": nc.sync, "act": nc.scalar, "pool": nc.gpsimd, "dve": nc.vector, "pe": nc.tensor}
        self.pending = {e: [] for e in ENGS}
        self.block = 0
        self.csem = {e: nc.alloc_semaphore("c_" + e) for e in ["act", "pool", "dve", "pe"]}
        self.ccount = {e: 0 for e in self.csem}
        self.nring = nring
        self.dsem = {q: [nc.alloc_semaphore("d_%s%d" % (q, i)) for i in range(nring)] for q in ["sp", "pool"]}
        self.duse = {q: [0] * nring for q in self.dsem}
        self.dnext = {q: 0 for q in self.dsem}
        self.ccsem = nc.alloc_semaphore("ccsem")
        self.cccount = 0
        self.known = {e: {} for e in ENGS}

    def _deps(self, op, reads, writes, parts=False):
        deps = {}
        is_async = op.dma or op.cc

        def add(d):
            if d is None:
                return
            d_async = d.dma or d.cc
            if not d_async and d.block != self.block:
                return
            if not d_async and not is_async and d.eng == op.eng and op.eng == "pe":
                return
            deps[id(d)] = d

        for b in reads:
            for ww in b.ws:
                add(ww)
        for b in writes:
            same_gen = parts and b.parts and not b.r and b.ws
            if same_gen:
                for gd in b.gen_deps:
                    add(gd)
            else:
                b.gen_deps = list(b.ws) + list(b.r)
                for ww in b.ws:
                    add(ww)
            for rr in b.r:
                add(rr)
        for d in deps.values():
            if not (d.dma or d.cc):
                d.sig = True
        op.waits = list(deps.values())
        for b in reads:
            if b in writes:
                continue
            if not is_async:
                b.r = [x for x in b.r if (x.dma or x.cc or x.eng != op.eng)]
            b.r.append(op)
        for b in writes:
            same_gen = parts and b.parts and not b.r and b.ws
            if same_gen:
                if not is_async:
                    b.ws = [x for x in b.ws if (x.dma or x.cc or x.eng != op.eng)]
                b.ws.append(op)
            else:
                b.ws = [op]
            b.parts = parts
            b.r = []

    def op(self, eng, fn, reads=(), writes=(), parts=False):
        o = Op(eng, fn, self.block)
        self._deps(o, list(reads), list(writes), parts)
        self.pending[eng].append(o)
        return o

    def dma(self, q, out, in_, reads=(), writes=(), parts=False, **kw):
        o = Op(q, lambda e: e.dma_start(out=out, in_=in_, **kw), self.block)
        o.dma = True
        try:
            o.store = "DRam" in type(out.tensor).__name__
        except Exception:
            o.store = True
        slot = self.dnext[q]
        self.dnext[q] = (slot + 1) % self.nring
        prev = self.duse[q][slot]
        o.sem = self.dsem[q][slot]
        if prev > 0:
            o.extra.append((o.sem, 16 * prev))
        self.duse[q][slot] = prev + 1
        o.val = 16 * (prev + 1)
        self._deps(o, list(reads), list(writes), parts)
        self.pending[q].append(o)
        return o

    def collective(self, fn, reads=(), writes=()):
        o = Op("pool", fn, self.block)
        o.cc = True
        self.cccount += 1
        o.sem = self.ccsem
        o.val = self.cccount
        self._deps(o, list(reads), list(writes))
        self.pending["pool"].append(o)
        return o

    def flush(self, final=False):
        if not final and not any(self.pending[e] for e in ENGS):
            return
        for e in ENGS:
            for o in self.pending[e]:
                if o.dma or o.cc:
                    continue
                if o.sig:
                    self.ccount[e] += 1
                    o.val = self.ccount[e]
        names = {"sp": "sync", "act": "scalar", "pool": "gpsimd", "dve": "vector", "pe": "tensor"}
        with self.nc.Block() as block:
            for e in ENGS:
                ops = self.pending[e]
                if not ops and not (final and e == "sp"):
                    continue

                def body(engine, ops=ops, e=e):
                    for o in ops:
                        w = {}
                        for (s, v) in o.extra:
                            w[s.num] = (s, max(v, w.get(s.num, (s, 0))[1]))
                        for d in o.waits:
                            s = d.sem if (d.dma or d.cc) else self.csem[d.eng]
                            w[s.num] = (s, max(d.val, w.get(s.num, (s, 0))[1]))
                        kn = self.known[e]
                        for (s, v) in w.values():
                            if kn.get(s.num, 0) >= v:
                                continue
                            engine.wait_ge(s, v)
                            kn[s.num] = v
                        ins = o.fn(engine)
                        if o.dma:
                            ins.then_inc(o.sem, 16)
                        elif o.cc:
                            ins.then_inc(o.sem, 1)
                        elif o.sig:
                            ins.then_inc(self.csem[e], 1)
                    tail = {}
                    for o in ops:
                        if o.dma and o.store:
                            tail[o.sem.num] = (o.sem, max(o.val, tail.get(o.sem.num, (o.sem, 0))[1]))
                    for (sm, v) in tail.values():
                        if self.known[e].get(sm.num, 0) >= v:
                            continue
                        engine.wait_ge(sm, v)
                        self.known[e][sm.num] = v
                    if final and e == "sp":
                        for q in self.dsem:
                            for i in range(self.nring):
                                if self.duse[q][i] > 0:
                                    engine.wait_ge(self.dsem[q][i], 16 * self.duse[q][i])

                getattr(block, names[e])(body)
        self.pending = {e: [] for e in ENGS}
        self.block += 1


def build_program(dbg=()):
    nc = bass.Bass("TRN2", target_bir_lowering=False)
    R = Rec(nc)
    dbg = set(dbg)
    dbg_outs = {}

    def din(name, shape, dt=F32):
        return nc.dram_tensor(name, list(shape), dt, kind="ExternalInput").ap()

    def dout(name, shape, dt=F32):
        return nc.dram_tensor(name, list(shape), dt, kind="ExternalOutput").ap()

    xin = din("xin", [NTOK, D])
    cT_d = din("cT", [128, 8, 2])
    wmod_d = din("w_mod_r", [12, 128, 8, 512])
    bmod_d = din("b_mod2", [2, 6144])
    n1c_d = din("norm1c", [128, 8])
    n2c_d = din("norm2c", [128, 8])
    wic_d = din("wic", [4, 128, 8, 384])
    wiqk_d = din("wiqk", [8, 128, 8, 128])
    wiv_d = din("wiv", [128, 8, 512])
    wio_d = din("wio", [128, 8, 512])
    wig_d = din("wig", [128, 8, 16])
    bg_d = din("bgate_row", [128, 16])
    csc_d = din("conv_sc", [128, 4, 4])
    mhn_d = din("mhn_row", [128, 512])
    wout_d = din("w_out_r", [128, 8, 1024])
    wup_d = din("w_up_r", [NJ, 128, 8, 256])
    cff_d = din("conv_ffn", [128, 44, 4])
    wdn_d = din("w_down_r", [128, NJ, 1024])
    fn_d = din("fn_row", [128, 1024])
    s0_d = din("s0", [128, 4, 129])
    m0_d = din("m0", [128, 4])
    sel_d = din("sel", [128, 2])
    ident_d = din("ident", [128, 128])
    anti_d = din("antiI", [128, 128])
    mk0_d = din("maskT0", [128, 128])
    mk1_d = din("maskT1", [128, 128])
    ones_d = din("ones", [128, 128])
    selj_d = din("selj", [2, 256])
    i2_d = din("i2", [2, 2])
    i8_d = din("i8", [8, 8])

    y_d = dout("y", [NTOK, D])
    oC_d = dout("oC", [2, 2, 4, 128, 128])
    on_d = dout("on", [2, 2, 4, 128])
    om_d = dout("om", [2, 2, 4])

    x1s_d = nc.dram_tensor("x1s", [NTOK, D], F32).ap()
    gate_d = nc.dram_tensor("gate_scr", [2, 2048], F32).ap()
    sx_in = nc.dram_tensor("sx_in", [128, 520], F32)
    sx_out = nc.dram_tensor("sx_out", [256, 520], F32)
    hx_in = nc.dram_tensor("hx_in", [64, 1024], F32)
    hx_out = nc.dram_tensor("hx_out", [128, 1024], F32)
    b_x1s = [Buf("x1s%d" % i) for i in range(NT)]
    b_gate_d = Buf("gate_d")
    b_sxin, b_sxout, b_hxin, b_hxout = Buf(), Buf(), Buf(), Buf()
    hx_scr = nc.dram_tensor("hx_scr", [64, D], F32).ap()
    b_hs = Buf()
    RG = [[0, 1], [2, 3], [4, 5], [6, 7]]

    def dump(name, ap, shape, buf, dt=F32):
        if name not in dbg:
            return
        t = dout("dbg_" + name, shape, dt)
        dbg_outs[name] = (shape, dt)
        R.dma("sp", t, ap, reads=[buf])

    def T(name, shape, dt=F32):
        return nc.sbuf_tensor(list(shape), dt)

    def PS(name, shape, dt=F32):
        return nc.psum_tensor(list(shape), dt)

    with Scope(nc) as _sc:
        ident_f = _sc.sb([128, 128])
        ones_f = _sc.sb([128, 128])
        tri0_f = _sc.sb([128, 128])
        tri1_f = _sc.sb([128, 128])
        ident_b = _sc.sb([128, 128], BF16)
        anti_b = _sc.sb([128, 128], BF16)
        mk0_b = _sc.sb([128, 128], BF16)
        mk1_b = _sc.sb([128, 128], BF16)
        selj = _sc.sb([2, 256])
        i2 = _sc.sb([2, 2])
        i8 = _sc.sb([8, 8])
        n1c = _sc.sb([128, 8])
        n2c = _sc.sb([128, 8])
        bg_row = _sc.sb([128, 16])
        csc = _sc.sb([128, 4, 4])
        cff = _sc.sb([128, 44, 4])
        sel = _sc.sb([128, 2])
        A1 = _sc.sb([128, 8, 2])
        B1 = _sc.sb([128, 8, 2])
        A2 = _sc.sb([128, 8, 2])
        B2 = _sc.sb([128, 8, 2])
        b_const = Buf("const")
        JN = _sc.sb([128, 2])

        def load_consts():
            for (t, d) in [(i2, i2_d), (n1c, n1c_d), (ident_f, ident_d), (ones_f, ones_d), (tri0_f, mk0_d), (tri1_f, mk1_d),
                           (selj, selj_d), (i8, i8_d), (n2c, n2c_d),
                           (bg_row, bg_d), (csc, csc_d), (cff, cff_d), (sel, sel_d)]:
                R.dma("sp", t[:], d, writes=[b_const], parts=True)
            for (t, d) in [(ident_b, ident_d), (anti_b, anti_d), (mk0_b, mk0_d), (mk1_b, mk1_d)]:
                R.dma("pool", t[:], d, writes=[b_const], parts=True)
            R.op("dve", lambda e: e.memset(JN[:, :], 0.0), reads=[b_const], writes=[b_const])
        b_mod = Buf("modcols")

        HT = nc.alloc_sbuf_tensor_at("HT", [128, 8, NTOK], BF16, offset=172032)
        b_ht = [Buf() for _ in range(NT)]
        HT_BYTES = 8 * NTOK * 2
        with Scope(nc) as _sc:
            cT_f = _sc.sb([128, 8, 2])
            cs_b = _sc.sb([128, 8, 2], BF16)
            bmod = _sc.sb([2, 6144])
            modrow = _sc.sb([2, 6144])
            wm0 = _sc.sb([128, 8, 512], BF16)
            wm1 = _sc.sb([128, 8, 512], BF16)
            wm2 = _sc.sb([128, 8, 512], BF16)
            tmpc = _sc.sb([128, 8, 2])
            pm0 = _sc.ps([128, 512])
            pm1 = _sc.ps([128, 512])
            pcol = _sc.ps([128, 512])
            b_ct, b_cs, b_bmod, b_modrow, b_tmpc = Buf(), Buf(), Buf(), Buf("modrow"), Buf()
            R.dma("sp", cT_f[:], cT_d, writes=[b_ct])
            R.dma("sp", bmod[:], bmod_d, writes=[b_bmod])
            R.op("act", lambda e: e.activation(out=cs_b[:], in_=cT_f[:], func=AF.Silu), reads=[b_ct], writes=[b_cs])
            wms = [wm0, wm1, wm2]
            b_wm = [Buf() for _ in range(3)]
            pms = [pm0, pm1]
            b_pm = [Buf(), Buf()]
            for n in range(3):
                R.dma("pool", wms[n][:], wmod_d[n], writes=[b_wm[n]])
            load_consts()
            b_pcol = Buf()

            def colv(vi):
                return pcol[:, vi * 16:(vi + 1) * 16].rearrange("p (a b) -> p a b", b=2)

            def wchunk(n):
                s = n % 3
                if n >= 3:
                    R.dma("pool", wms[s][:], wmod_d[n], writes=[b_wm[s]])
                p = n % 2
                for kc in range(8):
                    R.op("pe", lambda e, kc=kc: e.matmul(pms[p][0:2, :], lhsT=cs_b[:, kc, :], rhs=wms[s][:, kc, :], start=(kc == 0), stop=(kc == 7)),
                         reads=[b_cs, b_wm[s]], writes=[b_pm[p]])
                R.op("dve", lambda e: e.tensor_tensor(out=modrow[:, n * 512:(n + 1) * 512], in0=pms[p][0:2, :], in1=bmod[:, n * 512:(n + 1) * 512], op=ALU.add),
                     reads=[b_pm[p], b_bmod], writes=[b_modrow])

            def cols(Aout, Bout, vsh, vsc, ncol):
                offs = [0, 1024, 3072, 4096]
                for vi in (vsh, vsc):
                    for kc in range(8):
                        c0 = (vi * 8 + kc) * 2
                        off = offs[vi]
                        R.op("pe", lambda e, off=off, kc=kc, c0=c0: e.matmul(pcol[:, c0:c0 + 2], lhsT=modrow[0:2, off + kc * 128: off + (kc + 1) * 128],
                                                                            rhs=i2[:, :], start=True, stop=True),
                             reads=[b_modrow, b_const], writes=[b_pcol])
                R.op("dve", lambda e: e.tensor_scalar(out=tmpc[:], in0=colv(vsc), scalar1=1.0, scalar2=None, op0=ALU.add), reads=[b_pcol], writes=[b_tmpc])
                R.op("dve", lambda e: e.tensor_tensor(out=Aout[:], in0=tmpc[:], in1=ncol[:, :].unsqueeze(2).to_broadcast([128, 8, 2]), op=ALU.mult),
                     reads=[b_tmpc, b_const], writes=[b_mod])
                R.op("dve", lambda e: e.tensor_copy(out=Bout[:], in_=colv(vsh)), reads=[b_pcol], writes=[b_mod])

            bs1, bs2, nb = make_builder(nc, R, _sc, HT, b_ht, list(range(NT)), lambda t: xin[t * 128:(t + 1) * 128, :], [None] * NT,
                                        A1, B1, b_mod, ident_b, b_const, lambda t: 0 if t < 4 else 1, [128] * NT)
            bs1(0)
            for n in range(4):
                wchunk(n)
            cols(A1, B1, 0, 1, n1c)
            nextc = 4
            for i in range(nb):
                if i + 1 < nb:
                    bs1(i + 1)
                bs2(i)
                if i % 2 == 1 and nextc < 12:
                    wchunk(nextc)
                    if nextc == 5:
                        R.dma("sp", gate_d[:, 0:1024], modrow[:, 2048:3072], reads=[b_modrow], writes=[b_gate_d])
                    if nextc == 9:
                        cols(A2, B2, 2, 3, n2c)
                    nextc += 1
            while nextc < 12:
                wchunk(nextc)
                if nextc == 9:
                    cols(A2, B2, 2, 3, n2c)
                nextc += 1
            R.dma("sp", gate_d[:, 1024:2048], modrow[:, 5120:6144], reads=[b_modrow], writes=[b_gate_d])
            dump("modrow", modrow[:], [2, 6144], b_modrow)
            R.flush()

        if True:

            with Scope(nc) as _sc:
                QT = _sc.sb([128, NT, 4, 128], BF16)
                KT = _sc.sb([128, NT, 4, 128], BF16)
                V1 = _sc.sb([128, NT, 4, 130], BF16)
                OG = _sc.sb([128, NT, 512], BF16)
                YSC = _sc.sb([128, NT, 4, 128], BF16)
                GZ = _sc.sb([128, NT, 16])
                RR = _sc.sb([128, NT, 8])
                CUM = _sc.sb([128, NT, 8])
                FTOT = _sc.sb([128, NT, 8])
                RMAX = _sc.sb([128, NT, 8])
                G1ROW = _sc.sb([128, 2, 1024])
                b_qt = [Buf() for _ in range(NT)]
                b_kt = [Buf() for _ in range(NT)]
                b_v1 = [Buf() for _ in range(NT)]
                b_og = [Buf() for _ in range(NT)]
                b_ysc = [Buf() for _ in range(NT)]
                b_gz = [Buf() for _ in range(NT)]
                b_gs = [Buf() for _ in range(NT)]
                b_g1 = Buf("g1row")
                for j in range(2):
                    R.dma("sp", G1ROW[:, j, :], gate_d[j:j + 1, 0:1024].to_broadcast([128, 1024]), reads=[b_gate_d], writes=[b_g1])
                for t in range(NT):
                    R.op("pool", lambda e, t=t: e.memset(V1[:, t, :, 128:130], 1.0), writes=[b_v1[t]], parts=True)

                if True:
                    dump("HT", HT[:], [128, 8, NTOK], b_ht[0], BF16)
                    phase1(nc, R, locals())
                    R.flush()
                for nm, tt, bb, shp, dtt in [("QT", QT, b_qt, [128, NT, 4, 128], BF16), ("KT", KT, b_kt, [128, NT, 4, 128], BF16),
                                             ("V1", V1, b_v1, [128, NT, 4, 130], BF16), ("OG", OG, b_og, [128, NT, 512], BF16),
                                             ("YSC", YSC, b_ysc, [128, NT, 4, 128], BF16), ("GZ", GZ, b_gz, [128, NT, 16], F32),
                                             ("RR", RR, b_gs, [128, NT, 8], F32), ("CUM", CUM, b_gs, [128, NT, 8], F32),
                                             ("FTOT", FTOT, b_gs, [128, NT, 8], F32), ("RMAX", RMAX, b_gs, [128, NT, 8], F32)]:
                    dump(nm, tt[:], shp, bb[0], dtt)
                phase2(nc, R, locals())
                R.flush()

            ffn_all(nc, R, locals())
        R.flush(final=True)
    return nc, dbg_outs


def make_builder(nc, R, _sc, HT, b_ht, tiles, src_of, src_bufs, A, B, b_mod, ident_b, b_const, typ_of, ntoks, col_of=None, anti=None):
    xts = [_sc.sb([128, D], F32) for _ in range(3)]
    xns = [_sc.sb([128, D], BF16) for _ in range(2)]
    sqj = _sc.sb([128, D], BF16)
    ss = _sc.sb([128, 4], F32)
    rs = _sc.sb([128, 4], F32)
    pts = [[_sc.ps([128, 8, 128], BF16) for _ in range(2)] for _ in range(2)]
    ptf = [_sc.ps([128, 4, 128], F32) for _ in range(2)] if anti is not None else None
    b_xt = [Buf() for _ in range(3)]
    b_xn = [Buf() for _ in range(2)]
    b_pt = [[Buf(), Buf()] for _ in range(2)]
    b_ss = [Buf() for _ in range(4)]
    b_rs = [Buf() for _ in range(4)]
    b_sq = Buf()

    def s1(i):
        t = tiles[i]
        n = ntoks[i]
        xs, ns, q = i % 3, i % 2, i % 4
        R.dma("sp", xts[xs][0:n, :], src_of(t), reads=[src_bufs[i]] if src_bufs[i] is not None else [], writes=[b_xt[xs]])
        R.op("act", lambda e: e.activation(out=sqj[0:n, :], in_=xts[xs][0:n, :], func=AF.Square, accum_out=ss[0:n, q:q + 1]),
             reads=[b_xt[xs]], writes=[b_sq, b_ss[q]])
        R.op("act", lambda e: e.activation(out=rs[0:n, q:q + 1], in_=ss[0:n, q:q + 1], func=AF.Sqrt, bias=EPS, scale=1.0 / D),
             reads=[b_ss[q]], writes=[b_rs[q]])
        R.op("dve", lambda e: e.reciprocal(out=rs[0:n, q:q + 1], in_=rs[0:n, q:q + 1]), reads=[b_rs[q]], writes=[b_rs[q]])
        R.op("dve", lambda e: e.tensor_scalar(out=xns[ns][0:n, :], in0=xts[xs][0:n, :], scalar1=rs[0:n, q:q + 1], scalar2=None, op0=ALU.mult),
             reads=[b_xt[xs], b_rs[q]], writes=[b_xn[ns]])

    def s2(i):
        t = tiles[i]
        n = ntoks[i]
        ns, ps_ = i % 2, i % 2
        c0 = col_of(t) if col_of else t * 128
        for kc in range(8):
            eo, k2 = kc % 2, kc // 2
            if anti is None:
                R.op("pe", lambda e, kc=kc, eo=eo, k2=k2: e.transpose(pts[ps_][eo][:, k2, 0:n], xns[ns][0:n, kc * 128:(kc + 1) * 128], ident_b[0:n, 0:n]),
                     reads=[b_xn[ns], b_const], writes=[b_pt[ps_][eo]])
            else:
                R.op("pe", lambda e, kc=kc, eo=eo, k2=k2: e.matmul(ptf[eo][:, k2, 0:n], lhsT=xns[ns][0:n, kc * 128:(kc + 1) * 128], rhs=anti[0:n, 0:n], start=True, stop=True),
                     reads=[b_xn[ns], b_const], writes=[b_pt[ps_][eo]])
        j = typ_of(t)
        for kc in range(8):
            eo, k2 = kc % 2, kc // 2
            src_ps = pts[ps_][eo] if anti is None else ptf[eo]
            if eo == 0:
                R.op("act", lambda e, kc=kc, k2=k2, src_ps=src_ps: e.activation(out=HT[:, kc, c0:c0 + n], in_=src_ps[:, k2, 0:n], func=AF.Identity,
                                                                               scale=A[:, kc, j:j + 1], bias=B[:, kc, j:j + 1]),
                     reads=[b_pt[ps_][0], b_mod], writes=[b_ht[i]], parts=True)
            else:
                R.op("dve", lambda e, kc=kc, k2=k2, src_ps=src_ps: e.tensor_scalar(out=HT[:, kc, c0:c0 + n], in0=src_ps[:, k2, 0:n],
                                                                                  scalar1=A[:, kc, j:j + 1], scalar2=B[:, kc, j:j + 1], op0=ALU.mult, op1=ALU.add),
                     reads=[b_pt[ps_][1], b_mod], writes=[b_ht[i]], parts=True)

    return s1, s2, len(tiles)


def build_hT(nc, R, HT, b_ht, tiles, src_of, src_bufs, A, B, b_mod, ident_b, b_const, typ_of, ntoks, col_of=None, anti=None):
    with Scope(nc) as _sc:
        s1, s2, nt = make_builder(nc, R, _sc, HT, b_ht, tiles, src_of, src_bufs, A, B, b_mod, ident_b, b_const, typ_of, ntoks, col_of, anti)
        s1(0)
        for i in range(nt):
            if i + 1 < nt:
                s1(i + 1)
            s2(i)
        R.flush()


def phase1(nc, R, L):
    HT, b_ht = L["HT"], L["b_ht"]
    QT, KT, V1, OG, YSC, GZ, RR, CUM, FTOT, RMAX = (L[k] for k in ["QT", "KT", "V1", "OG", "YSC", "GZ", "RR", "CUM", "FTOT", "RMAX"])
    b_qt, b_kt, b_v1, b_og, b_ysc, b_gz, b_gs = (L[k] for k in ["b_qt", "b_kt", "b_v1", "b_og", "b_ysc", "b_gz", "b_gs"])
    csc, bg_row, b_const = L["csc"], L["bg_row"], L["b_const"]
    tri0_f, tri1_f, ones_f, ident_f, i8 = L["tri0_f"], L["tri1_f"], L["ones_f"], L["ident_f"], L["i8"]
    wic_d, wiqk_d, wiv_d, wio_d, wig_d = L["wic_d"], L["wiqk_d"], L["wiv_d"], L["wio_d"], L["wig_d"]
    with Scope(nc) as _sc:
        ws0 = _sc.sb([128, 8, 512], BF16)
        ws1 = _sc.sb([128, 8, 512], BF16)
        wgt = _sc.sb([128, 8, 16], BF16)
        scc0 = _sc.sb([128, 512], F32)
        scc1 = _sc.sb([128, 512], F32)
        u0 = _sc.sb([128, 512], F32)
        u1 = _sc.sb([128, 512], F32)
        ac0 = _sc.sb([128, 512], F32)
        ac1 = _sc.sb([128, 512], F32)
        LTALL = _sc.sb([128, NT, 8], F32)
        MX80 = _sc.sb([80, 2], F32)
        DG80 = _sc.sb([80, 2, 80], F32)
        p0 = _sc.ps([128, 512], F32)
        p1 = _sc.ps([128, 512], F32)
        p2 = _sc.ps([128, 512], F32)
        p3 = _sc.ps([128, 512], F32)
        p4 = _sc.ps([128, 512], F32)
        p5 = _sc.ps([128, 512], F32)
        p6 = _sc.ps([128, 512], F32)
        p7 = _sc.ps([128, 512], F32)
        assert nc.sbuf_bytes_remaining >= 8 * NTOK * 2 + 64, nc.sbuf_bytes_remaining
        ws = [ws0, ws1]
        b_ws = [Buf(), Buf()]
        pp = [p0, p1, p2, p3, p4, p5, p6, p7]
        b_pp = [Buf() for _ in range(8)]
        sccs, us, acs = [scc0, scc1], [u0, u1], [ac0, ac1]
        b_scc, b_u, b_ac = [Buf(), Buf()], [Buf(), Buf()], [Buf(), Buf()]
        wslot = [0]

        def next_w():
            s = wslot[0] % 2
            wslot[0] += 1
            return s

        def grp_bufs(bl, g):
            return [bl[4 * g + i] for i in range(4)]

        step = 0
        for cb in range(4):
            s = next_w()
            R.dma("pool", ws[s][:, :, 0:384], wic_d[cb], writes=[b_ws[s]])
            for g in range(5):
                pb = (step % 2) * 3
                k2 = step % 2
                step += 1
                hbufs = grp_bufs(b_ht, g)
                for part in range(3):
                    for kc in range(8):
                        R.op("pe", lambda e, s=s, part=part, kc=kc, g=g, pb=pb: e.matmul(pp[pb + part][:, :], lhsT=ws[s][:, kc, part * 128:(part + 1) * 128],
                                                                                     rhs=HT[:, kc, g * 512:(g + 1) * 512], start=(kc == 0), stop=(kc == 7)),
                             reads=[b_ws[s]] + hbufs, writes=[b_pp[pb + part]])
                W = 256 if g == 0 else 64

                def v3(ap, W=W):
                    return ap.rearrange("p (r w) -> p r w", w=W)

                R.op("act", lambda e, k2=k2, pb=pb: e.activation(out=sccs[k2][:, :], in_=pp[pb + 1][:, :], func=AF.Copy),
                     reads=[b_pp[pb + 1]], writes=[b_scc[k2]])
                R.op("dve", lambda e, k2=k2, pb=pb: e.tensor_tensor(out=us[k2][:, :], in0=pp[pb + 2][:, :], in1=sccs[k2][:, :], op=ALU.mult),
                     reads=[b_pp[pb + 2], b_scc[k2]], writes=[b_u[k2]])
                R.op("act", lambda e, k2=k2, cb=cb: e.activation(out=acs[k2][:, :], in_=us[k2][:, :], func=AF.Identity, scale=csc[:, cb, 1:2], bias=csc[:, cb, 3:4]),
                     reads=[b_u[k2], b_const], writes=[b_ac[k2]])
                R.op("dve", lambda e, k2=k2, cb=cb, v3=v3, W=W: e.scalar_tensor_tensor(out=v3(acs[k2][:, :])[:, :, 1:W], in0=v3(us[k2][:, :])[:, :, 0:W - 1], scalar=csc[:, cb, 0:1],
                                                                                      in1=v3(acs[k2][:, :])[:, :, 1:W], op0=ALU.mult, op1=ALU.add),
                     reads=[b_u[k2], b_ac[k2], b_const], writes=[b_ac[k2]])
                R.op("dve", lambda e, k2=k2, cb=cb, v3=v3, W=W: e.scalar_tensor_tensor(out=v3(acs[k2][:, :])[:, :, 0:W - 1], in0=v3(us[k2][:, :])[:, :, 1:W], scalar=csc[:, cb, 2:3],
                                                                                      in1=v3(acs[k2][:, :])[:, :, 0:W - 1], op0=ALU.mult, op1=ALU.add),
                     reads=[b_u[k2], b_ac[k2], b_const], writes=[b_ac[k2]])
                R.op("dve", lambda e, k2=k2, pb=pb, g=g, cb=cb: e.tensor_tensor(out=YSC[:, 4 * g:4 * g + 4, cb, :], in0=acs[k2][:, :].rearrange("p (a b) -> p a b", b=128),
                                                                              in1=pp[pb][:, :].rearrange("p (a b) -> p a b", b=128), op=ALU.mult),
                     reads=[b_ac[k2], b_pp[pb]], writes=grp_bufs(b_ysc, g), parts=True)
        for blk in range(8):
            s = next_w()
            R.dma("pool", ws[s][:, :, 0:128], wiqk_d[blk], writes=[b_ws[s]])
            hh = blk % 4
            for g in range(5):
                pb = 6 + (step % 2)
                step += 1
                hbufs = grp_bufs(b_ht, g)
                for kc in range(8):
                    R.op("pe", lambda e, s=s, kc=kc, g=g, pb=pb: e.matmul(pp[pb][:, :], lhsT=ws[s][:, kc, 0:128], rhs=HT[:, kc, g * 512:(g + 1) * 512],
                                                                     start=(kc == 0), stop=(kc == 7)),
                         reads=[b_ws[s]] + hbufs, writes=[b_pp[pb]])
                if blk < 4:
                    R.op("act", lambda e, pb=pb, g=g, hh=hh: e.activation(out=QT[:, 4 * g:4 * g + 4, hh, :], in_=pp[pb][:, :].rearrange("p (a b) -> p a b", b=128),
                                                                        func=AF.Copy, scale=float(DH ** -0.5)),
                         reads=[b_pp[pb]], writes=grp_bufs(b_qt, g), parts=True)
                else:
                    R.op("dve", lambda e, pb=pb, g=g, hh=hh: e.tensor_copy(out=KT[:, 4 * g:4 * g + 4, hh, :], in_=pp[pb][:, :].rearrange("p (a b) -> p a b", b=128)),
                         reads=[b_pp[pb]], writes=grp_bufs(b_kt, g), parts=True)
        for which in range(2):
            s = next_w()
            R.dma("pool", ws[s][:, :, :], wiv_d if which == 0 else wio_d, writes=[b_ws[s]])
            for t in range(NT):
                pb = step % 6
                step += 1
                for kc in range(8):
                    R.op("pe", lambda e, s=s, kc=kc, t=t, pb=pb: e.matmul(pp[pb][:, :], lhsT=HT[:, kc, t * 128:(t + 1) * 128], rhs=ws[s][:, kc, :],
                                                                     start=(kc == 0), stop=(kc == 7)),
                         reads=[b_ws[s], b_ht[t]], writes=[b_pp[pb]])
                if which == 0:
                    eng = "dve" if t % 2 == 0 else "act"
                    if eng == "dve":
                        R.op("dve", lambda e, pb=pb, t=t: e.tensor_copy(out=V1[:, t, :, 0:128], in_=pp[pb][:, :].rearrange("p (a b) -> p a b", b=128)),
                             reads=[b_pp[pb]], writes=[b_v1[t]], parts=True)
                    else:
                        R.op("act", lambda e, pb=pb, t=t: e.activation(out=V1[:, t, :, 0:128], in_=pp[pb][:, :].rearrange("p (a b) -> p a b", b=128), func=AF.Copy),
                             reads=[b_pp[pb]], writes=[b_v1[t]], parts=True)
                else:
                    R.op("act", lambda e, pb=pb, t=t: e.activation(out=OG[:, t, :], in_=pp[pb][:, :], func=AF.Sigmoid),
                         reads=[b_pp[pb]], writes=[b_og[t]])
        b_wg = Buf()
        R.dma("pool", wgt[:], wig_d, writes=[b_wg])
        b_lt = [Buf() for _ in range(4)]
        b_mx8 = [Buf() for _ in range(4)]
        b_dg8 = [Buf() for _ in range(4)]
        for t in range(NT):
            pb = step % 6
            step += 1
            q4 = t % 4
            for kc in range(8):
                R.op("pe", lambda e, kc=kc, t=t, pb=pb: e.matmul(pp[pb][:, 0:16], lhsT=HT[:, kc, t * 128:(t + 1) * 128], rhs=wgt[:, kc, :],
                                                                start=(kc == 0), stop=(kc == 7)),
                     reads=[b_wg, b_ht[t]], writes=[b_pp[pb]])
            R.op("dve", lambda e, pb=pb, t=t: e.tensor_tensor(out=GZ[:, t, :], in0=pp[pb][:, 0:16], in1=bg_row[:, :], op=ALU.add),
                 reads=[b_pp[pb], b_const], writes=[b_gz[t]])
        b_ltall, b_pga, b_mx80, b_dg80 = Buf(), Buf(), Buf(), Buf()
        gz5 = GZ[:, :, :].rearrange("p t (d y h) -> p t d y h", d=2, y=2)
        lt4 = LTALL[:, :, :].rearrange("p t (d h) -> p t d h", d=2)
        R.op("act", lambda e: e.activation(out=lt4, in_=gz5[:, :, :, 1, :], func=AF.Exp, scale=-1.0), reads=b_gz, writes=[b_ltall])
        R.op("act", lambda e: e.activation(out=LTALL[:, :, :], in_=LTALL[:, :, :], func=AF.Ln, bias=1.0, scale=1.0), reads=[b_ltall], writes=[b_ltall])
        pA, pB, pC = pp[6], pp[7], pp[0]
        R.op("pe", lambda e: e.matmul(pA[:, 0:80].rearrange("p (t h) -> p t h", h=4), lhsT=tri0_f[:, :], rhs=LTALL[:, :, 0:4], start=True, stop=True),
             reads=[b_ltall, b_const], writes=[b_pp[6]])
        R.op("pe", lambda e: e.matmul(pA[:, 80:160].rearrange("p (t h) -> p t h", h=4), lhsT=tri1_f[:, :], rhs=LTALL[:, :, 4:8], start=True, stop=True),
             reads=[b_ltall, b_const], writes=[b_pp[6]])
        R.op("pe", lambda e: e.matmul(pA[:, 160:320].rearrange("p (t h) -> p t h", h=8), lhsT=ones_f[:, :], rhs=LTALL[:, :, :], start=True, stop=True),
             reads=[b_ltall, b_const], writes=[b_pp[6]])
        R.op("dve", lambda e: e.tensor_copy(out=CUM[:, :, 0:4], in_=pA[:, 0:80].rearrange("p (t h) -> p t h", h=4)), reads=[b_pp[6]], writes=b_gs, parts=True)
        R.op("dve", lambda e: e.tensor_copy(out=CUM[:, :, 4:8], in_=pA[:, 80:160].rearrange("p (t h) -> p t h", h=4)), reads=[b_pp[6]], writes=b_gs, parts=True)
        R.op("dve", lambda e: e.tensor_copy(out=FTOT[:, :, :], in_=pA[:, 160:320].rearrange("p (t h) -> p t h", h=8)), reads=[b_pp[6]], writes=b_gs, parts=True)
        R.op("dve", lambda e: e.tensor_tensor(out=RR[:, :, :].rearrange("p t (d h) -> p t d h", d=2), in0=gz5[:, :, :, 0, :],
                                              in1=CUM[:, :, :].rearrange("p t (d h) -> p t d h", d=2), op=ALU.add),
             reads=b_gz + b_gs, writes=b_gs)
        for k in range(2):
            R.op("pe", lambda e, k=k: e.matmul(pB[0:80, k * 128:(k + 1) * 128], lhsT=RR[:, :, :].rearrange("p t h -> p (t h)")[:, k * 80:(k + 1) * 80], rhs=ident_f[:, :], start=True, stop=True),
                 reads=b_gs + [b_const], writes=[b_pp[7]])
        R.op("dve", lambda e: e.tensor_reduce(out=MX80[:, :], in_=pB[0:80, 0:256].rearrange("p (k n) -> p k n", k=2), axis=AX.X, op=ALU.max),
             reads=[b_pp[7]], writes=[b_mx80])
        for k in range(2):
            R.op("dve", lambda e, k=k: e.tensor_scalar(out=DG80[:, k, :], in0=ident_f[0:80, 0:80], scalar1=MX80[:, k:k + 1], scalar2=None, op0=ALU.mult),
                 reads=[b_mx80, b_const], writes=[b_dg80])
        for k in range(2):
            R.op("pe", lambda e, k=k: e.matmul(pC[:, k * 80:(k + 1) * 80], lhsT=ones_f[0:80, :], rhs=DG80[:, k, :], start=True, stop=True),
                 reads=[b_dg80, b_const], writes=[b_pp[0]])
        R.op("dve", lambda e: e.tensor_copy(out=RMAX[:, :, :], in_=pC[:, 0:160].rearrange("p (t h) -> p t h", h=8)), reads=[b_pp[0]], writes=b_gs)
        R.flush()


def phase2(nc, R, L):
    QT, KT, V1, OG, YSC, GZ, RR, CUM, FTOT, RMAX, G1ROW = (L[k] for k in ["QT", "KT", "V1", "OG", "YSC", "GZ", "RR", "CUM", "FTOT", "RMAX", "G1ROW"])
    b_qt, b_kt, b_v1, b_og, b_ysc, b_gs, b_g1 = (L[k] for k in ["b_qt", "b_kt", "b_v1", "b_og", "b_ysc", "b_gs", "b_g1"])
    b_const, ident_b, ident_f, mk0_b, mk1_b, sel = (L[k] for k in ["b_const", "ident_b", "ident_f", "mk0_b", "mk1_b", "sel"])
    xin, x1s_d, b_x1s = L["xin"], L["x1s_d"], L["b_x1s"]
    wout_d, mhn_d, s0_d, m0_d = L["wout_d"], L["mhn_d"], L["s0_d"], L["m0_d"]
    oC_d, on_d, om_d = L["oC_d"], L["on_d"], L["om_d"]
    sx_in, sx_out, hx_in, b_sxin, b_sxout, b_hxin, RG = (L[k] for k in ["sx_in", "sx_out", "hx_in", "b_sxin", "b_sxout", "b_hxin", "RG"])
    dump = L["dump"]
    masks = [mk0_b, mk1_b]
    with Scope(nc) as _sc:
        SNAP = _sc.sb([128, NT, 4, 130], BF16)
        WOUT = _sc.sb([128, 8, 1024], BF16)
        MHN = _sc.sb([128, 512], F32)
        MX = _sc.sb([128, NT, 8], F32)
        MPRE = _sc.sb([128, NT + 1, 8], F32)
        WG = _sc.sb([128, NT, 8], F32)
        DEC = _sc.sb([128, NT, 8], F32)
        CLP = _sc.sb([128, NT, 8], F32)
        TMPG = _sc.sb([128, NT, 8], F32)
        MFIN = _sc.sb([128, 3, 8], F32)
        SST = _sc.sb([128, 2, 4, 129], F32)
        SDB = _sc.sb([128, 4, 130], BF16)
        VW = _sc.sb([128, 2, 4, 130], BF16)
        KTK = _sc.sb([128, 4, 128], BF16)
        KTK2 = _sc.sb([128, 2, 4, 128], BF16)
        KTK2 = _sc.sb([128, 2, 4, 128], BF16)
        PT = _sc.sb([128, 2, 4, 128], BF16)
        DN = _sc.sb([128, 12], F32)
        HACC = _sc.sb([128, 2, 512], F32)
        HG0 = _sc.sb([128, 512], F32)
        HG = _sc.sb([128, 512], BF16)
        HMT = _sc.sb([128, 4, 128], BF16)
        SQ4 = _sc.sb([128, 4, 128], BF16)
        SS4 = _sc.sb([128, 2, 4], F32)
        XT = _sc.sb([128, 2, D], F32)
        X1 = _sc.sb([128, D], F32)
        CTR = _sc.sb([128, 2, 128], F32)
        SXG = _sc.sb([128, 2, 520], F32)
        SXT = _sc.sb([128, 520], F32)
        pST = _sc.ps([128, 512], F32)
        pND = _sc.ps([128, 3, 512], F32)
        pUP = _sc.ps([128, 2, 512], F32)
        pTR = _sc.ps([128, 1024], BF16)
        pWO = _sc.ps([128, 512], F32)
        b_snap = [Buf() for _ in range(NT)]
        b_wout, b_mhn = Buf(), Buf()
        R.dma("pool", WOUT[:], wout_d, writes=[b_wout])
        R.dma("sp", MHN[:], mhn_d, writes=[b_mhn])
        b_chain = Buf("chain")
        b_tmpg = Buf()
        b_mfin = Buf()
        b_sst = [[Buf() for _ in range(4)] for _ in range(2)]
        b_pst, b_pnd, b_ptr, b_pwo = Buf(), [Buf() for _ in range(3)], [Buf(), Buf()], Buf()
        _pup = Buf()
        b_pup = [_pup, _pup]
        b_sdb, b_vw, b_ktk, b_ptt = Buf(), [Buf(), Buf()], Buf(), [Buf(), Buf()]
        b_ktk2 = [Buf(), Buf()]
        b_ktk2 = [Buf(), Buf()]
        b_dn = Buf()
        b_hacc, b_ss4, b_xt = [[Buf() for _ in range(4)] for _ in range(2)], [Buf(), Buf()], [Buf(), Buf()]
        b_hg0, b_hg, b_hmt, b_sq, b_x1 = Buf(), Buf(), Buf(), Buf(), Buf()
        b_x1h = [Buf(), Buf()]
        b_sq4 = [Buf() for _ in range(4)]
        b_ctr = [Buf(), Buf()]
        b_sxg, b_sxt = Buf(), Buf()
        b_sxs = b_sxg
        cnt = {"ctr": 0}

        def nxt(k, n):
            v = cnt[k] % n
            cnt[k] += 1
            return v

        seqs = {"P0": [0, 1], "P1": [2, 3], "S": list(range(4, NT))}

        def d4(d):
            return slice(d * 4, d * 4 + 4)

        def nd_slot(i):
            return pND[:, i // 3, (i % 3) * 129:(i % 3) * 129 + 129]

        def up_slot(h):
            return pUP[:, h // 3, (h % 3) * 129:(h % 3) * 129 + 129]

        def chain(seq, d, init):
            tiles = seqs[seq] if d == 0 else seqs[seq][::-1]
            sidx = {"P0": 0, "P1": 1, "S": 2}[seq]
            first = tiles[0]
            if init is None:
                R.op("dve", lambda e: e.memset(MPRE[:, first, d4(d)], 0.0), writes=[b_chain])
            else:
                ap, bufs = init
                R.op("dve", lambda e: e.tensor_copy(out=MPRE[:, first, d4(d)], in_=ap), reads=bufs, writes=[b_chain])
            for i, c in enumerate(tiles):
                R.op("dve", lambda e, c=c: e.tensor_tensor(out=MX[:, c, d4(d)], in0=MPRE[:, c, d4(d)], in1=RMAX[:, c, d4(d)], op=ALU.max),
                     reads=[b_chain, b_gs[c]], writes=[b_chain])
                if i + 1 < len(tiles):
                    nx = tiles[i + 1]
                    R.op("dve", lambda e, c=c, nx=nx: e.tensor_tensor(out=MPRE[:, nx, d4(d)], in0=MX[:, c, d4(d)], in1=FTOT[:, c, d4(d)], op=ALU.subtract),
                         reads=[b_chain, b_gs[c]], writes=[b_chain])
                else:
                    R.op("dve", lambda e, c=c: e.tensor_tensor(out=MFIN[:, sidx, d4(d)], in0=MX[:, c, d4(d)], in1=FTOT[:, c, d4(d)], op=ALU.subtract),
                         reads=[b_chain, b_gs[c]], writes=[b_mfin])

        def weights(t0, t1, d):
            gs = [b_gs[t] for t in range(t0, t1)]
            for (dst, a, b_) in [(WG, RR, MX), (DEC, MPRE, MX), (CLP, CUM, MX)]:
                R.op("dve", lambda e, a=a, b_=b_: e.tensor_tensor(out=TMPG[:, t0:t1, d4(d)], in0=a[:, t0:t1, d4(d)], in1=b_[:, t0:t1, d4(d)], op=ALU.subtract),
                     reads=[b_chain] + gs, writes=[b_tmpg])
                R.op("act", lambda e, dst=dst: e.activation(out=dst[:, t0:t1, d4(d)], in_=TMPG[:, t0:t1, d4(d)], func=AF.Exp),
                     reads=[b_tmpg], writes=[b_chain])

        def st_vw(c, d):
            for h in range(4):
                col = d * 4 + h
                R.op("act", lambda e, h=h, col=col: e.activation(out=VW[:, d, h, 0:129], in_=V1[:, c, h, 0:129], func=AF.Identity, scale=WG[:, c, col:col + 1]),
                     reads=[b_v1[c], b_chain], writes=[b_vw[d]], parts=True)

        def st_ktr(c):
            for h in range(4):
                R.op("pe", lambda e, h=h: e.transpose(pTR[:, h * 128:(h + 1) * 128], KT[:, c, h, :], ident_b[:, :]), reads=[b_kt[c], b_const], writes=[b_ptr[0]])
            R.op("act", lambda e: e.activation(out=KTK[:, :, :], in_=pTR[:, 0:512].rearrange("p (a b) -> p a b", b=128), func=AF.Copy), reads=[b_ptr[0]], writes=[b_ktk])

        def st_upd(c, d):
            for h in range(4):
                R.op("pe", lambda e, h=h: e.matmul(up_slot(h), lhsT=KTK[:, h, :], rhs=VW[:, d, h, 0:129], start=True, stop=True),
                     reads=[b_ktk, b_vw[d]], writes=[b_pup[h // 3]])
            for h in range(4):
                col = d * 4 + h
                R.op("dve", lambda e, h=h, col=col: e.scalar_tensor_tensor(out=SST[:, d, h, :], in0=SST[:, d, h, :], scalar=DEC[:, c, col:col + 1], in1=up_slot(h),
                                                                            op0=ALU.mult, op1=ALU.add),
                     reads=[b_sst[d][h], b_chain, b_pup[h // 3]], writes=[b_sst[d][h]])

        pSTb = pST[:, :].bitcast(BF16)

        def state_only_pass(seq):
            cl = seqs[seq]

            def prep(c, par):
                trb = pTR if par == 0 else pSTb
                btr = b_ptr[0] if par == 0 else b_pst
                for h in range(4):
                    R.op("act", lambda e, h=h: e.activation(out=VW[:, par, h, 0:129], in_=V1[:, c, h, 0:129], func=AF.Identity, scale=WG[:, c, h:h + 1]),
                         reads=[b_v1[c], b_chain], writes=[b_vw[par]], parts=True)
                for h in range(4):
                    R.op("pe", lambda e, h=h: e.transpose(trb[:, h * 128:(h + 1) * 128], KT[:, c, h, :], ident_b[:, :]),
                         reads=[b_kt[c], b_const], writes=[btr])
                R.op("act", lambda e: e.activation(out=KTK2[:, par, :, :], in_=trb[:, 0:512].rearrange("p (a b) -> p a b", b=128), func=AF.Copy),
                     reads=[btr], writes=[b_ktk2[par]])

            def upd(c, par):
                def slot(h):
                    return up_slot(h) if par == 0 else nd_slot(h)
                bb = [b_pup[0]] if par == 0 else [b_pnd[0], b_pnd[1]]
                for h in range(4):
                    R.op("pe", lambda e, h=h: e.matmul(slot(h), lhsT=KTK2[:, par, h, :], rhs=VW[:, par, h, 0:129], start=True, stop=True),
                         reads=[b_ktk2[par], b_vw[par]], writes=bb)
                for h in range(4):
                    R.op("act", lambda e, h=h: e.activation(out=SNAP[:, c, h, 0:129], in_=SST[:, 0, h, :], func=AF.Identity, scale=DEC[:, c, h:h + 1]),
                         reads=[b_sst[0][h], b_chain], writes=[b_snap[c]], parts=True)
                for h in range(4):
                    R.op("dve", lambda e, h=h: e.scalar_tensor_tensor(out=SST[:, 0, h, :], in0=SST[:, 0, h, :], scalar=DEC[:, c, h:h + 1], in1=slot(h),
                                                                       op0=ALU.mult, op1=ALU.add),
                         reads=[b_sst[0][h], b_chain] + bb, writes=[b_sst[0][h]])

            prep(cl[0], 0)
            for k, c in enumerate(cl):
                if k + 1 < len(cl):
                    prep(cl[k + 1], (k + 1) % 2)
                upd(c, k % 2)

        def emit_state(seq, d):
            si = {"P0": 0, "P1": 1}[seq]
            for h in range(4):
                cs = nxt("ctr", 2)
                R.op("pe", lambda e, h=h: e.matmul(pWO[:, 0:128], lhsT=SST[:, d, h, 0:128], rhs=ident_f[:, :], start=True, stop=True),
                     reads=[b_sst[d][h], b_const], writes=[b_pwo])
                R.op("act", lambda e, cs=cs: e.activation(out=CTR[:, cs, :], in_=pWO[:, 0:128], func=AF.Copy), reads=[b_pwo], writes=[b_ctr[cs]])
                R.dma("sp", oC_d[si, d, h, :, :], CTR[:, cs, :], reads=[b_ctr[cs]])
                R.dma("sp", on_d[si, d, h, :].rearrange("(p o) -> p o", o=1), SST[:, d, h, 128:129], reads=[b_sst[d][h]])
            R.dma("sp", om_d[si, d, :].rearrange("(o h) -> o h", o=1), MFIN[0:1, si, d4(d)], reads=[b_mfin])

        def init_state(d, src=None):
            for h in range(4):
                if src is None:
                    R.op("dve", lambda e, h=h: e.memset(SST[:, d, h, :], 0.0), writes=[b_sst[d][h]])
                else:
                    ap_of, bufs = src
                    R.op("dve", lambda e, h=h: e.tensor_copy(out=SST[:, d, h, :], in_=ap_of(h)), reads=bufs, writes=[b_sst[d][h]])

        def full_pass(seq):
            typ = 1 if seq == "S" else 0
            cl = seqs[seq][::-1]

            def s1a(c, par):
                R.dma("sp", XT[:, par, :], xin[c * 128:(c + 1) * 128, :], writes=[b_xt[par]])
                for h in range(4):
                    R.op("act", lambda e, h=h: e.activation(out=SDB[:, h, 0:129], in_=SST[:, 1, h, :], func=AF.Identity, scale=DEC[:, c, 4 + h:5 + h]),
                         reads=[b_sst[1][h], b_chain], writes=[b_sdb], parts=True)
                st_vw(c, 0)
                st_vw(c, 1)
                for h in range(4):
                    R.op("pe", lambda e, h=h: e.matmul(pST[:, h * 128:(h + 1) * 128], lhsT=KT[:, c, h, :], rhs=QT[:, c, h, :], start=True, stop=True),
                         reads=[b_kt[c], b_qt[c]], writes=[b_pst])
                st_ktr(c)

            def s1b(c, par):
                for d in range(2):
                    R.op("dve", lambda e, d=d: e.tensor_tensor(out=PT[:, d, :, :], in0=pST[:, :].rearrange("p (a b) -> p a b", b=128),
                                                               in1=masks[d][:, :].unsqueeze(1).to_broadcast([128, 4, 128]), op=ALU.mult),
                         reads=[b_pst, b_const], writes=[b_ptt[d]])
                for d in range(2):
                    for h in range(4):
                        i = d * 4 + h
                        R.op("pe", lambda e, d=d, h=h, i=i: e.matmul(nd_slot(i), lhsT=PT[:, d, h, :], rhs=VW[:, d, h, 0:129], start=True, stop=False),
                             reads=[b_ptt[d], b_vw[d]], writes=[b_pnd[i // 3]])
                        if d == 0:
                            R.op("pe", lambda e, h=h, i=i: e.matmul(nd_slot(i), lhsT=QT[:, c, h, :], rhs=SNAP[:, c, h, 0:129], start=False, stop=True),
                                 reads=[b_qt[c], b_snap[c]], writes=[b_pnd[i // 3]])
                        else:
                            R.op("pe", lambda e, h=h, i=i: e.matmul(nd_slot(i), lhsT=QT[:, c, h, :], rhs=SDB[:, h, 0:129], start=False, stop=True),
                                 reads=[b_qt[c], b_sdb], writes=[b_pnd[i // 3]])
                st_upd(c, 1)

            def s1c(c, par):
                R.op("act", lambda e: e.activation(out=DN[:, 0:6].rearrange("p (b t) -> p b t", t=3), in_=pND[:, 0:2, 0:387].rearrange("p b (t w) -> p b t w", w=129)[:, :, :, 128],
                                                   func=AF.Abs),
                     reads=b_pnd, writes=[b_dn])
                R.op("act", lambda e: e.activation(out=DN[:, 6:8], in_=pND[:, 2, 0:258].rearrange("p (t w) -> p t w", w=129)[:, :, 128], func=AF.Abs),
                     reads=b_pnd, writes=[b_dn])
                R.op("dve", lambda e: e.tensor_tensor(out=DN[:, 0:8], in0=DN[:, 0:8], in1=CLP[:, c, :], op=ALU.max), reads=[b_dn, b_chain], writes=[b_dn])
                R.op("dve", lambda e: e.reciprocal(out=DN[:, 0:8], in_=DN[:, 0:8]), reads=[b_dn], writes=[b_dn])
                for h in range(4):
                    R.op("dve", lambda e, h=h: e.tensor_scalar(out=HACC[:, par, h * 128:(h + 1) * 128], in0=nd_slot(h)[:, 0:128], scalar1=DN[:, h:h + 1], scalar2=None, op0=ALU.mult),
                         reads=[b_pnd[h // 3], b_dn], writes=[b_hacc[par][h]])
                for h in range(4):
                    i = 4 + h
                    R.op("dve", lambda e, h=h, i=i: e.scalar_tensor_tensor(out=HACC[:, par, h * 128:(h + 1) * 128], in0=nd_slot(i)[:, 0:128], scalar=DN[:, i:i + 1],
                                                                            in1=HACC[:, par, h * 128:(h + 1) * 128], op0=ALU.mult, op1=ALU.add),
                         reads=[b_pnd[i // 3], b_dn, b_hacc[par][h]], writes=[b_hacc[par][h]])

            def hg0(c):
                R.op("pool", lambda e: e.tensor_tensor(out=HG0[:, :], in0=OG[:, c, :], in1=MHN[:, :], op=ALU.mult), reads=[b_og[c], b_mhn], writes=[b_hg0])

            def s2a(c, par):
                for h in range(4):
                    R.op("act", lambda e, h=h: e.activation(out=SQ4[:, h, :], in_=HACC[:, par, h * 128:(h + 1) * 128], func=AF.Square, accum_out=SS4[:, par, h:h + 1]),
                         reads=[b_hacc[par][h]], writes=[b_sq4[h], b_ss4[par]], parts=True)
                R.op("act", lambda e: e.activation(out=SS4[:, par, :], in_=SS4[:, par, :], func=AF.Sqrt, bias=EPS, scale=1.0 / DH),
                     reads=[b_ss4[par]], writes=[b_ss4[par]])
                R.op("dve", lambda e: e.reciprocal(out=SS4[:, par, :], in_=SS4[:, par, :]), reads=[b_ss4[par]], writes=[b_ss4[par]])
                for h in range(4):
                    R.op("dve", lambda e, h=h: e.scalar_tensor_tensor(out=HG[:, h * 128:(h + 1) * 128], in0=HACC[:, par, h * 128:(h + 1) * 128], scalar=SS4[:, par, h:h + 1],
                                                                       in1=HG0[:, h * 128:(h + 1) * 128], op0=ALU.mult, op1=ALU.mult),
                         reads=[b_hacc[par][h], b_ss4[par], b_hg0], writes=[b_hg], parts=True)

            def s2b(c, par):
                for h in range(4):
                    R.op("pe", lambda e, h=h: e.transpose(pTR[:, 512 + h * 128:512 + (h + 1) * 128], HG[:, h * 128:(h + 1) * 128], ident_b[:, :]),
                         reads=[b_hg, b_const], writes=[b_ptr[1]])
                R.op("act", lambda e: e.activation(out=HMT[:, :, :], in_=pTR[:, 512:1024].rearrange("p (a b) -> p a b", b=128), func=AF.Copy),
                     reads=[b_ptr[1]], writes=[b_hmt])
                wo_half(c, par, 0)

            def wo_half(c, par, half):
                for kb in range(8):
                    if kb < 4:
                        R.op("pe", lambda e, kb=kb: e.matmul(pWO[:, :], lhsT=YSC[:, c, kb, :], rhs=WOUT[:, kb, half * 512:(half + 1) * 512], start=(kb == 0), stop=False),
                             reads=[b_ysc[c], b_wout], writes=[b_pwo])
                    else:
                        R.op("pe", lambda e, kb=kb: e.matmul(pWO[:, :], lhsT=HMT[:, kb - 4, :], rhs=WOUT[:, kb, half * 512:(half + 1) * 512], start=False, stop=(kb == 7)),
                             reads=[b_hmt, b_wout], writes=[b_pwo])
                R.op("dve", lambda e: e.tensor_tensor(out=X1[:, half * 512:(half + 1) * 512], in0=pWO[:, :], in1=G1ROW[:, typ, half * 512:(half + 1) * 512], op=ALU.mult),
                     reads=[b_pwo, b_g1], writes=[b_x1h[half]])
                R.op("pool", lambda e: e.tensor_tensor(out=XT[:, par, half * 512:(half + 1) * 512], in0=XT[:, par, half * 512:(half + 1) * 512],
                                                       in1=X1[:, half * 512:(half + 1) * 512], op=ALU.add),
                     reads=[b_x1h[half], b_xt[par]], writes=[b_xt[par]])

            def s2c(c, par):
                wo_half(c, par, 1)
                R.dma("sp", x1s_d[c * 128:(c + 1) * 128, :], XT[:, par, :], reads=[b_xt[par]], writes=[b_x1s[c]])
                if c == NT - 1:
                    R.dma("sp", hx_in[:, :], XT[64:128, par, :], reads=[b_xt[par]], writes=[b_hxin])
                    hx_out, b_hxout, hx_scr, b_hs = L["hx_out"], L["b_hxout"], L["hx_scr"], L["b_hs"]
                    R.collective(lambda e: e.collective_compute("AllGather", ALU.bypass, replica_groups=RG, ins=[hx_in.ap().opt()], outs=[hx_out.ap().opt()]),
                                 reads=[b_hxin], writes=[b_hxout])

            prev = None
            for k, c in enumerate(cl):
                par = k % 2
                s1a(c, par)
                if prev is not None:
                    s2a(*prev)
                s1b(c, par)
                if prev is not None:
                    s2b(*prev)
                s1c(c, par)
                if prev is not None:
                    s2c(*prev)
                hg0(c)
                prev = (c, par)
            s2a(*prev)
            s2b(*prev)
            s2c(*prev)
            if seq == "S":
                hx_out, b_hxout, hx_scr, b_hs = L["hx_out"], L["b_hxout"], L["hx_scr"], L["b_hs"]
                for hf in range(2):
                    cs_ = slice(hf * 512, (hf + 1) * 512)
                    R.dma("pool", SXG[0:64, 0, 0:512], hx_out[0:64, cs_], reads=[b_hxout], writes=[b_sxg])
                    R.dma("pool", SXG[0:64, 1, 0:512], hx_out[64:128, cs_], reads=[b_hxout], writes=[b_sxg])
                    R.op("dve", lambda e: e.tensor_scalar(out=SXG[0:64, 0, 0:512], in0=SXG[0:64, 0, 0:512], scalar1=sel[0:64, 1:2], scalar2=None, op0=ALU.mult),
                         reads=[b_sxg, b_const], writes=[b_sxg])
                    R.op("dve", lambda e: e.tensor_scalar(out=SXG[0:64, 1, 0:512], in0=SXG[0:64, 1, 0:512], scalar1=sel[0:64, 0:1], scalar2=None, op0=ALU.mult),
                         reads=[b_sxg, b_const], writes=[b_sxg])
                    R.op("dve", lambda e: e.tensor_tensor(out=SXG[0:64, 0, 0:512], in0=SXG[0:64, 0, 0:512], in1=SXG[0:64, 1, 0:512], op=ALU.add),
                         reads=[b_sxg], writes=[b_sxg])
                    R.dma("pool", hx_scr[:, cs_], SXG[0:64, 0, 0:512], reads=[b_sxg], writes=[b_hs])

        chain("P0", 0, None)
        chain("P1", 0, None)
        R.dma("sp", SXT[:, 0:4], m0_d, writes=[b_sxt])
        chain("S", 0, (SXT[:, 0:4], [b_sxt]))
        weights(0, NT, 0)
        R.dma("sp", SXG[:, 0, 0:516].rearrange("p (h v) -> p h v", v=129), s0_d, writes=[b_sxs])
        init_state(0, (lambda h: SXG[:, 0, h * 129:(h + 1) * 129], [b_sxs]))
        state_only_pass("S")
        for h in range(4):
            R.op("dve", lambda e, h=h: e.tensor_copy(out=SXG[:, 0, h * 129:(h + 1) * 129], in_=SST[:, 0, h, :]), reads=[b_sst[0][h]], writes=[b_sxs])
        R.op("dve", lambda e: e.tensor_copy(out=SXG[:, 0, 516:520], in_=MFIN[:, 2, 0:4]), reads=[b_mfin], writes=[b_sxs])
        R.dma("pool", sx_in[:, :], SXG[:, 0, :], reads=[b_sxs], writes=[b_sxin])
        R.collective(lambda e: e.collective_compute("AllGather", ALU.bypass, replica_groups=RG, ins=[sx_in.ap().opt()], outs=[sx_out.ap().opt()]),
                     reads=[b_sxin], writes=[b_sxout])
        chain("P0", 1, None)
        chain("P1", 1, None)
        weights(0, 4, 1)
        for seq in ["P0", "P1"]:
            init_state(0)
            state_only_pass(seq)
            emit_state(seq, 0)
            init_state(1)
            full_pass(seq)
            emit_state(seq, 1)
        R.dma("pool", SXG[:, :, :], sx_out.ap().rearrange("(r p) n -> p r n", p=128), reads=[b_sxout], writes=[b_sxg])
        R.op("dve", lambda e: e.tensor_scalar(out=SXT[:, :], in0=SXG[:, 0, :], scalar1=sel[:, 1:2], scalar2=None, op0=ALU.mult), reads=[b_sxg, b_const], writes=[b_sxt])
        R.op("dve", lambda e: e.scalar_tensor_tensor(out=SXT[:, :], in0=SXG[:, 1, :], scalar=sel[:, 0:1], in1=SXT[:, :], op0=ALU.mult, op1=ALU.add),
             reads=[b_sxg, b_sxt, b_const], writes=[b_sxt])
        chain("S", 1, (SXT[:, 516:520], [b_sxt]))
        weights(4, NT, 1)
        init_state(1, (lambda h: SXT[:, h * 129:(h + 1) * 129], [b_sxt]))
        full_pass("S")
        dump("MX", MX[:], [128, NT, 8], b_chain)
        dump("WG", WG[:], [128, NT, 8], b_chain)
        dump("SNAP", SNAP[:], [128, NT, 4, 130], b_snap[0], BF16)
        R.flush()


def ffn_up(nc, R, L, _sc, seg, H2T, b_h2t, ACTS, b_acts, n_own, n_all, ntile, nbank, side=None):
    cff, b_const, wup_d = L["cff"], L["b_const"], L["wup_d"]
    U = _sc.sb([128, 2, n_all], F32)
    ACA = _sc.sb([128, 2, n_own], F32)
    ACG = _sc.sb([128, 2, n_own], F32)
    WU = _sc.sb([128, 3, 8, 256], BF16)
    fp = [_sc.ps([128, 512], F32) for _ in range(nbank)]
    b_fp = [Buf() for _ in range(nbank)]
    b_wu = [Buf() for _ in range(3)]
    b_u = [Buf(), Buf()]
    b_aca = [Buf(), Buf()]
    b_acg = [Buf(), Buf()]
    groups = [(g0, min(512, n_all - g0)) for g0 in range(0, n_all, 512)]
    pcount = 0

    def finish(j):
        g2 = j % 2
        R.op("act", lambda e: e.activation(out=ACG[:, g2, :], in_=ACG[:, g2, :], func=AF.Silu), reads=[b_acg[g2]], writes=[b_acg[g2]])
        R.op("pool", lambda e: e.tensor_tensor(out=ACTS[:, j, :], in0=ACG[:, g2, :], in1=ACA[:, g2, :], op=ALU.mult),
             reads=[b_acg[g2], b_aca[g2]], writes=[b_acts[j]])

    for j in range(min(2, NJ)):
        R.dma("pool", WU[:, j % 3, :, :], wup_d[j], writes=[b_wu[j % 3]])
    for j in range(NJ):
        s = j % 3
        if j + 2 < NJ:
            R.dma("pool", WU[:, (j + 2) % 3, :, :], wup_d[j + 2], writes=[b_wu[(j + 2) % 3]])
        a2 = j % 2
        for part in range(2):
            blk = j + part * NJ
            us = part
            acc = ACA[:, a2, :] if part == 0 else ACG[:, a2, :]
            b_acc = b_aca[a2] if part == 0 else b_acg[a2]
            for (g0, gn) in groups:
                pb = pcount % nbank
                pcount += 1
                hb = [b_h2t[min(ti, ntile)] for ti in range(g0 // 128, (g0 + gn + 127) // 128)]
                for kc in range(8):
                    R.op("pe", lambda e, kc=kc, g0=g0, gn=gn, pb=pb, s=s, part=part: e.matmul(fp[pb][:, 0:gn], lhsT=WU[:, s, kc, part * 128:(part + 1) * 128],
                                                                                         rhs=H2T[:, kc, g0:g0 + gn], start=(kc == 0), stop=(kc == 7)),
                         reads=[b_wu[s]] + hb, writes=[b_fp[pb]])
                R.op("act", lambda e, g0=g0, gn=gn, pb=pb, us=us: e.activation(out=U[:, us, g0:g0 + gn], in_=fp[pb][:, 0:gn], func=AF.Copy),
                     reads=[b_fp[pb]], writes=[b_u[us]], parts=True)
                if g0 < n_own:
                    on = min(gn, n_own - g0)
                    R.op("act", lambda e, g0=g0, on=on, pb=pb, acc=acc, blk=blk: e.activation(out=acc[:, g0:g0 + on], in_=fp[pb][:, 0:on], func=AF.Identity,
                                                                                         scale=cff[:, blk, 1:2], bias=cff[:, blk, 3:4]),
                         reads=[b_fp[pb], b_const], writes=[b_acc], parts=True)
            if seg == "prompt":
                def v3(ap):
                    return ap.rearrange("p (r w) -> p r w", w=256)
                R.op("dve", lambda e, acc=acc, us=us, blk=blk, v3=v3: e.scalar_tensor_tensor(out=v3(acc)[:, :, 1:256], in0=v3(U[:, us, 0:512])[:, :, 0:255], scalar=cff[:, blk, 0:1],
                                                                                        in1=v3(acc)[:, :, 1:256], op0=ALU.mult, op1=ALU.add),
                     reads=[b_u[us], b_acc, b_const], writes=[b_acc])
                R.op("dve", lambda e, acc=acc, us=us, blk=blk, v3=v3: e.scalar_tensor_tensor(out=v3(acc)[:, :, 0:255], in0=v3(U[:, us, 0:512])[:, :, 1:256], scalar=cff[:, blk, 2:3],
                                                                                        in1=v3(acc)[:, :, 0:255], op0=ALU.mult, op1=ALU.add),
                     reads=[b_u[us], b_acc, b_const], writes=[b_acc])
            else:
                R.op("dve", lambda e, acc=acc, us=us, blk=blk: e.scalar_tensor_tensor(out=acc[:, 64:2048], in0=U[:, us, 0:1984], scalar=cff[:, blk, 0:1],
                                                                                 in1=acc[:, 64:2048], op0=ALU.mult, op1=ALU.add),
                     reads=[b_u[us], b_acc, b_const], writes=[b_acc])
                R.op("dve", lambda e, acc=acc, us=us, blk=blk: e.scalar_tensor_tensor(out=acc[:, 0:2048], in0=U[:, us, 64:2112], scalar=cff[:, blk, 2:3],
                                                                                 in1=acc[:, 0:2048], op0=ALU.mult, op1=ALU.add),
                     reads=[b_u[us], b_acc, b_const], writes=[b_acc])
        if j >= 1:
            finish(j - 1)
        if side is not None:
            side(j)
    finish(NJ - 1)


def ffn_down_consts(nc, R, L, _sc, typ):
    wdn_d, fn_d, gate_d, b_gate_d = L["wdn_d"], L["fn_d"], L["gate_d"], L["b_gate_d"]
    WD = _sc.sb([128, NJ, D], BF16)
    FNR = _sc.sb([128, D], F32)
    G2ROW = _sc.sb([128, D], F32)
    b_wd = [Buf() for _ in range(NJ // 2)]
    b_fnr, b_g2 = Buf(), Buf()
    for jj in range(0, NJ, 2):
        R.dma("pool", WD[:, jj:jj + 2, :], wdn_d[:, jj:jj + 2, :], writes=[b_wd[jj // 2]])
    R.dma("sp", FNR[:], fn_d, writes=[b_fnr])
    R.dma("sp", G2ROW[:], gate_d[typ:typ + 1, 1024:2048].to_broadcast([128, 1024]), reads=[b_gate_d], writes=[b_g2])
    return WD, FNR, G2ROW, b_wd, b_fnr, b_g2


def ffn_down(nc, R, L, _sc, tiles, ACTS, b_acts, dc):
    x1s_d, b_x1s, y_d = L["x1s_d"], L["b_x1s"], L["y_d"]
    WD, FNR, G2ROW, b_wd, b_fnr, b_g2 = dc
    X1L = _sc.sb([128, 2, D], F32)
    X2 = _sc.sb([128, 2, D], F32)
    YT = _sc.sb([128, 2, D], F32)
    SQ2 = _sc.sb([128, D], BF16)
    SSD = _sc.sb([128, 4], F32)
    dp = [_sc.ps([128, 512], F32) for _ in range(4)]
    b_dp = [Buf() for _ in range(4)]
    b_sq2 = Buf()
    b_x1l, b_x2, b_yt = [Buf(), Buf()], [Buf(), Buf()], [Buf(), Buf()]
    b_ssd = [Buf() for _ in range(4)]
    pc = 0
    for i, t in enumerate(tiles):
        hs = i % 2
        q = i % 4
        R.dma("sp", X1L[:, hs, :], x1s_d[t * 128:(t + 1) * 128, :], reads=[b_x1s[t]], writes=[b_x1l[hs]])
        for half in range(2):
            pb = pc % 4
            pc += 1
            for j in range(NJ):
                R.op("pe", lambda e, j=j, i=i, half=half, pb=pb: e.matmul(dp[pb][:, :], lhsT=ACTS[:, j, i * 128:(i + 1) * 128], rhs=WD[:, j, half * 512:(half + 1) * 512],
                                                                        start=(j == 0), stop=(j == NJ - 1)),
                     reads=[b_acts[j], b_wd[j // 2]], writes=[b_dp[pb]])
            R.op("dve", lambda e, half=half, pb=pb, hs=hs: e.tensor_tensor(out=X2[:, hs, half * 512:(half + 1) * 512], in0=dp[pb][:, :], in1=G2ROW[:, half * 512:(half + 1) * 512], op=ALU.mult),
                 reads=[b_dp[pb], b_g2], writes=[b_x2[hs]], parts=True)
            R.op("pool", lambda e, half=half, hs=hs: e.tensor_tensor(out=X2[:, hs, half * 512:(half + 1) * 512], in0=X2[:, hs, half * 512:(half + 1) * 512],
                                                                   in1=X1L[:, hs, half * 512:(half + 1) * 512], op=ALU.add),
                 reads=[b_x2[hs], b_x1l[hs]], writes=[b_x2[hs]])
        R.op("act", lambda e, hs=hs, q=q: e.activation(out=SQ2[:, :], in_=X2[:, hs, :], func=AF.Square, accum_out=SSD[:, q:q + 1]),
             reads=[b_x2[hs]], writes=[b_sq2, b_ssd[q]])
        R.op("act", lambda e, q=q: e.activation(out=SSD[:, q:q + 1], in_=SSD[:, q:q + 1], func=AF.Sqrt, bias=EPS, scale=1.0 / D),
             reads=[b_ssd[q]], writes=[b_ssd[q]])
        R.op("dve", lambda e, q=q: e.reciprocal(out=SSD[:, q:q + 1], in_=SSD[:, q:q + 1]),
             reads=[b_ssd[q]], writes=[b_ssd[q]])
        R.op("dve", lambda e, hs=hs, q=q: e.scalar_tensor_tensor(out=YT[:, hs, :], in0=X2[:, hs, :], scalar=SSD[:, q:q + 1], in1=FNR[:, :], op0=ALU.mult, op1=ALU.mult),
             reads=[b_x2[hs], b_ssd[q], b_fnr], writes=[b_yt[hs]])
        R.dma("sp", y_d[t * 128:(t + 1) * 128, :], YT[:, hs, :], reads=[b_yt[hs]])


def ffn_all(nc, R, L):
    A2, B2, b_mod, ident_b, anti_b, b_const, sel = (L[k] for k in ["A2", "B2", "b_mod", "ident_b", "anti_b", "b_const", "sel"])
    x1s_d, b_x1s = L["x1s_d"], L["b_x1s"]
    hx_in, hx_out, b_hxin, b_hxout, RG = L["hx_in"], L["hx_out"], L["b_hxin"], L["b_hxout"], L["RG"]
    dump = L["dump"]
    p_tiles = [0, 1, 2, 3]
    s_tiles = list(range(4, NT))
    NS = len(s_tiles)
    hx_scr, b_hs = L["hx_scr"], L["b_hs"]
    with Scope(nc) as _so:
        H2Ts = _so.sb([128, 8, 2112], BF16)
        b_h2ts = [Buf() for _ in range(NS + 1)]
        with Scope(nc) as _sp:
            ACTSp = _sp.sb([128, NJ, 512], BF16)
            b_actsp = [Buf() for _ in range(NJ)]
            H2Tp = _sp.sb([128, 8, 512], BF16)
            b_h2tp = [Buf() for _ in range(5)]
            dcp = ffn_down_consts(nc, R, L, _sp, 0)
            with Scope(nc) as _s1:
                build_hT(nc, R, H2Tp, b_h2tp, p_tiles, lambda t: x1s_d[t * 128:(t + 1) * 128, :], [b_x1s[t] for t in p_tiles], A2, B2, b_mod, ident_b, b_const,
                         lambda t: 0, [128] * 4)
            with Scope(nc) as _s2:
                s1, s2, nb = make_builder(nc, R, _s2, H2Ts, b_h2ts, s_tiles, lambda t: x1s_d[t * 128:(t + 1) * 128, :], [b_x1s[t] for t in s_tiles],
                                          A2, B2, b_mod, ident_b, b_const, lambda t: 1, [128] * NS, col_of=lambda t: (t - 4) * 128)
                state = {"i": 0}
                s1(0)

                def side(j):
                    i = state["i"]
                    if i < nb:
                        if i + 1 < nb:
                            s1(i + 1)
                        s2(i)
                        state["i"] = i + 1

                ffn_up(nc, R, L, _s2, "prompt", H2Tp, b_h2tp, ACTSp, b_actsp, 512, 512, 4, 4, side=side)
                while state["i"] < nb:
                    side(0)
                R.flush()
            with Scope(nc) as _s3:
                ffn_down(nc, R, L, _s3, p_tiles, ACTSp, b_actsp, dcp)
                R.flush()
        build_hT(nc, R, H2Ts, [b_h2ts[NS]], [0], lambda t: hx_scr, [b_hs], A2, B2, b_mod, ident_b, b_const,
                 lambda t: 1, [64], col_of=lambda t: 2048, anti=anti_b)
        dump("H2T_sample", H2Ts[:], [128, 8, 2112], b_h2ts[0], BF16)
        with Scope(nc) as _ss:
            ACTSs = _ss.sb([128, NJ, 2048], BF16)
            b_actss = [Buf() for _ in range(NJ)]
            with Scope(nc) as _s4:
                ffn_up(nc, R, L, _s4, "sample", H2Ts, b_h2ts, ACTSs, b_actss, 2048, 2112, NS, 8)
                R.flush()
            dump("ACTS_sample", ACTSs[:], [128, NJ, 2048], b_actss[0], BF16)
            with Scope(nc) as _s5:
                dcs = ffn_down_consts(nc, R, L, _s5, 1)
                ffn_down(nc, R, L, _s5, s_tiles, ACTSs, b_actss, dcs)
                R.flush()


_CACHE = {}


def _consts():
    i = np.arange(128)
    ident = np.eye(128, dtype=np.float32)
    anti = np.zeros((128, 128), np.float32)
    anti[np.arange(64), 63 - np.arange(64)] = 1.0
    mk0 = (i[:, None] <= i[None, :]).astype(np.float32)
    mk1 = (i[:, None] >= i[None, :]).astype(np.float32)
    selj = np.zeros((2, 256), np.float32)
    selj[0, 0:128] = 1.0
    selj[1, 128:256] = 1.0
    return dict(ident=ident, antiI=anti, maskT0=mk0, maskT1=mk1, ones=np.ones((128, 128), np.float32),
                selj=selj, i2=np.eye(2, dtype=np.float32), i8=np.eye(8, dtype=np.float32))


def _kmajor(w):
    return np.ascontiguousarray(w.reshape(8, 128, -1).transpose(1, 0, 2))


def make_in_maps(x_prompt, x_sample, state_C, state_n, state_m, c, c_ctx, w_mod, b_mod, norm1, w_in, b_gate,
                 conv_sc_w, conv_sc_b, mh_norm, w_out, norm2, w_up, conv_ffn_w, conv_ffn_b, w_down, final_norm):
    f = np.float32
    A = lambda a: np.asarray(a, dtype=f)
    x_prompt, x_sample, state_C, state_n, state_m, c, c_ctx = map(A, (x_prompt, x_sample, state_C, state_n, state_m, c, c_ctx))
    w_mod, b_mod, norm1, w_in, b_gate = A(w_mod)[0], A(b_mod)[0], A(norm1)[0], A(w_in)[0], A(b_gate)[0]
    conv_sc_w, conv_sc_b, mh_norm, w_out, norm2 = A(conv_sc_w)[0], A(conv_sc_b)[0], A(mh_norm)[0], A(w_out)[0], A(norm2)[0]
    w_up, conv_ffn_w, conv_ffn_b, w_down, final_norm = A(w_up)[0], A(conv_ffn_w)[0], A(conv_ffn_b)[0], A(w_down)[0], A(final_norm)
    cst = _consts()
    shared = dict(cst)
    shared["w_mod_r"] = np.ascontiguousarray(_kmajor(w_mod).reshape(128, 8, 12, 512).transpose(2, 0, 1, 3))
    shared["b_mod2"] = np.ascontiguousarray(np.broadcast_to(b_mod[None, :], (2, 6144)))
    shared["norm1c"] = np.ascontiguousarray(norm1.reshape(8, 128).T)
    shared["norm2c"] = np.ascontiguousarray(norm2.reshape(8, 128).T)
    wk = _kmajor(w_in)
    shared["wic"] = np.ascontiguousarray(np.stack([np.concatenate([wk[:, :, cb * 128:(cb + 1) * 128], wk[:, :, 512 + cb * 128:512 + (cb + 1) * 128],
                                                                   wk[:, :, 1024 + cb * 128:1024 + (cb + 1) * 128]], axis=2) for cb in range(4)]))
    shared["wiqk"] = np.ascontiguousarray(np.stack([wk[:, :, 1536 + b * 128:1536 + (b + 1) * 128] for b in range(8)]))
    shared["wiv"] = np.ascontiguousarray(wk[:, :, 2560:3072])
    shared["wio"] = np.ascontiguousarray(wk[:, :, 3072:3584])
    shared["mhn_row"] = np.ascontiguousarray(np.broadcast_to(mh_norm[None, :], (128, 512)))
    shared["w_out_r"] = _kmajor(w_out)
    wu = _kmajor(w_up)
    shared["w_up_r"] = np.ascontiguousarray(np.stack([np.concatenate([wu[:, :, j * 128:(j + 1) * 128], wu[:, :, DFF + j * 128:DFF + (j + 1) * 128]], axis=2)
                                                      for j in range(NJ)]))
    shared["w_down_r"] = np.ascontiguousarray(w_down.reshape(NJ, 128, D).transpose(1, 0, 2))
    shared["fn_row"] = np.ascontiguousarray(np.broadcast_to(final_norm[None, :], (128, D)))
    maps = []
    for core in range(8):
        par = core % 2
        b = core // 2
        m = dict(shared)
        ps = [x_prompt[2 * core], x_prompt[2 * core + 1]]
        xs = x_sample[b, 0:2048] if par == 0 else x_sample[b, 2048:4096]
        if par == 1:
            ps = [p[::-1] for p in ps]
            xs = xs[::-1]
        m["xin"] = np.ascontiguousarray(np.concatenate(ps + [xs], axis=0))
        cv = np.stack([c_ctx, c[b]], axis=1)
        m["cT"] = np.ascontiguousarray(cv.reshape(8, 128, 2).transpose(1, 0, 2))
        gperm = [0, 1, 2, 3] if par == 0 else [2, 3, 0, 1]
        wg = wk[:, :, 3584:3600].reshape(128, 8, 4, 4)[:, :, gperm, :].reshape(128, 8, 16)
        m["wig"] = np.ascontiguousarray(wg)
        bg = b_gate[gperm, :].reshape(16)
        m["bgate_row"] = np.ascontiguousarray(np.broadcast_to(bg[None, :], (128, 16)))
        tap = [0, 1, 2] if par == 0 else [2, 1, 0]
        csc = np.stack([conv_sc_w[tap[0]], conv_sc_w[tap[1]], conv_sc_w[tap[2]], conv_sc_b], axis=1)
        m["conv_sc"] = np.ascontiguousarray(csc.reshape(4, 128, 4).transpose(1, 0, 2))
        cf = np.stack([conv_ffn_w[tap[0]], conv_ffn_w[tap[1]], conv_ffn_w[tap[2]], conv_ffn_b], axis=1)
        m["conv_ffn"] = np.ascontiguousarray(cf.reshape(44, 128, 4).transpose(1, 0, 2))
        dsel = par
        C0 = state_C[b, 0, dsel]
        n0 = state_n[b, 0, dsel]
        s0 = np.concatenate([C0.transpose(2, 0, 1), n0.T[:, :, None]], axis=2)
        m["s0"] = np.ascontiguousarray(s0)
        m["m0"] = np.ascontiguousarray(np.broadcast_to(state_m[b, 0, dsel][None, :], (128, 4)))
        sel = np.zeros((128, 2), f)
        sel[:, par] = 1.0
        m["sel"] = sel
        maps.append(m)
    return maps


def assemble(results):
    f = np.float32
    y_prompt = np.zeros((16, 256, D), f)
    y_sample = np.zeros((4, 4096, D), f)
    new_C = np.zeros((16, 1, 2, 4, 128, 128), f)
    new_n = np.zeros((16, 1, 2, 4, 128), f)
    new_m = np.zeros((16, 1, 2, 4), f)
    for core in range(8):
        r = results[core]
        par = core % 2
        b = core // 2
        y = np.asarray(r["y"], dtype=f)
        for i in range(2):
            yp = y[i * 256:(i + 1) * 256]
            y_prompt[2 * core + i] = yp[::-1] if par else yp
            for dl in range(2):
                dg = dl if par == 0 else 1 - dl
                new_C[2 * core + i, 0, dg] = r["oC"][i, dl]
                new_n[2 * core + i, 0, dg] = r["on"][i, dl]
                new_m[2 * core + i, 0, dg] = r["om"][i, dl]
        ys = y[512:2560]
        if par == 0:
            y_sample[b, 0:2048] = ys
        else:
            y_sample[b, 2048:4096] = ys[::-1]
    return (y_prompt, y_sample, new_C, new_n, new_m)


def kernel(**inputs):
    maps = make_in_maps(**inputs)
    if "nc" not in _CACHE:
        _CACHE["nc"] = build_program()[0]
    res = run_bass_kernel_spmd(_CACHE["nc"], maps, core_ids=list(range(8)))
    return assemble(res.results)
```

```python
import contextlib
import numpy as np
import concourse.bass as bass
import concourse.mybir as mybir
from concourse.bass_utils import run_bass_kernel_spmd

F32 = mybir.dt.float32
BF16 = mybir.dt.bfloat16
AF = mybir.ActivationFunctionType
ALU = mybir.AluOpType
AX = mybir.AxisListType

D = 1024
NT = 20
NTOK = 2560
DH = 128
NH = 4
DFF = 2816
NJ = 22
EPS = 1e-6
ENGS = ["sp", "act", "pool", "dve", "pe"]


class Scope:
    def __init__(self, nc):
        self.nc = nc
        self.stack = contextlib.ExitStack()

    def __enter__(self):
        self.stack.__enter__()
        return self

    def __exit__(self, *a):
        return self.stack.__exit__(*a)

    _n = [0]

    def sb(self, shape, dt=F32):
        Scope._n[0] += 1
        return self.stack.enter_context(self.nc.sbuf_tensor("sb%d" % Scope._n[0], list(shape), dt))

    def ps(self, shape, dt=F32):
        Scope._n[0] += 1
        return self.stack.enter_context(self.nc.psum_tensor("ps%d" % Scope._n[0], list(shape), dt))


class Op:
    __slots__ = ("eng", "fn", "waits", "sig", "val", "dma", "cc", "sem", "block", "extra", "store")

    def __init__(self, eng, fn, block):
        self.eng = eng
        self.fn = fn
        self.waits = []
        self.sig = False
        self.val = None
        self.dma = False
        self.cc = False
        self.sem = None
        self.block = block
        self.extra = []
        self.store = False


class Buf:
    __slots__ = ("name", "ws", "r", "parts", "gen_deps")

    def __init__(self, name=""):
        self.name = name
        self.ws = []
        self.r = []
        self.parts = False
        self.gen_deps = []


class Rec:
    def __init__(self, nc, nring=8):
        self.nc = nc
        self.engs = {"sp": nc.sync, "act": nc.scalar, "pool": nc.gpsimd, "dve": nc.vector, "pe": nc.tensor}
        self.pending = {e: [] for e in ENGS}
        self.block = 0
        self.csem = {e: nc.alloc_semaphore("c_" + e) for e in ["act", "pool", "dve", "pe"]}
        self.ccount = {e: 0 for e in self.csem}
        self.nring = nring
        self.dsem = {q: [nc.alloc_semaphore("d_%s%d" % (q, i)) for i in range(nring)] for q in ["sp", "pool"]}
        self.duse = {q: [0] * nring for q in self.dsem}
        self.dnext = {q: 0 for q in self.dsem}
        self.ccsem = nc.alloc_semaphore("ccsem")
        self.cccount = 0
        self.known = {e: {} for e in ENGS}

    def _deps(self, op, reads, writes, parts=False):
        deps = {}
        is_async = op.dma or op.cc

        def add(d):
            if d is None:
                return
            d_async = d.dma or d.cc
            if not d_async and d.block != self.block:
                return
            if not d_async and not is_async and d.eng == op.eng and op.eng == "pe":
                return
            deps[id(d)] = d

        for b in reads:
            for ww in b.ws:
                add(ww)
        for b in writes:
            same_gen = parts and b.parts and not b.r and b.ws
            if same_gen:
                for gd in b.gen_deps:
                    add(gd)
            else:
                b.gen_deps = list(b.ws) + list(b.r)
                for ww in b.ws:
                    add(ww)
            for rr in b.r:
                add(rr)
        for d in deps.values():
            if not (d.dma or d.cc):
                d.sig = True
        op.waits = list(deps.values())
        for b in reads:
            if b in writes:
                continue
            if not is_async:
                b.r = [x for x in b.r if (x.dma or x.cc or x.eng != op.eng)]
            b.r.append(op)
        for b in writes:
            same_gen = parts and b.parts and not b.r and b.ws
            if same_gen:
                if not is_async:
                    b.ws = [x for x in b.ws if (x.dma or x.cc or x.eng != op.eng)]
                b.ws.append(op)
            else:
                b.ws = [op]
            b.parts = parts
            b.r = []

    def op(self, eng, fn, reads=(), writes=(), parts=False):
        o = Op(eng, fn, self.block)
        self._deps(o, list(reads), list(writes), parts)
        self.pending[eng].append(o)
        return o

    def dma(self, q, out, in_, reads=(), writes=(), parts=False, **kw):
        o = Op(q, lambda e: e.dma_start(out=out, in_=in_, **kw), self.block)
        o.dma = True
        try:
            o.store = "DRam" in type(out.tensor).__name__
        except Exception:
            o.store = True
        slot = self.dnext[q]
        self.dnext[q] = (slot + 1) % self.nring
        prev = self.duse[q][slot]
        o.sem = self.dsem[q][slot]
        if prev > 0:
            o.extra.append((o.sem, 16 * prev))
        self.duse[q][slot] = prev + 1
        o.val = 16 * (prev + 1)
        self._deps(o, list(reads), list(writes), parts)
        self.pending[q].append(o)
        return o

    def collective(self, fn, reads=(), writes=()):
        o = Op("pool", fn, self.block)
        o.cc = True
        self.cccount += 1
        o.sem = self.ccsem
        o.val = self.cccount
        self._deps(o, list(reads), list(writes))
        self.pending["pool"].append(o)
        return o

    def flush(self, final=False):
        if not final and not any(self.pending[e] for e in ENGS):
            return
        for e in ENGS:
            for o in self.pending[e]:
                if o.dma or o.cc:
                    continue
                if o.sig:
                    self.ccount[e] += 1
                    o.val = self.ccount[e]
        names = {"sp": "sync", "act": "scalar", "pool": "gpsimd", "dve": "vector", "pe": "tensor"}
        with self.nc.Block() as block:
            for e in ENGS:
                ops = self.pending[e]
                if not ops and not (final and e == "sp"):
                    continue

                def body(engine, ops=ops, e=e):
                    for o in ops:
                        w = {}
                        for (s, v) in o.extra:
                            w[s.num] = (s, max(v, w.get(s.num, (s, 0))[1]))
                        for d in o.waits:
                            s = d.sem if (d.dma or d.cc) else self.csem[d.eng]
                            w[s.num] = (s, max(d.val, w.get(s.num, (s, 0))[1]))
                        kn = self.known[e]
                        for (s, v) in w.values():
                            if kn.get(s.num, 0) >= v:
                                continue
                            engine.wait_ge(s, v)
                            kn[s.num] = v
                        ins = o.fn(engine)
                        if o.dma:
                            ins.then_inc(o.sem, 16)
                        elif o.cc:
                            ins.then_inc(o.sem, 1)
                        elif o.sig:
                            ins.then_inc(self.csem[e], 1)
                    tail = {}
                    for o in ops:
                        if o.dma and o.store:
                            tail[o.sem.num] = (o.sem, max(o.val, tail.get(o.sem.num, (o.sem, 0))[1]))
                    for (sm, v) in tail.values():
                        if self.known[e].get(sm.num, 0) >= v:
                            continue
                        engine.wait_ge(sm, v)
                        self.known[e][sm.num] = v
                    if final and e == "sp":
                        for q in self.dsem:
                            for i in range(self.nring):
                                if self.duse[q][i] > 0:
                                    engine.wait_ge(self.dsem[q][i], 16 * self.duse[q][i])

                getattr(block, names[e])(body)
        self.pending = {e: [] for e in ENGS}
        self.block += 1


def build_program(dbg=()):
    nc = bass.Bass("TRN2", target_bir_lowering=False)
    R = Rec(nc)
    dbg = set(dbg)
    dbg_outs = {}

    def din(name, shape, dt=F32):
        return nc.dram_tensor(name, list(shape), dt, kind="ExternalInput").ap()

    def dout(name, shape, dt=F32):
        return nc.dram_tensor(name, list(shape), dt, kind="ExternalOutput").ap()

    xin = din("xin", [NTOK, D])
    cT_d = din("cT", [128, 8, 2])
    wmod_d = din("w_mod_r", [12, 128, 8, 512])
    bmod_d = din("b_mod2", [2, 6144])
    n1c_d = din("norm1c", [128, 8])
    n2c_d = din("norm2c", [128, 8])
    wic_d = din("wic", [4, 128, 8, 384])
    wiqk_d = din("wiqk", [8, 128, 8, 128])
    wiv_d = din("wiv", [128, 8, 512])
    wio_d = din("wio", [128, 8, 512])
    wig_d = din("wig", [128, 8, 16])
    bg_d = din("bgate_row", [128, 16])
    csc_d = din("conv_sc", [128, 4, 4])
    mhn_d = din("mhn_row", [128, 512])
    wout_d = din("w_out_r", [128, 8, 1024])
    wup_d = din("w_up_r", [NJ, 128, 8, 256])
    cff_d = din("conv_ffn", [128, 44, 4])
    wdn_d = din("w_down_r", [128, NJ, 1024])
    fn_d = din("fn_row", [128, 1024])
    s0_d = din("s0", [128, 4, 129])
    m0_d = din("m0", [128, 4])
    sel_d = din("sel", [128, 2])
    ident_d = din("ident", [128, 128])
    anti_d = din("antiI", [128, 128])
    mk0_d = din("maskT0", [128, 128])
    mk1_d = din("maskT1", [128, 128])
    ones_d = din("ones", [128, 128])
    selj_d = din("selj", [2, 256])
    i2_d = din("i2", [2, 2])
    i8_d = din("i8", [8, 8])

    y_d = dout("y", [NTOK, D])
    oC_d = dout("oC", [2, 2, 4, 128, 128])
    on_d = dout("on", [2, 2, 4, 128])
    om_d = dout("om", [2, 2, 4])

    x1s_d = nc.dram_tensor("x1s", [NTOK, D], F32).ap()
    gate_d = nc.dram_tensor("gate_scr", [2, 2048], F32).ap()
    sx_in = nc.dram_tensor("sx_in", [128, 520], F32)
    sx_out = nc.dram_tensor("sx_out", [256, 520], F32)
    hx_in = nc.dram_tensor("hx_in", [64, 1024], F32)
    hx_out = nc.dram_tensor("hx_out", [128, 1024], F32)
    b_x1s = [Buf("x1s%d" % i) for i in range(NT)]
    b_gate_d = Buf("gate_d")
    b_sxin, b_sxout, b_hxin, b_hxout = Buf(), Buf(), Buf(), Buf()
    hx_scr = nc.dram_tensor("hx_scr", [64, D], F32).ap()
    b_hs = Buf()
    RG = [[0, 1], [2, 3], [4, 5], [6, 7]]

    def dump(name, ap, shape, buf, dt=F32):
        if name not in dbg:
            return
        t = dout("dbg_" + name, shape, dt)
        dbg_outs[name] = (shape, dt)
        R.dma("sp", t, ap, reads=[buf])

    def T(name, shape, dt=F32):
        return nc.sbuf_tensor(list(shape), dt)

    def PS(name, shape, dt=F32):
        return nc.psum_tensor(list(shape), dt)

    with Scope(nc) as _sc:
        ident_f = _sc.sb([128, 128])
        ones_f = _sc.sb([128, 128])
        tri0_f = _sc.sb([128, 128])
        tri1_f = _sc.sb([128, 128])
        ident_b = _sc.sb([128, 128], BF16)
        anti_b = _sc.sb([128, 128], BF16)
        mk0_b = _sc.sb([128, 128], BF16)
        mk1_b = _sc.sb([128, 128], BF16)
        selj = _sc.sb([2, 256])
        i2 = _sc.sb([2, 2])
        i8 = _sc.sb([8, 8])
        n1c = _sc.sb([128, 8])
        n2c = _sc.sb([128, 8])
        bg_row = _sc.sb([128, 16])
        csc = _sc.sb([128, 4, 4])
        cff = _sc.sb([128, 44, 4])
        sel = _sc.sb([128, 2])
        A1 = _sc.sb([128, 8, 2])
        B1 = _sc.sb([128, 8, 2])
        A2 = _sc.sb([128, 8, 2])
        B2 = _sc.sb([128, 8, 2])
        b_const = Buf("const")
        JN = _sc.sb([128, 2])

        def load_consts():
            for (t, d) in [(i2, i2_d), (n1c, n1c_d), (ident_f, ident_d), (ones_f, ones_d), (tri0_f, mk0_d), (tri1_f, mk1_d),
                           (selj, selj_d), (i8, i8_d), (n2c, n2c_d),
                           (bg_row, bg_d), (csc, csc_d), (cff, cff_d), (sel, sel_d)]:
                R.dma("sp", t[:], d, writes=[b_const], parts=True)
            for (t, d) in [(ident_b, ident_d), (anti_b, anti_d), (mk0_b, mk0_d), (mk1_b, mk1_d)]:
                R.dma("pool", t[:], d, writes=[b_const], parts=True)
            R.op("dve", lambda e: e.memset(JN[:, :], 0.0), reads=[b_const], writes=[b_const])
        b_mod = Buf("modcols")

        HT = nc.alloc_sbuf_tensor_at("HT", [128, 8, NTOK], BF16, offset=172032)
        b_ht = [Buf() for _ in range(NT)]
        HT_BYTES = 8 * NTOK * 2
        with Scope(nc) as _sc:
            cT_f = _sc.sb([128, 8, 2])
            cs_b = _sc.sb([128, 8, 2], BF16)
            bmod = _sc.sb([2, 6144])
            modrow = _sc.sb([2, 6144])
            wm0 = _sc.sb([128, 8, 512], BF16)
            wm1 = _sc.sb([128, 8, 512], BF16)
            wm2 = _sc.sb([128, 8, 512], BF16)
            tmpc = _sc.sb([128, 8, 2])
            pm0 = _sc.ps([128, 512])
            pm1 = _sc.ps([128, 512])
            pcol = _sc.ps([128, 512])
            b_ct, b_cs, b_bmod, b_modrow, b_tmpc = Buf(), Buf(), Buf(), Buf("modrow"), Buf()
            R.dma("sp", cT_f[:], cT_d, writes=[b_ct])
            R.dma("sp", bmod[:], bmod_d, writes=[b_bmod])
            R.op("act", lambda e: e.activation(out=cs_b[:], in_=cT_f[:], func=AF.Silu), reads=[b_ct], writes=[b_cs])
            wms = [wm0, wm1, wm2]
            b_wm = [Buf() for _ in range(3)]
            pms = [pm0, pm1]
            b_pm = [Buf(), Buf()]
            for n in range(3):
                R.dma("pool", wms[n][:], wmod_d[n], writes=[b_wm[n]])
            load_consts()
            b_pcol = Buf()

            def colv(vi):
                return pcol[:, vi * 16:(vi + 1) * 16].rearrange("p (a b) -> p a b", b=2)

            def wchunk(n):
                s = n % 3
                if n >= 3:
                    R.dma("pool", wms[s][:], wmod_d[n], writes=[b_wm[s]])
                p = n % 2
                for kc in range(8):
                    R.op("pe", lambda e, kc=kc: e.matmul(pms[p][0:2, :], lhsT=cs_b[:, kc, :], rhs=wms[s][:, kc, :], start=(kc == 0), stop=(kc == 7)),
                         reads=[b_cs, b_wm[s]], writes=[b_pm[p]])
                R.op("dve", lambda e: e.tensor_tensor(out=modrow[:, n * 512:(n + 1) * 512], in0=pms[p][0:2, :], in1=bmod[:, n * 512:(n + 1) * 512], op=ALU.add),
                     reads=[b_pm[p], b_bmod], writes=[b_modrow])

            def cols(Aout, Bout, vsh, vsc, ncol):
                offs = [0, 1024, 3072, 4096]
                for vi in (vsh, vsc):
                    for kc in range(8):
                        c0 = (vi * 8 + kc) * 2
                        off = offs[vi]
                        R.op("pe", lambda e, off=off, kc=kc, c0=c0: e.matmul(pcol[:, c0:c0 + 2], lhsT=modrow[0:2, off + kc * 128: off + (kc + 1) * 128],
                                                                            rhs=i2[:, :], start=True, stop=True),
                             reads=[b_modrow, b_const], writes=[b_pcol])
                R.op("dve", lambda e: e.tensor_scalar(out=tmpc[:], in0=colv(vsc), scalar1=1.0, scalar2=None, op0=ALU.add), reads=[b_pcol], writes=[b_tmpc])
                R.op("dve", lambda e: e.tensor_tensor(out=Aout[:], in0=tmpc[:], in1=ncol[:, :].unsqueeze(2).to_broadcast([128, 8, 2]), op=ALU.mult),
                     reads=[b_tmpc, b_const], writes=[b_mod])
                R.op("dve", lambda e: e.tensor_copy(out=Bout[:], in_=colv(vsh)), reads=[b_pcol], writes=[b_mod])

            bs1, bs2, nb = make_builder(nc, R, _sc, HT, b_ht, list(range(NT)), lambda t: xin[t * 128:(t + 1) * 128, :], [None] * NT,
                                        A1, B1, b_mod, ident_b, b_const, lambda t: 0 if t < 4 else 1, [128] * NT)
            bs1(0)
            for n in range(4):
                wchunk(n)
            cols(A1, B1, 0, 1, n1c)
            nextc = 4
            for i in range(nb):
                if i + 1 < nb:
                    bs1(i + 1)
                bs2(i)
                if i % 2 == 1 and nextc < 12:
                    wchunk(nextc)
                    if nextc == 5:
                        R.dma("sp", gate_d[:, 0:1024], modrow[:, 2048:3072], reads=[b_modrow], writes=[b_gate_d])
                    if nextc == 9:
                        cols(A2, B2, 2, 3, n2c)
                    nextc += 1
            while nextc < 12:
                wchunk(nextc)
                if nextc == 9:
                    cols(A2, B2, 2, 3, n2c)
                nextc += 1
            R.dma("sp", gate_d[:, 1024:2048], modrow[:, 5120:6144], reads=[b_modrow], writes=[b_gate_d])
            dump("modrow", modrow[:], [2, 6144], b_modrow)
            R.flush()

        if True:

            with Scope(nc) as _sc:
                QT = _sc.sb([128, NT, 4, 128], BF16)
                KT = _sc.sb([128, NT, 4, 128], BF16)
                V1 = _sc.sb([128, NT, 4, 130], BF16)
                OG = _sc.sb([128, NT, 512], BF16)
                YSC = _sc.sb([128, NT, 4, 128], BF16)
                GZ = _sc.sb([128, NT, 16])
                RR = _sc.sb([128, NT, 8])
                CUM = _sc.sb([128, NT, 8])
                FTOT = _sc.sb([128, NT, 8])
                RMAX = _sc.sb([128, NT, 8])
                G1ROW = _sc.sb([128, 2, 1024])
                b_qt = [Buf() for _ in range(NT)]
                b_kt = [Buf() for _ in range(NT)]
                b_v1 = [Buf() for _ in range(NT)]
                b_og = [Buf() for _ in range(NT)]
                b_ysc = [Buf() for _ in range(NT)]
                b_gz = [Buf() for _ in range(NT)]
                b_gs = [Buf() for _ in range(NT)]
                b_g1 = Buf("g1row")
                for j in range(2):
                    R.dma("sp", G1ROW[:, j, :], gate_d[j:j + 1, 0:1024].to_broadcast([128, 1024]), reads=[b_gate_d], writes=[b_g1])
                for t in range(NT):
                    R.op("pool", lambda e, t=t: e.memset(V1[:, t, :, 128:130], 1.0), writes=[b_v1[t]], parts=True)

                if True:
                    dump("HT", HT[:], [128, 8, NTOK], b_ht[0], BF16)
                    phase1(nc, R, locals())
                    R.flush()
                for nm, tt, bb, shp, dtt in [("QT", QT, b_qt, [128, NT, 4, 128], BF16), ("KT", KT, b_kt, [128, NT, 4, 128], BF16),
                                             ("V1", V1, b_v1, [128, NT, 4, 130], BF16), ("OG", OG, b_og, [128, NT, 512], BF16),
                                             ("YSC", YSC, b_ysc, [128, NT, 4, 128], BF16), ("GZ", GZ, b_gz, [128, NT, 16], F32),
                                             ("RR", RR, b_gs, [128, NT, 8], F32), ("CUM", CUM, b_gs, [128, NT, 8], F32),
                                             ("FTOT", FTOT, b_gs, [128, NT, 8], F32), ("RMAX", RMAX, b_gs, [128, NT, 8], F32)]:
                    dump(nm, tt[:], shp, bb[0], dtt)
                phase2(nc, R, locals())
                R.flush()

            ffn_all(nc, R, locals())
        R.flush(final=True)
    return nc, dbg_outs


def make_builder(nc, R, _sc, HT, b_ht, tiles, src_of, src_bufs, A, B, b_mod, ident_b, b_const, typ_of, ntoks, col_of=None, anti=None):
    xts = [_sc.sb([128, D], F32) for _ in range(3)]
    xns = [_sc.sb([128, D], BF16) for _ in range(2)]
    sqj = _sc.sb([128, D], BF16)
    ss = _sc.sb([128, 4], F32)
    rs = _sc.sb([128, 4], F32)
    pts = [[_sc.ps([128, 8, 128], BF16) for _ in range(2)] for _ in range(2)]
    ptf = [_sc.ps([128, 4, 128], F32) for _ in range(2)] if anti is not None else None
    b_xt = [Buf() for _ in range(3)]
    b_xn = [Buf() for _ in range(2)]
    b_pt = [[Buf(), Buf()] for _ in range(2)]
    b_ss = [Buf() for _ in range(4)]
    b_rs = [Buf() for _ in range(4)]
    b_sq = Buf()

    def s1(i):
        t = tiles[i]
        n = ntoks[i]
        xs, ns, q = i % 3, i % 2, i % 4
        R.dma("sp", xts[xs][0:n, :], src_of(t), reads=[src_bufs[i]] if src_bufs[i] is not None else [], writes=[b_xt[xs]])
        R.op("act", lambda e: e.activation(out=sqj[0:n, :], in_=xts[xs][0:n, :], func=AF.Square, accum_out=ss[0:n, q:q + 1]),
             reads=[b_xt[xs]], writes=[b_sq, b_ss[q]])
        R.op("act", lambda e: e.activation(out=rs[0:n, q:q + 1], in_=ss[0:n, q:q + 1], func=AF.Sqrt, bias=EPS, scale=1.0 / D),
             reads=[b_ss[q]], writes=[b_rs[q]])
        R.op("dve", lambda e: e.reciprocal(out=rs[0:n, q:q + 1], in_=rs[0:n, q:q + 1]), reads=[b_rs[q]], writes=[b_rs[q]])
        R.op("dve", lambda e: e.tensor_scalar(out=xns[ns][0:n, :], in0=xts[xs][0:n, :], scalar1=rs[0:n, q:q + 1], scalar2=None, op0=ALU.mult),
             reads=[b_xt[xs], b_rs[q]], writes=[b_xn[ns]])

    def s2(i):
        t = tiles[i]
        n = ntoks[i]
        ns, ps_ = i % 2, i % 2
        c0 = col_of(t) if col_of else t * 128
        for kc in range(8):
            eo, k2 = kc % 2, kc // 2
            if anti is None:
                R.op("pe", lambda e, kc=kc, eo=eo, k2=k2: e.transpose(pts[ps_][eo][:, k2, 0:n], xns[ns][0:n, kc * 128:(kc + 1) * 128], ident_b[0:n, 0:n]),
                     reads=[b_xn[ns], b_const], writes=[b_pt[ps_][eo]])
            else:
                R.op("pe", lambda e, kc=kc, eo=eo, k2=k2: e.matmul(ptf[eo][:, k2, 0:n], lhsT=xns[ns][0:n, kc * 128:(kc + 1) * 128], rhs=anti[0:n, 0:n], start=True, stop=True),
                     reads=[b_xn[ns], b_const], writes=[b_pt[ps_][eo]])
        j = typ_of(t)
        for kc in range(8):
            eo, k2 = kc % 2, kc // 2
            src_ps = pts[ps_][eo] if anti is None else ptf[eo]
            if eo == 0:
                R.op("act", lambda e, kc=kc, k2=k2, src_ps=src_ps: e.activation(out=HT[:, kc, c0:c0 + n], in_=src_ps[:, k2, 0:n], func=AF.Identity,
                                                                               scale=A[:, kc, j:j + 1], bias=B[:, kc, j:j + 1]),
                     reads=[b_pt[ps_][0], b_mod], writes=[b_ht[i]], parts=True)
            else:
                R.op("dve", lambda e, kc=kc, k2=k2, src_ps=src_ps: e.tensor_scalar(out=HT[:, kc, c0:c0 + n], in0=src_ps[:, k2, 0:n],
                                                                                  scalar1=A[:, kc, j:j + 1], scalar2=B[:, kc, j:j + 1], op0=ALU.mult, op1=ALU.add),
                     reads=[b_pt[ps_][1], b_mod], writes=[b_ht[i]], parts=True)

    return s1, s2, len(tiles)


def build_hT(nc, R, HT, b_ht, tiles, src_of, src_bufs, A, B, b_mod, ident_b, b_const, typ_of, ntoks, col_of=None, anti=None):
    with Scope(nc) as _sc:
        s1, s2, nt = make_builder(nc, R, _sc, HT, b_ht, tiles, src_of, src_bufs, A, B, b_mod, ident_b, b_const, typ_of, ntoks, col_of, anti)
        s1(0)
        for i in range(nt):
            if i + 1 < nt:
                s1(i + 1)
            s2(i)
        R.flush()


def phase1(nc, R, L):
    HT, b_ht = L["HT"], L["b_ht"]
    QT, KT, V1, OG, YSC, GZ, RR, CUM, FTOT, RMAX = (L[k] for k in ["QT", "KT", "V1", "OG", "YSC", "GZ", "RR", "CUM", "FTOT", "RMAX"])
    b_qt, b_kt, b_v1, b_og, b_ysc, b_gz, b_gs = (L[k] for k in ["b_qt", "b_kt", "b_v1", "b_og", "b_ysc", "b_gz", "b_gs"])
    csc, bg_row, b_const = L["csc"], L["bg_row"], L["b_const"]
    tri0_f, tri1_f, ones_f, ident_f, i8 = L["tri0_f"], L["tri1_f"], L["ones_f"], L["ident_f"], L["i8"]
    wic_d, wiqk_d, wiv_d, wio_d, wig_d = L["wic_d"], L["wiqk_d"], L["wiv_d"], L["wio_d"], L["wig_d"]
    with Scope(nc) as _sc:
        ws0 = _sc.sb([128, 8, 512], BF16)
        ws1 = _sc.sb([128, 8, 512], BF16)
        wgt = _sc.sb([128, 8, 16], BF16)
        scc0 = _sc.sb([128, 512], F32)
        scc1 = _sc.sb([128, 512], F32)
        u0 = _sc.sb([128, 512], F32)
        u1 = _sc.sb([128, 512], F32)
        ac0 = _sc.sb([128, 512], F32)
        ac1 = _sc.sb([128, 512], F32)
        LTALL = _sc.sb([128, NT, 8], F32)
        MX80 = _sc.sb([80, 2], F32)
        DG80 = _sc.sb([80, 2, 80], F32)
        p0 = _sc.ps([128, 512], F32)
        p1 = _sc.ps([128, 512], F32)
        p2 = _sc.ps([128, 512], F32)
        p3 = _sc.ps([128, 512], F32)
        p4 = _sc.ps([128, 512], F32)
        p5 = _sc.ps([128, 512], F32)
        p6 = _sc.ps([128, 512], F32)
        p7 = _sc.ps([128, 512], F32)
        assert nc.sbuf_bytes_remaining >= 8 * NTOK * 2 + 64, nc.sbuf_bytes_remaining
        ws = [ws0, ws1]
        b_ws = [Buf(), Buf()]
        pp = [p0, p1, p2, p3, p4, p5, p6, p7]
        b_pp = [Buf() for _ in range(8)]
        sccs, us, acs = [scc0, scc1], [u0, u1], [ac0, ac1]
        b_scc, b_u, b_ac = [Buf(), Buf()], [Buf(), Buf()], [Buf(), Buf()]
        wslot = [0]

        def next_w():
            s = wslot[0] % 2
            wslot[0] += 1
            return s

        def grp_bufs(bl, g):
            return [bl[4 * g + i] for i in range(4)]

        step = 0
        for cb in range(4):
            s = next_w()
            R.dma("pool", ws[s][:, :, 0:384], wic_d[cb], writes=[b_ws[s]])
            for g in range(5):
                pb = (step % 2) * 3
                k2 = step % 2
                step += 1
                hbufs = grp_bufs(b_ht, g)
                for part in range(3):
                    for kc in range(8):
                        R.op("pe", lambda e, s=s, part=part, kc=kc, g=g, pb=pb: e.matmul(pp[pb + part][:, :], lhsT=ws[s][:, kc, part * 128:(part + 1) * 128],
                                                                                     rhs=HT[:, kc, g * 512:(g + 1) * 512], start=(kc == 0), stop=(kc == 7)),
                             reads=[b_ws[s]] + hbufs, writes=[b_pp[pb + part]])
                W = 256 if g == 0 else 64

                def v3(ap, W=W):
                    return ap.rearrange("p (r w) -> p r w", w=W)

                R.op("act", lambda e, k2=k2, pb=pb: e.activation(out=sccs[k2][:, :], in_=pp[pb + 1][:, :], func=AF.Copy),
                     reads=[b_pp[pb + 1]], writes=[b_scc[k2]])
                R.op("dve", lambda e, k2=k2, pb=pb: e.tensor_tensor(out=us[k2][:, :], in0=pp[pb + 2][:, :], in1=sccs[k2][:, :], op=ALU.mult),
                     reads=[b_pp[pb + 2], b_scc[k2]], writes=[b_u[k2]])
                R.op("act", lambda e, k2=k2, cb=cb: e.activation(out=acs[k2][:, :], in_=us[k2][:, :], func=AF.Identity, scale=csc[:, cb, 1:2], bias=csc[:, cb, 3:4]),
                     reads=[b_u[k2], b_const], writes=[b_ac[k2]])
                R.op("dve", lambda e, k2=k2, cb=cb, v3=v3, W=W: e.scalar_tensor_tensor(out=v3(acs[k2][:, :])[:, :, 1:W], in0=v3(us[k2][:, :])[:, :, 0:W - 1], scalar=csc[:, cb, 0:1],
                                                                                      in1=v3(acs[k2][:, :])[:, :, 1:W], op0=ALU.mult, op1=ALU.add),
                     reads=[b_u[k2], b_ac[k2], b_const], writes=[b_ac[k2]])
                R.op("dve", lambda e, k2=k2, cb=cb, v3=v3, W=W: e.scalar_tensor_tensor(out=v3(acs[k2][:, :])[:, :, 0:W - 1], in0=v3(us[k2][:, :])[:, :, 1:W], scalar=csc[:, cb, 2:3],
                                                                                      in1=v3(acs[k2][:, :])[:, :, 0:W - 1], op0=ALU.mult, op1=ALU.add),
                     reads=[b_u[k2], b_ac[k2], b_const], writes=[b_ac[k2]])
                R.op("dve", lambda e, k2=k2, pb=pb, g=g, cb=cb: e.tensor_tensor(out=YSC[:, 4 * g:4 * g + 4, cb, :], in0=acs[k2][:, :].rearrange("p (a b) -> p a b", b=128),
                                                                              in1=pp[pb][:, :].rearrange("p (a b) -> p a b", b=128), op=ALU.mult),
                     reads=[b_ac[k2], b_pp[pb]], writes=grp_bufs(b_ysc, g), parts=True)
        for blk in range(8):
            s = next_w()
            R.dma("pool", ws[s][:, :, 0:128], wiqk_d[blk], writes=[b_ws[s]])
            hh = blk % 4
            for g in range(5):
                pb = 6 + (step % 2)
                step += 1
                hbufs = grp_bufs(b_ht, g)
                for kc in range(8):
                    R.op("pe", lambda e, s=s, kc=kc, g=g, pb=pb: e.matmul(pp[pb][:, :], lhsT=ws[s][:, kc, 0:128], rhs=HT[:, kc, g * 512:(g + 1) * 512],
                                                                     start=(kc == 0), stop=(kc == 7)),
                         reads=[b_ws[s]] + hbufs, writes=[b_pp[pb]])
                if blk < 4:
                    R.op("act", lambda e, pb=pb, g=g, hh=hh: e.activation(out=QT[:, 4 * g:4 * g + 4, hh, :], in_=pp[pb][:, :].rearrange("p (a b) -> p a b", b=128),
                                                                        func=AF.Copy, scale=float(DH ** -0.5)),
                         reads=[b_pp[pb]], writes=grp_bufs(b_qt, g), parts=True)
                else:
                    R.op("dve", lambda e, pb=pb, g=g, hh=hh: e.tensor_copy(out=KT[:, 4 * g:4 * g + 4, hh, :], in_=pp[pb][:, :].rearrange("p (a b) -> p a b", b=128)),
                         reads=[b_pp[pb]], writes=grp_bufs(b_kt, g), parts=True)
        for which in range(2):
            s = next_w()
            R.dma("pool", ws[s][:, :, :], wiv_d if which == 0 else wio_d, writes=[b_ws[s]])
            for t in range(NT):
                pb = step % 6
                step += 1
                for kc in range(8):
                    R.op("pe", lambda e, s=s, kc=kc, t=t, pb=pb: e.matmul(pp[pb][:, :], lhsT=HT[:, kc, t * 128:(t + 1) * 128], rhs=ws[s][:, kc, :],
                                                                     start=(kc == 0), stop=(kc == 7)),
                         reads=[b_ws[s], b_ht[t]], writes=[b_pp[pb]])
                if which == 0:
                    eng = "dve" if t % 2 == 0 else "act"
                    if eng == "dve":
                        R.op("dve", lambda e, pb=pb, t=t: e.tensor_copy(out=V1[:, t, :, 0:128], in_=pp[pb][:, :].rearrange("p (a b) -> p a b", b=128)),
                             reads=[b_pp[pb]], writes=[b_v1[t]], parts=True)
                    else:
                        R.op("act", lambda e, pb=pb, t=t: e.activation(out=V1[:, t, :, 0:128], in_=pp[pb][:, :].rearrange("p (a b) -> p a b", b=128), func=AF.Copy),
                             reads=[b_pp[pb]], writes=[b_v1[t]], parts=True)
                else:
                    R.op("act", lambda e, pb=pb, t=t: e.activation(out=OG[:, t, :], in_=pp[pb][:, :], func=AF.Sigmoid),
                         reads=[b_pp[pb]], writes=[b_og[t]])
        b_wg = Buf()
        R.dma("pool", wgt[:], wig_d, writes=[b_wg])
        b_lt = [Buf() for _ in range(4)]
        b_mx8 = [Buf() for _ in range(4)]
        b_dg8 = [Buf() for _ in range(4)]
        for t in range(NT):
            pb = step % 6
            step += 1
            q4 = t % 4
            for kc in range(8):
                R.op("pe", lambda e, kc=kc, t=t, pb=pb: e.matmul(pp[pb][:, 0:16], lhsT=HT[:, kc, t * 128:(t + 1) * 128], rhs=wgt[:, kc, :],
                                                                start=(kc == 0), stop=(kc == 7)),
                     reads=[b_wg, b_ht[t]], writes=[b_pp[pb]])
            R.op("dve", lambda e, pb=pb, t=t: e.tensor_tensor(out=GZ[:, t, :], in0=pp[pb][:, 0:16], in1=bg_row[:, :], op=ALU.add),
                 reads=[b_pp[pb], b_const], writes=[b_gz[t]])
        b_ltall, b_pga, b_mx80, b_dg80 = Buf(), Buf(), Buf(), Buf()
        gz5 = GZ[:, :, :].rearrange("p t (d y h) -> p t d y h", d=2, y=2)
        lt4 = LTALL[:, :, :].rearrange("p t (d h) -> p t d h", d=2)
        R.op("act", lambda e: e.activation(out=lt4, in_=gz5[:, :, :, 1, :], func=AF.Exp, scale=-1.0), reads=b_gz, writes=[b_ltall])
        R.op("act", lambda e: e.activation(out=LTALL[:, :, :], in_=LTALL[:, :, :], func=AF.Ln, bias=1.0, scale=1.0), reads=[b_ltall], writes=[b_ltall])
        pA, pB, pC = pp[6], pp[7], pp[0]
        R.op("pe", lambda e: e.matmul(pA[:, 0:80].rearrange("p (t h) -> p t h", h=4), lhsT=tri0_f[:, :], rhs=LTALL[:, :, 0:4], start=True, stop=True),
             reads=[b_ltall, b_const], writes=[b_pp[6]])
        R.op("pe", lambda e: e.matmul(pA[:, 80:160].rearrange("p (t h) -> p t h", h=4), lhsT=tri1_f[:, :], rhs=LTALL[:, :, 4:8], start=True, stop=True),
             reads=[b_ltall, b_const], writes=[b_pp[6]])
        R.op("pe", lambda e: e.matmul(pA[:, 160:320].rearrange("p (t h) -> p t h", h=8), lhsT=ones_f[:, :], rhs=LTALL[:, :, :], start=True, stop=True),
             reads=[b_ltall, b_const], writes=[b_pp[6]])
        R.op("dve", lambda e: e.tensor_copy(out=CUM[:, :, 0:4], in_=pA[:, 0:80].rearrange("p (t h) -> p t h", h=4)), reads=[b_pp[6]], writes=b_gs, parts=True)
        R.op("dve", lambda e: e.tensor_copy(out=CUM[:, :, 4:8], in_=pA[:, 80:160].rearrange("p (t h) -> p t h", h=4)), reads=[b_pp[6]], writes=b_gs, parts=True)
        R.op("dve", lambda e: e.tensor_copy(out=FTOT[:, :, :], in_=pA[:, 160:320].rearrange("p (t h) -> p t h", h=8)), reads=[b_pp[6]], writes=b_gs, parts=True)
        R.op("dve", lambda e: e.tensor_tensor(out=RR[:, :, :].rearrange("p t (d h) -> p t d h", d=2), in0=gz5[:, :, :, 0, :],
                                              in1=CUM[:, :, :].rearrange("p t (d h) -> p t d h", d=2), op=ALU.add),
             reads=b_gz + b_gs, writes=b_gs)
        for k in range(2):
            R.op("pe", lambda e, k=k: e.matmul(pB[0:80, k * 128:(k + 1) * 128], lhsT=RR[:, :, :].rearrange("p t h -> p (t h)")[:, k * 80:(k + 1) * 80], rhs=ident_f[:, :], start=True, stop=True),
                 reads=b_gs + [b_const], writes=[b_pp[7]])
        R.op("dve", lambda e: e.tensor_reduce(out=MX80[:, :], in_=pB[0:80, 0:256].rearrange("p (k n) -> p k n", k=2), axis=AX.X, op=ALU.max),
             reads=[b_pp[7]], writes=[b_mx80])
        for k in range(2):
            R.op("dve", lambda e, k=k: e.tensor_scalar(out=DG80[:, k, :], in0=ident_f[0:80, 0:80], scalar1=MX80[:, k:k + 1], scalar2=None, op0=ALU.mult),
                 reads=[b_mx80, b_const], writes=[b_dg80])
        for k in range(2):
            R.op("pe", lambda e, k=k: e.matmul(pC[:, k * 80:(k + 1) * 80], lhsT=ones_f[0:80, :], rhs=DG80[:, k, :], start=True, stop=True),
                 reads=[b_dg80, b_const], writes=[b_pp[0]])
        R.op("dve", lambda e: e.tensor_copy(out=RMAX[:, :, :], in_=pC[:, 0:160].rearrange("p (t h) -> p t h", h=8)), reads=[b_pp[0]], writes=b_gs)
        R.flush()


def phase2(nc, R, L):
    QT, KT, V1, OG, YSC, GZ, RR, CUM, FTOT, RMAX, G1ROW = (L[k] for k in ["QT", "KT", "V1", "OG", "YSC", "GZ", "RR", "CUM", "FTOT", "RMAX", "G1ROW"])
    b_qt, b_kt, b_v1, b_og, b_ysc, b_gs, b_g1 = (L[k] for k in ["b_qt", "b_kt", "b_v1", "b_og", "b_ysc", "b_gs", "b_g1"])
    b_const, ident_b, ident_f, mk0_b, mk1_b, sel = (L[k] for k in ["b_const", "ident_b", "ident_f", "mk0_b", "mk1_b", "sel"])
    xin, x1s_d, b_x1s = L["xin"], L["x1s_d"], L["b_x1s"]
    wout_d, mhn_d, s0_d, m0_d = L["wout_d"], L["mhn_d"], L["s0_d"], L["m0_d"]
    oC_d, on_d, om_d = L["oC_d"], L["on_d"], L["om_d"]
    sx_in, sx_out, hx_in, b_sxin, b_sxout, b_hxin, RG = (L[k] for k in ["sx_in", "sx_out", "hx_in", "b_sxin", "b_sxout", "b_hxin", "RG"])
    dump = L["dump"]
    masks = [mk0_b, mk1_b]
    with Scope(nc) as _sc:
        SNAP = _sc.sb([128, NT, 4, 130], BF16)
        WOUT = _sc.sb([128, 8, 1024], BF16)
        MHN = _sc.sb([128, 512], F32)
        MX = _sc.sb([128, NT, 8], F32)
        MPRE = _sc.sb([128, NT + 1, 8], F32)
        WG = _sc.sb([128, NT, 8], F32)
        DEC = _sc.sb([128, NT, 8], F32)
        CLP = _sc.sb([128, NT, 8], F32)
        TMPG = _sc.sb([128, NT, 8], F32)
        MFIN = _sc.sb([128, 3, 8], F32)
        SST = _sc.sb([128, 2, 4, 129], F32)
        SDB = _sc.sb([128, 4, 130], BF16)
        VW = _sc.sb([128, 2, 4, 130], BF16)
        KTK = _sc.sb([128, 4, 128], BF16)
        KTK2 = _sc.sb([128, 2, 4, 128], BF16)
        KTK2 = _sc.sb([128, 2, 4, 128], BF16)
        PT = _sc.sb([128, 2, 4, 128], BF16)
        DN = _sc.sb([128, 12], F32)
        HACC = _sc.sb([128, 2, 512], F32)
        HG0 = _sc.sb([128, 512], F32)
        HG = _sc.sb([128, 512], BF16)
        HMT = _sc.sb([128, 4, 128], BF16)
        SQ4 = _sc.sb([128, 4, 128], BF16)
        SS4 = _sc.sb([128, 2, 4], F32)
        XT = _sc.sb([128, 2, D], F32)
        X1 = _sc.sb([128, D], F32)
        CTR = _sc.sb([128, 2, 128], F32)
        SXG = _sc.sb([128, 2, 520], F32)
        SXT = _sc.sb([128, 520], F32)
        pST = _sc.ps([128, 512], F32)
        pND = _sc.ps([128, 3, 512], F32)
        pUP = _sc.ps([128, 2, 512], F32)
        pTR = _sc.ps([128, 1024], BF16)
        pWO = _sc.ps([128, 512], F32)
        b_snap = [Buf() for _ in range(NT)]
        b_wout, b_mhn = Buf(), Buf()
        R.dma("pool", WOUT[:], wout_d, writes=[b_wout])
        R.dma("sp", MHN[:], mhn_d, writes=[b_mhn])
        b_chain = Buf("chain")
        b_tmpg = Buf()
        b_mfin = Buf()
        b_sst = [[Buf() for _ in range(4)] for _ in range(2)]
        b_pst, b_pnd, b_ptr, b_pwo = Buf(), [Buf() for _ in range(3)], [Buf(), Buf()], Buf()
        _pup = Buf()
        b_pup = [_pup, _pup]
        b_sdb, b_vw, b_ktk, b_ptt = Buf(), [Buf(), Buf()], Buf(), [Buf(), Buf()]
        b_ktk2 = [Buf(), Buf()]
        b_ktk2 = [Buf(), Buf()]
        b_dn = Buf()
        b_hacc, b_ss4, b_xt = [[Buf() for _ in range(4)] for _ in range(2)], [Buf(), Buf()], [Buf(), Buf()]
        b_hg0, b_hg, b_hmt, b_sq, b_x1 = Buf(), Buf(), Buf(), Buf(), Buf()
        b_x1h = [Buf(), Buf()]
        b_sq4 = [Buf() for _ in range(4)]
        b_ctr = [Buf(), Buf()]
        b_sxg, b_sxt = Buf(), Buf()
        b_sxs = b_sxg
        cnt = {"ctr": 0}

        def nxt(k, n):
            v = cnt[k] % n
            cnt[k] += 1
            return v

        seqs = {"P0": [0, 1], "P1": [2, 3], "S": list(range(4, NT))}

        def d4(d):
            return slice(d * 4, d * 4 + 4)

        def nd_slot(i):
            return pND[:, i // 3, (i % 3) * 129:(i % 3) * 129 + 129]

        def up_slot(h):
            return pUP[:, h // 3, (h % 3) * 129:(h % 3) * 129 + 129]

        def chain(seq, d, init):
            tiles = seqs[seq] if d == 0 else seqs[seq][::-1]
            sidx = {"P0": 0, "P1": 1, "S": 2}[seq]
            first = tiles[0]
            if init is None:
                R.op("dve", lambda e: e.memset(MPRE[:, first, d4(d)], 0.0), writes=[b_chain])
            else:
                ap, bufs = init
                R.op("dve", lambda e: e.tensor_copy(out=MPRE[:, first, d4(d)], in_=ap), reads=bufs, writes=[b_chain])
            for i, c in enumerate(tiles):
                R.op("dve", lambda e, c=c: e.tensor_tensor(out=MX[:, c, d4(d)], in0=MPRE[:, c, d4(d)], in1=RMAX[:, c, d4(d)], op=ALU.max),
                     reads=[b_chain, b_gs[c]], writes=[b_chain])
                if i + 1 < len(tiles):
                    nx = tiles[i + 1]
                    R.op("dve", lambda e, c=c, nx=nx: e.tensor_tensor(out=MPRE[:, nx, d4(d)], in0=MX[:, c, d4(d)], in1=FTOT[:, c, d4(d)], op=ALU.subtract),
                         reads=[b_chain, b_gs[c]], writes=[b_chain])
                else:
                    R.op("dve", lambda e, c=c: e.tensor_tensor(out=MFIN[:, sidx, d4(d)], in0=MX[:, c, d4(d)], in1=FTOT[:, c, d4(d)], op=ALU.subtract),
                         reads=[b_chain, b_gs[c]], writes=[b_mfin])

        def weights(t0, t1, d):
            gs = [b_gs[t] for t in range(t0, t1)]
            for (dst, a, b_) in [(WG, RR, MX), (DEC, MPRE, MX), (CLP, CUM, MX)]:
                R.op("dve", lambda e, a=a, b_=b_: e.tensor_tensor(out=TMPG[:, t0:t1, d4(d)], in0=a[:, t0:t1, d4(d)], in1=b_[:, t0:t1, d4(d)], op=ALU.subtract),
                     reads=[b_chain] + gs, writes=[b_tmpg])
                R.op("act", lambda e, dst=dst: e.activation(out=dst[:, t0:t1, d4(d)], in_=TMPG[:, t0:t1, d4(d)], func=AF.Exp),
                     reads=[b_tmpg], writes=[b_chain])

        def st_vw(c, d):
            for h in range(4):
                col = d * 4 + h
                R.op("act", lambda e, h=h, col=col: e.activation(out=VW[:, d, h, 0:129], in_=V1[:, c, h, 0:129], func=AF.Identity, scale=WG[:, c, col:col + 1]),
                     reads=[b_v1[c], b_chain], writes=[b_vw[d]], parts=True)

        def st_ktr(c):
            for h in range(4):
                R.op("pe", lambda e, h=h: e.transpose(pTR[:, h * 128:(h + 1) * 128], KT[:, c, h, :], ident_b[:, :]), reads=[b_kt[c], b_const], writes=[b_ptr[0]])
            R.op("act", lambda e: e.activation(out=KTK[:, :, :], in_=pTR[:, 0:512].rearrange("p (a b) -> p a b", b=128), func=AF.Copy), reads=[b_ptr[0]], writes=[b_ktk])

        def st_upd(c, d):
            for h in range(4):
                R.op("pe", lambda e, h=h: e.matmul(up_slot(h), lhsT=KTK[:, h, :], rhs=VW[:, d, h, 0:129], start=True, stop=True),
                     reads=[b_ktk, b_vw[d]], writes=[b_pup[h // 3]])
            for h in range(4):
                col = d * 4 + h
                R.op("dve", lambda e, h=h, col=col: e.scalar_tensor_tensor(out=SST[:, d, h, :], in0=SST[:, d, h, :], scalar=DEC[:, c, col:col + 1], in1=up_slot(h),
                                                                            op0=ALU.mult, op1=ALU.add),
                     reads=[b_sst[d][h], b_chain, b_pup[h // 3]], writes=[b_sst[d][h]])

        pSTb = pST[:, :].bitcast(BF16)

        def state_only_pass(seq):
            cl = seqs[seq]

            def prep(c, par):
                trb = pTR if par == 0 else pSTb
                btr = b_ptr[0] if par == 0 else b_pst
                for h in range(4):
                    R.op("act", lambda e, h=h: e.activation(out=VW[:, par, h, 0:129], in_=V1[:, c, h, 0:129], func=AF.Identity, scale=WG[:, c, h:h + 1]),
                         reads=[b_v1[c], b_chain], writes=[b_vw[par]], parts=True)
                for h in range(4):
                    R.op("pe", lambda e, h=h: e.transpose(trb[:, h * 128:(h + 1) * 128], KT[:, c, h, :], ident_b[:, :]),
                         reads=[b_kt[c], b_const], writes=[btr])
                R.op("act", lambda e: e.activation(out=KTK2[:, par, :, :], in_=trb[:, 0:512].rearrange("p (a b) -> p a b", b=128), func=AF.Copy),
                     reads=[btr], writes=[b_ktk2[par]])

            def upd(c, par):
                def slot(h):
                    return up_slot(h) if par == 0 else nd_slot(h)
                bb = [b_pup[0]] if par == 0 else [b_pnd[0], b_pnd[1]]
                for h in range(4):
                    R.op("pe", lambda e, h=h: e.matmul(slot(h), lhsT=KTK2[:, par, h, :], rhs=VW[:, par, h, 0:129], start=True, stop=True),
                         reads=[b_ktk2[par], b_vw[par]], writes=bb)
                for h in range(4):
                    R.op("act", lambda e, h=h: e.activation(out=SNAP[:, c, h, 0:129], in_=SST[:, 0, h, :], func=AF.Identity, scale=DEC[:, c, h:h + 1]),
                         reads=[b_sst[0][h], b_chain], writes=[b_snap[c]], parts=True)
                for h in range(4):
                    R.op("dve", lambda e, h=h: e.scalar_tensor_tensor(out=SST[:, 0, h, :], in0=SST[:, 0, h, :], scalar=DEC[:, c, h:h + 1], in1=slot(h),
                                                                       op0=ALU.mult, op1=ALU.add),
                         reads=[b_sst[0][h], b_chain] + bb, writes=[b_sst[0][h]])

            prep(cl[0], 0)
            for k, c in enumerate(cl):
                if k + 1 < len(cl):
                    prep(cl[k + 1], (k + 1) % 2)
                upd(c, k % 2)

        def emit_state(seq, d):
            si = {"P0": 0, "P1": 1}[seq]
            for h in range(4):
                cs = nxt("ctr", 2)
                R.op("pe", lambda e, h=h: e.matmul(pWO[:, 0:128], lhsT=SST[:, d, h, 0:128], rhs=ident_f[:, :], start=True, stop=True),
                     reads=[b_sst[d][h], b_const], writes=[b_pwo])
                R.op("act", lambda e, cs=cs: e.activation(out=CTR[:, cs, :], in_=pWO[:, 0:128], func=AF.Copy), reads=[b_pwo], writes=[b_ctr[cs]])
                R.dma("sp", oC_d[si, d, h, :, :], CTR[:, cs, :], reads=[b_ctr[cs]])
                R.dma("sp", on_d[si, d, h, :].rearrange("(p o) -> p o", o=1), SST[:, d, h, 128:129], reads=[b_sst[d][h]])
            R.dma("sp", om_d[si, d, :].rearrange("(o h) -> o h", o=1), MFIN[0:1, si, d4(d)], reads=[b_mfin])

        def init_state(d, src=None):
            for h in range(4):
                if src is None:
                    R.op("dve", lambda e, h=h: e.memset(SST[:, d, h, :], 0.0), writes=[b_sst[d][h]])
                else:
                    ap_of, bufs = src
                    R.op("dve", lambda e, h=h: e.tensor_copy(out=SST[:, d, h, :], in_=ap_of(h)), reads=bufs, writes=[b_sst[d][h]])

        def full_pass(seq):
            typ = 1 if seq == "S" else 0
            cl = seqs[seq][::-1]

            def s1a(c, par):
                R.dma("sp", XT[:, par, :], xin[c * 128:(c + 1) * 128, :], writes=[b_xt[par]])
                for h in range(4):
                    R.op("act", lambda e, h=h: e.activation(out=SDB[:, h, 0:129], in_=SST[:, 1, h, :], func=AF.Identity, scale=DEC[:, c, 4 + h:5 + h]),
                         reads=[b_sst[1][h], b_chain], writes=[b_sdb], parts=True)
                st_vw(c, 0)
                st_vw(c, 1)
                for h in range(4):
                    R.op("pe", lambda e, h=h: e.matmul(pST[:, h * 128:(h + 1) * 128], lhsT=KT[:, c, h, :], rhs=QT[:, c, h, :], start=True, stop=True),
                         reads=[b_kt[c], b_qt[c]], writes=[b_pst])
                st_ktr(c)

            def s1b(c, par):
                for d in range(2):
                    R.op("dve", lambda e, d=d: e.tensor_tensor(out=PT[:, d, :, :], in0=pST[:, :].rearrange("p (a b) -> p a b", b=128),
                                                               in1=masks[d][:, :].unsqueeze(1).to_broadcast([128, 4, 128]), op=ALU.mult),
                         reads=[b_pst, b_const], writes=[b_ptt[d]])
                for d in range(2):
                    for h in range(4):
                        i = d * 4 + h
                        R.op("pe", lambda e, d=d, h=h, i=i: e.matmul(nd_slot(i), lhsT=PT[:, d, h, :], rhs=VW[:, d, h, 0:129], start=True, stop=False),
                             reads=[b_ptt[d], b_vw[d]], writes=[b_pnd[i // 3]])
                        if d == 0:
                            R.op("pe", lambda e, h=h, i=i: e.matmul(nd_slot(i), lhsT=QT[:, c, h, :], rhs=SNAP[:, c, h, 0:129], start=False, stop=True),
                                 reads=[b_qt[c], b_snap[c]], writes=[b_pnd[i // 3]])
                        else:
                            R.op("pe", lambda e, h=h, i=i: e.matmul(nd_slot(i), lhsT=QT[:, c, h, :], rhs=SDB[:, h, 0:129], start=False, stop=True),
                                 reads=[b_qt[c], b_sdb], writes=[b_pnd[i // 3]])
                st_upd(c, 1)

            def s1c(c, par):
                R.op("act", lambda e: e.activation(out=DN[:, 0:6].rearrange("p (b t) -> p b t", t=3), in_=pND[:, 0:2, 0:387].rearrange("p b (t w) -> p b t w", w=129)[:, :, :, 128],
                                                   func=AF.Abs),
                     reads=b_pnd, writes=[b_dn])
                R.op("act", lambda e: e.activation(out=DN[:, 6:8], in_=pND[:, 2, 0:258].rearrange("p (t w) -> p t w", w=129)[:, :, 128], func=AF.Abs),
                     reads=b_pnd, writes=[b_dn])
                R.op("dve", lambda e: e.tensor_tensor(out=DN[:, 0:8], in0=DN[:, 0:8], in1=CLP[:, c, :], op=ALU.max), reads=[b_dn, b_chain], writes=[b_dn])
                R.op("dve", lambda e: e.reciprocal(out=DN[:, 0:8], in_=DN[:, 0:8]), reads=[b_dn], writes=[b_dn])
                for h in range(4):
                    R.op("dve", lambda e, h=h: e.tensor_scalar(out=HACC[:, par, h * 128:(h + 1) * 128], in0=nd_slot(h)[:, 0:128], scalar1=DN[:, h:h + 1], scalar2=None, op0=ALU.mult),
                         reads=[b_pnd[h // 3], b_dn], writes=[b_hacc[par][h]])
                for h in range(4):
                    i = 4 + h
                    R.op("dve", lambda e, h=h, i=i: e.scalar_tensor_tensor(out=HACC[:, par, h * 128:(h + 1) * 128], in0=nd_slot(i)[:, 0:128], scalar=DN[:, i:i + 1],
                                                                            in1=HACC[:, par, h * 128:(h + 1) * 128], op0=ALU.mult, op1=ALU.add),
                         reads=[b_pnd[i // 3], b_dn, b_hacc[par][h]], writes=[b_hacc[par][h]])

            def hg0(c):
                R.op("pool", lambda e: e.tensor_tensor(out=HG0[:, :], in0=OG[:, c, :], in1=MHN[:, :], op=ALU.mult), reads=[b_og[c], b_mhn], writes=[b_hg0])

            def s2a(c, par):
                for h in range(4):
                    R.op("act", lambda e, h=h: e.activation(out=SQ4[:, h, :], in_=HACC[:, par, h * 128:(h + 1) * 128], func=AF.Square, accum_out=SS4[:, par, h:h + 1]),
                         reads=[b_hacc[par][h]], writes=[b_sq4[h], b_ss4[par]], parts=True)
                R.op("act", lambda e: e.activation(out=SS4[:, par, :], in_=SS4[:, par, :], func=AF.Sqrt, bias=EPS, scale=1.0 / DH),
                     reads=[b_ss4[par]], writes=[b_ss4[par]])
                R.op("dve", lambda e: e.reciprocal(out=SS4[:, par, :], in_=SS4[:, par, :]), reads=[b_ss4[par]], writes=[b_ss4[par]])
                for h in range(4):
                    R.op("dve", lambda e, h=h: e.scalar_tensor_tensor(out=HG[:, h * 128:(h + 1) * 128], in0=HACC[:, par, h * 128:(h + 1) * 128], scalar=SS4[:, par, h:h + 1],
                                                                       in1=HG0[:, h * 128:(h + 1) * 128], op0=ALU.mult, op1=ALU.mult),
                         reads=[b_hacc[par][h], b_ss4[par], b_hg0], writes=[b_hg], parts=True)

            def s2b(c, par):
                for h in range(4):
                    R.op("pe", lambda e, h=h: e.transpose(pTR[:, 512 + h * 128:512 + (h + 1) * 128], HG[:, h * 128:(h + 1) * 128], ident_b[:, :]),
                         reads=[b_hg, b_const], writes=[b_ptr[1]])
                R.op("act", lambda e: e.activation(out=HMT[:, :, :], in_=pTR[:, 512:1024].rearrange("p (a b) -> p a b", b=128), func=AF.Copy),
                     reads=[b_ptr[1]], writes=[b_hmt])
                wo_half(c, par, 0)

            def wo_half(c, par, half):
                for kb in range(8):
                    if kb < 4:
                        R.op("pe", lambda e, kb=kb: e.matmul(pWO[:, :], lhsT=YSC[:, c, kb, :], rhs=WOUT[:, kb, half * 512:(half + 1) * 512], start=(kb == 0), stop=False),
                             reads=[b_ysc[c], b_wout], writes=[b_pwo])
                    else:
                        R.op("pe", lambda e, kb=kb: e.matmul(pWO[:, :], lhsT=HMT[:, kb - 4, :], rhs=WOUT[:, kb, half * 512:(half + 1) * 512], start=False, stop=(kb == 7)),
                             reads=[b_hmt, b_wout], writes=[b_pwo])
                R.op("dve", lambda e: e.tensor_tensor(out=X1[:, half * 512:(half + 1) * 512], in0=pWO[:, :], in1=G1ROW[:, typ, half * 512:(half + 1) * 512], op=ALU.mult),
                     reads=[b_pwo, b_g1], writes=[b_x1h[half]])
                R.op("pool", lambda e: e.tensor_tensor(out=XT[:, par, half * 512:(half + 1) * 512], in0=XT[:, par, half * 512:(half + 1) * 512],
                                                       in1=X1[:, half * 512:(half + 1) * 512], op=ALU.add),
                     reads=[b_x1h[half], b_xt[par]], writes=[b_xt[par]])

            def s2c(c, par):
                wo_half(c, par, 1)
                R.dma("sp", x1s_d[c * 128:(c + 1) * 128, :], XT[:, par, :], reads=[b_xt[par]], writes=[b_x1s[c]])
                if c == NT - 1:
                    R.dma("sp", hx_in[:, :], XT[64:128, par, :], reads=[b_xt[par]], writes=[b_hxin])
                    hx_out, b_hxout, hx_scr, b_hs = L["hx_out"], L["b_hxout"], L["hx_scr"], L["b_hs"]
                    R.collective(lambda e: e.collective_compute("AllGather", ALU.bypass, replica_groups=RG, ins=[hx_in.ap().opt()], outs=[hx_out.ap().opt()]),
                                 reads=[b_hxin], writes=[b_hxout])

            prev = None
            for k, c in enumerate(cl):
                par = k % 2
                s1a(c, par)
                if prev is not None:
                    s2a(*prev)
                s1b(c, par)
                if prev is not None:
                    s2b(*prev)
                s1c(c, par)
                if prev is not None:
                    s2c(*prev)
                hg0(c)
                prev = (c, par)
            s2a(*prev)
            s2b(*prev)
            s2c(*prev)
            if seq == "S":
                hx_out, b_hxout, hx_scr, b_hs = L["hx_out"], L["b_hxout"], L["hx_scr"], L["b_hs"]
                for hf in range(2):
                    cs_ = slice(hf * 512, (hf + 1) * 512)
                    R.dma("pool", SXG[0:64, 0, 0:512], hx_out[0:64, cs_], reads=[b_hxout], writes=[b_sxg])
                    R.dma("pool", SXG[0:64, 1, 0:512], hx_out[64:128, cs_], reads=[b_hxout], writes=[b_sxg])
                    R.op("dve", lambda e: e.tensor_scalar(out=SXG[0:64, 0, 0:512], in0=SXG[0:64, 0, 0:512], scalar1=sel[0:64, 1:2], scalar2=None, op0=ALU.mult),
                         reads=[b_sxg, b_const], writes=[b_sxg])
                    R.op("dve", lambda e: e.tensor_scalar(out=SXG[0:64, 1, 0:512], in0=SXG[0:64, 1, 0:512], scalar1=sel[0:64, 0:1], scalar2=None, op0=ALU.mult),
                         reads=[b_sxg, b_const], writes=[b_sxg])
                    R.op("dve", lambda e: e.tensor_tensor(out=SXG[0:64, 0, 0:512], in0=SXG[0:64, 0, 0:512], in1=SXG[0:64, 1, 0:512], op=ALU.add),
                         reads=[b_sxg], writes=[b_sxg])
                    R.dma("pool", hx_scr[:, cs_], SXG[0:64, 0, 0:512], reads=[b_sxg], writes=[b_hs])

        chain("P0", 0, None)
        chain("P1", 0, None)
        R.dma("sp", SXT[:, 0:4], m0_d, writes=[b_sxt])
        chain("S", 0, (SXT[:, 0:4], [b_sxt]))
        weights(0, NT, 0)
        R.dma("sp", SXG[:, 0, 0:516].rearrange("p (h v) -> p h v", v=129), s0_d, writes=[b_sxs])
        init_state(0, (lambda h: SXG[:, 0, h * 129:(h + 1) * 129], [b_sxs]))
        state_only_pass("S")
        for h in range(4):
            R.op("dve", lambda e, h=h: e.tensor_copy(out=SXG[:, 0, h * 129:(h + 1) * 129], in_=SST[:, 0, h, :]), reads=[b_sst[0][h]], writes=[b_sxs])
        R.op("dve", lambda e: e.tensor_copy(out=SXG[:, 0, 516:520], in_=MFIN[:, 2, 0:4]), reads=[b_mfin], writes=[b_sxs])
        R.dma("pool", sx_in[:, :], SXG[:, 0, :], reads=[b_sxs], writes=[b_sxin])
        R.collective(lambda e: e.collective_compute("AllGather", ALU.bypass, replica_groups=RG, ins=[sx_in.ap().opt()], outs=[sx_out.ap().opt()]),
                     reads=[b_sxin], writes=[b_sxout])
        chain("P0", 1, None)
        chain("P1", 1, None)
        weights(0, 4, 1)
        for seq in ["P0", "P1"]:
            init_state(0)
            state_only_pass(seq)
            emit_state(seq, 0)
            init_state(1)
            full_pass(seq)
            emit_state(seq, 1)
        R.dma("pool", SXG[:, :, :], sx_out.ap().rearrange("(r p) n -> p r n", p=128), reads=[b_sxout], writes=[b_sxg])
        R.op("dve", lambda e: e.tensor_scalar(out=SXT[:, :], in0=SXG[:, 0, :], scalar1=sel[:, 1:2], scalar2=None, op0=ALU.mult), reads=[b_sxg, b_const], writes=[b_sxt])
        R.op("dve", lambda e: e.scalar_tensor_tensor(out=SXT[:, :], in0=SXG[:, 1, :], scalar=sel[:, 0:1], in1=SXT[:, :], op0=ALU.mult, op1=ALU.add),
             reads=[b_sxg, b_sxt, b_const], writes=[b_sxt])
        chain("S", 1, (SXT[:, 516:520], [b_sxt]))
        weights(4, NT, 1)
        init_state(1, (lambda h: SXT[:, h * 129:(h + 1) * 129], [b_sxt]))
        full_pass("S")
        dump("MX", MX[:], [128, NT, 8], b_chain)
        dump("WG", WG[:], [128, NT, 8], b_chain)
        dump("SNAP", SNAP[:], [128, NT, 4, 130], b_snap[0], BF16)
        R.flush()


def ffn_up(nc, R, L, _sc, seg, H2T, b_h2t, ACTS, b_acts, n_own, n_all, ntile, nbank, side=None):
    cff, b_const, wup_d = L["cff"], L["b_const"], L["wup_d"]
    U = _sc.sb([128, 2, n_all], F32)
    ACA = _sc.sb([128, 2, n_own], F32)
    ACG = _sc.sb([128, 2, n_own], F32)
    WU = _sc.sb([128, 3, 8, 256], BF16)
    fp = [_sc.ps([128, 512], F32) for _ in range(nbank)]
    b_fp = [Buf() for _ in range(nbank)]
    b_wu = [Buf() for _ in range(3)]
    b_u = [Buf(), Buf()]
    b_aca = [Buf(), Buf()]
    b_acg = [Buf(), Buf()]
    groups = [(g0, min(512, n_all - g0)) for g0 in range(0, n_all, 512)]
    pcount = 0

    def finish(j):
        g2 = j % 2
        R.op("act", lambda e: e.activation(out=ACG[:, g2, :], in_=ACG[:, g2, :], func=AF.Silu), reads=[b_acg[g2]], writes=[b_acg[g2]])
        R.op("pool", lambda e: e.tensor_tensor(out=ACTS[:, j, :], in0=ACG[:, g2, :], in1=ACA[:, g2, :], op=ALU.mult),
             reads=[b_acg[g2], b_aca[g2]], writes=[b_acts[j]])

    for j in range(min(2, NJ)):
        R.dma("pool", WU[:, j % 3, :, :], wup_d[j], writes=[b_wu[j % 3]])
    for j in range(NJ):
        s = j % 3
        if j + 2 < NJ:
            R.dma("pool", WU[:, (j + 2) % 3, :, :], wup_d[j + 2], writes=[b_wu[(j + 2) % 3]])
        a2 = j % 2
        for part in range(2):
            blk = j + part * NJ
            us = part
            acc = ACA[:, a2, :] if part == 0 else ACG[:, a2, :]
            b_acc = b_aca[a2] if part == 0 else b_acg[a2]
            for (g0, gn) in groups:
                pb = pcount % nbank
                pcount += 1
                hb = [b_h2t[min(ti, ntile)] for ti in range(g0 // 128, (g0 + gn + 127) // 128)]
                for kc in range(8):
                    R.op("pe", lambda e, kc=kc, g0=g0, gn=gn, pb=pb, s=s, part=part: e.matmul(fp[pb][:, 0:gn], lhsT=WU[:, s, kc, part * 128:(part + 1) * 128],
                                                                                         rhs=H2T[:, kc, g0:g0 + gn], start=(kc == 0), stop=(kc == 7)),
                         reads=[b_wu[s]] + hb, writes=[b_fp[pb]])
                R.op("act", lambda e, g0=g0, gn=gn, pb=pb, us=us: e.activation(out=U[:, us, g0:g0 + gn], in_=fp[pb][:, 0:gn], func=AF.Copy),
                     reads=[b_fp[pb]], writes=[b_u[us]], parts=True)
                if g0 < n_own:
                    on = min(gn, n_own - g0)
                    R.op("act", lambda e, g0=g0, on=on, pb=pb, acc=acc, blk=blk: e.activation(out=acc[:, g0:g0 + on], in_=fp[pb][:, 0:on], func=AF.Identity,
                                                                                         scale=cff[:, blk, 1:2], bias=cff[:, blk, 3:4]),
                         reads=[b_fp[pb], b_const], writes=[b_acc], parts=True)
            if seg == "prompt":
                def v3(ap):
                    return ap.rearrange("p (r w) -> p r w", w=256)
                R.op("dve", lambda e, acc=acc, us=us, blk=blk, v3=v3: e.scalar_tensor_tensor(out=v3(acc)[:, :, 1:256], in0=v3(U[:, us, 0:512])[:, :, 0:255], scalar=cff[:, blk, 0:1],
                                                                                        in1=v3(acc)[:, :, 1:256], op0=ALU.mult, op1=ALU.add),
                     reads=[b_u[us], b_acc, b_const], writes=[b_acc])
                R.op("dve", lambda e, acc=acc, us=us, blk=blk, v3=v3: e.scalar_tensor_tensor(out=v3(acc)[:, :, 0:255], in0=v3(U[:, us, 0:512])[:, :, 1:256], scalar=cff[:, blk, 2:3],
                                                                                        in1=v3(acc)[:, :, 0:255], op0=ALU.mult, op1=ALU.add),
                     reads=[b_u[us], b_acc, b_const], writes=[b_acc])
            else:
                R.op("dve", lambda e, acc=acc, us=us, blk=blk: e.scalar_tensor_tensor(out=acc[:, 64:2048], in0=U[:, us, 0:1984], scalar=cff[:, blk, 0:1],
                                                                                 in1=acc[:, 64:2048], op0=ALU.mult, op1=ALU.add),
                     reads=[b_u[us], b_acc, b_const], writes=[b_acc])
                R.op("dve", lambda e, acc=acc, us=us, blk=blk: e.scalar_tensor_tensor(out=acc[:, 0:2048], in0=U[:, us, 64:2112], scalar=cff[:, blk, 2:3],
                                                                                 in1=acc[:, 0:2048], op0=ALU.mult, op1=ALU.add),
                     reads=[b_u[us], b_acc, b_const], writes=[b_acc])
        if j >= 1:
            finish(j - 1)
        if side is not None:
            side(j)
    finish(NJ - 1)


def ffn_down_consts(nc, R, L, _sc, typ):
    wdn_d, fn_d, gate_d, b_gate_d = L["wdn_d"], L["fn_d"], L["gate_d"], L["b_gate_d"]
    WD = _sc.sb([128, NJ, D], BF16)
    FNR = _sc.sb([128, D], F32)
    G2ROW = _sc.sb([128, D], F32)
    b_wd = [Buf() for _ in range(NJ // 2)]
    b_fnr, b_g2 = Buf(), Buf()
    for jj in range(0, NJ, 2):
        R.dma("pool", WD[:, jj:jj + 2, :], wdn_d[:, jj:jj + 2, :], writes=[b_wd[jj // 2]])
    R.dma("sp", FNR[:], fn_d, writes=[b_fnr])
    R.dma("sp", G2ROW[:], gate_d[typ:typ + 1, 1024:2048].to_broadcast([128, 1024]), reads=[b_gate_d], writes=[b_g2])
    return WD, FNR, G2ROW, b_wd, b_fnr, b_g2


def ffn_down(nc, R, L, _sc, tiles, ACTS, b_acts, dc):
    x1s_d, b_x1s, y_d = L["x1s_d"], L["b_x1s"], L["y_d"]
    WD, FNR, G2ROW, b_wd, b_fnr, b_g2 = dc
    X1L = _sc.sb([128, 2, D], F32)
    X2 = _sc.sb([128, 2, D], F32)
    YT = _sc.sb([128, 2, D], F32)
    SQ2 = _sc.sb([128, D], BF16)
    SSD = _sc.sb([128, 4], F32)
    dp = [_sc.ps([128, 512], F32) for _ in range(4)]
    b_dp = [Buf() for _ in range(4)]
    b_sq2 = Buf()
    b_x1l, b_x2, b_yt = [Buf(), Buf()], [Buf(), Buf()], [Buf(), Buf()]
    b_ssd = [Buf() for _ in range(4)]
    pc = 0
    for i, t in enumerate(tiles):
        hs = i % 2
        q = i % 4
        R.dma("sp", X1L[:, hs, :], x1s_d[t * 128:(t + 1) * 128, :], reads=[b_x1s[t]], writes=[b_x1l[hs]])
        for half in range(2):
            pb = pc % 4
            pc += 1
            for j in range(NJ):
                R.op("pe", lambda e, j=j, i=i, half=half, pb=pb: e.matmul(dp[pb][:, :], lhsT=ACTS[:, j, i * 128:(i + 1) * 128], rhs=WD[:, j, half * 512:(half + 1) * 512],
                                                                        start=(j == 0), stop=(j == NJ - 1)),
                     reads=[b_acts[j], b_wd[j // 2]], writes=[b_dp[pb]])
            R.op("dve", lambda e, half=half, pb=pb, hs=hs: e.tensor_tensor(out=X2[:, hs, half * 512:(half + 1) * 512], in0=dp[pb][:, :], in1=G2ROW[:, half * 512:(half + 1) * 512], op=ALU.mult),
                 reads=[b_dp[pb], b_g2], writes=[b_x2[hs]], parts=True)
            R.op("dve", lambda e, half=half, hs=hs: e.tensor_tensor(out=X2[:, hs, half * 512:(half + 1) * 512], in0=X2[:, hs, half * 512:(half + 1) * 512],
                                                                  in1=X1L[:, hs, half * 512:(half + 1) * 512], op=ALU.add),
                 reads=[b_x2[hs], b_x1l[hs]], writes=[b_x2[hs]])
        R.op("act", lambda e, hs=hs, q=q: e.activation(out=SQ2[:, :], in_=X2[:, hs, :], func=AF.Square, accum_out=SSD[:, q:q + 1]),
             reads=[b_x2[hs]], writes=[b_sq2, b_ssd[q]])
        R.op("act", lambda e, q=q: e.activation(out=SSD[:, q:q + 1], in_=SSD[:, q:q + 1], func=AF.Sqrt, bias=EPS, scale=1.0 / D),
             reads=[b_ssd[q]], writes=[b_ssd[q]])
        R.op("dve", lambda e, q=q: e.reciprocal(out=SSD[:, q:q + 1], in_=SSD[:, q:q + 1]),
             reads=[b_ssd[q]], writes=[b_ssd[q]])
        R.op("dve", lambda e, hs=hs, q=q: e.scalar_tensor_tensor(out=YT[:, hs, :], in0=X2[:, hs, :], scalar=SSD[:, q:q + 1], in1=FNR[:, :], op0=ALU.mult, op1=ALU.mult),
             reads=[b_x2[hs], b_ssd[q], b_fnr], writes=[b_yt[hs]])
        R.dma("sp", y_d[t * 128:(t + 1) * 128, :], YT[:, hs, :], reads=[b_yt[hs]])


def ffn_all(nc, R, L):
    A2, B2, b_mod, ident_b, anti_b, b_const, sel = (L[k] for k in ["A2", "B2", "b_mod", "ident_b", "anti_b", "b_const", "sel"])
    x1s_d, b_x1s = L["x1s_d"], L["b_x1s"]
    hx_in, hx_out, b_hxin, b_hxout, RG = L["hx_in"], L["hx_out"], L["b_hxin"], L["b_hxout"], L["RG"]
    dump = L["dump"]
    p_tiles = [0, 1, 2, 3]
    s_tiles = list(range(4, NT))
    NS = len(s_tiles)
    hx_scr, b_hs = L["hx_scr"], L["b_hs"]
    with Scope(nc) as _so:
        H2Ts = _so.sb([128, 8, 2112], BF16)
        b_h2ts = [Buf() for _ in range(NS + 1)]
        with Scope(nc) as _sp:
            ACTSp = _sp.sb([128, NJ, 512], BF16)
            b_actsp = [Buf() for _ in range(NJ)]
            H2Tp = _sp.sb([128, 8, 512], BF16)
            b_h2tp = [Buf() for _ in range(5)]
            dcp = ffn_down_consts(nc, R, L, _sp, 0)
            with Scope(nc) as _s1:
                build_hT(nc, R, H2Tp, b_h2tp, p_tiles, lambda t: x1s_d[t * 128:(t + 1) * 128, :], [b_x1s[t] for t in p_tiles], A2, B2, b_mod, ident_b, b_const,
                         lambda t: 0, [128] * 4)
            with Scope(nc) as _s2:
                s1, s2, nb = make_builder(nc, R, _s2, H2Ts, b_h2ts, s_tiles, lambda t: x1s_d[t * 128:(t + 1) * 128, :], [b_x1s[t] for t in s_tiles],
                                          A2, B2, b_mod, ident_b, b_const, lambda t: 1, [128] * NS, col_of=lambda t: (t - 4) * 128)
                state = {"i": 0}
                s1(0)

                def side(j):
                    i = state["i"]
                    if i < nb:
                        if i + 1 < nb:
                            s1(i + 1)
                        s2(i)
                        state["i"] = i + 1

                ffn_up(nc, R, L, _s2, "prompt", H2Tp, b_h2tp, ACTSp, b_actsp, 512, 512, 4, 4, side=side)
                while state["i"] < nb:
                    side(0)
                R.flush()
            with Scope(nc) as _s3:
                ffn_down(nc, R, L, _s3, p_tiles, ACTSp, b_actsp, dcp)
                R.flush()
        build_hT(nc, R, H2Ts, [b_h2ts[NS]], [0], lambda t: hx_scr, [b_hs], A2, B2, b_mod, ident_b, b_const,
                 lambda t: 1, [64], col_of=lambda t: 2048, anti=anti_b)
        dump("H2T_sample", H2Ts[:], [128, 8, 2112], b_h2ts[0], BF16)
        with Scope(nc) as _ss:
            ACTSs = _ss.sb([128, NJ, 2048], BF16)
            b_actss = [Buf() for _ in range(NJ)]
            with Scope(nc) as _s4:
                ffn_up(nc, R, L, _s4, "sample", H2Ts, b_h2ts, ACTSs, b_actss, 2048, 2112, NS, 8)
                R.flush()
            dump("ACTS_sample", ACTSs[:], [128, NJ, 2048], b_actss[0], BF16)
            with Scope(nc) as _s5:
                dcs = ffn_down_consts(nc, R, L, _s5, 1)
                ffn_down(nc, R, L, _s5, s_tiles, ACTSs, b_actss, dcs)
                R.flush()


_CACHE = {}


def _consts():
    i = np.arange(128)
    ident = np.eye(128, dtype=np.float32)
    anti = np.zeros((128, 128), np.float32)
    anti[np.arange(64), 63 - np.arange(64)] = 1.0
    mk0 = (i[:, None] <= i[None, :]).astype(np.float32)
    mk1 = (i[:, None] >= i[None, :]).astype(np.float32)
    selj = np.zeros((2, 256), np.float32)
    selj[0, 0:128] = 1.0
    selj[1, 128:256] = 1.0
    return dict(ident=ident, antiI=anti, maskT0=mk0, maskT1=mk1, ones=np.ones((128, 128), np.float32),
                selj=selj, i2=np.eye(2, dtype=np.float32), i8=np.eye(8, dtype=np.float32))


def _kmajor(w):
    return np.ascontiguousarray(w.reshape(8, 128, -1).transpose(1, 0, 2))


def make_in_maps(x_prompt, x_sample, state_C, state_n, state_m, c, c_ctx, w_mod, b_mod, norm1, w_in, b_gate,
                 conv_sc_w, conv_sc_b, mh_norm, w_out, norm2, w_up, conv_ffn_w, conv_ffn_b, w_down, final_norm):
    f = np.float32
    A = lambda a: np.asarray(a, dtype=f)
    x_prompt, x_sample, state_C, state_n, state_m, c, c_ctx = map(A, (x_prompt, x_sample, state_C, state_n, state_m, c, c_ctx))
    w_mod, b_mod, norm1, w_in, b_gate = A(w_mod)[0], A(b_mod)[0], A(norm1)[0], A(w_in)[0], A(b_gate)[0]
    conv_sc_w, conv_sc_b, mh_norm, w_out, norm2 = A(conv_sc_w)[0], A(conv_sc_b)[0], A(mh_norm)[0], A(w_out)[0], A(norm2)[0]
    w_up, conv_ffn_w, conv_ffn_b, w_down, final_norm = A(w_up)[0], A(conv_ffn_w)[0], A(conv_ffn_b)[0], A(w_down)[0], A(final_norm)
    cst = _consts()
    shared = dict(cst)
    shared["w_mod_r"] = np.ascontiguousarray(_kmajor(w_mod).reshape(128, 8, 12, 512).transpose(2, 0, 1, 3))
    shared["b_mod2"] = np.ascontiguousarray(np.broadcast_to(b_mod[None, :], (2, 6144)))
    shared["norm1c"] = np.ascontiguousarray(norm1.reshape(8, 128).T)
    shared["norm2c"] = np.ascontiguousarray(norm2.reshape(8, 128).T)
    wk = _kmajor(w_in)
    shared["wic"] = np.ascontiguousarray(np.stack([np.concatenate([wk[:, :, cb * 128:(cb + 1) * 128], wk[:, :, 512 + cb * 128:512 + (cb + 1) * 128],
                                                                   wk[:, :, 1024 + cb * 128:1024 + (cb + 1) * 128]], axis=2) for cb in range(4)]))
    shared["wiqk"] = np.ascontiguousarray(np.stack([wk[:, :, 1536 + b * 128:1536 + (b + 1) * 128] for b in range(8)]))
    shared["wiv"] = np.ascontiguousarray(wk[:, :, 2560:3072])
    shared["wio"] = np.ascontiguousarray(wk[:, :, 3072:3584])
    shared["mhn_row"] = np.ascontiguousarray(np.broadcast_to(mh_norm[None, :], (128, 512)))
    shared["w_out_r"] = _kmajor(w_out)
    wu = _kmajor(w_up)
    shared["w_up_r"] = np.ascontiguousarray(np.stack([np.concatenate([wu[:, :, j * 128:(j + 1) * 128], wu[:, :, DFF + j * 128:DFF + (j + 1) * 128]], axis=2)
                                                      for j in range(NJ)]))
    shared["w_down_r"] = np.ascontiguousarray(w_down.reshape(NJ, 128, D).transpose(1, 0, 2))
    shared["fn_row"] = np.ascontiguousarray(np.broadcast_to(final_norm[None, :], (128, D)))
    maps = []
    for core in range(8):
        par = core % 2
        b = core // 2
        m = dict(shared)
        ps = [x_prompt[2 * core], x_prompt[2 * core + 1]]
        xs = x_sample[b, 0:2048] if par == 0 else x_sample[b, 2048:4096]
        if par == 1:
            ps = [p[::-1] for p in ps]
            xs = xs[::-1]
        m["xin"] = np.ascontiguousarray(np.concatenate(ps + [xs], axis=0))
        cv = np.stack([c_ctx, c[b]], axis=1)
        m["cT"] = np.ascontiguousarray(cv.reshape(8, 128, 2).transpose(1, 0, 2))
        gperm = [0, 1, 2, 3] if par == 0 else [2, 3, 0, 1]
        wg = wk[:, :, 3584:3600].reshape(128, 8, 4, 4)[:, :, gperm, :].reshape(128, 8, 16)
        m["wig"] = np.ascontiguousarray(wg)
        bg = b_gate[gperm, :].reshape(16)
        m["bgate_row"] = np.ascontiguousarray(np.broadcast_to(bg[None, :], (128, 16)))
        tap = [0, 1, 2] if par == 0 else [2, 1, 0]
        csc = np.stack([conv_sc_w[tap[0]], conv_sc_w[tap[1]], conv_sc_w[tap[2]], conv_sc_b], axis=1)
        m["conv_sc"] = np.ascontiguousarray(csc.reshape(4, 128, 4).transpose(1, 0, 2))
        cf = np.stack([conv_ffn_w[tap[0]], conv_ffn_w[tap[1]], conv_ffn_w[tap[2]], conv_ffn_b], axis=1)
        m["conv_ffn"] = np.ascontiguousarray(cf.reshape(44, 128, 4).transpose(1, 0, 2))
        dsel = par
        C0 = state_C[b, 0, dsel]
        n0 = state_n[b, 0, dsel]
        s0 = np.concatenate([C0.transpose(2, 0, 1), n0.T[:, :, None]], axis=2)
        m["s0"] = np.ascontiguousarray(s0)
        m["m0"] = np.ascontiguousarray(np.broadcast_to(state_m[b, 0, dsel][None, :], (128, 4)))
        sel = np.zeros((128, 2), f)
        sel[:, par] = 1.0
        m["sel"] = sel
        maps.append(m)
    return maps


def assemble(results):
    f = np.float32
    y_prompt = np.zeros((16, 256, D), f)
    y_sample = np.zeros((4, 4096, D), f)
    new_C = np.zeros((16, 1, 2, 4, 128, 128), f)
    new_n = np.zeros((16, 1, 2, 4, 128), f)
    new_m = np.zeros((16, 1, 2, 4), f)
    for core in range(8):
        r = results[core]
        par = core % 2
        b = core // 2
        y = np.asarray(r["y"], dtype=f)
        for i in range(2):
            yp = y[i * 256:(i + 1) * 256]
            y_prompt[2 * core + i] = yp[::-1] if par else yp
            for dl in range(2):
                dg = dl if par == 0 else 1 - dl
                new_C[2 * core + i, 0, dg] = r["oC"][i, dl]
                new_n[2 * core + i, 0, dg] = r["on"][i, dl]
                new_m[2 * core + i, 0, dg] = r["om"][i, dl]
        ys = y[512:2560]
        if par == 0:
            y_sample[b, 0:2048] = ys
        else:
            y_sample[b, 2048:4096] = ys[::-1]
    return (y_prompt, y_sample, new_C, new_n, new_m)


def kernel(**inputs):
    maps = make_in_maps(**inputs)
    if "nc" not in _CACHE:
        _CACHE["nc"] = build_program()[0]
    res = run_bass_kernel_spmd(_CACHE["nc"], maps, core_ids=list(range(8)))
    return assemble(res.results)
```
